# Optimizing a Trainium2 kernel written in Bass

```python
import math
import jax, jax.numpy as jnp
from jax import lax
import numpy as np

D_MODEL = 1024
BATCH = 16
SEQ = 4096
DEPTH = 2
DEC_BATCH = 4
DEC_SEQ = 8192
PAST_LEN = 128

GRID_W = 64
HEAD_DIM = 64
N_BRANCH = 3
D_ATTN = 512
N_HEADS_ATTN = D_ATTN // HEAD_DIM
WIN_H_MAX = 8
WIN_W = 16
Q_COL_BLOCK = 16
K_COL_BLOCK = 32
N_COL_BLOCKS = GRID_W // Q_COL_BLOCK
D_SSM = 1024
N_HEADS_SSM = D_SSM // HEAD_DIM
SSM_GROUPS = 2
HEADS_PER_GROUP = N_HEADS_SSM // SSM_GROUPS
SSM_STATE = 128
SSM_CONV = 5
SSM_CHUNK = 128
SSM_CONV_DIM = D_SSM + 2 * SSM_GROUPS * SSM_STATE
D_RWKV = 512
N_HEADS_RWKV = D_RWKV // HEAD_DIM
LORA_DECAY = 64
LORA_ICLR = 64
LORA_GATE = 160
RWKV_PROJ = 3 * D_RWKV + 2 * LORA_DECAY + 2 * LORA_ICLR + LORA_GATE
MIX_W = D_ATTN + D_SSM + D_RWKV
IN_W = 3 * D_ATTN + D_SSM + SSM_CONV_DIM + 2 * N_HEADS_SSM + RWKV_PROJ + N_BRANCH * D_MODEL
FF_DIM = 2816
RMS_EPS = 1e-6
GN_EPS = 64e-5

kernel_name = "hybrid_natten_ssd_rwkv7_encoder"


def rms_norm(x, g):
    xf = x.astype(jnp.float32)
    y = xf * lax.rsqrt(jnp.mean(xf * xf, axis=-1, keepdims=True) + RMS_EPS)
    return (y * g.astype(jnp.float32)).astype(x.dtype)


def swiglu_ffn(x, w_in, w_out):
    gate, up = jnp.split(x @ w_in, 2, axis=-1)
    return (jax.nn.silu(gate) * up) @ w_out


def neighbourhood_attention(q, k, v, rpb):
    b, t, _ = q.shape
    rows = t // GRID_W
    kh = min(WIN_H_MAX, rows)
    h, d = N_HEADS_ATTN, HEAD_DIM
    qg = q.reshape(b, rows, N_COL_BLOCKS, Q_COL_BLOCK, h, d) * (d ** -0.5)
    q_cols = np.arange(GRID_W).reshape(N_COL_BLOCKS, Q_COL_BLOCK)
    k_start = np.clip(q_cols[:, 0] - WIN_W // 2, 0, GRID_W - K_COL_BLOCK)
    k_cols = k_start[:, None] + np.arange(K_COL_BLOCK)
    w_start = np.clip(q_cols - WIN_W // 2, 0, GRID_W - WIN_W)
    kc = k_cols[:, None, :]
    col_mask = (kc >= w_start[:, :, None]) & (kc < w_start[:, :, None] + WIN_W)
    dc_idx = np.clip(kc - q_cols[:, :, None] + WIN_W - 1, 0, 2 * WIN_W - 2)
    col_bias = rpb.astype(jnp.float32)[:, :, dc_idx]
    mask_b = col_mask[:, :, None, :]
    kg = k.reshape(b, rows, GRID_W, h, d)[:, :, k_cols]
    vg = v.reshape(b, rows, GRID_W, h, d)[:, :, k_cols]

    def row_block(r):
        rs = jnp.clip(r - kh // 2, 0, rows - kh)
        k_blk = lax.dynamic_slice_in_dim(kg, rs, kh, axis=1)
        v_blk = lax.dynamic_slice_in_dim(vg, rs, kh, axis=1)
        q_r = lax.dynamic_index_in_dim(qg, r, axis=1, keepdims=False)
        s = jnp.einsum('bjqhd,brjkhd->bhjqrk', q_r, k_blk).astype(jnp.float32)
        dr_idx = rs + jnp.arange(kh) - r + WIN_H_MAX - 1
        bias = jnp.take(col_bias, dr_idx, axis=1).transpose(0, 2, 3, 1, 4)
        s = jnp.where(mask_b, s + bias[None], -jnp.inf)
        p = jax.nn.softmax(s, axis=(-2, -1)).astype(v.dtype)
        return jnp.einsum('bhjqrk,brjkhd->bjqhd', p, v_blk)

    out = lax.map(row_block, jnp.arange(rows))
    return out.transpose(1, 0, 2, 3, 4, 5).reshape(b, t, h * d)


def centred_depthwise_conv(x, w, bias):
    c = x.shape[-1]
    pad = w.shape[0] // 2
    y = lax.conv_general_dilated(x, w[:, None, :].astype(x.dtype), window_strides=(1,),
                                 padding=[(pad, pad)], dimension_numbers=('NWC', 'WIO', 'NWC'),
                                 feature_group_count=c)
    return y + bias


def ssd_chunked(x, log_a, bm, cm):
    b, t, g, e, p = x.shape
    n = bm.shape[-1]
    L = SSM_CHUNK
    nc = t // L
    x = x.reshape(b, nc, L, g, e, p)
    bm = bm.reshape(b, nc, L, g, n)
    cm = cm.reshape(b, nc, L, g, n)
    a_cs = jnp.cumsum(log_a.reshape(b, nc, L, g, e).transpose(0, 1, 3, 4, 2), axis=-1)
    seg = a_cs[..., :, None] - a_cs[..., None, :]
    lower = np.tril(np.ones((L, L), dtype=bool))
    decay = jnp.exp(jnp.where(lower, seg, -jnp.inf))
    cb = jnp.einsum('bclgn,bcsgn->bcgls', cm, bm)
    y_diag = jnp.einsum('bcgels,bcsgep->bclgep', cb[:, :, :, None] * decay, x)
    decay_to_end = jnp.exp(a_cs[..., -1:] - a_cs).transpose(0, 1, 4, 2, 3)
    states = jnp.einsum('bcsgn,bcsgep->bcgepn', bm, x * decay_to_end[..., None])
    chunk_decay = jnp.exp(a_cs[..., -1])

    def carry_state(hs, inp):
        s_c, d_c = inp
        return hs * d_c[..., None, None] + s_c, hs

    h0 = jnp.zeros((b, g, e, p, n), states.dtype)
    _, prev = lax.scan(carry_state, h0, (jnp.moveaxis(states, 1, 0), jnp.moveaxis(chunk_decay, 1, 0)))
    prev = jnp.moveaxis(prev, 0, 1)
    decay_in = jnp.exp(a_cs).transpose(0, 1, 4, 2, 3)
    y_off = jnp.einsum('bclgn,bcgepn->bclgep', cm, prev) * decay_in[..., None]
    return (y_diag + y_off).reshape(b, t, g, e, p)


def mamba2_branch(z, xbc, dt_raw, conv_w, conv_b, dt_bias, a_log, d_skip, norm_g):
    f32 = jnp.float32
    b, t, _ = z.shape
    xbc = jax.nn.silu(centred_depthwise_conv(xbc, conv_w, conv_b))
    xs, bm, cm = jnp.split(xbc, [D_SSM, D_SSM + SSM_GROUPS * SSM_STATE], axis=-1)
    xs = xs.reshape(b, t, SSM_GROUPS, HEADS_PER_GROUP, HEAD_DIM).astype(f32)
    bm = bm.reshape(b, t, SSM_GROUPS, SSM_STATE).astype(f32)
    cm = cm.reshape(b, t, SSM_GROUPS, SSM_STATE).astype(f32)
    dt = jax.nn.softplus(dt_raw.astype(f32).reshape(b, t, 2, N_HEADS_SSM) + dt_bias.astype(f32))
    a = -jnp.exp(a_log.astype(f32))
    outs = []
    for direction in range(2):
        dt_d = dt[:, :, direction].reshape(b, t, SSM_GROUPS, HEADS_PER_GROUP)
        x_in = xs * dt_d[..., None]
        log_a = dt_d * a[direction].reshape(SSM_GROUPS, HEADS_PER_GROUP)
        if direction == 0:
            y_d = ssd_chunked(x_in, log_a, bm, cm)
        else:
            y_d = jnp.flip(ssd_chunked(jnp.flip(x_in, 1), jnp.flip(log_a, 1),
                                       jnp.flip(bm, 1), jnp.flip(cm, 1)), 1)
        d_d = d_skip[direction].astype(f32).reshape(SSM_GROUPS, HEADS_PER_GROUP)[..., None]
        outs.append(y_d + d_d * xs)
    y = (outs[0] + outs[1]).reshape(b, t, SSM_GROUPS, HEADS_PER_GROUP * HEAD_DIM)
    y = y * jax.nn.silu(z.astype(f32).reshape(b, t, SSM_GROUPS, HEADS_PER_GROUP * HEAD_DIM))
    y = y * lax.rsqrt(jnp.mean(y * y, axis=-1, keepdims=True) + RMS_EPS)
    return (y.reshape(b, t, D_SSM) * norm_g.astype(f32)).astype(z.dtype)


def centred_token_shift(p, mu):
    prev = jnp.pad(p[:, :-1], ((0, 0), (1, 0), (0, 0)))
    nxt = jnp.pad(p[:, 1:], ((0, 0), (0, 1), (0, 0)))
    return p + mu[0] * (prev - p) + mu[1] * (nxt - p)


def rwkv7_branch(proj, mu, w0, w_up, a0, a_up, g_up, k_k, k_a, r_k, ln_g, ln_b):
    f32 = jnp.float32
    b, t, _ = proj.shape
    c, h, d = D_RWKV, N_HEADS_RWKV, HEAD_DIM
    p = centred_token_shift(proj.astype(f32), mu.astype(f32))
    r, k, v, wd, ad, gd = jnp.split(
        p, [c, 2 * c, 3 * c, 3 * c + 2 * LORA_DECAY, 3 * c + 2 * LORA_DECAY + 2 * LORA_ICLR], axis=-1)
    wd = wd.reshape(b, t, 2, LORA_DECAY)
    ad = ad.reshape(b, t, 2, LORA_ICLR)
    log_w = -jax.nn.softplus(-(w0.astype(f32)[:, None, None, :]
                               + jnp.einsum('btnl,nlc->nbtc', jnp.tanh(wd), w_up.astype(f32)))) - 0.5
    decay = jnp.exp(-jnp.exp(log_w))
    iclr = jax.nn.sigmoid(a0.astype(f32)[:, None, None, :]
                          + jnp.einsum('btnl,nlc->nbtc', ad, a_up.astype(f32)))
    g = jax.nn.sigmoid(gd) @ g_up.astype(f32)
    kk = (k * k_k.astype(f32)).reshape(b, t, h, d)
    kk = kk / jnp.maximum(jnp.sqrt(jnp.sum(kk * kk, axis=-1, keepdims=True)), 1e-12)
    kk = kk.reshape(b, t, c)
    k_mod = k[None] * (1.0 + (iclr - 1.0) * k_a.astype(f32))
    a_vec = jnp.broadcast_to(-kk, (2, b, t, c))
    b_vec = kk[None] * iclr

    def to_dirs(u):
        u = jnp.stack([u[0], jnp.flip(u[1], axis=1)])
        return u.reshape(2, b, t, h, d).transpose(2, 0, 1, 3, 4)

    r2 = jnp.broadcast_to(r, (2, b, t, c))
    v2 = jnp.broadcast_to(v, (2, b, t, c))
    seq_in = (to_dirs(r2), to_dirs(decay), to_dirs(k_mod), to_dirs(v2), to_dirs(a_vec), to_dirs(b_vec))

    def step(s, inp):
        r_t, w_t, k_t, v_t, a_t, b_t = inp
        sa = jnp.einsum('nbhij,nbhj->nbhi', s, a_t)
        s = s * w_t[..., None, :] + sa[..., :, None] * b_t[..., None, :] + v_t[..., :, None] * k_t[..., None, :]
        return s, jnp.einsum('nbhij,nbhj->nbhi', s, r_t)

    s0 = jnp.zeros((2, b, h, d, d), f32)
    _, ys = lax.scan(step, s0, seq_in)
    ys = ys.transpose(1, 2, 0, 3, 4)
    y = ys[0] + jnp.flip(ys[1], axis=1)
    mean = jnp.mean(y, axis=-1, keepdims=True)
    var = jnp.mean(jnp.square(y - mean), axis=-1, keepdims=True)
    y = ((y - mean) * lax.rsqrt(var + GN_EPS)).reshape(b, t, c) * ln_g.astype(f32) + ln_b.astype(f32)
    rk = jnp.sum((r[None] * k_mod).reshape(2, b, t, h, d) * r_k.astype(f32), axis=-1, keepdims=True)
    bonus = (rk[0] + rk[1]) * v.reshape(b, t, h, d)
    return ((y + bonus.reshape(b, t, c)) * g).astype(proj.dtype)


def hybrid_mixer(hn, w_in, rpb, conv_w, conv_b, dt_bias, a_log, d_skip, ssm_norm_g,
                 mu, w0, w_up, a0, a_up, g_up, k_k, k_a, r_k, ln_g, ln_b, w_branch, w_out):
    b, t, _ = hn.shape
    proj = hn @ w_in
    sizes = [D_ATTN, D_ATTN, D_ATTN, D_SSM, SSM_CONV_DIM, 2 * N_HEADS_SSM, RWKV_PROJ, N_BRANCH * D_MODEL]
    offsets = np.cumsum(sizes)[:-1].tolist()
    q, k, v, z, xbc, dt_raw, rw, gates = jnp.split(proj, offsets, axis=-1)
    y_attn = neighbourhood_attention(q, k, v, rpb)
    y_ssm = mamba2_branch(z, xbc, dt_raw, conv_w, conv_b, dt_bias, a_log, d_skip, ssm_norm_g)
    y_rwkv = rwkv7_branch(rw, mu, w0, w_up, a0, a_up, g_up, k_k, k_a, r_k, ln_g, ln_b)
    gates = jax.nn.sigmoid(gates.astype(jnp.float32)).astype(hn.dtype).reshape(b, t, N_BRANCH, D_MODEL)
    merged = (gates[:, :, 0] * (y_attn @ w_branch[:D_ATTN])
              + gates[:, :, 1] * (y_ssm @ w_branch[D_ATTN:D_ATTN + D_SSM])
              + gates[:, :, 2] * (y_rwkv @ w_branch[D_ATTN + D_SSM:]))
    return merged @ w_out


def run_trunk(x, p):
    for i in range(DEPTH):
        g = p['norm_g'][i]
        h = swiglu_ffn(rms_norm(x, g[0]), p['ff_w_in'][i, 0], p['ff_w_out'][i, 0])
        x = x + 0.5 * rms_norm(h, g[1])
        h = hybrid_mixer(rms_norm(x, g[2]), p['w_in'][i], p['attn_rpb'][i],
                         p['ssm_conv_w'][i], p['ssm_conv_b'][i], p['ssm_dt_bias'][i], p['ssm_a_log'][i],
                         p['ssm_d'][i], p['ssm_norm_g'][i],
                         p['rwkv_mu'][i], p['rwkv_w0'][i], p['rwkv_w_up'][i], p['rwkv_a0'][i],
                         p['rwkv_a_up'][i], p['rwkv_g_up'][i], p['rwkv_k_k'][i], p['rwkv_k_a'][i],
                         p['rwkv_r_k'][i], p['rwkv_ln_g'][i], p['rwkv_ln_b'][i],
                         p['w_branch'][i], p['w_out'][i])
        x = x + rms_norm(h, g[3])
        h = swiglu_ffn(rms_norm(x, g[4]), p['ff_w_in'][i, 1], p['ff_w_out'][i, 1])
        x = x + 0.5 * rms_norm(h, g[5])
    return x


def setup_inputs(seed: int = 0) -> dict:
    key = jax.random.key(seed)
    ks = jax.random.split(key, 26)
    f32 = jnp.float32

    def nrm(k, shape, scale):
        return scale * jax.random.normal(k, shape, f32)

    dt0 = jnp.exp(jax.random.uniform(ks[8], (DEPTH, 2, N_HEADS_SSM), f32, math.log(1e-3), math.log(1e-1)))
    return {
        "x_prompt": nrm(ks[0], (BATCH, SEQ, D_MODEL), 1.0),
        "x_sample": nrm(ks[1], (DEC_BATCH, DEC_SEQ, D_MODEL), 1.0),
        "norm_g": 1.0 + nrm(ks[2], (DEPTH, 6, D_MODEL), 0.02),
        "ff_w_in": nrm(ks[3], (DEPTH, 2, D_MODEL, 2 * FF_DIM), D_MODEL ** -0.5),
        "ff_w_out": nrm(ks[4], (DEPTH, 2, FF_DIM, D_MODEL), FF_DIM ** -0.5),
        "w_in": nrm(ks[5], (DEPTH, D_MODEL, IN_W), D_MODEL ** -0.5),
        "attn_rpb": nrm(ks[6], (DEPTH, N_HEADS_ATTN, 2 * WIN_H_MAX - 1, 2 * WIN_W - 1), 0.1),
        "ssm_conv_w": nrm(ks[7], (DEPTH, SSM_CONV, SSM_CONV_DIM), SSM_CONV ** -0.5),
        "ssm_conv_b": nrm(ks[9], (DEPTH, SSM_CONV_DIM), 0.02),
        "ssm_dt_bias": dt0 + jnp.log(-jnp.expm1(-dt0)),
        "ssm_a_log": jnp.log(jax.random.uniform(ks[10], (DEPTH, 2, N_HEADS_SSM), f32, 1.0, 16.0)),
        "ssm_d": 1.0 + nrm(ks[11], (DEPTH, 2, N_HEADS_SSM), 0.1),
        "ssm_norm_g": 1.0 + nrm(ks[12], (DEPTH, D_SSM), 0.02),
        "rwkv_mu": jax.random.uniform(ks[13], (DEPTH, 2, RWKV_PROJ), f32, 0.0, 0.5),
        "rwkv_w0": jax.random.uniform(ks[14], (DEPTH, 2, D_RWKV), f32, -6.0, -1.0),
        "rwkv_w_up": nrm(ks[15], (DEPTH, 2, LORA_DECAY, D_RWKV), 0.5 * LORA_DECAY ** -0.5),
        "rwkv_a0": nrm(ks[16], (DEPTH, 2, D_RWKV), 0.1),
        "rwkv_a_up": nrm(ks[17], (DEPTH, 2, LORA_ICLR, D_RWKV), 0.5 * LORA_ICLR ** -0.5),
        "rwkv_g_up": nrm(ks[18], (DEPTH, LORA_GATE, D_RWKV), LORA_GATE ** -0.5),
        "rwkv_k_k": 0.85 + nrm(ks[19], (DEPTH, D_RWKV), 0.02),
        "rwkv_k_a": 1.0 + nrm(ks[20], (DEPTH, D_RWKV), 0.02),
        "rwkv_r_k": nrm(ks[21], (DEPTH, N_HEADS_RWKV, HEAD_DIM), 0.1),
        "rwkv_ln_g": 1.0 + nrm(ks[22], (DEPTH, D_RWKV), 0.02),
        "rwkv_ln_b": nrm(ks[23], (DEPTH, D_RWKV), 0.02),
        "w_branch": nrm(ks[24], (DEPTH, MIX_W, D_MODEL), D_ATTN ** -0.5),
        "w_out": nrm(ks[25], (DEPTH, D_MODEL, D_MODEL), D_MODEL ** -0.5),
    }


def reference(x_prompt, x_sample, norm_g, ff_w_in, ff_w_out, w_in, attn_rpb, ssm_conv_w, ssm_conv_b,
              ssm_dt_bias, ssm_a_log, ssm_d, ssm_norm_g, rwkv_mu, rwkv_w0, rwkv_w_up, rwkv_a0,
              rwkv_a_up, rwkv_g_up, rwkv_k_k, rwkv_k_a, rwkv_r_k, rwkv_ln_g, rwkv_ln_b, w_branch, w_out):
    params = dict(norm_g=norm_g, ff_w_in=ff_w_in, ff_w_out=ff_w_out, w_in=w_in, attn_rpb=attn_rpb,
                  ssm_conv_w=ssm_conv_w, ssm_conv_b=ssm_conv_b, ssm_dt_bias=ssm_dt_bias,
                  ssm_a_log=ssm_a_log, ssm_d=ssm_d, ssm_norm_g=ssm_norm_g, rwkv_mu=rwkv_mu,
                  rwkv_w0=rwkv_w0, rwkv_w_up=rwkv_w_up, rwkv_a0=rwkv_a0, rwkv_a_up=rwkv_a_up,
                  rwkv_g_up=rwkv_g_up, rwkv_k_k=rwkv_k_k, rwkv_k_a=rwkv_k_a, rwkv_r_k=rwkv_r_k,
                  rwkv_ln_g=rwkv_ln_g, rwkv_ln_b=rwkv_ln_b, w_branch=w_branch, w_out=w_out)
    y_prompt = run_trunk(x_prompt, params)
    y_sample = run_trunk(x_sample, params)
    return (y_prompt, y_sample)
```

```python
import numpy as np
import ml_dtypes
from contextlib import ExitStack
import concourse.bass as bass
import concourse.mybir as mybir
from concourse.bass_utils import run_bass_kernel_spmd

F32 = mybir.dt.float32
BF16 = mybir.dt.bfloat16
AF = mybir.ActivationFunctionType
ALU = mybir.AluOpType
AX = mybir.AxisListType

D = 1024
FF = 2816
DEPTH = 2
EPS = 1e-6


class Res:
    __slots__ = ("w", "r")

    def __init__(self):
        self.w = None
        self.r = []


class Tl:
    def __init__(self, t, res=None):
        self.t = t
        self.res = res or Res()

    def __getitem__(self, idx):
        return self.t[idx]


class Sched:
    EPOCH = 60000
    NDMA = {"sp": 24, "pool": 12, "act": 8}

    def __init__(self, nc, es):
        self.nc = nc
        self.es = es
        self.names = ["sp", "act", "dve", "pool", "pe"]
        self.ops = {k: [] for k in self.names}
        self.n = {k: 0 for k in self.names}
        self.sems = {k: [] for k in self.names}
        self.seen = {k: {} for k in self.names}
        self.dsem = {}
        self.dval = {}
        self.drr = {}
        for q, n in self.NDMA.items():
            self.dsem[q] = [es.enter_context(nc.semaphore(f"d{q}{i}")) for i in range(n)]
            self.dval[q] = [0] * n
            self.drr[q] = 0
        self.last = {k: None for k in self.names}

    def _next_ev(self, e):
        n = self.n[e]
        ep = n // self.EPOCH
        while len(self.sems[e]) <= ep:
            self.sems[e].append(self.es.enter_context(self.nc.semaphore(f"s{e}{len(self.sems[e])}")))
        self.n[e] += 1
        ev = (self.sems[e][ep], n % self.EPOCH + 1, e)
        self.last[e] = ev
        return ev

    def _deps(self, e, reads, writes):
        deps = []
        for r in reads:
            r = r.res if isinstance(r, Tl) else r
            if r.w is not None:
                deps.append((r.w, 0))
        for w in writes:
            w = w.res if isinstance(w, Tl) else w
            if w.w is not None:
                deps.append((w.w, 0))
            for ev in w.r:
                deps.append((ev, 1))
        waits = []
        seen = self.seen[e]
        for (sem, val, src), war in deps:
            if src == e:
                if e == "pe" or war:
                    continue
            k = id(sem)
            if seen.get(k, 0) >= val:
                continue
            seen[k] = val
            waits.append((sem, val))
        return waits

    def _commit(self, ev, reads, writes):
        for r in reads:
            r = r.res if isinstance(r, Tl) else r
            r.r.append(ev)
        for w in writes:
            w = w.res if isinstance(w, Tl) else w
            w.w = ev
            w.r = []

    def op(self, e, fn, reads=(), writes=()):
        waits = self._deps(e, reads, writes)
        ev = self._next_ev(e)
        self.ops[e].append((waits, fn, ev, 1))
        self._commit(ev, reads, writes)

    def dma(self, q, out, in_, reads=(), writes=(), slow=False):
        waits = self._deps(q, reads, writes)
        i = self.drr[q]
        self.drr[q] = (i + 1) % len(self.dsem[q])
        sem = self.dsem[q][i]
        prev = self.dval[q][i]
        if prev > 0 and self.seen[q].get(id(sem), 0) < prev:
            waits.append((sem, prev))
            self.seen[q][id(sem)] = prev
        self.dval[q][i] = prev + 16
        ev = (sem, prev + 16, "dma")
        if slow:
            fn = (lambda e, o=out, i_=in_: e.dma_start(out=o, in_=i_, allow_slow_non_contiguous=True))
        else:
            fn = (lambda e, o=out, i_=in_: e.dma_start(out=o, in_=i_))
        self.ops[q].append((waits, fn, ev, 16))
        self._commit(ev, reads, writes)

    def barrier(self):
        evs = [self.last[k] for k in self.names if self.last[k] is not None]
        for q in self.dsem:
            for sem, v in zip(self.dsem[q], self.dval[q]):
                if v > 0:
                    evs.append((sem, v, "dma"))
        for e in self.names:
            waits = []
            for sem, val, src in evs:
                if src == e:
                    continue
                if self.seen[e].get(id(sem), 0) >= val:
                    continue
                self.seen[e][id(sem)] = val
                waits.append((sem, val))
            if waits:
                self.ops[e].append((waits, None, None, 0))

    def emit(self):
        block = self.es.enter_context(self.nc.Block())
        decos = {"sp": block.sync, "act": block.scalar, "dve": block.vector, "pool": block.gpsimd,
                 "pe": block.tensor}
        for name in self.names:
            ops = self.ops[name]

            def body(e, ops=ops):
                for waits, fn, ev, inc in ops:
                    for (s, v) in waits:
                        e.wait_ge(s, v)
                    if fn is not None:
                        fn(e).then_inc(ev[0], inc)

            decos[name](body)


class Ctx:
    BASE = 16640
    ARENA = 224 * 1024

    def __init__(self, nc, es):
        self.nc = nc
        self.es = es
        self.S = Sched(nc, es)
        self.off = self.BASE
        self.uid = 0
        self.keep = self.BASE
        self.ps = [Tl(es.enter_context(nc.psum_tensor(f"ps{i}", [128, 512], F32))) for i in range(8)]
        self.psi = 0
        self.psi6 = 0

    def alloc(self, shape, dtype, keep=False):
        nbytes = int(np.prod(shape[1:])) * (2 if dtype == BF16 else 4)
        nbytes = (nbytes + 31) // 32 * 32
        assert self.off + nbytes <= self.ARENA, f"SBUF arena overflow {self.off}+{nbytes}"
        self.uid += 1
        t = self.nc.alloc_sbuf_tensor_at(f"t{self.uid}", list(shape), dtype, offset=self.off)
        self.off += nbytes
        if keep:
            self.keep = self.off
        return Tl(t)

    def new_pass(self):
        self.S.barrier()
        self.off = self.keep

    def psum(self):
        p = self.ps[self.psi]
        self.psi = (self.psi + 1) % 8
        return p


class Pool:
    def __init__(self, K, n, shape, dtype):
        self.t = [K.alloc(shape, dtype) for _ in range(n)]
        self.i = 0

    def get(self):
        t = self.t[self.i]
        self.i = (self.i + 1) % len(self.t)
        return t


def norm_transpose(K, xt, xres, gbc, hnT, col0, P):
    S = K.S
    sm = P["small"].get()
    junk = P["junk"].get()
    S.op("act", lambda e: e.activation(out=junk[:, :], in_=xt, func=AF.Square, accum_out=sm[:, 0:1]),
         reads=[xres], writes=[junk, sm])
    S.op("act", lambda e: e.activation(out=sm[:, 1:2], in_=sm[:, 0:1], func=AF.Sqrt, bias=K.epsc[:, 0:1],
                                       scale=1.0 / D), reads=[sm, K.epsc], writes=[sm])
    S.op("dve", lambda e: e.reciprocal(out=sm[:, 2:3], in_=sm[:, 1:2]), reads=[sm], writes=[sm])
    hn = P["hn"].get()
    S.op("dve", lambda e: e.scalar_tensor_tensor(out=hn[:, :], in0=xt, scalar=sm[:, 2:3], in1=gbc[:, :],
                                                 op0=ALU.mult, op1=ALU.mult),
         reads=[xres, sm, gbc], writes=[hn])
    pt = K.psum()
    ptb = pt.t[:, :].bitcast(BF16)
    for kc in range(8):
        S.op("pe", lambda e, kc=kc: e.transpose(ptb[:, kc * 128:(kc + 1) * 128],
                                                hn[:, kc * 128:(kc + 1) * 128], K.ident[:, :]),
             reads=[hn, K.ident], writes=[pt])
    S.op("act", lambda e: e.copy(out=hnT[:, :, col0:col0 + 128],
                                 in_=ptb.rearrange("p (k c) -> p k c", k=8)), reads=[pt], writes=[hnT])


def ffn_pass(K, xin, xout, gin_ap, gout_ap, win_bf, wout_bf, ntok):
    S = K.S
    K.new_pass()
    gin = K.alloc([128, D], F32)
    gout = K.alloc([128, D], F32)
    S.dma("sp", gin[:, :], gin_ap.broadcast_to([128, D]), writes=[gin])
    S.dma("sp", gout[:, :], gout_ap.broadcast_to([128, D]), writes=[gout])
    wout = K.alloc([128, 22, D], BF16)
    wo_v = wout_bf.rearrange("(kc p) n -> p kc n", p=128)
    for kc in range(0, 22, 2):
        S.dma("sp", wout[:, kc:kc + 2, :], wo_v[:, kc:kc + 2, :], writes=[wout])
    xp = Pool(K, 2, [128, 4, D], F32)
    hnTp = Pool(K, 2, [128, 8, 512], BF16)
    hT = K.alloc([128, 22, 512], BF16)
    wp = Pool(K, 3, [128, 8, 2, 256], BF16)
    sg = Pool(K, 2, [128, 512], F32)
    tt = Pool(K, 2, [128, D], F32)
    P = {"small": Pool(K, 8, [128, 8], F32), "junk": Pool(K, 2, [128, D], BF16), "hn": Pool(K, 2, [128, D], BF16)}
    win_v = win_bf.rearrange("(kc p) n -> p kc n", p=128)
    for mt in range(ntok // 512):
        x = xp.get()
        S.dma("sp", x[:, :, :], xin[mt * 512:(mt + 1) * 512, :].rearrange("(s p) d -> p s d", p=128),
              writes=[x])
        hnT = hnTp.get()
        for s in range(4):
            norm_transpose(K, x[:, s, :], x, gin, hnT, s * 128, P)
        for j in range(11):
            w = wp.get()
            S.dma("sp", w[:, :, 0, :], win_v[:, :, j * 256:(j + 1) * 256], writes=[w])
            S.dma("sp", w[:, :, 1, :], win_v[:, :, FF + j * 256:FF + (j + 1) * 256], writes=[w])
            for c in range(2):
                pg = K.psum()
                pu = K.psum()
                for gi, pp in enumerate((pg, pu)):
                    for kc in range(8):
                        S.op("pe", lambda e, pp=pp, gi=gi, kc=kc, c=c, w=w, hnT=hnT: e.matmul(
                            pp[:, :], w[:, kc, gi, c * 128:(c + 1) * 128], hnT[:, kc, :],
                            start=(kc == 0), stop=(kc == 7)), reads=[w, hnT], writes=[pp])
                sgt = sg.get()
                S.op("act", lambda e, sgt=sgt, pg=pg: e.activation(out=sgt[:, :], in_=pg[:, :], func=AF.Silu),
                     reads=[pg], writes=[sgt])
                S.op("dve", lambda e, sgt=sgt, pu=pu, ch=j * 2 + c: e.tensor_tensor(
                    out=hT[:, ch, :], in0=sgt[:, :], in1=pu[:, :], op=ALU.mult),
                     reads=[sgt, pu], writes=[hT])
        for s in range(4):
            pp = [K.psum(), K.psum()]
            for nf in range(2):
                for kc in range(22):
                    S.op("pe", lambda e, p_=pp[nf], nf=nf, kc=kc, s=s: e.matmul(
                        p_[:, :], hT[:, kc, s * 128:(s + 1) * 128], wout[:, kc, nf * 512:(nf + 1) * 512],
                        start=(kc == 0), stop=(kc == 21)), reads=[hT, wout], writes=[pp[nf]])
            sm = P["small"].get()
            junk = P["junk"].get()
            for nf in range(2):
                S.op("act", lambda e, nf=nf, junk=junk, sm=sm, p_=pp[nf]: e.activation(
                    out=junk[:, 0:512], in_=p_[:, :], func=AF.Square, accum_out=sm[:, nf:nf + 1]),
                     reads=[pp[nf]], writes=[junk, sm])
            S.op("dve", lambda e, sm=sm: e.tensor_tensor(out=sm[:, 2:3], in0=sm[:, 0:1], in1=sm[:, 1:2],
                                                         op=ALU.add), reads=[sm], writes=[sm])
            S.op("act", lambda e, sm=sm: e.activation(out=sm[:, 3:4], in_=sm[:, 2:3], func=AF.Sqrt,
                                                      bias=K.epsc[:, 1:2], scale=4.0 / D),
                 reads=[sm, K.epsc], writes=[sm])
            S.op("dve", lambda e, sm=sm: e.reciprocal(out=sm[:, 4:5], in_=sm[:, 3:4]), reads=[sm], writes=[sm])
            t = tt.get()
            for nf in range(2):
                S.op("dve", lambda e, nf=nf, t=t, sm=sm, p_=pp[nf]: e.scalar_tensor_tensor(
                    out=t[:, nf * 512:(nf + 1) * 512], in0=p_[:, :], scalar=sm[:, 4:5],
                    in1=gout[:, nf * 512:(nf + 1) * 512], op0=ALU.mult, op1=ALU.mult),
                     reads=[pp[nf], sm, gout], writes=[t])
            S.op("pool", lambda e, t=t, x=x, s=s: e.tensor_tensor(out=x[:, s, :], in0=x[:, s, :], in1=t[:, :],
                                                                  op=ALU.add), reads=[t, x], writes=[x])
        S.dma("pool", xout[mt * 512:(mt + 1) * 512, :].rearrange("(s p) d -> p s d", p=128), x[:, :, :],
              reads=[x])


IN_W = 9152
OFF_Q, OFF_K, OFF_V, OFF_Z, OFF_XBC, OFF_DT, OFF_RW, OFF_G = 0, 512, 1024, 1536, 2560, 4096, 4128, 6080


def inproj_pass(K, xin, g_ap, w_bf, SC, ntok):
    S = K.S
    K.new_pass()
    gin = K.alloc([128, D], F32)
    S.dma("sp", gin[:, :], g_ap.broadcast_to([128, D]), writes=[gin])
    xp = Pool(K, 2, [128, 4, D], F32)
    hnTp = Pool(K, 2, [128, 8, 512], BF16)
    wp = Pool(K, 3, [128, 8, 512], BF16)
    ofm = Pool(K, 3, [128, 512], F32)
    obf = Pool(K, 3, [128, 512], BF16)
    P = {"small": Pool(K, 8, [128, 8], F32), "junk": Pool(K, 2, [128, D], BF16), "hn": Pool(K, 2, [128, D], BF16)}
    w_v = w_bf.rearrange("(kc p) n -> p kc n", p=128)
    blocks = []
    for c0 in range(0, 1024, 512):
        blocks.append((c0, 512, "F"))
    blocks.append((OFF_V, 512, "T"))
    blocks += [(OFF_Z, 512, "T"), (OFF_Z + 512, 512, "T")]
    blocks += [(OFF_XBC + i * 512, 512, "F") for i in range(3)]
    blocks.append((OFF_DT, 32, "T"))
    blocks += [(OFF_RW, 512, "F"), (OFF_RW + 512, 512, "F"), (OFF_RW + 1024, 512, "F"), (OFF_RW + 1536, 416, "F")]
    blocks += [(OFF_G + i * 512, 512, "T") for i in range(6)]
    for mt in range(ntok // 512):
        t0 = mt * 512
        x = xp.get()
        S.dma("sp", x[:, :, :], xin[t0:t0 + 512, :].rearrange("(s p) d -> p s d", p=128), writes=[x])
        hnT = hnTp.get()
        for s in range(4):
            norm_transpose(K, x[:, s, :], x, gin, hnT, s * 128, P)
        for (c0, ncol, mode) in blocks:
            w = wp.get()
            S.dma("sp", w[:, :, 0:ncol], w_v[:, :, c0:c0 + ncol], writes=[w])
            if mode == "F":
                for fc in range((ncol + 127) // 128):
                    nf = min(128, ncol - fc * 128)
                    pp = K.psum()
                    for kc in range(8):
                        S.op("pe", lambda e, pp=pp, w=w, kc=kc, fc=fc, nf=nf, hnT=hnT: e.matmul(
                            pp[0:nf, :], w[:, kc, fc * 128:fc * 128 + nf], hnT[:, kc, :],
                            start=(kc == 0), stop=(kc == 7)), reads=[w, hnT], writes=[pp])
                    f0 = c0 + fc * 128
                    if f0 < 1024:
                        o = obf.get()
                        sc = 0.125 if f0 < 512 else 1.0
                        S.op("act", lambda e, o=o, pp=pp, sc=sc: e.activation(out=o[:, :], in_=pp[:, :],
                                                                              func=AF.Copy, scale=sc),
                             reads=[pp], writes=[o])
                        dst = SC["QT"] if f0 < 512 else SC["KT"]
                        r0 = f0 % 512
                        S.dma("pool", dst[r0:r0 + 128, t0:t0 + 512], o[:, :], reads=[o])
                    else:
                        o = ofm.get()
                        S.op("act", lambda e, o=o, pp=pp, nf=nf: e.copy(out=o[0:nf, :], in_=pp[0:nf, :]),
                             reads=[pp], writes=[o])
                        if f0 < OFF_DT:
                            r0 = f0 - OFF_XBC
                            S.dma("pool", SC["XBCT"][r0:r0 + nf, t0:t0 + 512], o[0:nf, :], reads=[o])
                        else:
                            r0 = f0 - OFF_RW
                            S.dma("pool", SC["RWT"][r0:r0 + nf, t0:t0 + 512], o[0:nf, :], reads=[o])
            else:
                for s in range(4):
                    pp = K.psum()
                    for kc in range(8):
                        S.op("pe", lambda e, pp=pp, w=w, kc=kc, s=s, ncol=ncol, hnT=hnT: e.matmul(
                            pp[:, 0:ncol], hnT[:, kc, s * 128:(s + 1) * 128], w[:, kc, 0:ncol],
                            start=(kc == 0), stop=(kc == 7)), reads=[w, hnT], writes=[pp])
                    tk = t0 + s * 128
                    if c0 == OFF_V:
                        o = obf.get()
                        S.op("act", lambda e, o=o, pp=pp: e.copy(out=o[:, :], in_=pp[:, :]), reads=[pp], writes=[o])
                        S.dma("pool", SC["V"][tk:tk + 128, :], o[:, :], reads=[o])
                    elif c0 >= OFF_G:
                        o = ofm.get()
                        S.op("act", lambda e, o=o, pp=pp: e.activation(out=o[:, :], in_=pp[:, :], func=AF.Sigmoid),
                             reads=[pp], writes=[o])
                        S.dma("pool", SC["G"][tk:tk + 128, c0 - OFF_G:c0 - OFF_G + 512], o[:, :], reads=[o])
                    elif c0 == OFF_DT:
                        o = ofm.get()
                        S.op("act", lambda e, o=o, pp=pp: e.copy(out=o[:, 0:32], in_=pp[:, 0:32]), reads=[pp], writes=[o])
                        S.dma("pool", SC["DT"][tk:tk + 128, :], o[:, 0:32], reads=[o])
                    else:
                        o = ofm.get()
                        S.op("act", lambda e, o=o, pp=pp: e.copy(out=o[:, :], in_=pp[:, :]), reads=[pp], writes=[o])
                        S.dma("pool", SC["Z"][tk:tk + 128, c0 - OFF_Z:c0 - OFF_Z + 512], o[:, :], reads=[o])


def attn_plan(nseg, seglen):
    R = seglen // 64
    nrow = nseg * R

    def rs_of(g, link):
        seg, r = divmod(g, R)
        if link and seg < 2 and nseg >= 2:
            return int(np.clip(g - 4, 0, 2 * R - 8))
        return seg * R + int(np.clip(r - 4, 0, R - 8))

    qc = np.arange(64)
    wst = np.clip(qc - 8, 0, 48)
    pats = {}
    plan = []
    ids = {}
    for P in range(nrow // 2):
        kps = set()
        for g in (2 * P, 2 * P + 1):
            for link in (False, True):
                rs = rs_of(g, link)
                for kr in range(rs, rs + 8):
                    kps.add(kr // 2)
        ent = []
        for KP in sorted(kps):
            both = []
            for link in (False, True):
                idx = np.full((2, 64, 2, 64), -1, np.int32)
                for qr2 in range(2):
                    g = 2 * P + qr2
                    rs = rs_of(g, link)
                    for kr2 in range(2):
                        kr = 2 * KP + kr2
                        if not (rs <= kr < rs + 8):
                            continue
                        dr = kr - g + 7
                        kc = np.arange(64)[:, None]
                        ok = (kc >= wst[None, :]) & (kc < wst[None, :] + 16)
                        dc = np.clip(kc - qc[None, :] + 15, 0, 30)
                        idx[kr2, :, qr2, :] = np.where(ok, dr * 31 + dc, -1)
                both.append(idx.reshape(128, 128))
            key = both[0].tobytes() + both[1].tobytes()
            if key not in ids:
                ids[key] = len(ids)
                pats.setdefault(False, []).append(both[0])
                pats.setdefault(True, []).append(both[1])
            ent.append((KP, ids[key]))
        plan.append(ent)
    return plan, {k: np.stack(v) for k, v in pats.items()}


def attn_tables(rpb, pat):
    flat = rpb.reshape(rpb.shape[0], 8, 15 * 31)
    g = flat[:, :, np.clip(pat, 0, None)]
    g = np.where(pat[None, None] >= 0, g, np.float32(-30000.0)).astype(np.float32)
    return np.ascontiguousarray(g.transpose(0, 2, 3, 1, 4))


def attn_pass(K, SC, tab, plan):
    import os
    ALV = int(os.environ.get("MK_ALV", "3"))
    S = K.S
    K.new_pass()
    qp = Pool(K, 2, [64, 8, 128], BF16)
    kp = Pool(K, 3, [64, 8, 128], BF16)
    vp = Pool(K, 3, [128, 8, 80], BF16)
    tp = Pool(K, 3, [128, 8, 128], F32)
    sp_ = Pool(K, 2, [128, 512], F32)
    ptp = Pool(K, 3, [128, 8, 128], BF16)
    yp = Pool(K, 2, [128, 8, 64], BF16)
    sm = Pool(K, 2, [128, 8], F32)
    for v in vp.t:
        S.op("pool", lambda e, v=v: e.memset(v[:, :, 64:65], 1.0), writes=[v])
    QTv = SC["QT"].rearrange("(c p) t -> p c t", p=64)
    KTv = SC["KT"].rearrange("(c p) t -> p c t", p=64)
    for P, ent in enumerate(plan):
        q = qp.get()
        S.dma("sp", q[:, :, :], QTv[:, :, P * 128:(P + 1) * 128], writes=[q])
        ob = [K.ps[6], K.ps[7]]
        for ki, (KP, cfg) in enumerate(ent):
            k = kp.get()
            S.dma("sp", k[:, :, :], KTv[:, :, KP * 128:(KP + 1) * 128], writes=[k])
            v = vp.get()
            S.dma("sp", v[:, :, 0:64], SC["V"][KP * 128:(KP + 1) * 128, :].rearrange("t (h d) -> t h d", h=8),
                  writes=[v])
            tb = tp.get()
            S.dma("sp", tb[:, :, :], tab[cfg], writes=[tb])
            pt = ptp.get()
            for hb in range(2):
                pp = K.ps[K.psi6]
                K.psi6 = (K.psi6 + 1) % 6
                for hh in range(4):
                    h = hb * 4 + hh
                    S.op("pe", lambda e, pp=pp, k=k, q=q, h=h, hh=hh: e.matmul(
                        pp[:, hh * 128:(hh + 1) * 128], k[:, h, :], q[:, h, :], start=True, stop=True),
                        reads=[k, q], writes=[pp])
                sb = sp_.get()
                S.op("dve", lambda e, sb=sb, pp=pp, tb=tb, hb=hb: e.tensor_tensor(
                    out=sb[:, :], in0=pp[:, :], in1=tb[:, hb * 4:(hb + 1) * 4, :].rearrange("p a b -> p (a b)"),
                    op=ALU.add), reads=[pp, tb], writes=[sb])
                S.op("act", lambda e, sb=sb, pt=pt, hb=hb: e.activation(
                    out=pt[:, hb * 4:(hb + 1) * 4, :].rearrange("p a b -> p (a b)"), in_=sb[:, :], func=AF.Exp),
                    reads=[sb], writes=[pt])
            for h in range(8 if ALV >= 2 else 0):
                o_ = ob[h // 4]
                hh = h % 4
                S.op("pe", lambda e, o_=o_, pt=pt, v=v, h=h, hh=hh, ki=ki, n=len(ent): e.matmul(
                    o_[:, hh * 128:hh * 128 + 65], pt[:, h, :], v[:, h, 0:65], start=(ki == 0 and hh == 0), stop=(ki == n - 1)),
                    reads=[pt, v], writes=[o_])
        rc = sm.get()
        y = yp.get()
        for hb in range(2 if ALV >= 3 else 0):
            ov = ob[hb][:, :].rearrange("p (h d) -> p h d", h=4)
            S.op("dve", lambda e, rc=rc, ov=ov, hb=hb: e.reciprocal(out=rc[:, hb * 4:(hb + 1) * 4], in_=ov[:, :, 64]),
                 reads=[ob[hb]], writes=[rc])
            S.op("dve", lambda e, rc=rc, ov=ov, hb=hb, y=y: e.tensor_tensor(
                out=y[:, hb * 4:(hb + 1) * 4, :], in0=ov[:, :, 0:64],
                in1=rc[:, hb * 4:(hb + 1) * 4].unsqueeze(2).broadcast_to([128, 4, 64]), op=ALU.mult),
                reads=[ob[hb], rc], writes=[y])
        if ALV >= 3:
            S.dma("pool", SC["YA"][P * 128:(P + 1) * 128, :], y[:, :, :].rearrange("p h d -> p (h d)"), reads=[y])


def merge_pass(K, SC, xio, g_ap, wbr_bf, wout_bf, ntok):
    S = K.S
    K.new_pass()
    g3 = K.alloc([128, D], F32)
    S.dma("sp", g3[:, :], g_ap.broadcast_to([128, D]), writes=[g3])
    wbr = K.alloc([128, 16, D], BF16)
    wbv = wbr_bf.rearrange("(kc p) n -> p kc n", p=128)
    for kc in range(0, 16, 4):
        S.dma("sp", wbr[:, kc:kc + 4, :], wbv[:, kc:kc + 4, :], writes=[wbr])
    wo = K.alloc([128, 8, D], BF16)
    S.dma("sp", wo[:, :, :], wout_bf.rearrange("(kc p) n -> p kc n", p=128), writes=[wo])
    ybp = Pool(K, 2, [128, 2048], BF16)
    gp = Pool(K, 2, [128, 3, D], F32)
    xp = Pool(K, 2, [128, D], F32)
    yTp = Pool(K, 2, [128, 16, 128], BF16)
    mp = Pool(K, 2, [128, D], F32)
    tmp = Pool(K, 2, [128, 512], F32)
    mbp = Pool(K, 2, [128, D], BF16)
    mTp = Pool(K, 2, [128, 8, 128], BF16)
    tt = Pool(K, 2, [128, D], F32)
    sm = Pool(K, 4, [128, 8], F32)
    junk = Pool(K, 2, [128, 512], BF16)
    for t in range(ntok // 128):
        r0 = t * 128
        yb = ybp.get()
        S.dma("sp", yb[:, 0:512], SC["YA"][r0:r0 + 128, :], writes=[yb])
        S.dma("sp", yb[:, 512:1536], SC["YS"][r0:r0 + 128, :], writes=[yb])
        S.dma("sp", yb[:, 1536:2048], SC["YR"][r0:r0 + 128, :], writes=[yb])
        gt = gp.get()
        S.dma("sp", gt[:, :, :], SC["G"][r0:r0 + 128, :].rearrange("t (b d) -> t b d", b=3), writes=[gt])
        x = xp.get()
        S.dma("sp", x[:, :], xio[r0:r0 + 128, :], writes=[x])
        yT = yTp.get()
        for half in range(2):
            pt = K.psum()
            ptb = pt.t[:, :].bitcast(BF16)
            for c in range(8):
                kc = half * 8 + c
                S.op("pe", lambda e, ptb=ptb, c=c, kc=kc, yb=yb: e.transpose(
                    ptb[:, c * 128:(c + 1) * 128], yb[:, kc * 128:(kc + 1) * 128], K.ident[:, :]),
                    reads=[yb, K.ident], writes=[pt])
            S.op("act", lambda e, yT=yT, ptb=ptb, half=half: e.copy(
                out=yT[:, half * 8:(half + 1) * 8, :], in_=ptb.rearrange("p (k c) -> p k c", k=8)),
                reads=[pt], writes=[yT])
        m = mp.get()
        for b, (k0, nk) in enumerate(((0, 4), (4, 8), (12, 4))):
            for nf in range(2):
                pp = K.psum()
                for kc in range(nk):
                    S.op("pe", lambda e, pp=pp, yT=yT, kc=kc, k0=k0, nf=nf, nk=nk: e.matmul(
                        pp[:, :], yT[:, k0 + kc, :], wbr[:, k0 + kc, nf * 512:(nf + 1) * 512],
                        start=(kc == 0), stop=(kc == nk - 1)), reads=[yT, wbr], writes=[pp])
                if b == 0:
                    S.op("dve", lambda e, m=m, pp=pp, gt=gt, nf=nf, b=b: e.tensor_tensor(
                        out=m[:, nf * 512:(nf + 1) * 512], in0=pp[:, :], in1=gt[:, b, nf * 512:(nf + 1) * 512],
                        op=ALU.mult), reads=[pp, gt], writes=[m])
                else:
                    tm = tmp.get()
                    S.op("dve", lambda e, tm=tm, pp=pp, gt=gt, nf=nf, b=b: e.tensor_tensor(
                        out=tm[:, :], in0=pp[:, :], in1=gt[:, b, nf * 512:(nf + 1) * 512], op=ALU.mult),
                        reads=[pp, gt], writes=[tm])
                    S.op("pool", lambda e, tm=tm, m=m, nf=nf: e.tensor_tensor(
                        out=m[:, nf * 512:(nf + 1) * 512], in0=m[:, nf * 512:(nf + 1) * 512], in1=tm[:, :],
                        op=ALU.add), reads=[tm, m], writes=[m])
        mb = mbp.get()
        S.op("act", lambda e, mb=mb, m=m: e.copy(out=mb[:, :], in_=m[:, :]), reads=[m], writes=[mb])
        mT = mTp.get()
        pt = K.psum()
        ptb = pt.t[:, :].bitcast(BF16)
        for c in range(8):
            S.op("pe", lambda e, ptb=ptb, c=c, mb=mb: e.transpose(
                ptb[:, c * 128:(c + 1) * 128], mb[:, c * 128:(c + 1) * 128], K.ident[:, :]),
                reads=[mb, K.ident], writes=[pt])
        S.op("act", lambda e, mT=mT, ptb=ptb: e.copy(out=mT[:, :, :], in_=ptb.rearrange("p (k c) -> p k c", k=8)),
             reads=[pt], writes=[mT])
        pp = [K.psum(), K.psum()]
        for nf in range(2):
            for kc in range(8):
                S.op("pe", lambda e, p_=pp[nf], mT=mT, kc=kc, nf=nf: e.matmul(
                    p_[:, :], mT[:, kc, :], wo[:, kc, nf * 512:(nf + 1) * 512], start=(kc == 0), stop=(kc == 7)),
                    reads=[mT, wo], writes=[pp[nf]])
        s_ = sm.get()
        jk = junk.get()
        for nf in range(2):
            S.op("act", lambda e, nf=nf, jk=jk, s_=s_, p_=pp[nf]: e.activation(
                out=jk[:, :], in_=p_[:, :], func=AF.Square, accum_out=s_[:, nf:nf + 1]),
                reads=[pp[nf]], writes=[jk, s_])
        S.op("dve", lambda e, s_=s_: e.tensor_tensor(out=s_[:, 2:3], in0=s_[:, 0:1], in1=s_[:, 1:2], op=ALU.add),
             reads=[s_], writes=[s_])
        S.op("act", lambda e, s_=s_: e.activation(out=s_[:, 3:4], in_=s_[:, 2:3], func=AF.Sqrt,
                                                  bias=K.epsc[:, 0:1], scale=1.0 / D),
             reads=[s_, K.epsc], writes=[s_])
        S.op("dve", lambda e, s_=s_: e.reciprocal(out=s_[:, 4:5], in_=s_[:, 3:4]), reads=[s_], writes=[s_])
        t_ = tt.get()
        for nf in range(2):
            S.op("dve", lambda e, nf=nf, t_=t_, s_=s_, p_=pp[nf]: e.scalar_tensor_tensor(
                out=t_[:, nf * 512:(nf + 1) * 512], in0=p_[:, :], scalar=s_[:, 4:5],
                in1=g3[:, nf * 512:(nf + 1) * 512], op0=ALU.mult, op1=ALU.mult),
                reads=[pp[nf], s_, g3], writes=[t_])
        S.op("pool", lambda e, t_=t_, x=x: e.tensor_tensor(out=x[:, :], in0=x[:, :], in1=t_[:, :], op=ALU.add),
             reads=[t_, x], writes=[x])
        S.dma("pool", xio[r0:r0 + 128, :], x[:, :], reads=[x])


def ssd_setup(K, A, l):
    S = K.S
    C = {}
    C["cw"] = K.alloc([128, 12, 5], F32)
    for k in range(5):
        S.dma("sp", C["cw"][:, :, k], A["ssm_conv_w"][l, k].rearrange("(fc p) -> p fc", p=128), writes=[C["cw"]],
              slow=True)
    C["cb"] = K.alloc([128, 12], F32)
    S.dma("sp", C["cb"][:, :], A["ssm_conv_b"][l].rearrange("(fc p) -> p fc", p=128), writes=[C["cb"]], slow=True)
    C["dtb"] = K.alloc([128, 32], F32)
    S.dma("sp", C["dtb"][:, :], A["ssm_dt_bias"][l].rearrange("a b -> (a b)").unsqueeze(0).broadcast_to([128, 32]),
          writes=[C["dtb"]])
    C["abc"] = K.alloc([128, 32], F32)
    S.dma("sp", C["abc"][:, :], A["ssm_a_log"][l].rearrange("a b -> (a b)").unsqueeze(0).broadcast_to([128, 32]),
          writes=[C["abc"]])
    S.op("act", lambda e: e.activation(out=C["abc"][:, :], in_=C["abc"][:, :], func=AF.Exp), reads=[C["abc"]],
         writes=[C["abc"]])
    S.op("dve", lambda e: e.tensor_scalar(out=C["abc"][:, :], in0=C["abc"][:, :], scalar1=-1.0, scalar2=0.0,
                                          op0=ALU.mult, op1=ALU.add), reads=[C["abc"]], writes=[C["abc"]])
    dsk = K.alloc([128, 32], F32)
    S.dma("sp", dsk[:, :], A["ssm_d"][l].rearrange("a b -> (a b)").unsqueeze(0).broadcast_to([128, 32]), writes=[dsk])
    C["dsum"] = K.alloc([128, 16], F32)
    S.op("dve", lambda e: e.tensor_tensor(out=C["dsum"][:, :], in0=dsk[:, 0:16], in1=dsk[:, 16:32], op=ALU.add),
         reads=[dsk], writes=[C["dsum"]])
    C["dI"] = K.alloc([128, 16, 128], F32)
    for h in range(16):
        S.op("dve", lambda e, h=h: e.tensor_scalar(out=C["dI"][:, h, :], in0=K.identf[:, :], scalar1=C["dsum"][:, h:h + 1],
                                                   scalar2=0.0, op0=ALU.mult, op1=ALU.add),
             reads=[K.identf, C["dsum"]], writes=[C["dI"]])
    C["ng"] = K.alloc([128, 1024], F32)
    S.dma("sp", C["ng"][:, :], A["ssm_norm_g"][l:l + 1, :].broadcast_to([128, 1024]), writes=[C["ng"]])
    return C


import os as _os
NB = int(_os.environ.get('MK_NB', '2'))


def ssd_pools(K):
    P = {}
    P["xw"] = Pool(K, NB, [128, 12, 132], F32)
    P["acc"] = Pool(K, 1, [128, 12, 128], F32)
    P["tmp"] = Pool(K, 1, [128, 12, 128], F32)
    P["xbcT"] = Pool(K, NB, [128, 12, 128], BF16)
    P["xs"] = Pool(K, NB, [128, 1024], BF16)
    P["bm"] = Pool(K, NB, [128, 256], BF16)
    P["dt"] = Pool(K, NB, [128, 32], F32)
    P["v"] = Pool(K, NB, [128, 256], F32)
    return P


def ssd_prep(K, SC, C, P, c, nchunk, cps, need_cm=True):
    S = K.S
    t0 = c * 128
    ntok = nchunk * 128
    seg, cis = divmod(c, cps)
    xw = P["xw"].get()
    XB = SC["XBCT"].rearrange("(fc p) t -> p fc t", p=128)
    lo, hi = max(t0 - 2, 0), min(t0 + 130, ntok)
    S.dma("sp", xw[:, :, lo - (t0 - 2):hi - (t0 - 2)], XB[:, :, lo:hi], writes=[xw])
    if t0 == 0:
        S.op("pool", lambda e: e.memset(xw[:, :, 0:2], 0.0), writes=[xw])
    elif cis == 0:
        S.op("pool", lambda e, seg=seg: e.tensor_scalar(out=xw[:, :, 0:2], in0=xw[:, :, 0:2],
                                                        scalar1=K.flags[:, seg - 1:seg], scalar2=0.0,
                                                        op0=ALU.mult, op1=ALU.add), reads=[xw, K.flags], writes=[xw])
    if t0 + 130 > ntok:
        S.op("pool", lambda e: e.memset(xw[:, :, 130:132], 0.0), writes=[xw])
    elif cis == cps - 1:
        S.op("pool", lambda e, seg=seg: e.tensor_scalar(out=xw[:, :, 130:132], in0=xw[:, :, 130:132],
                                                        scalar1=K.flags[:, seg:seg + 1], scalar2=0.0,
                                                        op0=ALU.mult, op1=ALU.add), reads=[xw, K.flags], writes=[xw])
    acc = P["acc"].get()
    tmp = P["tmp"].get()
    cw = C["cw"]
    for k in range(5):
        wk = cw[:, :, k:k + 1].broadcast_to([128, 12, 128])
        if k == 0:
            S.op("dve", lambda e, wk=wk: e.tensor_tensor(out=acc[:, :, :], in0=xw[:, :, 0:128], in1=wk, op=ALU.mult),
                 reads=[xw, cw], writes=[acc])
        else:
            S.op("pool", lambda e, wk=wk, k=k: e.tensor_tensor(out=tmp[:, :, :], in0=xw[:, :, k:k + 128], in1=wk,
                                                               op=ALU.mult), reads=[xw, cw], writes=[tmp])
            S.op("dve", lambda e: e.tensor_tensor(out=acc[:, :, :], in0=acc[:, :, :], in1=tmp[:, :, :], op=ALU.add),
                 reads=[acc, tmp], writes=[acc])
    xbcT = P["xbcT"].get()
    for fc in range(12):
        S.op("act", lambda e, fc=fc: e.activation(out=xbcT[:, fc, :], in_=acc[:, fc, :], func=AF.Silu,
                                                  bias=C["cb"][:, fc:fc + 1]), reads=[acc, C["cb"]], writes=[xbcT])
    xs = P["xs"].get()
    pt = K.psum()
    ptb = pt.t[:, :].bitcast(BF16)
    for j in range(8):
        S.op("pe", lambda e, j=j: e.transpose(ptb[:, j * 128:(j + 1) * 128], xbcT[:, j, :], K.ident[:, :]),
             reads=[xbcT, K.ident], writes=[pt])
    S.op("act", lambda e: e.copy(out=xs[:, :], in_=ptb[:, :]), reads=[pt], writes=[xs])
    bm = P["bm"].get()
    pt2 = K.psum()
    ptb2 = pt2.t[:, :].bitcast(BF16)
    for g in range(2):
        S.op("pe", lambda e, g=g: e.transpose(ptb2[:, g * 128:(g + 1) * 128], xbcT[:, 8 + g, :], K.ident[:, :]),
             reads=[xbcT, K.ident], writes=[pt2])
    S.op("act", lambda e: e.copy(out=bm[:, :], in_=ptb2[:, 0:256]), reads=[pt2], writes=[bm])
    dt = P["dt"].get()
    S.dma("sp", dt[:, :], SC["DT"][t0:t0 + 128, :], writes=[dt])
    V = P["v"].get()
    S.op("dve", lambda e: e.tensor_tensor(out=V[:, 0:32], in0=dt[:, :], in1=C["dtb"][:, :], op=ALU.add),
         reads=[dt, C["dtb"]], writes=[V])
    S.op("act", lambda e: e.activation(out=V[:, 0:32], in_=V[:, 0:32], func=AF.Exp), reads=[V], writes=[V])
    S.op("act", lambda e: e.activation(out=V[:, 0:32], in_=V[:, 0:32], func=AF.Ln, bias=K.onec[:, 0:1]),
         reads=[V, K.onec], writes=[V])
    S.op("act", lambda e: e.activation(out=V[:, 32:64], in_=V[:, 0:32], func=AF.Ln), reads=[V], writes=[V])
    S.op("dve", lambda e: e.tensor_tensor(out=V[:, 64:96], in0=V[:, 0:32], in1=C["abc"][:, :], op=ALU.mult),
         reads=[V, C["abc"]], writes=[V])
    pc = K.psum()
    S.op("pe", lambda e: e.matmul(pc[:, 0:16], K.uinc[:, :], V[:, 64:80], start=True, stop=True),
         reads=[K.uinc, V], writes=[pc])
    S.op("pe", lambda e: e.matmul(pc[:, 16:32], K.uexc[:, :], V[:, 80:96], start=True, stop=True),
         reads=[K.uexc, V], writes=[pc])
    S.op("pe", lambda e: e.matmul(pc[:, 32:64], K.onesf[:, :], V[:, 64:96], start=True, stop=True),
         reads=[K.onesf, V], writes=[pc])
    S.op("dve", lambda e: e.tensor_copy(out=V[:, 96:160], in_=pc[:, 0:64]), reads=[pc], writes=[V])
    S.op("dve", lambda e: e.tensor_tensor(out=V[:, 160:176], in0=V[:, 32:48], in1=V[:, 96:112], op=ALU.subtract),
         reads=[V], writes=[V])
    S.op("dve", lambda e: e.tensor_tensor(out=V[:, 176:192], in0=V[:, 48:64], in1=V[:, 112:128], op=ALU.add),
         reads=[V], writes=[V])
    S.op("dve", lambda e: e.tensor_tensor(out=V[:, 192:208], in0=V[:, 160:176], in1=V[:, 128:144], op=ALU.add),
         reads=[V], writes=[V])
    S.op("act", lambda e: e.activation(out=V[:, 192:208], in_=V[:, 192:208], func=AF.Exp), reads=[V], writes=[V])
    S.op("act", lambda e: e.activation(out=V[:, 208:224], in_=V[:, 176:192], func=AF.Exp), reads=[V], writes=[V])
    S.op("act", lambda e: e.activation(out=V[:, 224:240], in_=V[:, 96:112], func=AF.Exp), reads=[V], writes=[V])
    S.op("dve", lambda e: e.tensor_tensor(out=V[:, 240:256], in0=V[:, 144:160], in1=V[:, 112:128], op=ALU.subtract),
         reads=[V], writes=[V])
    S.op("act", lambda e: e.activation(out=V[:, 240:256], in_=V[:, 240:256], func=AF.Exp), reads=[V], writes=[V])
    S.op("act", lambda e: e.activation(out=V[:, 128:160], in_=V[:, 128:160], func=AF.Exp), reads=[V], writes=[V])
    return {"xbcT": xbcT, "xs": xs, "bm": bm, "V": V}


def ssd_state_step(K, T, H, wcol, ecol, PS):
    S = K.S
    xs, bm, V = T["xs"], T["bm"], T["V"]
    xd = PS["xd"].get()
    S.op("pool", lambda e: e.tensor_tensor(out=xd[:, :].rearrange("p (h d) -> p h d", h=16),
                                           in0=xs[:, :].rearrange("p (h d) -> p h d", h=16),
                                           in1=V[:, wcol:wcol + 16].unsqueeze(2).broadcast_to([128, 16, 64]),
                                           op=ALU.mult), reads=[xs, V], writes=[xd])
    S.op("dve", lambda e: e.tensor_tensor(out=H[:, :].rearrange("p (h d) -> p h d", h=16),
                                          in0=H[:, :].rearrange("p (h d) -> p h d", h=16),
                                          in1=V[:, ecol:ecol + 16].unsqueeze(2).broadcast_to([128, 16, 64]),
                                          op=ALU.mult), reads=[H, V], writes=[H])
    for g in range(2):
        pp = K.psum()
        S.op("pe", lambda e, pp=pp, g=g: e.matmul(pp[:, :], bm[:, g * 128:(g + 1) * 128], xd[:, g * 512:(g + 1) * 512],
                                                  start=True, stop=True), reads=[bm, xd], writes=[pp])
        S.op("dve", lambda e, pp=pp, g=g: e.tensor_tensor(out=H[:, g * 512:(g + 1) * 512], in0=H[:, g * 512:(g + 1) * 512],
                                                          in1=pp[:, :], op=ALU.add), reads=[pp, H], writes=[H])


def ssd_bwd_pass(K, SC, C, nchunk, cps):
    S = K.S
    P = ssd_pools(K)
    PS = {"xd": Pool(K, NB, [128, 1024], BF16)}
    H = K.alloc([128, 1024], F32)
    hbp = Pool(K, NB, [128, 1024], BF16)
    S.op("dve", lambda e: e.memset(H[:, :], 0.0), writes=[H])
    def body(c):
        seg, cis = divmod(c, cps)
        if cis == cps - 1 and c != nchunk - 1:
            S.op("dve", lambda e, seg=seg: e.tensor_scalar(out=H[:, :], in0=H[:, :], scalar1=K.flags[:, seg:seg + 1],
                                                           scalar2=0.0, op0=ALU.mult, op1=ALU.add),
                 reads=[H, K.flags], writes=[H])
        T = ssd_prep(K, SC, C, P, c, nchunk, cps)
        ssd_state_step(K, T, H, 208, 144, PS)
        hb = hbp.get()
        S.op("act", lambda e, hb=hb: e.copy(out=hb[:, :], in_=H[:, :]), reads=[H], writes=[hb])
        S.dma("pool", SC["HB"][c], hb[:, :], reads=[hb])

    for c in range(nchunk - 1, -1, -1):
        body(c)


def ssd_fwd_pass(K, SC, C, nchunk, cps):
    S = K.S
    P = ssd_pools(K)
    PS = {"xd": Pool(K, NB, [128, 1024], BF16)}
    H = K.alloc([128, 1024], F32)
    Hbf = K.alloc([128, 1024], BF16)
    S.op("dve", lambda e: e.memset(H[:, :], 0.0), writes=[H])
    hbp = Pool(K, NB, [128, 1024], BF16)
    cbp = Pool(K, NB, [128, 4, 128], F32)
    tq = Pool(K, NB, [128, 4, 128], F32)
    eq = Pool(K, NB, [128, 4, 128], F32)
    mq = Pool(K, NB, [128, 4, 128], F32)
    Mp = Pool(K, NB, [128, 16, 128], BF16)
    yp = Pool(K, NB, [128, 1024], F32)
    t1p = Pool(K, NB, [128, 512], F32)
    zp = Pool(K, NB, [128, 1024], F32)
    ybp = Pool(K, NB, [128, 1024], BF16)
    smp = Pool(K, 4, [128, 8], F32)
    jk = Pool(K, 1, [128, 512], BF16)
    def body(c):
        seg, cis = divmod(c, cps)
        t0 = c * 128
        if cis == 0 and c != 0:
            S.op("dve", lambda e, seg=seg: e.tensor_scalar(out=H[:, :], in0=H[:, :], scalar1=K.flags[:, seg - 1:seg],
                                                           scalar2=0.0, op0=ALU.mult, op1=ALU.add),
                 reads=[H, K.flags], writes=[H])
        S.op("act", lambda e: e.copy(out=Hbf[:, :], in_=H[:, :]), reads=[H], writes=[Hbf])
        hb = hbp.get()
        if c == nchunk - 1:
            S.op("pool", lambda e, hb=hb: e.memset(hb[:, :], 0.0), writes=[hb])
        else:
            S.dma("sp", hb[:, :], SC["HB"][c + 1], writes=[hb])
            if cis == cps - 1:
                S.op("pool", lambda e, hb=hb, seg=seg: e.tensor_scalar(
                    out=hb[:, :], in0=hb[:, :], scalar1=K.flags[:, seg:seg + 1], scalar2=0.0, op0=ALU.mult,
                    op1=ALU.add), reads=[hb, K.flags], writes=[hb])
        T = ssd_prep(K, SC, C, P, c, nchunk, cps)
        xbcT, xs, V = T["xbcT"], T["xs"], T["V"]
        cbm = cbp.get()
        for g in range(2):
            pp = K.psum()
            S.op("pe", lambda e, pp=pp, g=g: e.matmul(pp[:, 0:128], xbcT[:, 8 + g, :], xbcT[:, 10 + g, :],
                                                      start=True, stop=True), reads=[xbcT], writes=[pp])
            S.op("dve", lambda e, pp=pp, g=g: e.tensor_tensor(out=cbm[:, 2 * g, :], in0=pp[:, 0:128], in1=K.mskf[:, :],
                                                              op=ALU.mult), reads=[pp, K.mskf], writes=[cbm])
            S.op("dve", lambda e, pp=pp, g=g: e.tensor_tensor(out=cbm[:, 2 * g + 1, :], in0=pp[:, 0:128],
                                                              in1=K.mskb[:, :], op=ALU.mult),
                 reads=[pp, K.mskb], writes=[cbm])
        M = Mp.get()
        for qd in range(4):
            g = qd // 2
            mqs = []
            for d in range(2):
                pa = K.psum()
                for hh in range(4):
                    h = qd * 4 + hh
                    col = 64 + d * 16 + h
                    S.op("pe", lambda e, pa=pa, hh=hh, col=col, d=d: e.matmul(
                        pa[:, hh * 128:(hh + 1) * 128], V[:, col:col + 1].broadcast_to([128, 128]),
                        (K.uinc if d == 0 else K.uexc)[:, :], start=True, stop=True),
                        reads=[V, K.uinc, K.uexc], writes=[pa])
                t = tq.get()
                pav = pa[:, :].rearrange("p (a b) -> p a b", a=4)
                if d == 0:
                    vb = V[:, 96 + qd * 4:96 + qd * 4 + 4].unsqueeze(2).broadcast_to([128, 4, 128])
                    S.op("dve", lambda e, t=t, pav=pav, vb=vb: e.tensor_tensor(out=t[:, :, :], in0=pav, in1=vb,
                                                                               op=ALU.subtract),
                         reads=[pa, V], writes=[t])
                else:
                    vb = V[:, 112 + qd * 4:112 + qd * 4 + 4].unsqueeze(2).broadcast_to([128, 4, 128])
                    S.op("dve", lambda e, t=t, pav=pav, vb=vb: e.tensor_tensor(out=t[:, :, :], in0=vb, in1=pav,
                                                                               op=ALU.subtract),
                         reads=[pa, V], writes=[t])
                S.op("dve", lambda e, t=t: e.tensor_scalar(out=t[:, :, :], in0=t[:, :, :], scalar1=0.0, scalar2=0.0,
                                                            op0=ALU.min, op1=ALU.add), reads=[t], writes=[t])
                E = eq.get()
                for hh in range(4):
                    h = qd * 4 + hh
                    S.op("act", lambda e, E=E, t=t, hh=hh, h=h, d=d: e.activation(
                        out=E[:, hh, :], in_=t[:, hh, :], func=AF.Exp, bias=V[:, 32 + d * 16 + h:33 + d * 16 + h]),
                        reads=[t, V], writes=[E])
                m_ = mq.get()
                S.op("dve", lambda e, m_=m_, E=E, g=g, d=d: e.tensor_tensor(
                    out=m_[:, :, :], in0=E[:, :, :], in1=cbm[:, 2 * g + d:2 * g + d + 1, :].broadcast_to([128, 4, 128]),
                    op=ALU.mult), reads=[E, cbm], writes=[m_])
                mqs.append(m_)
            S.op("dve", lambda e, a=mqs[0], b=mqs[1]: e.tensor_tensor(out=a[:, :, :], in0=a[:, :, :], in1=b[:, :, :],
                                                                       op=ALU.add), reads=[mqs[0], mqs[1]], writes=[mqs[0]])
            S.op("dve", lambda e, a=mqs[0], qd=qd: e.tensor_tensor(out=M[:, qd * 4:(qd + 1) * 4, :], in0=a[:, :, :],
                                                                    in1=C["dI"][:, qd * 4:(qd + 1) * 4, :], op=ALU.add),
                 reads=[mqs[0], C["dI"]], writes=[M])
        yi = [K.psum(), K.psum()]
        yo = [[K.psum(), K.psum()], [K.psum(), K.psum()]]
        for h in range(16):
            g, hh = divmod(h, 8)
            S.op("pe", lambda e, h=h, g=g, hh=hh: e.matmul(yi[g][:, hh * 64:(hh + 1) * 64], M[:, h, :],
                                                           xs[:, h * 64:(h + 1) * 64], start=True, stop=True),
                 reads=[M, xs], writes=[yi[g]])
        for d, Hs in enumerate((Hbf, hb)):
            for g in range(2):
                S.op("pe", lambda e, d=d, g=g, Hs=Hs: e.matmul(yo[d][g][:, :], xbcT[:, 10 + g, :],
                                                               Hs[:, g * 512:(g + 1) * 512], start=True, stop=True),
                     reads=[xbcT, Hs], writes=[yo[d][g]])
        y = yp.get()
        for g in range(2):
            t1 = t1p.get()
            scf = V[:, 224 + g * 8:232 + g * 8].unsqueeze(2).broadcast_to([128, 8, 64])
            scb = V[:, 240 + g * 8:248 + g * 8].unsqueeze(2).broadcast_to([128, 8, 64])
            S.op("dve", lambda e, t1=t1, g=g, scf=scf: e.tensor_tensor(
                out=t1[:, :].rearrange("p (h d) -> p h d", h=8), in0=yo[0][g][:, :].rearrange("p (h d) -> p h d", h=8),
                in1=scf, op=ALU.mult), reads=[yo[0][g], V], writes=[t1])
            S.op("dve", lambda e, t1=t1, g=g: e.tensor_tensor(out=t1[:, :], in0=t1[:, :], in1=yi[g][:, :], op=ALU.add),
                 reads=[t1, yi[g]], writes=[t1])
            S.op("dve", lambda e, g=g, scb=scb, y=y: e.tensor_tensor(
                out=y[:, g * 512:(g + 1) * 512].rearrange("p (h d) -> p h d", h=8),
                in0=yo[1][g][:, :].rearrange("p (h d) -> p h d", h=8), in1=scb, op=ALU.mult),
                reads=[yo[1][g], V], writes=[y])
            S.op("pool", lambda e, g=g, t1=t1, y=y: e.tensor_tensor(out=y[:, g * 512:(g + 1) * 512],
                                                                    in0=y[:, g * 512:(g + 1) * 512], in1=t1[:, :],
                                                                    op=ALU.add), reads=[t1, y], writes=[y])
        if "DY" in SC:
            S.dma("pool", SC["DY"][t0:t0 + 128, :], y[:, :], reads=[y])
            S.dma("pool", SC["DV"][t0:t0 + 128, :], V[:, :], reads=[V])
            S.dma("pool", SC["DM"][c], M[:, :, :], reads=[M])
        z = zp.get()
        S.dma("sp", z[:, :], SC["Z"][t0:t0 + 128, :], writes=[z])
        S.op("act", lambda e, z=z: e.activation(out=z[:, :], in_=z[:, :], func=AF.Silu), reads=[z], writes=[z])
        S.op("pool", lambda e, z=z, y=y: e.tensor_tensor(out=y[:, :], in0=y[:, :], in1=z[:, :], op=ALU.mult),
             reads=[y, z], writes=[y])
        sm = smp.get()
        j_ = jk.get()
        yb = ybp.get()
        for g in range(2):
            S.op("act", lambda e, g=g, sm=sm, j_=j_, y=y: e.activation(
                out=j_[:, :], in_=y[:, g * 512:(g + 1) * 512], func=AF.Square, accum_out=sm[:, g:g + 1]),
                reads=[y], writes=[j_, sm])
        S.op("act", lambda e, sm=sm: e.activation(out=sm[:, 2:4], in_=sm[:, 0:2], func=AF.Sqrt, bias=K.epsc[:, 0:1],
                                                  scale=1.0 / 512), reads=[sm, K.epsc], writes=[sm])
        S.op("dve", lambda e, sm=sm: e.reciprocal(out=sm[:, 4:6], in_=sm[:, 2:4]), reads=[sm], writes=[sm])
        for g in range(2):
            S.op("dve", lambda e, g=g, sm=sm, y=y, yb=yb: e.scalar_tensor_tensor(
                out=yb[:, g * 512:(g + 1) * 512], in0=y[:, g * 512:(g + 1) * 512], scalar=sm[:, 4 + g:5 + g],
                in1=C["ng"][:, g * 512:(g + 1) * 512], op0=ALU.mult, op1=ALU.mult),
                reads=[y, sm, C["ng"]], writes=[yb])
        S.dma("pool", SC["YS"][t0:t0 + 128, :], yb[:, :], reads=[yb])
        ssd_state_step(K, T, H, 192, 128, PS)

    for c in range(nchunk):
        body(c)


RL = float(_os.environ.get('MK_RL', '99'))
LW_C = 0.6065306597126334
GN_EPS = 64e-5


def rwkv_setup(K, A, l):
    S = K.S
    C = {}

    def ld(name, shape, dt, q, src, slow=False):
        C[name] = K.alloc(shape, dt)
        S.dma(q, C[name][tuple(slice(None) for _ in shape)], src, writes=[C[name]], slow=slow)

    ld("wup", [64, 2, 512], BF16, "pool", A["rwkv_w_up"][l].rearrange("n l c -> l n c"))
    ld("aup", [64, 2, 512], BF16, "pool", A["rwkv_a_up"][l].rearrange("n l c -> l n c"))
    C["gup"] = K.alloc([64, 3, 512], BF16)
    S.dma("pool", C["gup"][:, 0:2, :], A["rwkv_g_up"][l][0:128, :].rearrange("(q l) c -> l q c", l=64), writes=[C["gup"]])
    S.dma("pool", C["gup"][0:32, 2, :], A["rwkv_g_up"][l][128:160, :], writes=[C["gup"]])
    C["w0"] = K.alloc([128, 2, 512], F32)
    for n in range(2):
        S.dma("sp", C["w0"][:, n, :], A["rwkv_w0"][l][n:n + 1, :].broadcast_to([128, 512]), writes=[C["w0"]])
    C["a0"] = K.alloc([64, 2, 8], F32)
    for n in range(2):
        S.dma("sp", C["a0"][:, n, :], A["rwkv_a0"][l][n].rearrange("(h j) -> j h", j=64), writes=[C["a0"]], slow=True)
    for nm, src in (("kk_", A["rwkv_k_k"][l]), ("ka", A["rwkv_k_a"][l])):
        C[nm] = K.alloc([64, 8], F32)
        S.dma("sp", C[nm][:, :], src.rearrange("(h j) -> j h", j=64), writes=[C[nm]], slow=True)
    C["rk"] = K.alloc([64, 8], F32)
    S.dma("sp", C["rk"][:, :], A["rwkv_r_k"][l].rearrange("h j -> j h"), writes=[C["rk"]], slow=True)
    C["omka"] = K.alloc([64, 8], F32)
    S.op("dve", lambda e: e.tensor_scalar(out=C["omka"][:, :], in0=C["ka"][:, :], scalar1=-1.0, scalar2=1.0,
                                          op0=ALU.mult, op1=ALU.add), reads=[C["ka"]], writes=[C["omka"]])
    C["mu"] = K.alloc([64, 3, 31], F32)
    S.op("dve", lambda e: e.memset(C["mu"][:, :, :], 0.0), writes=[C["mu"]])
    for m in range(2):
        S.dma("sp", C["mu"][:, m, 0:30], A["rwkv_mu"][l][m, 0:1920].rearrange("(g j) -> j g", j=64), writes=[C["mu"]],
              slow=True)
        S.dma("sp", C["mu"][0:32, m, 30:31], A["rwkv_mu"][l][m, 1920:1952].rearrange("(g j) -> j g", j=32),
              writes=[C["mu"]], slow=True)
    S.op("dve", lambda e: e.tensor_tensor(out=C["mu"][:, 2, :], in0=C["mu"][:, 0, :], in1=C["mu"][:, 1, :], op=ALU.add),
         reads=[C["mu"]], writes=[C["mu"]])
    S.op("dve", lambda e: e.tensor_scalar(out=C["mu"][:, 2, :], in0=C["mu"][:, 2, :], scalar1=-1.0, scalar2=1.0,
                                          op0=ALU.mult, op1=ALU.add), reads=[C["mu"]], writes=[C["mu"]])
    C["lng"] = K.alloc([128, 512], F32)
    S.dma("sp", C["lng"][:, :], A["rwkv_ln_g"][l:l + 1, :].broadcast_to([128, 512]), writes=[C["lng"]])
    C["lnb"] = K.alloc([128, 512], F32)
    S.dma("sp", C["lnb"][:, :], A["rwkv_ln_b"][l:l + 1, :].broadcast_to([128, 512]), writes=[C["lnb"]])
    C["gneps"] = K.alloc([128, 1], F32)
    S.op("dve", lambda e: e.memset(C["gneps"][:, :], GN_EPS), writes=[C["gneps"]])
    return C


def rwkv_pools(K):
    P = {}
    P["X"] = Pool(K, 2, [64, 31, 130], F32)
    P["Pt"] = Pool(K, 1, [64, 31, 128], F32)
    P["tmp31"] = Pool(K, 1, [64, 31, 128], F32)
    P["tw"] = Pool(K, 2, [64, 128], BF16)
    P["ad"] = Pool(K, 2, [64, 128], BF16)
    P["sg"] = Pool(K, 2, [128, 512], F32)
    P["f8"] = Pool(K, 11, [64, 8, 128], F32)
    P["b8"] = Pool(K, 8, [64, 8, 128], BF16)
    P["tok"] = Pool(K, 6, [128, 512], BF16)
    P["vf"] = Pool(K, 2, [128, 512], F32)
    P["pl"] = Pool(K, 2, [64, 8], F32)
    P["mat"] = Pool(K, 8, [128, 4, 128], BF16)
    P["res"] = Pool(K, 12, [128, 4, 128], BF16)
    P["w"] = Pool(K, 2, [128, 512], F32)
    P["rkc"] = Pool(K, 2, [128, 8], F32)
    return P


def rwkv_prep(K, SC, C, P, c, nchunk, cps, n, want_g):
    S = K.S
    t0 = c * 128
    ntok = nchunk * 128
    seg, cis = divmod(c, cps)
    X = P["X"].get()
    lo, hi = max(t0 - 1, 0), min(t0 + 129, ntok)
    a_, b_ = lo - (t0 - 1), hi - (t0 - 1)
    S.dma("sp", X[:, 0:30, a_:b_], SC["RWT"][0:1920, lo:hi].rearrange("(g j) t -> j g t", j=64), writes=[X])
    S.dma("sp", X[0:32, 30, a_:b_], SC["RWT"][1920:1952, lo:hi], writes=[X])
    if t0 == 0:
        S.op("pool", lambda e: e.memset(X[:, :, 0:1], 0.0), writes=[X])
    elif cis == 0:
        S.op("pool", lambda e: e.tensor_scalar(out=X[:, :, 0:1], in0=X[:, :, 0:1], scalar1=K.flags[0:64, seg - 1:seg],
                                               scalar2=0.0, op0=ALU.mult, op1=ALU.add), reads=[X, K.flags], writes=[X])
    if t0 + 129 > ntok:
        S.op("pool", lambda e: e.memset(X[:, :, 129:130], 0.0), writes=[X])
    elif cis == cps - 1:
        S.op("pool", lambda e: e.tensor_scalar(out=X[:, :, 129:130], in0=X[:, :, 129:130],
                                               scalar1=K.flags[0:64, seg:seg + 1], scalar2=0.0, op0=ALU.mult,
                                               op1=ALU.add), reads=[X, K.flags], writes=[X])
    Pt = P["Pt"].get()
    tmp = P["tmp31"].get()
    mu = C["mu"]
    bc = lambda m: mu[:, m, :].unsqueeze(2).broadcast_to([64, 31, 128])
    S.op("dve", lambda e: e.tensor_tensor(out=Pt[:, :, :], in0=X[:, :, 1:129], in1=bc(2), op=ALU.mult),
         reads=[X, mu], writes=[Pt])
    S.op("pool", lambda e: e.tensor_tensor(out=tmp[:, :, :], in0=X[:, :, 0:128], in1=bc(0), op=ALU.mult),
         reads=[X, mu], writes=[tmp])
    S.op("dve", lambda e: e.tensor_tensor(out=Pt[:, :, :], in0=Pt[:, :, :], in1=tmp[:, :, :], op=ALU.add),
         reads=[Pt, tmp], writes=[Pt])
    S.op("pool", lambda e: e.tensor_tensor(out=tmp[:, :, :], in0=X[:, :, 2:130], in1=bc(1), op=ALU.mult),
         reads=[X, mu], writes=[tmp])
    S.op("dve", lambda e: e.tensor_tensor(out=Pt[:, :, :], in0=Pt[:, :, :], in1=tmp[:, :, :], op=ALU.add),
         reads=[Pt, tmp], writes=[Pt])
    if RL <= 1:
        return None
    rT, kT, vT = Pt[:, 0:8, :], Pt[:, 8:16, :], Pt[:, 16:24, :]
    tw = P["tw"].get()
    S.op("act", lambda e: e.activation(out=tw[:, :], in_=Pt[:, 24 + n, :], func=AF.Tanh), reads=[Pt], writes=[tw])
    pu = K.psum()
    S.op("pe", lambda e: e.matmul(pu[:, :], tw[:, :], C["wup"][:, n, :], start=True, stop=True),
         reads=[tw, C["wup"]], writes=[pu])
    sg = P["sg"].get()
    S.op("dve", lambda e: e.tensor_tensor(out=sg[:, :], in0=pu[:, :], in1=C["w0"][:, n, :], op=ALU.add),
         reads=[pu, C["w0"]], writes=[sg])
    S.op("act", lambda e: e.activation(out=sg[:, :], in_=sg[:, :], func=AF.Sigmoid), reads=[sg], writes=[sg])
    if RL <= 2:
        return None
    uin = K.uinc if n == 0 else K.mskb
    uex = K.uexc if n == 0 else K.ugt
    eP, eN, ePx = P["f8"].get(), P["f8"].get(), P["f8"].get()
    for hq in range(2):
        pi, px = K.psum(), K.psum()
        for hh in range(4):
            h = hq * 4 + hh
            S.op("pe", lambda e, pi=pi, hh=hh, h=h: e.matmul(pi[0:64, hh * 128:(hh + 1) * 128], sg[:, h * 64:(h + 1) * 64],
                                                             uin[:, :], start=True, stop=True),
                 reads=[sg, uin], writes=[pi])
            S.op("pe", lambda e, px=px, hh=hh, h=h: e.matmul(px[0:64, hh * 128:(hh + 1) * 128], sg[:, h * 64:(h + 1) * 64],
                                                             uex[:, :], start=True, stop=True),
                 reads=[sg, uex], writes=[px])
        sl = slice(hq * 4, hq * 4 + 4)
        S.op("act", lambda e, pi=pi, sl=sl: e.activation(out=eP[:, sl, :].rearrange("p a b -> p (a b)"), in_=pi[0:64, :],
                                                         func=AF.Exp, scale=-LW_C), reads=[pi], writes=[eP])
        S.op("act", lambda e, pi=pi, sl=sl: e.activation(out=eN[:, sl, :].rearrange("p a b -> p (a b)"), in_=pi[0:64, :],
                                                         func=AF.Exp, scale=LW_C), reads=[pi], writes=[eN])
        S.op("act", lambda e, px=px, sl=sl: e.activation(out=ePx[:, sl, :].rearrange("p a b -> p (a b)"), in_=px[0:64, :],
                                                         func=AF.Exp, scale=-LW_C), reads=[px], writes=[ePx])
    last = 127 if n == 0 else 0
    pl = P["pl"].get()
    S.op("dve", lambda e: e.tensor_copy(out=pl[:, :], in_=eP[:, :, last]), reads=[eP], writes=[pl])
    if RL <= 3:
        return None
    ad = P["ad"].get()
    S.op("act", lambda e: e.copy(out=ad[:, :], in_=Pt[:, 26 + n, :]), reads=[Pt], writes=[ad])
    ic = P["f8"].get()
    for hq in range(2):
        pa = K.psum()
        for hh in range(4):
            h = hq * 4 + hh
            S.op("pe", lambda e, pa=pa, hh=hh, h=h: e.matmul(pa[0:64, hh * 128:(hh + 1) * 128],
                                                             C["aup"][:, n, h * 64:(h + 1) * 64], ad[:, :],
                                                             start=True, stop=True), reads=[C["aup"], ad], writes=[pa])
        sl = slice(hq * 4, hq * 4 + 4)
        S.op("dve", lambda e, pa=pa, sl=sl: e.tensor_tensor(
            out=ic[:, sl, :], in0=pa[0:64, :].rearrange("p (a b) -> p a b", a=4),
            in1=C["a0"][:, n, sl].unsqueeze(2).broadcast_to([64, 4, 128]), op=ALU.add), reads=[pa, C["a0"]], writes=[ic])
    S.op("act", lambda e: e.activation(out=ic[:, :, :], in_=ic[:, :, :], func=AF.Sigmoid), reads=[ic], writes=[ic])
    if RL <= 4:
        return None
    kk = P["f8"].get()
    sq = P["f8"].get()
    h8 = lambda t_: t_[:, :].unsqueeze(2).broadcast_to([64, 8, 128])
    S.op("dve", lambda e: e.tensor_tensor(out=kk[:, :, :], in0=kT, in1=h8(C["kk_"]), op=ALU.mult),
         reads=[Pt, C["kk_"]], writes=[kk])
    S.op("pool", lambda e: e.tensor_tensor(out=sq[:, :, :], in0=kk[:, :, :], in1=kk[:, :, :], op=ALU.mult),
         reads=[kk], writes=[sq])
    for hq in range(2):
        pn = K.psum()
        sl = slice(hq * 4, hq * 4 + 4)
        S.op("pe", lambda e, pn=pn, sl=sl: e.matmul(pn[0:64, :], K.onesf[0:64, 0:64],
                                                    sq[:, sl, :].rearrange("p a b -> p (a b)"), start=True, stop=True),
             reads=[K.onesf, sq], writes=[pn])
        S.op("act", lambda e, pn=pn, sl=sl: e.activation(out=sq[:, sl, :].rearrange("p a b -> p (a b)"), in_=pn[0:64, :],
                                                         func=AF.Sqrt), reads=[pn], writes=[sq])
    S.op("dve", lambda e: e.tensor_scalar(out=sq[:, :, :], in0=sq[:, :, :], scalar1=1e-12, scalar2=0.0, op0=ALU.max,
                                          op1=ALU.add), reads=[sq], writes=[sq])
    S.op("dve", lambda e: e.reciprocal(out=sq[:, :, :], in_=sq[:, :, :]), reads=[sq], writes=[sq])
    S.op("dve", lambda e: e.tensor_tensor(out=kk[:, :, :], in0=kk[:, :, :], in1=sq[:, :, :], op=ALU.mult),
         reads=[kk, sq], writes=[kk])
    if RL <= 5:
        return None
    km = P["f8"].get()
    S.op("pool", lambda e: e.tensor_tensor(out=km[:, :, :], in0=ic[:, :, :], in1=h8(C["ka"]), op=ALU.mult),
         reads=[ic, C["ka"]], writes=[km])
    S.op("pool", lambda e: e.tensor_tensor(out=km[:, :, :], in0=km[:, :, :], in1=h8(C["omka"]), op=ALU.add),
         reads=[km, C["omka"]], writes=[km])
    S.op("pool", lambda e: e.tensor_tensor(out=km[:, :, :], in0=km[:, :, :], in1=kT, op=ALU.mult),
         reads=[km, Pt], writes=[km])
    bb = P["f8"].get()
    S.op("dve", lambda e: e.tensor_tensor(out=bb[:, :, :], in0=kk[:, :, :], in1=ic[:, :, :], op=ALU.mult),
         reads=[kk, ic], writes=[bb])
    RtT, AtT, BhT, KhT = [P["b8"].get() for _ in range(4)]
    BbT, KbT = P["f8"].get(), P["f8"].get()
    S.op("dve", lambda e: e.tensor_tensor(out=RtT[:, :, :], in0=rT, in1=eP[:, :, :], op=ALU.mult),
         reads=[Pt, eP], writes=[RtT])
    S.op("dve", lambda e: e.scalar_tensor_tensor(out=AtT[:, :, :], in0=kk[:, :, :], scalar=-1.0, in1=ePx[:, :, :],
                                                 op0=ALU.mult, op1=ALU.mult), reads=[kk, ePx], writes=[AtT])
    S.op("pool", lambda e: e.tensor_tensor(out=bb[:, :, :], in0=bb[:, :, :], in1=eN[:, :, :], op=ALU.mult),
         reads=[bb, eN], writes=[bb])
    S.op("pool", lambda e: e.tensor_tensor(out=sq[:, :, :], in0=km[:, :, :], in1=eN[:, :, :], op=ALU.mult),
         reads=[km, eN, sq], writes=[sq])
    S.op("act", lambda e: e.copy(out=BhT[:, :, :], in_=bb[:, :, :]), reads=[bb], writes=[BhT])
    S.op("act", lambda e: e.copy(out=KhT[:, :, :], in_=sq[:, :, :]), reads=[sq], writes=[KhT])
    plb = pl[:, :].unsqueeze(2).broadcast_to([64, 8, 128])
    S.op("dve", lambda e: e.tensor_tensor(out=BbT[:, :, :], in0=bb[:, :, :], in1=plb, op=ALU.mult),
         reads=[bb, pl], writes=[BbT])
    S.op("dve", lambda e: e.tensor_tensor(out=KbT[:, :, :], in0=sq[:, :, :], in1=plb, op=ALU.mult),
         reads=[sq, pl], writes=[KbT])
    if RL <= 6:
        return None
    pv = K.psum()
    for h in range(8):
        S.op("pe", lambda e, h=h: e.matmul(pv[:, h * 64:(h + 1) * 64], Pt[:, 16 + h, :], K.identf[0:64, 0:64],
                                           start=True, stop=True), reads=[Pt, K.identf], writes=[pv])
    if RL <= 6.2:
        return None
    Vf = P["vf"].get()
    Vb = P["tok"].get()
    S.op("act", lambda e: e.copy(out=Vf[:, :], in_=pv[:, :]), reads=[pv], writes=[Vf])
    S.op("dve", lambda e: e.tensor_copy(out=Vb[:, :], in_=Vf[:, :]), reads=[Vf], writes=[Vb])
    if RL <= 6.5:
        return None
    outs = []
    for src in (BbT, KbT):
        pb = K.psum()
        for h in range(8):
            S.op("pe", lambda e, h=h, src=src, pb=pb: e.matmul(pb[:, h * 64:(h + 1) * 64], src[:, h, :],
                                                               K.identf[0:64, 0:64], start=True, stop=True),
                 reads=[src, K.identf], writes=[pb])
        o = P["tok"].get()
        S.op("act", lambda e, o=o, pb=pb: e.copy(out=o[:, :], in_=pb[:, :]), reads=[pb], writes=[o])
        outs.append(o)
    T = {"RtT": RtT, "AtT": AtT, "BhT": BhT, "KhT": KhT, "Vb": Vb, "Vf": Vf, "Bb": outs[0], "Kb": outs[1], "pl": pl}
    if RL <= 7:
        return None
    S.op("dve", lambda e: e.tensor_tensor(out=km[:, :, :], in0=km[:, :, :], in1=rT, op=ALU.mult),
         reads=[km, Pt], writes=[km])
    S.op("dve", lambda e: e.tensor_tensor(out=km[:, :, :], in0=km[:, :, :], in1=h8(C["rk"]), op=ALU.mult),
         reads=[km, C["rk"]], writes=[km])
    pr = K.psum()
    for h in range(8):
        S.op("pe", lambda e, h=h: e.matmul(pr[:, h:h + 1], km[:, h, :], K.onesf[0:64, 0:1], start=True, stop=True),
             reads=[km, K.onesf], writes=[pr])
    rkc = P["rkc"].get()
    S.op("dve", lambda e: e.tensor_copy(out=rkc[:, :], in_=pr[:, 0:8]), reads=[pr], writes=[rkc])
    T["rk"] = rkc
    if RL <= 8:
        return None
    if want_g:
        sgd = P["b8"].get()
        S.op("act", lambda e: e.activation(out=sgd[:, 0:3, :], in_=Pt[:, 28:31, :], func=AF.Sigmoid), reads=[Pt],
             writes=[sgd])
        pg = K.psum()
        for q in range(3):
            rows = 64 if q < 2 else 32
            S.op("pe", lambda e, q=q, rows=rows: e.matmul(pg[:, :], sgd[0:rows, q, :], C["gup"][0:rows, q, :],
                                                          start=(q == 0), stop=(q == 2)), reads=[sgd, C["gup"]], writes=[pg])
        gt = P["w"].get()
        S.op("act", lambda e: e.copy(out=gt[:, :], in_=pg[:, :]), reads=[pg], writes=[gt])
        T["g"] = gt
    return T


def rwkv_intra(K, P, T, n):
    S = K.S
    AtT, BhT, KhT, RtT = T["AtT"], T["BhT"], T["KhT"], T["RtT"]
    m_strict_sr = K.ugt if n == 0 else K.uexc
    m_strict_rs = K.uexc if n == 0 else K.ugt
    m_incl_st = K.uinc if n == 0 else K.mskb
    res = {"TT": [], "AakT": [], "ArbT": [], "ArkT": []}

    def prod(lhs, rhs, hq, mask, pool="mat"):
        pp = K.psum()
        for hh in range(4):
            h = hq * 4 + hh
            S.op("pe", lambda e, hh=hh, h=h: e.matmul(pp[:, hh * 128:(hh + 1) * 128], lhs[:, h, :], rhs[:, h, :],
                                                      start=True, stop=True), reads=[lhs, rhs], writes=[pp])
        o = P[pool].get()
        S.op("dve", lambda e: e.tensor_tensor(out=o[:, :, :], in0=pp[:, :].rearrange("p (a b) -> p a b", a=4),
                                              in1=mask[:, :].unsqueeze(1).broadcast_to([128, 4, 128]), op=ALU.mult),
             reads=[pp, mask], writes=[o])
        return o

    def mm4(lhs, rhs, addto=None, eng="act", pool="mat"):
        pp = K.psum()
        for hh in range(4):
            S.op("pe", lambda e, hh=hh: e.matmul(pp[:, hh * 128:(hh + 1) * 128], lhs[:, hh, :], rhs[:, hh, :],
                                                 start=True, stop=True), reads=[lhs, rhs], writes=[pp])
        o = P[pool].get()
        if addto is None:
            S.op(eng, (lambda e: e.copy(out=o[:, :, :].rearrange("p a b -> p (a b)"), in_=pp[:, :])) if eng == "act" else
                 (lambda e: e.tensor_copy(out=o[:, :, :].rearrange("p a b -> p (a b)"), in_=pp[:, :])),
                 reads=[pp], writes=[o])
        else:
            S.op("dve", lambda e: e.tensor_tensor(out=o[:, :, :].rearrange("p a b -> p (a b)"), in0=pp[:, :],
                                                  in1=addto[:, :, :].rearrange("p a b -> p (a b)"), op=ALU.add),
                 reads=[pp, addto], writes=[o])
        return o

    for hq in range(2):
        M = prod(AtT, BhT, hq, m_strict_sr)
        MT = prod(BhT, AtT, hq, m_strict_rs)
        TT = P["mat"].get()
        S.op("pool", lambda e, TT=TT, MT=MT: e.tensor_tensor(
            out=TT[:, :, :], in0=MT[:, :, :], in1=K.ident[:, :].unsqueeze(1).broadcast_to([128, 4, 128]), op=ALU.add),
            reads=[MT, K.ident], writes=[TT])
        for k in range(1, 7):
            M2 = mm4(MT, M, eng="act")
            MT2 = mm4(M, MT, eng="dve") if k < 6 else None
            TT = mm4(M2, TT, addto=TT, pool=("res" if k == 6 else "mat"))
            M, MT = M2, MT2
        res["TT"].append(TT)
        res["AakT"].append(prod(KhT, AtT, hq, m_strict_rs, pool="res"))
        res["ArbT"].append(prod(BhT, RtT, hq, m_incl_st, pool="res"))
        res["ArkT"].append(prod(KhT, RtT, hq, m_incl_st, pool="res"))
    return res


def rwkv_seq(K, P, T, I_, St, Sb):
    S = K.S
    AtT, RtT, Vb, Bb, Kb, pl = T["AtT"], T["RtT"], T["Vb"], T["Bb"], T["Kb"], T["pl"]
    pw = K.psum()
    for h in range(8):
        hq, hh = divmod(h, 4)
        S.op("pe", lambda e, h=h: e.matmul(pw[:, h * 64:(h + 1) * 64], AtT[:, h, :], Sb[:, h, :], start=True, stop=False),
             reads=[AtT, Sb], writes=[pw])
        S.op("pe", lambda e, h=h, hq=hq, hh=hh: e.matmul(pw[:, h * 64:(h + 1) * 64], I_["AakT"][hq][:, hh, :],
                                                         Vb[:, h * 64:(h + 1) * 64], start=False, stop=True),
             reads=[I_["AakT"][hq], Vb], writes=[pw])
    Wb = P["tok"].get()
    S.op("act", lambda e: e.copy(out=Wb[:, :], in_=pw[:, :]), reads=[pw], writes=[Wb])
    pu = K.psum()
    for h in range(8):
        hq, hh = divmod(h, 4)
        S.op("pe", lambda e, h=h, hq=hq, hh=hh: e.matmul(pu[:, h * 64:(h + 1) * 64], I_["TT"][hq][:, hh, :],
                                                         Wb[:, h * 64:(h + 1) * 64], start=True, stop=True),
             reads=[I_["TT"][hq], Wb], writes=[pu])
    Ub = P["tok"].get()
    S.op("dve", lambda e: e.tensor_copy(out=Ub[:, :], in_=pu[:, :]), reads=[pu], writes=[Ub])
    py = K.psum()
    for h in range(8):
        hq, hh = divmod(h, 4)
        S.op("pe", lambda e, h=h: e.matmul(py[:, h * 64:(h + 1) * 64], RtT[:, h, :], Sb[:, h, :], start=True, stop=False),
             reads=[RtT, Sb], writes=[py])
        S.op("pe", lambda e, h=h, hq=hq, hh=hh: e.matmul(py[:, h * 64:(h + 1) * 64], I_["ArbT"][hq][:, hh, :],
                                                         Ub[:, h * 64:(h + 1) * 64], start=False, stop=False),
             reads=[I_["ArbT"][hq], Ub], writes=[py])
        S.op("pe", lambda e, h=h, hq=hq, hh=hh: e.matmul(py[:, h * 64:(h + 1) * 64], I_["ArkT"][hq][:, hh, :],
                                                         Vb[:, h * 64:(h + 1) * 64], start=False, stop=True),
             reads=[I_["ArkT"][hq], Vb], writes=[py])
    ps = K.psum()
    for h in range(8):
        S.op("pe", lambda e, h=h: e.matmul(ps[0:64, h * 64:(h + 1) * 64], Bb[:, h * 64:(h + 1) * 64],
                                           Ub[:, h * 64:(h + 1) * 64], start=True, stop=False),
             reads=[Bb, Ub], writes=[ps])
        S.op("pe", lambda e, h=h: e.matmul(ps[0:64, h * 64:(h + 1) * 64], Kb[:, h * 64:(h + 1) * 64],
                                           Vb[:, h * 64:(h + 1) * 64], start=False, stop=True),
             reads=[Kb, Vb], writes=[ps])
    S.op("dve", lambda e: e.tensor_tensor(out=St[:, :, :], in0=St[:, :, :],
                                          in1=pl[:, :].unsqueeze(2).broadcast_to([64, 8, 64]), op=ALU.mult),
         reads=[St, pl], writes=[St])
    S.op("dve", lambda e: e.tensor_tensor(out=St[:, :, :], in0=St[:, :, :],
                                          in1=ps[0:64, :].rearrange("p (a b) -> p a b", a=8), op=ALU.add),
         reads=[St, ps], writes=[St])
    S.op("act", lambda e: e.copy(out=Sb[:, :, :], in_=St[:, :, :]), reads=[St], writes=[Sb])
    return py


def rwkv_dir_pass(K, SC, C, nchunk, cps, n):
    S = K.S
    P = rwkv_pools(K)
    St = K.alloc([64, 8, 64], F32)
    Sb = K.alloc([64, 8, 64], BF16)
    S.op("dve", lambda e: e.memset(St[:, :, :], 0.0), writes=[St])
    S.op("dve", lambda e: e.memset(Sb[:, :, :], 0.0), writes=[Sb])
    ybp = Pool(K, 2, [128, 520], F32)
    y1p = Pool(K, 2, [128, 520], F32)
    yw = Pool(K, 2, [128, 8, 64], F32)
    yc = Pool(K, 2, [128, 8, 64], F32)
    smp = Pool(K, 4, [128, 32], F32)
    outp = Pool(K, 2, [128, 512], BF16)

    def body(c):
        seg, cis = divmod(c, cps)
        t0 = c * 128
        first = (cis == 0) if n == 0 else (cis == cps - 1)
        edge = (c == 0) if n == 0 else (c == nchunk - 1)
        if first and not edge:
            fl = K.flags[0:64, seg - 1:seg] if n == 0 else K.flags[0:64, seg:seg + 1]
            S.op("dve", lambda e: e.tensor_scalar(out=St[:, :, :], in0=St[:, :, :], scalar1=fl, scalar2=0.0,
                                                  op0=ALU.mult, op1=ALU.add), reads=[St, K.flags], writes=[St])
            S.op("act", lambda e: e.copy(out=Sb[:, :, :], in_=St[:, :, :]), reads=[St], writes=[Sb])
        T = rwkv_prep(K, SC, C, P, c, nchunk, cps, n, want_g=(n == 0))
        if T is None or RL <= 9:
            return
        I_ = rwkv_intra(K, P, T, n)
        if RL <= 10:
            return
        py = rwkv_seq(K, P, T, I_, St, Sb)
        if RL <= 11:
            return
        if n == 1:
            yb = ybp.get()
            S.op("act", lambda e: e.copy(out=yb[:, 0:512], in_=py[:, :]), reads=[py], writes=[yb])
            S.op("dve", lambda e: e.tensor_copy(out=yb[:, 512:520], in_=T["rk"][:, :]), reads=[T["rk"]], writes=[yb])
            S.dma("pool", SC["YB"][t0:t0 + 128, :], yb[:, :], reads=[yb])
            return
        y1 = y1p.get()
        S.dma("sp", y1[:, :], SC["YB"][t0:t0 + 128, :], writes=[y1])
        y = yw.get()
        yv = y[:, :, :].rearrange("p a b -> p (a b)")
        S.op("dve", lambda e: e.tensor_tensor(out=yv, in0=py[:, :], in1=y1[:, 0:512], op=ALU.add),
             reads=[py, y1], writes=[y])
        sm = smp.get()
        S.op("dve", lambda e: e.tensor_reduce(out=sm[:, 0:8], in_=y[:, :, :], axis=AX.X, op=ALU.add),
             reads=[y], writes=[sm])
        S.op("dve", lambda e: e.tensor_scalar(out=sm[:, 0:8], in0=sm[:, 0:8], scalar1=1.0 / 64, scalar2=0.0,
                                              op0=ALU.mult, op1=ALU.add), reads=[sm], writes=[sm])
        ycn = yc.get()
        S.op("dve", lambda e: e.tensor_tensor(out=ycn[:, :, :], in0=y[:, :, :],
                                              in1=sm[:, 0:8].unsqueeze(2).broadcast_to([128, 8, 64]), op=ALU.subtract),
             reads=[y, sm], writes=[ycn])
        S.op("pool", lambda e: e.tensor_tensor(out=y[:, :, :], in0=ycn[:, :, :], in1=ycn[:, :, :], op=ALU.mult),
             reads=[ycn], writes=[y])
        S.op("dve", lambda e: e.tensor_reduce(out=sm[:, 8:16], in_=y[:, :, :], axis=AX.X, op=ALU.add),
             reads=[y], writes=[sm])
        S.op("act", lambda e: e.activation(out=sm[:, 8:16], in_=sm[:, 8:16], func=AF.Sqrt, bias=C["gneps"][:, 0:1],
                                           scale=1.0 / 64), reads=[sm, C["gneps"]], writes=[sm])
        S.op("dve", lambda e: e.reciprocal(out=sm[:, 8:16], in_=sm[:, 8:16]), reads=[sm], writes=[sm])
        S.op("dve", lambda e: e.tensor_tensor(out=ycn[:, :, :], in0=ycn[:, :, :],
                                              in1=sm[:, 8:16].unsqueeze(2).broadcast_to([128, 8, 64]), op=ALU.mult),
             reads=[ycn, sm], writes=[ycn])
        ycv = ycn[:, :, :].rearrange("p a b -> p (a b)")
        S.op("pool", lambda e: e.tensor_tensor(out=ycv, in0=ycv, in1=C["lng"][:, :], op=ALU.mult),
             reads=[ycn, C["lng"]], writes=[ycn])
        S.op("pool", lambda e: e.tensor_tensor(out=ycv, in0=ycv, in1=C["lnb"][:, :], op=ALU.add),
             reads=[ycn, C["lnb"]], writes=[ycn])
        S.op("dve", lambda e: e.tensor_tensor(out=sm[:, 16:24], in0=T["rk"][:, :], in1=y1[:, 512:520], op=ALU.add),
             reads=[T["rk"], y1], writes=[sm])
        S.op("dve", lambda e: e.tensor_tensor(out=y[:, :, :], in0=T["Vf"][:, :].rearrange("p (a b) -> p a b", a=8),
                                              in1=sm[:, 16:24].unsqueeze(2).broadcast_to([128, 8, 64]), op=ALU.mult),
             reads=[T["Vf"], sm, y], writes=[y])
        S.op("pool", lambda e: e.tensor_tensor(out=ycv, in0=ycv, in1=yv, op=ALU.add), reads=[ycn, y], writes=[ycn])
        o = outp.get()
        S.op("dve", lambda e: e.tensor_tensor(out=o[:, :], in0=ycv, in1=T["g"][:, :], op=ALU.mult),
             reads=[ycn, T["g"]], writes=[o])
        S.dma("pool", SC["YR"][t0:t0 + 128, :], o[:, :], reads=[o])

    order = range(nchunk) if n == 0 else range(nchunk - 1, -1, -1)
    for c in order:
        body(c)


def cast_weights(K, src, dst, rows, cols):
    for r in range(0, rows, 128):
        K.S.dma("pool", dst[r:r + 128, :], src[r:r + 128, :])


def zero_fill(K, dst, rows, cols, dt):
    z = K.alloc([128, cols], dt)
    K.S.op("pool", lambda e: e.memset(z[:, :], 0.0), writes=[z])
    for r in range(0, rows, 128):
        K.S.dma("pool", dst[r:r + 128, :], z[:, :], reads=[z])


def build(ntok, seglen=4096, depth=DEPTH, mixer=True):
    nc = bass.Bass("TRN2", target_bir_lowering=False)
    es = ExitStack()
    A = {}
    nseg = ntok // seglen
    plan, pats = attn_plan(nseg, seglen)
    ncfg = pats[False].shape[0]

    def inp(name, shape, dt=F32):
        A[name] = nc.dram_tensor(name, list(shape), dt, kind="ExternalInput").ap()
        return A[name]

    import os
    dbg = os.environ.get("MK_DBG", "").split(",")

    def scr(name, shape, dt=F32):
        if name in dbg:
            return nc.dram_tensor(name, list(shape), dt, kind="ExternalOutput").ap()
        return nc.dram_tensor(name, list(shape), dt).ap()

    x = inp("x", [ntok, D])
    inp("norm_g", [DEPTH, 6, D])
    inp("ff_w_in", [DEPTH, 2, D, 2 * FF])
    inp("ff_w_out", [DEPTH, 2, FF, D])
    inp("w_in", [DEPTH, D, IN_W])
    inp("w_branch", [DEPTH, 2048, D])
    inp("w_out", [DEPTH, D, D])
    inp("atab", [DEPTH, ncfg, 128, 8, 128])
    inp("ssm_conv_w", [DEPTH, 5, 1536])
    inp("ssm_conv_b", [DEPTH, 1536])
    inp("ssm_dt_bias", [DEPTH, 2, 16])
    inp("ssm_a_log", [DEPTH, 2, 16])
    inp("ssm_d", [DEPTH, 2, 16])
    inp("ssm_norm_g", [DEPTH, 1024])
    inp("flags", [128, 4])
    inp("cf32", [128, 7, 128])
    inp("rwkv_mu", [DEPTH, 2, 1952])
    inp("rwkv_w0", [DEPTH, 2, 512])
    inp("rwkv_w_up", [DEPTH, 2, 64, 512])
    inp("rwkv_a0", [DEPTH, 2, 512])
    inp("rwkv_a_up", [DEPTH, 2, 64, 512])
    inp("rwkv_g_up", [DEPTH, 160, 512])
    inp("rwkv_k_k", [DEPTH, 512])
    inp("rwkv_k_a", [DEPTH, 512])
    inp("rwkv_r_k", [DEPTH, 8, 64])
    inp("rwkv_ln_g", [DEPTH, 512])
    inp("rwkv_ln_b", [DEPTH, 512])
    inp("ident", [128, 128], BF16)
    y = nc.dram_tensor("y", [ntok, D], F32, kind="ExternalOutput").ap()
    xs = scr("xs", [ntok, D])
    wi_bf = [[scr(f"wi{l}{f}", [D, 2 * FF], BF16) for f in range(2)] for l in range(DEPTH)]
    wo_bf = [[scr(f"wo{l}{f}", [FF, D], BF16) for f in range(2)] for l in range(DEPTH)]
    win_bf = [scr(f"win{l}", [D, IN_W], BF16) for l in range(DEPTH)]
    wbr_bf = [scr(f"wbr{l}", [2048, D], BF16) for l in range(DEPTH)]
    wout_bf = [scr(f"wout{l}", [D, D], BF16) for l in range(DEPTH)]
    SC = {"QT": scr("QT", [512, ntok], BF16), "KT": scr("KT", [512, ntok], BF16), "V": scr("V", [ntok, 512], BF16),
          "Z": scr("Z", [ntok, 1024]), "DT": scr("DT", [ntok, 32]), "G": scr("G", [ntok, 3072]),
          "XBCT": scr("XBCT", [1536, ntok]), "RWT": scr("RWT", [1952, ntok]),
          "HB": scr("HB", [ntok // 128, 128, 1024], BF16), "YB": scr("YB", [ntok, 520]),
          "YA": scr("YA", [ntok, 512], BF16), "YS": scr("YS", [ntok, 1024], BF16), "YR": scr("YR", [ntok, 512], BF16)}
    if "DY" in dbg:
        SC["DY"] = scr("DY", [ntok, 1024]); SC["DV"] = scr("DV", [ntok, 256]); SC["DM"] = scr("DM", [ntok // 128, 128, 16, 128], BF16)
    K = Ctx(nc, es)
    S = K.S
    K.ident = K.alloc([128, 128], BF16, keep=True)
    S.dma("sp", K.ident[:, :], A["ident"][:, :], writes=[K.ident])
    cf = K.alloc([128, 7, 128], F32, keep=True)
    S.dma("sp", cf[:, :, :], A["cf32"][:, :, :], writes=[cf])
    K.identf, K.uinc, K.uexc, K.onesf, K.mskf, K.mskb, K.ugt = [Tl(cf.t[:, i, :], cf.res) for i in range(7)]
    K.flags = K.alloc([128, 4], F32, keep=True)
    S.dma("sp", K.flags[:, :], A["flags"][:, :], writes=[K.flags])
    K.onec = K.alloc([128, 4], F32, keep=True)
    S.op("dve", lambda e: e.memset(K.onec[:, :], 1.0), writes=[K.onec])
    K.epsc = K.alloc([128, 4], F32, keep=True)
    S.op("dve", lambda e: e.memset(K.epsc[:, 0:1], EPS), writes=[K.epsc])
    S.op("dve", lambda e: e.memset(K.epsc[:, 1:2], 4.0 * EPS), writes=[K.epsc])
    for l in range(depth):
        for f in range(2):
            cast_weights(K, A["ff_w_in"][l, f], wi_bf[l][f], D, 2 * FF)
            cast_weights(K, A["ff_w_out"][l, f], wo_bf[l][f], FF, D)
        if mixer:
            cast_weights(K, A["w_in"][l], win_bf[l], D, IN_W)
            cast_weights(K, A["w_branch"][l], wbr_bf[l], 2048, D)
            cast_weights(K, A["w_out"][l], wout_bf[l], D, D)
    if mixer:
        zero_fill(K, SC["YS"], ntok, 1024, BF16)
        zero_fill(K, SC["YR"], ntok, 512, BF16)
    cur = x
    for l in range(depth):
        g = A["norm_g"][l]
        ffn_pass(K, cur, xs, g[0:1, :], g[1:2, :], wi_bf[l][0], wo_bf[l][0], ntok)
        cur = xs
        if mixer:
            import os
            st = os.environ.get("MK_STAGES", "iasrm")
            if "i" in st:
                inproj_pass(K, xs, g[2:3, :], win_bf[l], SC, ntok)
            if "a" in st:
                attn_pass(K, SC, A["atab"][l], plan)
            if "s" in st:
                K.new_pass()
                Cs = ssd_setup(K, A, l)
                ssd_bwd_pass(K, SC, Cs, ntok // 128, seglen // 128)
                K.new_pass()
                Cs = ssd_setup(K, A, l)
                ssd_fwd_pass(K, SC, Cs, ntok // 128, seglen // 128)
            if "r" in st:
                for n_ in (1, 0):
                    K.new_pass()
                    Cr = rwkv_setup(K, A, l)
                    rwkv_dir_pass(K, SC, Cr, ntok // 128, seglen // 128, n_)
            if "m" in st:
                merge_pass(K, SC, xs, g[3:4, :], wbr_bf[l], wout_bf[l], ntok)
        last = (l == depth - 1)
        ffn_pass(K, cur, y if last else xs, g[4:5, :], g[5:6, :], wi_bf[l][1], wo_bf[l][1], ntok)
    S.barrier()
    S.emit()
    es.close()
    return nc


def consts():
    i = np.arange(128)
    s_, l_ = i[:, None], i[None, :]
    cf = np.stack([np.eye(128), s_ <= l_, s_ < l_, np.ones((128, 128)), s_ <= l_, s_ >= l_, s_ > l_]).astype(np.float32)
    return {"ident": np.eye(128, dtype=np.float32).astype(ml_dtypes.bfloat16),
            "cf32": np.ascontiguousarray(cf.transpose(1, 0, 2))}


def flags_for(link):
    f = np.zeros((128, 4), np.float32)
    f[:, 0] = 1.0 if link else 0.0
    return f


NTOK = 12288
_NC_CACHE = {}


def _assign():
    segs = []
    for c in range(4):
        segs.append([("s", c, 0), ("s", c, 1), ("p", c, 0)])
    for c in range(4, 8):
        b = 4 + (c - 4) * 3
        segs.append([("p", b, 0), ("p", b + 1, 0), ("p", b + 2, 0)])
    return segs


def kernel(**inputs):
    xp = np.asarray(inputs["x_prompt"], dtype=np.float32)
    xs = np.asarray(inputs["x_sample"], dtype=np.float32)
    segs = _assign()
    if "nc" not in _NC_CACHE:
        _NC_CACHE["nc"] = build(NTOK, seglen=4096)
    nc = _NC_CACHE["nc"]
    shared = {k: np.ascontiguousarray(np.asarray(inputs[k], dtype=np.float32))
              for k in ("norm_g", "ff_w_in", "ff_w_out", "w_in", "w_branch", "w_out", "ssm_conv_w", "ssm_conv_b",
                        "ssm_dt_bias", "ssm_a_log", "ssm_d", "ssm_norm_g", "rwkv_mu", "rwkv_w0", "rwkv_w_up",
                        "rwkv_a0", "rwkv_a_up", "rwkv_g_up", "rwkv_k_k", "rwkv_k_a", "rwkv_r_k", "rwkv_ln_g",
                        "rwkv_ln_b")}
    shared.update(consts())
    plan, pats = attn_plan(3, 4096)
    rpb = np.asarray(inputs["attn_rpb"], dtype=np.float32)
    tabs = {link: attn_tables(rpb, pats[link]) for link in (False, True)}
    in_maps = []
    for c in range(8):
        parts = []
        for kind, b, h in segs[c]:
            parts.append(xs[b, h * 4096:(h + 1) * 4096] if kind == "s" else xp[b])
        m = {"x": np.ascontiguousarray(np.concatenate(parts, axis=0)), "atab": tabs[c < 4],
             "flags": flags_for(c < 4)}
        m.update(shared)
        in_maps.append(m)
    res = run_bass_kernel_spmd(nc, in_maps, core_ids=list(range(8)))
    yp = np.empty_like(xp)
    ys = np.empty_like(xs)
    for c in range(8):
        y = res.results[c]["y"]
        for i, (kind, b, h) in enumerate(segs[c]):
            blk = y[i * 4096:(i + 1) * 4096]
            if kind == "s":
                ys[b, h * 4096:(h + 1) * 4096] = blk
            else:
                yp[b] = blk
    return yp, ys
```

```python
import numpy as np
import ml_dtypes
from contextlib import ExitStack
import concourse.bass as bass
import concourse.mybir as mybir
from concourse.bass_utils import run_bass_kernel_spmd

F32 = mybir.dt.float32
BF16 = mybir.dt.bfloat16
AF = mybir.ActivationFunctionType
ALU = mybir.AluOpType
AX = mybir.AxisListType

D = 1024
FF = 2816
DEPTH = 2
EPS = 1e-6


class Res:
    __slots__ = ("w", "r")

    def __init__(self):
        self.w = None
        self.r = []


class Tl:
    def __init__(self, t, res=None):
        self.t = t
        self.res = res or Res()

    def __getitem__(self, idx):
        return self.t[idx]


class Sched:
    EPOCH = 60000
    NDMA = {"sp": 24, "pool": 12, "act": 8}

    def __init__(self, nc, es):
        self.nc = nc
        self.es = es
        self.names = ["sp", "act", "dve", "pool", "pe"]
        self.ops = {k: [] for k in self.names}
        self.n = {k: 0 for k in self.names}
        self.sems = {k: [] for k in self.names}
        self.seen = {k: {} for k in self.names}
        self.dsem = {}
        self.dval = {}
        self.drr = {}
        for q, n in self.NDMA.items():
            self.dsem[q] = [es.enter_context(nc.semaphore(f"d{q}{i}")) for i in range(n)]
            self.dval[q] = [0] * n
            self.drr[q] = 0
        self.last = {k: None for k in self.names}

    def _next_ev(self, e):
        n = self.n[e]
        ep = n // self.EPOCH
        while len(self.sems[e]) <= ep:
            self.sems[e].append(self.es.enter_context(self.nc.semaphore(f"s{e}{len(self.sems[e])}")))
        self.n[e] += 1
        ev = (self.sems[e][ep], n % self.EPOCH + 1, e)
        self.last[e] = ev
        return ev

    def _deps(self, e, reads, writes):
        deps = []
        for r in reads:
            r = r.res if isinstance(r, Tl) else r
            if r.w is not None:
                deps.append((r.w, 0))
        for w in writes:
            w = w.res if isinstance(w, Tl) else w
            if w.w is not None:
                deps.append((w.w, 0))
            for ev in w.r:
                deps.append((ev, 1))
        waits = []
        seen = self.seen[e]
        for (sem, val, src), war in deps:
            if src == e:
                if e == "pe" or war:
                    continue
            k = id(sem)
            if seen.get(k, 0) >= val:
                continue
            seen[k] = val
            waits.append((sem, val))
        return waits

    def _commit(self, ev, reads, writes):
        for r in reads:
            r = r.res if isinstance(r, Tl) else r
            r.r.append(ev)
        for w in writes:
            w = w.res if isinstance(w, Tl) else w
            w.w = ev
            w.r = []

    def op(self, e, fn, reads=(), writes=()):
        waits = self._deps(e, reads, writes)
        ev = self._next_ev(e)
        self.ops[e].append((waits, fn, ev, 1))
        self._commit(ev, reads, writes)

    def dma(self, q, out, in_, reads=(), writes=(), slow=False):
        waits = self._deps(q, reads, writes)
        i = self.drr[q]
        self.drr[q] = (i + 1) % len(self.dsem[q])
        sem = self.dsem[q][i]
        prev = self.dval[q][i]
        if prev > 0 and self.seen[q].get(id(sem), 0) < prev:
            waits.append((sem, prev))
            self.seen[q][id(sem)] = prev
        self.dval[q][i] = prev + 16
        ev = (sem, prev + 16, "dma")
        if slow:
            fn = (lambda e, o=out, i_=in_: e.dma_start(out=o, in_=i_, allow_slow_non_contiguous=True))
        else:
            fn = (lambda e, o=out, i_=in_: e.dma_start(out=o, in_=i_))
        self.ops[q].append((waits, fn, ev, 16))
        self._commit(ev, reads, writes)

    def barrier(self):
        evs = [self.last[k] for k in self.names if self.last[k] is not None]
        for q in self.dsem:
            for sem, v in zip(self.dsem[q], self.dval[q]):
                if v > 0:
                    evs.append((sem, v, "dma"))
        for e in self.names:
            waits = []
            for sem, val, src in evs:
                if src == e:
                    continue
                if self.seen[e].get(id(sem), 0) >= val:
                    continue
                self.seen[e][id(sem)] = val
                waits.append((sem, val))
            if waits:
                self.ops[e].append((waits, None, None, 0))

    def emit(self):
        block = self.es.enter_context(self.nc.Block())
        decos = {"sp": block.sync, "act": block.scalar, "dve": block.vector, "pool": block.gpsimd,
                 "pe": block.tensor}
        for name in self.names:
            ops = self.ops[name]

            def body(e, ops=ops):
                for waits, fn, ev, inc in ops:
                    for (s, v) in waits:
                        e.wait_ge(s, v)
                    if fn is not None:
                        fn(e).then_inc(ev[0], inc)

            decos[name](body)


class Ctx:
    BASE = 16640
    ARENA = 224 * 1024

    def __init__(self, nc, es):
        self.nc = nc
        self.es = es
        self.S = Sched(nc, es)
        self.off = self.BASE
        self.uid = 0
        self.keep = self.BASE
        self.ps = [Tl(es.enter_context(nc.psum_tensor(f"ps{i}", [128, 512], F32))) for i in range(8)]
        self.psi = 0
        self.psi6 = 0

    def alloc(self, shape, dtype, keep=False):
        nbytes = int(np.prod(shape[1:])) * (2 if dtype == BF16 else 4)
        nbytes = (nbytes + 31) // 32 * 32
        assert self.off + nbytes <= self.ARENA, f"SBUF arena overflow {self.off}+{nbytes}"
        self.uid += 1
        t = self.nc.alloc_sbuf_tensor_at(f"t{self.uid}", list(shape), dtype, offset=self.off)
        self.off += nbytes
        if keep:
            self.keep = self.off
        return Tl(t)

    def new_pass(self):
        self.S.barrier()
        self.off = self.keep

    def psum(self):
        p = self.ps[self.psi]
        self.psi = (self.psi + 1) % 8
        return p


class Pool:
    def __init__(self, K, n, shape, dtype):
        self.t = [K.alloc(shape, dtype) for _ in range(n)]
        self.i = 0

    def get(self):
        t = self.t[self.i]
        self.i = (self.i + 1) % len(self.t)
        return t


def norm_transpose(K, xt, xres, gbc, hnT, col0, P):
    S = K.S
    sm = P["small"].get()
    junk = P["junk"].get()
    S.op("act", lambda e: e.activation(out=junk[:, :], in_=xt, func=AF.Square, accum_out=sm[:, 0:1]),
         reads=[xres], writes=[junk, sm])
    S.op("act", lambda e: e.activation(out=sm[:, 1:2], in_=sm[:, 0:1], func=AF.Sqrt, bias=K.epsc[:, 0:1],
                                       scale=1.0 / D), reads=[sm, K.epsc], writes=[sm])
    S.op("dve", lambda e: e.reciprocal(out=sm[:, 2:3], in_=sm[:, 1:2]), reads=[sm], writes=[sm])
    hn = P["hn"].get()
    S.op("dve", lambda e: e.scalar_tensor_tensor(out=hn[:, :], in0=xt, scalar=sm[:, 2:3], in1=gbc[:, :],
                                                 op0=ALU.mult, op1=ALU.mult),
         reads=[xres, sm, gbc], writes=[hn])
    pt = K.psum()
    ptb = pt.t[:, :].bitcast(BF16)
    for kc in range(8):
        S.op("pe", lambda e, kc=kc: e.transpose(ptb[:, kc * 128:(kc + 1) * 128],
                                                hn[:, kc * 128:(kc + 1) * 128], K.ident[:, :]),
             reads=[hn, K.ident], writes=[pt])
    S.op("act", lambda e: e.copy(out=hnT[:, :, col0:col0 + 128],
                                 in_=ptb.rearrange("p (k c) -> p k c", k=8)), reads=[pt], writes=[hnT])


def ffn_pass(K, xin, xout, gin_ap, gout_ap, win_bf, wout_bf, ntok):
    S = K.S
    K.new_pass()
    gin = K.alloc([128, D], F32)
    gout = K.alloc([128, D], F32)
    S.dma("sp", gin[:, :], gin_ap.broadcast_to([128, D]), writes=[gin])
    S.dma("sp", gout[:, :], gout_ap.broadcast_to([128, D]), writes=[gout])
    wout = K.alloc([128, 22, D], BF16)
    wo_v = wout_bf.rearrange("(kc p) n -> p kc n", p=128)
    for kc in range(0, 22, 2):
        S.dma("sp", wout[:, kc:kc + 2, :], wo_v[:, kc:kc + 2, :], writes=[wout])
    xp = Pool(K, 2, [128, 4, D], F32)
    hnTp = Pool(K, 2, [128, 8, 512], BF16)
    hT = K.alloc([128, 22, 512], BF16)
    wp = Pool(K, 3, [128, 8, 2, 256], BF16)
    sg = Pool(K, 2, [128, 512], F32)
    tt = Pool(K, 2, [128, D], F32)
    P = {"small": Pool(K, 8, [128, 8], F32), "junk": Pool(K, 2, [128, D], BF16), "hn": Pool(K, 2, [128, D], BF16)}
    win_v = win_bf.rearrange("(kc p) n -> p kc n", p=128)
    for mt in range(ntok // 512):
        x = xp.get()
        S.dma("sp", x[:, :, :], xin[mt * 512:(mt + 1) * 512, :].rearrange("(s p) d -> p s d", p=128),
              writes=[x])
        hnT = hnTp.get()
        for s in range(4):
            norm_transpose(K, x[:, s, :], x, gin, hnT, s * 128, P)
        for j in range(11):
            w = wp.get()
            S.dma("sp", w[:, :, 0, :], win_v[:, :, j * 256:(j + 1) * 256], writes=[w])
            S.dma("sp", w[:, :, 1, :], win_v[:, :, FF + j * 256:FF + (j + 1) * 256], writes=[w])
            for c in range(2):
                pg = K.psum()
                pu = K.psum()
                for gi, pp in enumerate((pg, pu)):
                    for kc in range(8):
                        S.op("pe", lambda e, pp=pp, gi=gi, kc=kc, c=c, w=w, hnT=hnT: e.matmul(
                            pp[:, :], w[:, kc, gi, c * 128:(c + 1) * 128], hnT[:, kc, :],
                            start=(kc == 0), stop=(kc == 7)), reads=[w, hnT], writes=[pp])
                sgt = sg.get()
                S.op("act", lambda e, sgt=sgt, pg=pg: e.activation(out=sgt[:, :], in_=pg[:, :], func=AF.Silu),
                     reads=[pg], writes=[sgt])
                S.op("dve", lambda e, sgt=sgt, pu=pu, ch=j * 2 + c: e.tensor_tensor(
                    out=hT[:, ch, :], in0=sgt[:, :], in1=pu[:, :], op=ALU.mult),
                     reads=[sgt, pu], writes=[hT])
        for s in range(4):
            pp = [K.psum(), K.psum()]
            for nf in range(2):
                for kc in range(22):
                    S.op("pe", lambda e, p_=pp[nf], nf=nf, kc=kc, s=s: e.matmul(
                        p_[:, :], hT[:, kc, s * 128:(s + 1) * 128], wout[:, kc, nf * 512:(nf + 1) * 512],
                        start=(kc == 0), stop=(kc == 21)), reads=[hT, wout], writes=[pp[nf]])
            sm = P["small"].get()
            junk = P["junk"].get()
            for nf in range(2):
                S.op("act", lambda e, nf=nf, junk=junk, sm=sm, p_=pp[nf]: e.activation(
                    out=junk[:, 0:512], in_=p_[:, :], func=AF.Square, accum_out=sm[:, nf:nf + 1]),
                     reads=[pp[nf]], writes=[junk, sm])
            S.op("dve", lambda e, sm=sm: e.tensor_tensor(out=sm[:, 2:3], in0=sm[:, 0:1], in1=sm[:, 1:2],
                                                         op=ALU.add), reads=[sm], writes=[sm])
            S.op("act", lambda e, sm=sm: e.activation(out=sm[:, 3:4], in_=sm[:, 2:3], func=AF.Sqrt,
                                                      bias=K.epsc[:, 1:2], scale=4.0 / D),
                 reads=[sm, K.epsc], writes=[sm])
            S.op("dve", lambda e, sm=sm: e.reciprocal(out=sm[:, 4:5], in_=sm[:, 3:4]), reads=[sm], writes=[sm])
            t = tt.get()
            for nf in range(2):
                S.op("dve", lambda e, nf=nf, t=t, sm=sm, p_=pp[nf]: e.scalar_tensor_tensor(
                    out=t[:, nf * 512:(nf + 1) * 512], in0=p_[:, :], scalar=sm[:, 4:5],
                    in1=gout[:, nf * 512:(nf + 1) * 512], op0=ALU.mult, op1=ALU.mult),
                     reads=[pp[nf], sm, gout], writes=[t])
            S.op("pool", lambda e, t=t, x=x, s=s: e.tensor_tensor(out=x[:, s, :], in0=x[:, s, :], in1=t[:, :],
                                                                  op=ALU.add), reads=[t, x], writes=[x])
        S.dma("pool", xout[mt * 512:(mt + 1) * 512, :].rearrange("(s p) d -> p s d", p=128), x[:, :, :],
              reads=[x])


IN_W = 9152
OFF_Q, OFF_K, OFF_V, OFF_Z, OFF_XBC, OFF_DT, OFF_RW, OFF_G = 0, 512, 1024, 1536, 2560, 4096, 4128, 6080


def inproj_pass(K, xin, g_ap, w_bf, SC, ntok):
    S = K.S
    K.new_pass()
    gin = K.alloc([128, D], F32)
    S.dma("sp", gin[:, :], g_ap.broadcast_to([128, D]), writes=[gin])
    xp = Pool(K, 2, [128, 4, D], F32)
    hnTp = Pool(K, 2, [128, 8, 512], BF16)
    wp = Pool(K, 3, [128, 8, 512], BF16)
    ofm = Pool(K, 3, [128, 512], F32)
    obf = Pool(K, 3, [128, 512], BF16)
    P = {"small": Pool(K, 8, [128, 8], F32), "junk": Pool(K, 2, [128, D], BF16), "hn": Pool(K, 2, [128, D], BF16)}
    w_v = w_bf.rearrange("(kc p) n -> p kc n", p=128)
    blocks = []
    for c0 in range(0, 1024, 512):
        blocks.append((c0, 512, "F"))
    blocks.append((OFF_V, 512, "T"))
    blocks += [(OFF_Z, 512, "T"), (OFF_Z + 512, 512, "T")]
    blocks += [(OFF_XBC + i * 512, 512, "F") for i in range(3)]
    blocks.append((OFF_DT, 32, "T"))
    blocks += [(OFF_RW, 512, "F"), (OFF_RW + 512, 512, "F"), (OFF_RW + 1024, 512, "F"), (OFF_RW + 1536, 416, "F")]
    blocks += [(OFF_G + i * 512, 512, "T") for i in range(6)]
    for mt in range(ntok // 512):
        t0 = mt * 512
        x = xp.get()
        S.dma("sp", x[:, :, :], xin[t0:t0 + 512, :].rearrange("(s p) d -> p s d", p=128), writes=[x])
        hnT = hnTp.get()
        for s in range(4):
            norm_transpose(K, x[:, s, :], x, gin, hnT, s * 128, P)
        for (c0, ncol, mode) in blocks:
            w = wp.get()
            S.dma("sp", w[:, :, 0:ncol], w_v[:, :, c0:c0 + ncol], writes=[w])
            if mode == "F":
                for fc in range((ncol + 127) // 128):
                    nf = min(128, ncol - fc * 128)
                    pp = K.psum()
                    for kc in range(8):
                        S.op("pe", lambda e, pp=pp, w=w, kc=kc, fc=fc, nf=nf, hnT=hnT: e.matmul(
                            pp[0:nf, :], w[:, kc, fc * 128:fc * 128 + nf], hnT[:, kc, :],
                            start=(kc == 0), stop=(kc == 7)), reads=[w, hnT], writes=[pp])
                    f0 = c0 + fc * 128
                    if f0 < 1024:
                        o = obf.get()
                        sc = 0.125 if f0 < 512 else 1.0
                        S.op("act", lambda e, o=o, pp=pp, sc=sc: e.activation(out=o[:, :], in_=pp[:, :],
                                                                              func=AF.Copy, scale=sc),
                             reads=[pp], writes=[o])
                        dst = SC["QT"] if f0 < 512 else SC["KT"]
                        r0 = f0 % 512
                        S.dma("pool", dst[r0:r0 + 128, t0:t0 + 512], o[:, :], reads=[o])
                    else:
                        o = ofm.get()
                        S.op("act", lambda e, o=o, pp=pp, nf=nf: e.copy(out=o[0:nf, :], in_=pp[0:nf, :]),
                             reads=[pp], writes=[o])
                        if f0 < OFF_DT:
                            r0 = f0 - OFF_XBC
                            S.dma("pool", SC["XBCT"][r0:r0 + nf, t0:t0 + 512], o[0:nf, :], reads=[o])
                        else:
                            r0 = f0 - OFF_RW
                            S.dma("pool", SC["RWT"][r0:r0 + nf, t0:t0 + 512], o[0:nf, :], reads=[o])
            else:
                for s in range(4):
                    pp = K.psum()
                    for kc in range(8):
                        S.op("pe", lambda e, pp=pp, w=w, kc=kc, s=s, ncol=ncol, hnT=hnT: e.matmul(
                            pp[:, 0:ncol], hnT[:, kc, s * 128:(s + 1) * 128], w[:, kc, 0:ncol],
                            start=(kc == 0), stop=(kc == 7)), reads=[w, hnT], writes=[pp])
                    tk = t0 + s * 128
                    if c0 == OFF_V:
                        o = obf.get()
                        S.op("act", lambda e, o=o, pp=pp: e.copy(out=o[:, :], in_=pp[:, :]), reads=[pp], writes=[o])
                        S.dma("pool", SC["V"][tk:tk + 128, :], o[:, :], reads=[o])
                    elif c0 >= OFF_G:
                        o = ofm.get()
                        S.op("act", lambda e, o=o, pp=pp: e.activation(out=o[:, :], in_=pp[:, :], func=AF.Sigmoid),
                             reads=[pp], writes=[o])
                        S.dma("pool", SC["G"][tk:tk + 128, c0 - OFF_G:c0 - OFF_G + 512], o[:, :], reads=[o])
                    elif c0 == OFF_DT:
                        o = ofm.get()
                        S.op("act", lambda e, o=o, pp=pp: e.copy(out=o[:, 0:32], in_=pp[:, 0:32]), reads=[pp], writes=[o])
                        S.dma("pool", SC["DT"][tk:tk + 128, :], o[:, 0:32], reads=[o])
                    else:
                        o = ofm.get()
                        S.op("act", lambda e, o=o, pp=pp: e.copy(out=o[:, :], in_=pp[:, :]), reads=[pp], writes=[o])
                        S.dma("pool", SC["Z"][tk:tk + 128, c0 - OFF_Z:c0 - OFF_Z + 512], o[:, :], reads=[o])


def attn_plan(nseg, seglen):
    R = seglen // 64
    nrow = nseg * R

    def rs_of(g, link):
        seg, r = divmod(g, R)
        if link and seg < 2 and nseg >= 2:
            return int(np.clip(g - 4, 0, 2 * R - 8))
        return seg * R + int(np.clip(r - 4, 0, R - 8))

    qc = np.arange(64)
    wst = np.clip(qc - 8, 0, 48)
    pats = {}
    plan = []
    ids = {}
    for P in range(nrow // 2):
        kps = set()
        for g in (2 * P, 2 * P + 1):
            for link in (False, True):
                rs = rs_of(g, link)
                for kr in range(rs, rs + 8):
                    kps.add(kr // 2)
        ent = []
        for KP in sorted(kps):
            both = []
            for link in (False, True):
                idx = np.full((2, 64, 2, 64), -1, np.int32)
                for qr2 in range(2):
                    g = 2 * P + qr2
                    rs = rs_of(g, link)
                    for kr2 in range(2):
                        kr = 2 * KP + kr2
                        if not (rs <= kr < rs + 8):
                            continue
                        dr = kr - g + 7
                        kc = np.arange(64)[:, None]
                        ok = (kc >= wst[None, :]) & (kc < wst[None, :] + 16)
                        dc = np.clip(kc - qc[None, :] + 15, 0, 30)
                        idx[kr2, :, qr2, :] = np.where(ok, dr * 31 + dc, -1)
                both.append(idx.reshape(128, 128))
            key = both[0].tobytes() + both[1].tobytes()
            if key not in ids:
                ids[key] = len(ids)
                pats.setdefault(False, []).append(both[0])
                pats.setdefault(True, []).append(both[1])
            ent.append((KP, ids[key]))
        plan.append(ent)
    return plan, {k: np.stack(v) for k, v in pats.items()}


def attn_tables(rpb, pat):
    flat = rpb.reshape(rpb.shape[0], 8, 15 * 31)
    g = flat[:, :, np.clip(pat, 0, None)]
    g = np.where(pat[None, None] >= 0, g, np.float32(-30000.0)).astype(np.float32)
    return np.ascontiguousarray(g.transpose(0, 2, 3, 1, 4))


def attn_pass(K, SC, tab, plan):
    import os
    ALV = int(os.environ.get("MK_ALV", "3"))
    S = K.S
    K.new_pass()
    qp = Pool(K, 2, [64, 8, 128], BF16)
    kp = Pool(K, 3, [64, 8, 128], BF16)
    vp = Pool(K, 3, [128, 8, 80], BF16)
    tp = Pool(K, 3, [128, 8, 128], F32)
    sp_ = Pool(K, 2, [128, 512], F32)
    ptp = Pool(K, 3, [128, 8, 128], BF16)
    yp = Pool(K, 2, [128, 8, 64], BF16)
    sm = Pool(K, 2, [128, 8], F32)
    for v in vp.t:
        S.op("pool", lambda e, v=v: e.memset(v[:, :, 64:65], 1.0), writes=[v])
    QTv = SC["QT"].rearrange("(c p) t -> p c t", p=64)
    KTv = SC["KT"].rearrange("(c p) t -> p c t", p=64)
    for P, ent in enumerate(plan):
        q = qp.get()
        S.dma("sp", q[:, :, :], QTv[:, :, P * 128:(P + 1) * 128], writes=[q])
        ob = [K.ps[6], K.ps[7]]
        for ki, (KP, cfg) in enumerate(ent):
            k = kp.get()
            S.dma("sp", k[:, :, :], KTv[:, :, KP * 128:(KP + 1) * 128], writes=[k])
            v = vp.get()
            S.dma("sp", v[:, :, 0:64], SC["V"][KP * 128:(KP + 1) * 128, :].rearrange("t (h d) -> t h d", h=8),
                  writes=[v])
            tb = tp.get()
            S.dma("sp", tb[:, :, :], tab[cfg], writes=[tb])
            pt = ptp.get()
            for hb in range(2):
                pp = K.ps[K.psi6]
                K.psi6 = (K.psi6 + 1) % 6
                for hh in range(4):
                    h = hb * 4 + hh
                    S.op("pe", lambda e, pp=pp, k=k, q=q, h=h, hh=hh: e.matmul(
                        pp[:, hh * 128:(hh + 1) * 128], k[:, h, :], q[:, h, :], start=True, stop=True),
                        reads=[k, q], writes=[pp])
                sb = sp_.get()
                S.op("dve", lambda e, sb=sb, pp=pp, tb=tb, hb=hb: e.tensor_tensor(
                    out=sb[:, :], in0=pp[:, :], in1=tb[:, hb * 4:(hb + 1) * 4, :].rearrange("p a b -> p (a b)"),
                    op=ALU.add), reads=[pp, tb], writes=[sb])
                S.op("act", lambda e, sb=sb, pt=pt, hb=hb: e.activation(
                    out=pt[:, hb * 4:(hb + 1) * 4, :].rearrange("p a b -> p (a b)"), in_=sb[:, :], func=AF.Exp),
                    reads=[sb], writes=[pt])
            for h in range(8 if ALV >= 2 else 0):
                o_ = ob[h // 4]
                hh = h % 4
                S.op("pe", lambda e, o_=o_, pt=pt, v=v, h=h, hh=hh, ki=ki, n=len(ent): e.matmul(
                    o_[:, hh * 128:hh * 128 + 65], pt[:, h, :], v[:, h, 0:65], start=(ki == 0 and hh == 0), stop=(ki == n - 1)),
                    reads=[pt, v], writes=[o_])
        rc = sm.get()
        y = yp.get()
        for hb in range(2 if ALV >= 3 else 0):
            ov = ob[hb][:, :].rearrange("p (h d) -> p h d", h=4)
            S.op("dve", lambda e, rc=rc, ov=ov, hb=hb: e.reciprocal(out=rc[:, hb * 4:(hb + 1) * 4], in_=ov[:, :, 64]),
                 reads=[ob[hb]], writes=[rc])
            S.op("dve", lambda e, rc=rc, ov=ov, hb=hb, y=y: e.tensor_tensor(
                out=y[:, hb * 4:(hb + 1) * 4, :], in0=ov[:, :, 0:64],
                in1=rc[:, hb * 4:(hb + 1) * 4].unsqueeze(2).broadcast_to([128, 4, 64]), op=ALU.mult),
                reads=[ob[hb], rc], writes=[y])
        if ALV >= 3:
            S.dma("pool", SC["YA"][P * 128:(P + 1) * 128, :], y[:, :, :].rearrange("p h d -> p (h d)"), reads=[y])


def merge_pass(K, SC, xio, g_ap, wbr_bf, wout_bf, ntok):
    S = K.S
    K.new_pass()
    g3 = K.alloc([128, D], F32)
    S.dma("sp", g3[:, :], g_ap.broadcast_to([128, D]), writes=[g3])
    wbr = K.alloc([128, 16, D], BF16)
    wbv = wbr_bf.rearrange("(kc p) n -> p kc n", p=128)
    for kc in range(0, 16, 4):
        S.dma("sp", wbr[:, kc:kc + 4, :], wbv[:, kc:kc + 4, :], writes=[wbr])
    wo = K.alloc([128, 8, D], BF16)
    S.dma("sp", wo[:, :, :], wout_bf.rearrange("(kc p) n -> p kc n", p=128), writes=[wo])
    ybp = Pool(K, 2, [128, 2048], BF16)
    gp = Pool(K, 2, [128, 3, D], F32)
    xp = Pool(K, 2, [128, D], F32)
    yTp = Pool(K, 2, [128, 16, 128], BF16)
    mp = Pool(K, 2, [128, D], F32)
    tmp = Pool(K, 2, [128, 512], F32)
    mbp = Pool(K, 2, [128, D], BF16)
    mTp = Pool(K, 2, [128, 8, 128], BF16)
    tt = Pool(K, 2, [128, D], F32)
    sm = Pool(K, 4, [128, 8], F32)
    junk = Pool(K, 2, [128, 512], BF16)
    for t in range(ntok // 128):
        r0 = t * 128
        yb = ybp.get()
        S.dma("sp", yb[:, 0:512], SC["YA"][r0:r0 + 128, :], writes=[yb])
        S.dma("sp", yb[:, 512:1536], SC["YS"][r0:r0 + 128, :], writes=[yb])
        S.dma("sp", yb[:, 1536:2048], SC["YR"][r0:r0 + 128, :], writes=[yb])
        gt = gp.get()
        S.dma("sp", gt[:, :, :], SC["G"][r0:r0 + 128, :].rearrange("t (b d) -> t b d", b=3), writes=[gt])
        x = xp.get()
        S.dma("sp", x[:, :], xio[r0:r0 + 128, :], writes=[x])
        yT = yTp.get()
        for half in range(2):
            pt = K.psum()
            ptb = pt.t[:, :].bitcast(BF16)
            for c in range(8):
                kc = half * 8 + c
                S.op("pe", lambda e, ptb=ptb, c=c, kc=kc, yb=yb: e.transpose(
                    ptb[:, c * 128:(c + 1) * 128], yb[:, kc * 128:(kc + 1) * 128], K.ident[:, :]),
                    reads=[yb, K.ident], writes=[pt])
            S.op("act", lambda e, yT=yT, ptb=ptb, half=half: e.copy(
                out=yT[:, half * 8:(half + 1) * 8, :], in_=ptb.rearrange("p (k c) -> p k c", k=8)),
                reads=[pt], writes=[yT])
        m = mp.get()
        for b, (k0, nk) in enumerate(((0, 4), (4, 8), (12, 4))):
            for nf in range(2):
                pp = K.psum()
                for kc in range(nk):
                    S.op("pe", lambda e, pp=pp, yT=yT, kc=kc, k0=k0, nf=nf, nk=nk: e.matmul(
                        pp[:, :], yT[:, k0 + kc, :], wbr[:, k0 + kc, nf * 512:(nf + 1) * 512],
                        start=(kc == 0), stop=(kc == nk - 1)), reads=[yT, wbr], writes=[pp])
                if b == 0:
                    S.op("dve", lambda e, m=m, pp=pp, gt=gt, nf=nf, b=b: e.tensor_tensor(
                        out=m[:, nf * 512:(nf + 1) * 512], in0=pp[:, :], in1=gt[:, b, nf * 512:(nf + 1) * 512],
                        op=ALU.mult), reads=[pp, gt], writes=[m])
                else:
                    tm = tmp.get()
                    S.op("dve", lambda e, tm=tm, pp=pp, gt=gt, nf=nf, b=b: e.tensor_tensor(
                        out=tm[:, :], in0=pp[:, :], in1=gt[:, b, nf * 512:(nf + 1) * 512], op=ALU.mult),
                        reads=[pp, gt], writes=[tm])
                    S.op("pool", lambda e, tm=tm, m=m, nf=nf: e.tensor_tensor(
                        out=m[:, nf * 512:(nf + 1) * 512], in0=m[:, nf * 512:(nf + 1) * 512], in1=tm[:, :],
                        op=ALU.add), reads=[tm, m], writes=[m])
        mb = mbp.get()
        S.op("act", lambda e, mb=mb, m=m: e.copy(out=mb[:, :], in_=m[:, :]), reads=[m], writes=[mb])
        mT = mTp.get()
        pt = K.psum()
        ptb = pt.t[:, :].bitcast(BF16)
        for c in range(8):
            S.op("pe", lambda e, ptb=ptb, c=c, mb=mb: e.transpose(
                ptb[:, c * 128:(c + 1) * 128], mb[:, c * 128:(c + 1) * 128], K.ident[:, :]),
                reads=[mb, K.ident], writes=[pt])
        S.op("act", lambda e, mT=mT, ptb=ptb: e.copy(out=mT[:, :, :], in_=ptb.rearrange("p (k c) -> p k c", k=8)),
             reads=[pt], writes=[mT])
        pp = [K.psum(), K.psum()]
        for nf in range(2):
            for kc in range(8):
                S.op("pe", lambda e, p_=pp[nf], mT=mT, kc=kc, nf=nf: e.matmul(
                    p_[:, :], mT[:, kc, :], wo[:, kc, nf * 512:(nf + 1) * 512], start=(kc == 0), stop=(kc == 7)),
                    reads=[mT, wo], writes=[pp[nf]])
        s_ = sm.get()
        jk = junk.get()
        for nf in range(2):
            S.op("act", lambda e, nf=nf, jk=jk, s_=s_, p_=pp[nf]: e.activation(
                out=jk[:, :], in_=p_[:, :], func=AF.Square, accum_out=s_[:, nf:nf + 1]),
                reads=[pp[nf]], writes=[jk, s_])
        S.op("dve", lambda e, s_=s_: e.tensor_tensor(out=s_[:, 2:3], in0=s_[:, 0:1], in1=s_[:, 1:2], op=ALU.add),
             reads=[s_], writes=[s_])
        S.op("act", lambda e, s_=s_: e.activation(out=s_[:, 3:4], in_=s_[:, 2:3], func=AF.Sqrt,
                                                  bias=K.epsc[:, 0:1], scale=1.0 / D),
             reads=[s_, K.epsc], writes=[s_])
        S.op("dve", lambda e, s_=s_: e.reciprocal(out=s_[:, 4:5], in_=s_[:, 3:4]), reads=[s_], writes=[s_])
        t_ = tt.get()
        for nf in range(2):
            S.op("dve", lambda e, nf=nf, t_=t_, s_=s_, p_=pp[nf]: e.scalar_tensor_tensor(
                out=t_[:, nf * 512:(nf + 1) * 512], in0=p_[:, :], scalar=s_[:, 4:5],
                in1=g3[:, nf * 512:(nf + 1) * 512], op0=ALU.mult, op1=ALU.mult),
                reads=[pp[nf], s_, g3], writes=[t_])
        S.op("pool", lambda e, t_=t_, x=x: e.tensor_tensor(out=x[:, :], in0=x[:, :], in1=t_[:, :], op=ALU.add),
             reads=[t_, x], writes=[x])
        S.dma("pool", xio[r0:r0 + 128, :], x[:, :], reads=[x])


def ssd_setup(K, A, l):
    S = K.S
    C = {}
    C["cw"] = K.alloc([128, 12, 5], F32)
    for k in range(5):
        S.dma("sp", C["cw"][:, :, k], A["ssm_conv_w"][l, k].rearrange("(fc p) -> p fc", p=128), writes=[C["cw"]],
              slow=True)
    C["cb"] = K.alloc([128, 12], F32)
    S.dma("sp", C["cb"][:, :], A["ssm_conv_b"][l].rearrange("(fc p) -> p fc", p=128), writes=[C["cb"]], slow=True)
    C["dtb"] = K.alloc([128, 32], F32)
    S.dma("sp", C["dtb"][:, :], A["ssm_dt_bias"][l].rearrange("a b -> (a b)").unsqueeze(0).broadcast_to([128, 32]),
          writes=[C["dtb"]])
    C["abc"] = K.alloc([128, 32], F32)
    S.dma("sp", C["abc"][:, :], A["ssm_a_log"][l].rearrange("a b -> (a b)").unsqueeze(0).broadcast_to([128, 32]),
          writes=[C["abc"]])
    S.op("act", lambda e: e.activation(out=C["abc"][:, :], in_=C["abc"][:, :], func=AF.Exp), reads=[C["abc"]],
         writes=[C["abc"]])
    S.op("dve", lambda e: e.tensor_scalar(out=C["abc"][:, :], in0=C["abc"][:, :], scalar1=-1.0, scalar2=0.0,
                                          op0=ALU.mult, op1=ALU.add), reads=[C["abc"]], writes=[C["abc"]])
    dsk = K.alloc([128, 32], F32)
    S.dma("sp", dsk[:, :], A["ssm_d"][l].rearrange("a b -> (a b)").unsqueeze(0).broadcast_to([128, 32]), writes=[dsk])
    C["dsum"] = K.alloc([128, 16], F32)
    S.op("dve", lambda e: e.tensor_tensor(out=C["dsum"][:, :], in0=dsk[:, 0:16], in1=dsk[:, 16:32], op=ALU.add),
         reads=[dsk], writes=[C["dsum"]])
    C["dI"] = K.alloc([128, 16, 128], F32)
    for h in range(16):
        S.op("dve", lambda e, h=h: e.tensor_scalar(out=C["dI"][:, h, :], in0=K.identf[:, :], scalar1=C["dsum"][:, h:h + 1],
                                                   scalar2=0.0, op0=ALU.mult, op1=ALU.add),
             reads=[K.identf, C["dsum"]], writes=[C["dI"]])
    C["ng"] = K.alloc([128, 1024], F32)
    S.dma("sp", C["ng"][:, :], A["ssm_norm_g"][l:l + 1, :].broadcast_to([128, 1024]), writes=[C["ng"]])
    return C


import os as _os
NB = int(_os.environ.get('MK_NB', '2'))


def ssd_pools(K):
    P = {}
    P["xw"] = Pool(K, NB, [128, 12, 132], F32)
    P["acc"] = Pool(K, 1, [128, 12, 128], F32)
    P["tmp"] = Pool(K, 1, [128, 12, 128], F32)
    P["xbcT"] = Pool(K, NB, [128, 12, 128], BF16)
    P["xs"] = Pool(K, NB, [128, 1024], BF16)
    P["bm"] = Pool(K, NB, [128, 256], BF16)
    P["dt"] = Pool(K, NB, [128, 32], F32)
    P["v"] = Pool(K, NB, [128, 256], F32)
    return P


def ssd_prep(K, SC, C, P, c, nchunk, cps, need_cm=True):
    S = K.S
    t0 = c * 128
    ntok = nchunk * 128
    seg, cis = divmod(c, cps)
    xw = P["xw"].get()
    XB = SC["XBCT"].rearrange("(fc p) t -> p fc t", p=128)
    lo, hi = max(t0 - 2, 0), min(t0 + 130, ntok)
    S.dma("sp", xw[:, :, lo - (t0 - 2):hi - (t0 - 2)], XB[:, :, lo:hi], writes=[xw])
    if t0 == 0:
        S.op("pool", lambda e: e.memset(xw[:, :, 0:2], 0.0), writes=[xw])
    elif cis == 0:
        S.op("pool", lambda e, seg=seg: e.tensor_scalar(out=xw[:, :, 0:2], in0=xw[:, :, 0:2],
                                                        scalar1=K.flags[:, seg - 1:seg], scalar2=0.0,
                                                        op0=ALU.mult, op1=ALU.add), reads=[xw, K.flags], writes=[xw])
    if t0 + 130 > ntok:
        S.op("pool", lambda e: e.memset(xw[:, :, 130:132], 0.0), writes=[xw])
    elif cis == cps - 1:
        S.op("pool", lambda e, seg=seg: e.tensor_scalar(out=xw[:, :, 130:132], in0=xw[:, :, 130:132],
                                                        scalar1=K.flags[:, seg:seg + 1], scalar2=0.0,
                                                        op0=ALU.mult, op1=ALU.add), reads=[xw, K.flags], writes=[xw])
    acc = P["acc"].get()
    tmp = P["tmp"].get()
    cw = C["cw"]
    for k in range(5):
        wk = cw[:, :, k:k + 1].broadcast_to([128, 12, 128])
        if k == 0:
            S.op("dve", lambda e, wk=wk: e.tensor_tensor(out=acc[:, :, :], in0=xw[:, :, 0:128], in1=wk, op=ALU.mult),
                 reads=[xw, cw], writes=[acc])
        else:
            S.op("pool", lambda e, wk=wk, k=k: e.tensor_tensor(out=tmp[:, :, :], in0=xw[:, :, k:k + 128], in1=wk,
                                                               op=ALU.mult), reads=[xw, cw], writes=[tmp])
            S.op("dve", lambda e: e.tensor_tensor(out=acc[:, :, :], in0=acc[:, :, :], in1=tmp[:, :, :], op=ALU.add),
                 reads=[acc, tmp], writes=[acc])
    xbcT = P["xbcT"].get()
    for fc in range(12):
        S.op("act", lambda e, fc=fc: e.activation(out=xbcT[:, fc, :], in_=acc[:, fc, :], func=AF.Silu,
                                                  bias=C["cb"][:, fc:fc + 1]), reads=[acc, C["cb"]], writes=[xbcT])
    xs = P["xs"].get()
    pt = K.psum()
    ptb = pt.t[:, :].bitcast(BF16)
    for j in range(8):
        S.op("pe", lambda e, j=j: e.transpose(ptb[:, j * 128:(j + 1) * 128], xbcT[:, j, :], K.ident[:, :]),
             reads=[xbcT, K.ident], writes=[pt])
    S.op("act", lambda e: e.copy(out=xs[:, :], in_=ptb[:, :]), reads=[pt], writes=[xs])
    bm = P["bm"].get()
    pt2 = K.psum()
    ptb2 = pt2.t[:, :].bitcast(BF16)
    for g in range(2):
        S.op("pe", lambda e, g=g: e.transpose(ptb2[:, g * 128:(g + 1) * 128], xbcT[:, 8 + g, :], K.ident[:, :]),
             reads=[xbcT, K.ident], writes=[pt2])
    S.op("act", lambda e: e.copy(out=bm[:, :], in_=ptb2[:, 0:256]), reads=[pt2], writes=[bm])
    dt = P["dt"].get()
    S.dma("sp", dt[:, :], SC["DT"][t0:t0 + 128, :], writes=[dt])
    V = P["v"].get()
    S.op("dve", lambda e: e.tensor_tensor(out=V[:, 0:32], in0=dt[:, :], in1=C["dtb"][:, :], op=ALU.add),
         reads=[dt, C["dtb"]], writes=[V])
    S.op("act", lambda e: e.activation(out=V[:, 0:32], in_=V[:, 0:32], func=AF.Exp), reads=[V], writes=[V])
    S.op("act", lambda e: e.activation(out=V[:, 0:32], in_=V[:, 0:32], func=AF.Ln, bias=K.onec[:, 0:1]),
         reads=[V, K.onec], writes=[V])
    S.op("act", lambda e: e.activation(out=V[:, 32:64], in_=V[:, 0:32], func=AF.Ln), reads=[V], writes=[V])
    S.op("dve", lambda e: e.tensor_tensor(out=V[:, 64:96], in0=V[:, 0:32], in1=C["abc"][:, :], op=ALU.mult),
         reads=[V, C["abc"]], writes=[V])
    pc = K.psum()
    S.op("pe", lambda e: e.matmul(pc[:, 0:16], K.uinc[:, :], V[:, 64:80], start=True, stop=True),
         reads=[K.uinc, V], writes=[pc])
    S.op("pe", lambda e: e.matmul(pc[:, 16:32], K.uexc[:, :], V[:, 80:96], start=True, stop=True),
         reads=[K.uexc, V], writes=[pc])
    S.op("pe", lambda e: e.matmul(pc[:, 32:64], K.onesf[:, :], V[:, 64:96], start=True, stop=True),
         reads=[K.onesf, V], writes=[pc])
    S.op("dve", lambda e: e.tensor_copy(out=V[:, 96:160], in_=pc[:, 0:64]), reads=[pc], writes=[V])
    S.op("dve", lambda e: e.tensor_tensor(out=V[:, 160:176], in0=V[:, 32:48], in1=V[:, 96:112], op=ALU.subtract),
         reads=[V], writes=[V])
    S.op("dve", lambda e: e.tensor_tensor(out=V[:, 176:192], in0=V[:, 48:64], in1=V[:, 112:128], op=ALU.add),
         reads=[V], writes=[V])
    S.op("dve", lambda e: e.tensor_tensor(out=V[:, 192:208], in0=V[:, 160:176], in1=V[:, 128:144], op=ALU.add),
         reads=[V], writes=[V])
    S.op("act", lambda e: e.activation(out=V[:, 192:208], in_=V[:, 192:208], func=AF.Exp), reads=[V], writes=[V])
    S.op("act", lambda e: e.activation(out=V[:, 208:224], in_=V[:, 176:192], func=AF.Exp), reads=[V], writes=[V])
    S.op("act", lambda e: e.activation(out=V[:, 224:240], in_=V[:, 96:112], func=AF.Exp), reads=[V], writes=[V])
    S.op("dve", lambda e: e.tensor_tensor(out=V[:, 240:256], in0=V[:, 144:160], in1=V[:, 112:128], op=ALU.subtract),
         reads=[V], writes=[V])
    S.op("act", lambda e: e.activation(out=V[:, 240:256], in_=V[:, 240:256], func=AF.Exp), reads=[V], writes=[V])
    S.op("act", lambda e: e.activation(out=V[:, 128:160], in_=V[:, 128:160], func=AF.Exp), reads=[V], writes=[V])
    return {"xbcT": xbcT, "xs": xs, "bm": bm, "V": V}


def ssd_state_step(K, T, H, wcol, ecol, PS):
    S = K.S
    xs, bm, V = T["xs"], T["bm"], T["V"]
    xd = PS["xd"].get()
    S.op("pool", lambda e: e.tensor_tensor(out=xd[:, :].rearrange("p (h d) -> p h d", h=16),
                                           in0=xs[:, :].rearrange("p (h d) -> p h d", h=16),
                                           in1=V[:, wcol:wcol + 16].unsqueeze(2).broadcast_to([128, 16, 64]),
                                           op=ALU.mult), reads=[xs, V], writes=[xd])
    S.op("dve", lambda e: e.tensor_tensor(out=H[:, :].rearrange("p (h d) -> p h d", h=16),
                                          in0=H[:, :].rearrange("p (h d) -> p h d", h=16),
                                          in1=V[:, ecol:ecol + 16].unsqueeze(2).broadcast_to([128, 16, 64]),
                                          op=ALU.mult), reads=[H, V], writes=[H])
    for g in range(2):
        pp = K.psum()
        S.op("pe", lambda e, pp=pp, g=g: e.matmul(pp[:, :], bm[:, g * 128:(g + 1) * 128], xd[:, g * 512:(g + 1) * 512],
                                                  start=True, stop=True), reads=[bm, xd], writes=[pp])
        S.op("dve", lambda e, pp=pp, g=g: e.tensor_tensor(out=H[:, g * 512:(g + 1) * 512], in0=H[:, g * 512:(g + 1) * 512],
                                                          in1=pp[:, :], op=ALU.add), reads=[pp, H], writes=[H])


def ssd_bwd_pass(K, SC, C, nchunk, cps):
    S = K.S
    P = ssd_pools(K)
    PS = {"xd": Pool(K, NB, [128, 1024], BF16)}
    H = K.alloc([128, 1024], F32)
    hbp = Pool(K, NB, [128, 1024], BF16)
    S.op("dve", lambda e: e.memset(H[:, :], 0.0), writes=[H])
    def body(c):
        seg, cis = divmod(c, cps)
        if cis == cps - 1 and c != nchunk - 1:
            S.op("dve", lambda e, seg=seg: e.tensor_scalar(out=H[:, :], in0=H[:, :], scalar1=K.flags[:, seg:seg + 1],
                                                           scalar2=0.0, op0=ALU.mult, op1=ALU.add),
                 reads=[H, K.flags], writes=[H])
        T = ssd_prep(K, SC, C, P, c, nchunk, cps)
        ssd_state_step(K, T, H, 208, 144, PS)
        hb = hbp.get()
        S.op("act", lambda e, hb=hb: e.copy(out=hb[:, :], in_=H[:, :]), reads=[H], writes=[hb])
        S.dma("pool", SC["HB"][c], hb[:, :], reads=[hb])

    for c in range(nchunk - 1, -1, -1):
        body(c)


def ssd_fwd_pass(K, SC, C, nchunk, cps):
    S = K.S
    P = ssd_pools(K)
    PS = {"xd": Pool(K, NB, [128, 1024], BF16)}
    H = K.alloc([128, 1024], F32)
    Hbf = K.alloc([128, 1024], BF16)
    S.op("dve", lambda e: e.memset(H[:, :], 0.0), writes=[H])
    hbp = Pool(K, NB, [128, 1024], BF16)
    cbp = Pool(K, NB, [128, 4, 128], F32)
    tq = Pool(K, NB, [128, 4, 128], F32)
    eq = Pool(K, NB, [128, 4, 128], F32)
    mq = Pool(K, NB, [128, 4, 128], F32)
    Mp = Pool(K, NB, [128, 16, 128], BF16)
    yp = Pool(K, NB, [128, 1024], F32)
    t1p = Pool(K, NB, [128, 512], F32)
    zp = Pool(K, NB, [128, 1024], F32)
    ybp = Pool(K, NB, [128, 1024], BF16)
    smp = Pool(K, 4, [128, 8], F32)
    jk = Pool(K, 1, [128, 512], BF16)
    def body(c):
        seg, cis = divmod(c, cps)
        t0 = c * 128
        if cis == 0 and c != 0:
            S.op("dve", lambda e, seg=seg: e.tensor_scalar(out=H[:, :], in0=H[:, :], scalar1=K.flags[:, seg - 1:seg],
                                                           scalar2=0.0, op0=ALU.mult, op1=ALU.add),
                 reads=[H, K.flags], writes=[H])
        S.op("act", lambda e: e.copy(out=Hbf[:, :], in_=H[:, :]), reads=[H], writes=[Hbf])
        hb = hbp.get()
        if c == nchunk - 1:
            S.op("pool", lambda e, hb=hb: e.memset(hb[:, :], 0.0), writes=[hb])
        else:
            S.dma("sp", hb[:, :], SC["HB"][c + 1], writes=[hb])
            if cis == cps - 1:
                S.op("pool", lambda e, hb=hb, seg=seg: e.tensor_scalar(
                    out=hb[:, :], in0=hb[:, :], scalar1=K.flags[:, seg:seg + 1], scalar2=0.0, op0=ALU.mult,
                    op1=ALU.add), reads=[hb, K.flags], writes=[hb])
        T = ssd_prep(K, SC, C, P, c, nchunk, cps)
        xbcT, xs, V = T["xbcT"], T["xs"], T["V"]
        cbm = cbp.get()
        for g in range(2):
            pp = K.psum()
            S.op("pe", lambda e, pp=pp, g=g: e.matmul(pp[:, 0:128], xbcT[:, 8 + g, :], xbcT[:, 10 + g, :],
                                                      start=True, stop=True), reads=[xbcT], writes=[pp])
            S.op("dve", lambda e, pp=pp, g=g: e.tensor_tensor(out=cbm[:, 2 * g, :], in0=pp[:, 0:128], in1=K.mskf[:, :],
                                                              op=ALU.mult), reads=[pp, K.mskf], writes=[cbm])
            S.op("dve", lambda e, pp=pp, g=g: e.tensor_tensor(out=cbm[:, 2 * g + 1, :], in0=pp[:, 0:128],
                                                              in1=K.mskb[:, :], op=ALU.mult),
                 reads=[pp, K.mskb], writes=[cbm])
        M = Mp.get()
        for qd in range(4):
            g = qd // 2
            mqs = []
            for d in range(2):
                pa = K.psum()
                for hh in range(4):
                    h = qd * 4 + hh
                    col = 64 + d * 16 + h
                    S.op("pe", lambda e, pa=pa, hh=hh, col=col, d=d: e.matmul(
                        pa[:, hh * 128:(hh + 1) * 128], V[:, col:col + 1].broadcast_to([128, 128]),
                        (K.uinc if d == 0 else K.uexc)[:, :], start=True, stop=True),
                        reads=[V, K.uinc, K.uexc], writes=[pa])
                t = tq.get()
                pav = pa[:, :].rearrange("p (a b) -> p a b", a=4)
                if d == 0:
                    vb = V[:, 96 + qd * 4:96 + qd * 4 + 4].unsqueeze(2).broadcast_to([128, 4, 128])
                    S.op("dve", lambda e, t=t, pav=pav, vb=vb: e.tensor_tensor(out=t[:, :, :], in0=pav, in1=vb,
                                                                               op=ALU.subtract),
                         reads=[pa, V], writes=[t])
                else:
                    vb = V[:, 112 + qd * 4:112 + qd * 4 + 4].unsqueeze(2).broadcast_to([128, 4, 128])
                    S.op("dve", lambda e, t=t, pav=pav, vb=vb: e.tensor_tensor(out=t[:, :, :], in0=vb, in1=pav,
                                                                               op=ALU.subtract),
                         reads=[pa, V], writes=[t])
                S.op("dve", lambda e, t=t: e.tensor_scalar(out=t[:, :, :], in0=t[:, :, :], scalar1=0.0, scalar2=0.0,
                                                            op0=ALU.min, op1=ALU.add), reads=[t], writes=[t])
                E = eq.get()
                for hh in range(4):
                    h = qd * 4 + hh
                    S.op("act", lambda e, E=E, t=t, hh=hh, h=h, d=d: e.activation(
                        out=E[:, hh, :], in_=t[:, hh, :], func=AF.Exp, bias=V[:, 32 + d * 16 + h:33 + d * 16 + h]),
                        reads=[t, V], writes=[E])
                m_ = mq.get()
                S.op("dve", lambda e, m_=m_, E=E, g=g, d=d: e.tensor_tensor(
                    out=m_[:, :, :], in0=E[:, :, :], in1=cbm[:, 2 * g + d:2 * g + d + 1, :].broadcast_to([128, 4, 128]),
                    op=ALU.mult), reads=[E, cbm], writes=[m_])
                mqs.append(m_)
            S.op("dve", lambda e, a=mqs[0], b=mqs[1]: e.tensor_tensor(out=a[:, :, :], in0=a[:, :, :], in1=b[:, :, :],
                                                                       op=ALU.add), reads=[mqs[0], mqs[1]], writes=[mqs[0]])
            S.op("dve", lambda e, a=mqs[0], qd=qd: e.tensor_tensor(out=M[:, qd * 4:(qd + 1) * 4, :], in0=a[:, :, :],
                                                                    in1=C["dI"][:, qd * 4:(qd + 1) * 4, :], op=ALU.add),
                 reads=[mqs[0], C["dI"]], writes=[M])
        yi = [K.psum(), K.psum()]
        yo = [[K.psum(), K.psum()], [K.psum(), K.psum()]]
        for h in range(16):
            g, hh = divmod(h, 8)
            S.op("pe", lambda e, h=h, g=g, hh=hh: e.matmul(yi[g][:, hh * 64:(hh + 1) * 64], M[:, h, :],
                                                           xs[:, h * 64:(h + 1) * 64], start=True, stop=True),
                 reads=[M, xs], writes=[yi[g]])
        for d, Hs in enumerate((Hbf, hb)):
            for g in range(2):
                S.op("pe", lambda e, d=d, g=g, Hs=Hs: e.matmul(yo[d][g][:, :], xbcT[:, 10 + g, :],
                                                               Hs[:, g * 512:(g + 1) * 512], start=True, stop=True),
                     reads=[xbcT, Hs], writes=[yo[d][g]])
        y = yp.get()
        for g in range(2):
            t1 = t1p.get()
            scf = V[:, 224 + g * 8:232 + g * 8].unsqueeze(2).broadcast_to([128, 8, 64])
            scb = V[:, 240 + g * 8:248 + g * 8].unsqueeze(2).broadcast_to([128, 8, 64])
            S.op("dve", lambda e, t1=t1, g=g, scf=scf: e.tensor_tensor(
                out=t1[:, :].rearrange("p (h d) -> p h d", h=8), in0=yo[0][g][:, :].rearrange("p (h d) -> p h d", h=8),
                in1=scf, op=ALU.mult), reads=[yo[0][g], V], writes=[t1])
            S.op("dve", lambda e, t1=t1, g=g: e.tensor_tensor(out=t1[:, :], in0=t1[:, :], in1=yi[g][:, :], op=ALU.add),
                 reads=[t1, yi[g]], writes=[t1])
            S.op("dve", lambda e, g=g, scb=scb, y=y: e.tensor_tensor(
                out=y[:, g * 512:(g + 1) * 512].rearrange("p (h d) -> p h d", h=8),
                in0=yo[1][g][:, :].rearrange("p (h d) -> p h d", h=8), in1=scb, op=ALU.mult),
                reads=[yo[1][g], V], writes=[y])
            S.op("pool", lambda e, g=g, t1=t1, y=y: e.tensor_tensor(out=y[:, g * 512:(g + 1) * 512],
                                                                    in0=y[:, g * 512:(g + 1) * 512], in1=t1[:, :],
                                                                    op=ALU.add), reads=[t1, y], writes=[y])
        if "DY" in SC:
            S.dma("pool", SC["DY"][t0:t0 + 128, :], y[:, :], reads=[y])
            S.dma("pool", SC["DV"][t0:t0 + 128, :], V[:, :], reads=[V])
            S.dma("pool", SC["DM"][c], M[:, :, :], reads=[M])
        z = zp.get()
        S.dma("sp", z[:, :], SC["Z"][t0:t0 + 128, :], writes=[z])
        S.op("act", lambda e, z=z: e.activation(out=z[:, :], in_=z[:, :], func=AF.Silu), reads=[z], writes=[z])
        S.op("pool", lambda e, z=z, y=y: e.tensor_tensor(out=y[:, :], in0=y[:, :], in1=z[:, :], op=ALU.mult),
             reads=[y, z], writes=[y])
        sm = smp.get()
        j_ = jk.get()
        yb = ybp.get()
        for g in range(2):
            S.op("act", lambda e, g=g, sm=sm, j_=j_, y=y: e.activation(
                out=j_[:, :], in_=y[:, g * 512:(g + 1) * 512], func=AF.Square, accum_out=sm[:, g:g + 1]),
                reads=[y], writes=[j_, sm])
        S.op("act", lambda e, sm=sm: e.activation(out=sm[:, 2:4], in_=sm[:, 0:2], func=AF.Sqrt, bias=K.epsc[:, 0:1],
                                                  scale=1.0 / 512), reads=[sm, K.epsc], writes=[sm])
        S.op("dve", lambda e, sm=sm: e.reciprocal(out=sm[:, 4:6], in_=sm[:, 2:4]), reads=[sm], writes=[sm])
        for g in range(2):
            S.op("dve", lambda e, g=g, sm=sm, y=y, yb=yb: e.scalar_tensor_tensor(
                out=yb[:, g * 512:(g + 1) * 512], in0=y[:, g * 512:(g + 1) * 512], scalar=sm[:, 4 + g:5 + g],
                in1=C["ng"][:, g * 512:(g + 1) * 512], op0=ALU.mult, op1=ALU.mult),
                reads=[y, sm, C["ng"]], writes=[yb])
        S.dma("pool", SC["YS"][t0:t0 + 128, :], yb[:, :], reads=[yb])
        ssd_state_step(K, T, H, 192, 128, PS)

    for c in range(nchunk):
        body(c)


RL = float(_os.environ.get('MK_RL', '99'))
LW_C = 0.6065306597126334
GN_EPS = 64e-5


def rwkv_setup(K, A, l):
    S = K.S
    C = {}

    def ld(name, shape, dt, q, src, slow=False):
        C[name] = K.alloc(shape, dt)
        S.dma(q, C[name][tuple(slice(None) for _ in shape)], src, writes=[C[name]], slow=slow)

    ld("wup", [64, 2, 512], BF16, "pool", A["rwkv_w_up"][l].rearrange("n l c -> l n c"))
    ld("aup", [64, 2, 512], BF16, "pool", A["rwkv_a_up"][l].rearrange("n l c -> l n c"))
    C["gup"] = K.alloc([64, 3, 512], BF16)
    S.dma("pool", C["gup"][:, 0:2, :], A["rwkv_g_up"][l][0:128, :].rearrange("(q l) c -> l q c", l=64), writes=[C["gup"]])
    S.dma("pool", C["gup"][0:32, 2, :], A["rwkv_g_up"][l][128:160, :], writes=[C["gup"]])
    C["w0"] = K.alloc([128, 2, 512], F32)
    for n in range(2):
        S.dma("sp", C["w0"][:, n, :], A["rwkv_w0"][l][n:n + 1, :].broadcast_to([128, 512]), writes=[C["w0"]])
    C["a0"] = K.alloc([64, 2, 8], F32)
    for n in range(2):
        S.dma("sp", C["a0"][:, n, :], A["rwkv_a0"][l][n].rearrange("(h j) -> j h", j=64), writes=[C["a0"]], slow=True)
    for nm, src in (("kk_", A["rwkv_k_k"][l]), ("ka", A["rwkv_k_a"][l])):
        C[nm] = K.alloc([64, 8], F32)
        S.dma("sp", C[nm][:, :], src.rearrange("(h j) -> j h", j=64), writes=[C[nm]], slow=True)
    C["rk"] = K.alloc([64, 8], F32)
    S.dma("sp", C["rk"][:, :], A["rwkv_r_k"][l].rearrange("h j -> j h"), writes=[C["rk"]], slow=True)
    C["omka"] = K.alloc([64, 8], F32)
    S.op("dve", lambda e: e.tensor_scalar(out=C["omka"][:, :], in0=C["ka"][:, :], scalar1=-1.0, scalar2=1.0,
                                          op0=ALU.mult, op1=ALU.add), reads=[C["ka"]], writes=[C["omka"]])
    C["mu"] = K.alloc([64, 3, 31], F32)
    S.op("dve", lambda e: e.memset(C["mu"][:, :, :], 0.0), writes=[C["mu"]])
    for m in range(2):
        S.dma("sp", C["mu"][:, m, 0:30], A["rwkv_mu"][l][m, 0:1920].rearrange("(g j) -> j g", j=64), writes=[C["mu"]],
              slow=True)
        S.dma("sp", C["mu"][0:32, m, 30:31], A["rwkv_mu"][l][m, 1920:1952].rearrange("(g j) -> j g", j=32),
              writes=[C["mu"]], slow=True)
    S.op("dve", lambda e: e.tensor_tensor(out=C["mu"][:, 2, :], in0=C["mu"][:, 0, :], in1=C["mu"][:, 1, :], op=ALU.add),
         reads=[C["mu"]], writes=[C["mu"]])
    S.op("dve", lambda e: e.tensor_scalar(out=C["mu"][:, 2, :], in0=C["mu"][:, 2, :], scalar1=-1.0, scalar2=1.0,
                                          op0=ALU.mult, op1=ALU.add), reads=[C["mu"]], writes=[C["mu"]])
    C["lng"] = K.alloc([128, 512], F32)
    S.dma("sp", C["lng"][:, :], A["rwkv_ln_g"][l:l + 1, :].broadcast_to([128, 512]), writes=[C["lng"]])
    C["lnb"] = K.alloc([128, 512], F32)
    S.dma("sp", C["lnb"][:, :], A["rwkv_ln_b"][l:l + 1, :].broadcast_to([128, 512]), writes=[C["lnb"]])
    C["gneps"] = K.alloc([128, 1], F32)
    S.op("dve", lambda e: e.memset(C["gneps"][:, :], GN_EPS), writes=[C["gneps"]])
    return C


def shift_pass(K, SC, A, l, ntok, seglen):
    S = K.S
    K.new_pass()
    mu = K.alloc([128, 3, 16], F32)
    S.op("dve", lambda e: e.memset(mu[:, :, :], 0.0), writes=[mu])
    for m in range(2):
        S.dma("sp", mu[:, m, 0:15], A["rwkv_mu"][l][m, 0:1920].rearrange("(g j) -> j g", j=128), writes=[mu], slow=True)
        S.dma("sp", mu[0:32, m, 15:16], A["rwkv_mu"][l][m, 1920:1952].rearrange("(g j) -> j g", j=32), writes=[mu],
              slow=True)
    S.op("dve", lambda e: e.tensor_tensor(out=mu[:, 2, :], in0=mu[:, 0, :], in1=mu[:, 1, :], op=ALU.add),
         reads=[mu], writes=[mu])
    S.op("dve", lambda e: e.tensor_scalar(out=mu[:, 2, :], in0=mu[:, 2, :], scalar1=-1.0, scalar2=1.0, op0=ALU.mult,
                                          op1=ALU.add), reads=[mu], writes=[mu])
    xp = Pool(K, 2, [128, 16, 514], F32)
    ap = Pool(K, 2, [128, 16, 512], F32)
    RW = SC["RWT"]

    def body(mt):
        t0 = mt * 512
        X = xp.get()
        lo, hi = max(t0 - 1, 0), min(t0 + 513, ntok)
        a_, b_ = lo - (t0 - 1), hi - (t0 - 1)
        S.dma("sp", X[:, 0:15, a_:b_], RW[0:1920, lo:hi].rearrange("(g j) t -> j g t", j=128), writes=[X])
        S.dma("sp", X[0:32, 15, a_:b_], RW[1920:1952, lo:hi], writes=[X])
        seg = t0 // seglen
        if t0 == 0:
            S.op("pool", lambda e: e.memset(X[:, :, 0:1], 0.0), writes=[X])
        elif t0 % seglen == 0:
            S.op("pool", lambda e: e.tensor_scalar(out=X[:, :, 0:1], in0=X[:, :, 0:1], scalar1=K.flags[:, seg - 1:seg],
                                                   scalar2=0.0, op0=ALU.mult, op1=ALU.add), reads=[X, K.flags], writes=[X])
        if t0 + 512 >= ntok:
            S.op("pool", lambda e: e.memset(X[:, :, 513:514], 0.0), writes=[X])
        elif (t0 + 512) % seglen == 0:
            S.op("pool", lambda e: e.tensor_scalar(out=X[:, :, 513:514], in0=X[:, :, 513:514],
                                                   scalar1=K.flags[:, seg:seg + 1], scalar2=0.0, op0=ALU.mult,
                                                   op1=ALU.add), reads=[X, K.flags], writes=[X])
        acc = ap.get()
        for fc in range(16):
            eng = "dve"
            S.op("pool", lambda e, fc=fc: e.tensor_scalar(out=acc[:, fc, :], in0=X[:, fc, 1:513], scalar1=mu[:, 2, fc:fc + 1],
                                                       scalar2=0.0, op0=ALU.mult, op1=ALU.add),
                 reads=[X, mu], writes=[acc])
            S.op(eng, lambda e, fc=fc: e.scalar_tensor_tensor(out=acc[:, fc, :], in0=X[:, fc, 0:512],
                                                              scalar=mu[:, 0, fc:fc + 1], in1=acc[:, fc, :],
                                                              op0=ALU.mult, op1=ALU.add), reads=[X, mu, acc], writes=[acc])
            S.op(eng, lambda e, fc=fc: e.scalar_tensor_tensor(out=acc[:, fc, :], in0=X[:, fc, 2:514],
                                                              scalar=mu[:, 1, fc:fc + 1], in1=acc[:, fc, :],
                                                              op0=ALU.mult, op1=ALU.add), reads=[X, mu, acc], writes=[acc])
        S.dma("pool", SC["RWS"][0:1920, t0:t0 + 512].rearrange("(g j) t -> j g t", j=128), acc[:, 0:15, :], reads=[acc])
        S.dma("pool", SC["RWS"][1920:1952, t0:t0 + 512], acc[0:32, 15, :], reads=[acc])

    for mt in range(ntok // 512):
        body(mt)


def rwkv_pools(K):
    P = {}
    P["Pt"] = Pool(K, 2, [64, 31, 128], F32)
    P["tw"] = Pool(K, 2, [64, 128], BF16)
    P["ad"] = Pool(K, 2, [64, 128], BF16)
    P["sg"] = Pool(K, 2, [128, 512], F32)
    P["f8"] = Pool(K, 11, [64, 8, 128], F32)
    P["b8"] = Pool(K, 8, [64, 8, 128], BF16)
    P["tok"] = Pool(K, 6, [128, 512], BF16)
    P["vf"] = Pool(K, 2, [128, 512], F32)
    P["pl"] = Pool(K, 2, [64, 8], F32)
    P["mat"] = Pool(K, 16, [128, 4, 128], BF16)
    P["res"] = Pool(K, 12, [128, 4, 128], BF16)
    P["w"] = Pool(K, 2, [128, 512], F32)
    P["rkc"] = Pool(K, 2, [128, 8], F32)
    return P


def rwkv_prep(K, SC, C, P, c, nchunk, cps, n, want_g):
    S = K.S
    t0 = c * 128
    ntok = nchunk * 128
    seg, cis = divmod(c, cps)
    Pt = P["Pt"].get()
    S.dma("sp", Pt[:, 0:30, :], SC["RWS"][0:1920, t0:t0 + 128].rearrange("(g j) t -> j g t", j=64), writes=[Pt])
    S.dma("sp", Pt[0:32, 30, :], SC["RWS"][1920:1952, t0:t0 + 128], writes=[Pt])
    if RL <= 1:
        return None
    rT, kT, vT = Pt[:, 0:8, :], Pt[:, 8:16, :], Pt[:, 16:24, :]
    tw = P["tw"].get()
    S.op("act", lambda e: e.activation(out=tw[:, :], in_=Pt[:, 24 + n, :], func=AF.Tanh), reads=[Pt], writes=[tw])
    pu = K.psum()
    S.op("pe", lambda e: e.matmul(pu[:, :], tw[:, :], C["wup"][:, n, :], start=True, stop=True),
         reads=[tw, C["wup"]], writes=[pu])
    sg = P["sg"].get()
    S.op("dve", lambda e: e.tensor_tensor(out=sg[:, :], in0=pu[:, :], in1=C["w0"][:, n, :], op=ALU.add),
         reads=[pu, C["w0"]], writes=[sg])
    S.op("act", lambda e: e.activation(out=sg[:, :], in_=sg[:, :], func=AF.Sigmoid), reads=[sg], writes=[sg])
    if RL <= 2:
        return None
    uin = K.uinc if n == 0 else K.mskb
    uex = K.uexc if n == 0 else K.ugt
    eP, eN, ePx = P["f8"].get(), P["f8"].get(), P["f8"].get()
    for hq in range(2):
        pi, px = K.psum(), K.psum()
        for hh in range(4):
            h = hq * 4 + hh
            S.op("pe", lambda e, pi=pi, hh=hh, h=h: e.matmul(pi[0:64, hh * 128:(hh + 1) * 128], sg[:, h * 64:(h + 1) * 64],
                                                             uin[:, :], start=True, stop=True),
                 reads=[sg, uin], writes=[pi])
            S.op("pe", lambda e, px=px, hh=hh, h=h: e.matmul(px[0:64, hh * 128:(hh + 1) * 128], sg[:, h * 64:(h + 1) * 64],
                                                             uex[:, :], start=True, stop=True),
                 reads=[sg, uex], writes=[px])
        sl = slice(hq * 4, hq * 4 + 4)
        S.op("act", lambda e, pi=pi, sl=sl: e.activation(out=eP[:, sl, :].rearrange("p a b -> p (a b)"), in_=pi[0:64, :],
                                                         func=AF.Exp, scale=-LW_C), reads=[pi], writes=[eP])
        S.op("act", lambda e, pi=pi, sl=sl: e.activation(out=eN[:, sl, :].rearrange("p a b -> p (a b)"), in_=pi[0:64, :],
                                                         func=AF.Exp, scale=LW_C), reads=[pi], writes=[eN])
        S.op("act", lambda e, px=px, sl=sl: e.activation(out=ePx[:, sl, :].rearrange("p a b -> p (a b)"), in_=px[0:64, :],
                                                         func=AF.Exp, scale=-LW_C), reads=[px], writes=[ePx])
    last = 127 if n == 0 else 0
    pl = P["pl"].get()
    S.op("dve", lambda e: e.tensor_copy(out=pl[:, :], in_=eP[:, :, last]), reads=[eP], writes=[pl])
    if RL <= 3:
        return None
    ad = P["ad"].get()
    S.op("act", lambda e: e.copy(out=ad[:, :], in_=Pt[:, 26 + n, :]), reads=[Pt], writes=[ad])
    ic = P["f8"].get()
    for hq in range(2):
        pa = K.psum()
        for hh in range(4):
            h = hq * 4 + hh
            S.op("pe", lambda e, pa=pa, hh=hh, h=h: e.matmul(pa[0:64, hh * 128:(hh + 1) * 128],
                                                             C["aup"][:, n, h * 64:(h + 1) * 64], ad[:, :],
                                                             start=True, stop=True), reads=[C["aup"], ad], writes=[pa])
        sl = slice(hq * 4, hq * 4 + 4)
        S.op("dve", lambda e, pa=pa, sl=sl: e.tensor_tensor(
            out=ic[:, sl, :], in0=pa[0:64, :].rearrange("p (a b) -> p a b", a=4),
            in1=C["a0"][:, n, sl].unsqueeze(2).broadcast_to([64, 4, 128]), op=ALU.add), reads=[pa, C["a0"]], writes=[ic])
    S.op("act", lambda e: e.activation(out=ic[:, :, :], in_=ic[:, :, :], func=AF.Sigmoid), reads=[ic], writes=[ic])
    if RL <= 4:
        return None
    kk = P["f8"].get()
    sq = P["f8"].get()
    h8 = lambda t_: t_[:, :].unsqueeze(2).broadcast_to([64, 8, 128])
    S.op("dve", lambda e: e.tensor_tensor(out=kk[:, :, :], in0=kT, in1=h8(C["kk_"]), op=ALU.mult),
         reads=[Pt, C["kk_"]], writes=[kk])
    S.op("pool", lambda e: e.tensor_tensor(out=sq[:, :, :], in0=kk[:, :, :], in1=kk[:, :, :], op=ALU.mult),
         reads=[kk], writes=[sq])
    for hq in range(2):
        pn = K.psum()
        sl = slice(hq * 4, hq * 4 + 4)
        S.op("pe", lambda e, pn=pn, sl=sl: e.matmul(pn[0:64, :], K.onesf[0:64, 0:64],
                                                    sq[:, sl, :].rearrange("p a b -> p (a b)"), start=True, stop=True),
             reads=[K.onesf, sq], writes=[pn])
        S.op("dve", lambda e, pn=pn, sl=sl: e.tensor_scalar(out=sq[:, sl, :].rearrange("p a b -> p (a b)"), in0=pn[0:64, :],
                                                            scalar1=1e-24, scalar2=0.0, op0=ALU.max, op1=ALU.add),
             reads=[pn], writes=[sq])
    S.op("act", lambda e: e.activation(out=sq[:, :, :], in_=sq[:, :, :], func=AF.Ln), reads=[sq], writes=[sq])
    S.op("act", lambda e: e.activation(out=sq[:, :, :], in_=sq[:, :, :], func=AF.Exp, scale=-0.5), reads=[sq], writes=[sq])
    S.op("dve", lambda e: e.tensor_tensor(out=kk[:, :, :], in0=kk[:, :, :], in1=sq[:, :, :], op=ALU.mult),
         reads=[kk, sq], writes=[kk])
    if RL <= 5:
        return None
    km = P["f8"].get()
    S.op("pool", lambda e: e.tensor_tensor(out=km[:, :, :], in0=ic[:, :, :], in1=h8(C["ka"]), op=ALU.mult),
         reads=[ic, C["ka"]], writes=[km])
    S.op("pool", lambda e: e.tensor_tensor(out=km[:, :, :], in0=km[:, :, :], in1=h8(C["omka"]), op=ALU.add),
         reads=[km, C["omka"]], writes=[km])
    S.op("pool", lambda e: e.tensor_tensor(out=km[:, :, :], in0=km[:, :, :], in1=kT, op=ALU.mult),
         reads=[km, Pt], writes=[km])
    bb = P["f8"].get()
    S.op("dve", lambda e: e.tensor_tensor(out=bb[:, :, :], in0=kk[:, :, :], in1=ic[:, :, :], op=ALU.mult),
         reads=[kk, ic], writes=[bb])
    RtT, AtT, BhT, KhT = [P["b8"].get() for _ in range(4)]
    BbT, KbT = P["f8"].get(), P["f8"].get()
    S.op("dve", lambda e: e.tensor_tensor(out=RtT[:, :, :], in0=rT, in1=eP[:, :, :], op=ALU.mult),
         reads=[Pt, eP], writes=[RtT])
    S.op("dve", lambda e: e.scalar_tensor_tensor(out=AtT[:, :, :], in0=kk[:, :, :], scalar=-1.0, in1=ePx[:, :, :],
                                                 op0=ALU.mult, op1=ALU.mult), reads=[kk, ePx], writes=[AtT])
    S.op("pool", lambda e: e.tensor_tensor(out=bb[:, :, :], in0=bb[:, :, :], in1=eN[:, :, :], op=ALU.mult),
         reads=[bb, eN], writes=[bb])
    S.op("pool", lambda e: e.tensor_tensor(out=sq[:, :, :], in0=km[:, :, :], in1=eN[:, :, :], op=ALU.mult),
         reads=[km, eN, sq], writes=[sq])
    S.op("act", lambda e: e.copy(out=BhT[:, :, :], in_=bb[:, :, :]), reads=[bb], writes=[BhT])
    S.op("act", lambda e: e.copy(out=KhT[:, :, :], in_=sq[:, :, :]), reads=[sq], writes=[KhT])
    plb = pl[:, :].unsqueeze(2).broadcast_to([64, 8, 128])
    S.op("dve", lambda e: e.tensor_tensor(out=BbT[:, :, :], in0=bb[:, :, :], in1=plb, op=ALU.mult),
         reads=[bb, pl], writes=[BbT])
    S.op("dve", lambda e: e.tensor_tensor(out=KbT[:, :, :], in0=sq[:, :, :], in1=plb, op=ALU.mult),
         reads=[sq, pl], writes=[KbT])
    if RL <= 6:
        return None
    pv = K.psum()
    for h in range(8):
        S.op("pe", lambda e, h=h: e.matmul(pv[:, h * 64:(h + 1) * 64], Pt[:, 16 + h, :], K.identf[0:64, 0:64],
                                           start=True, stop=True), reads=[Pt, K.identf], writes=[pv])
    if RL <= 6.2:
        return None
    Vf = P["vf"].get()
    Vb = P["tok"].get()
    S.op("act", lambda e: e.copy(out=Vf[:, :], in_=pv[:, :]), reads=[pv], writes=[Vf])
    S.op("dve", lambda e: e.tensor_copy(out=Vb[:, :], in_=Vf[:, :]), reads=[Vf], writes=[Vb])
    if RL <= 6.5:
        return None
    outs = []
    for src in (BbT, KbT):
        pb = K.psum()
        for h in range(8):
            S.op("pe", lambda e, h=h, src=src, pb=pb: e.matmul(pb[:, h * 64:(h + 1) * 64], src[:, h, :],
                                                               K.identf[0:64, 0:64], start=True, stop=True),
                 reads=[src, K.identf], writes=[pb])
        o = P["tok"].get()
        S.op("act", lambda e, o=o, pb=pb: e.copy(out=o[:, :], in_=pb[:, :]), reads=[pb], writes=[o])
        outs.append(o)
    T = {"RtT": RtT, "AtT": AtT, "BhT": BhT, "KhT": KhT, "Vb": Vb, "Vf": Vf, "Bb": outs[0], "Kb": outs[1], "pl": pl}
    if RL <= 7:
        return None
    S.op("dve", lambda e: e.tensor_tensor(out=km[:, :, :], in0=km[:, :, :], in1=rT, op=ALU.mult),
         reads=[km, Pt], writes=[km])
    S.op("dve", lambda e: e.tensor_tensor(out=km[:, :, :], in0=km[:, :, :], in1=h8(C["rk"]), op=ALU.mult),
         reads=[km, C["rk"]], writes=[km])
    pr = K.psum()
    for h in range(8):
        S.op("pe", lambda e, h=h: e.matmul(pr[:, h:h + 1], km[:, h, :], K.onesf[0:64, 0:1], start=True, stop=True),
             reads=[km, K.onesf], writes=[pr])
    rkc = P["rkc"].get()
    S.op("dve", lambda e: e.tensor_copy(out=rkc[:, :], in_=pr[:, 0:8]), reads=[pr], writes=[rkc])
    T["rk"] = rkc
    if RL <= 8:
        return None
    if want_g:
        sgd = P["b8"].get()
        S.op("act", lambda e: e.activation(out=sgd[:, 0:3, :], in_=Pt[:, 28:31, :], func=AF.Sigmoid), reads=[Pt],
             writes=[sgd])
        pg = K.psum()
        for q in range(3):
            rows = 64 if q < 2 else 32
            S.op("pe", lambda e, q=q, rows=rows: e.matmul(pg[:, :], sgd[0:rows, q, :], C["gup"][0:rows, q, :],
                                                          start=(q == 0), stop=(q == 2)), reads=[sgd, C["gup"]], writes=[pg])
        gt = P["w"].get()
        S.op("act", lambda e: e.copy(out=gt[:, :], in_=pg[:, :]), reads=[pg], writes=[gt])
        T["g"] = gt
    return T


def rwkv_intra(K, P, T, n):
    S = K.S
    AtT, BhT, KhT, RtT = T["AtT"], T["BhT"], T["KhT"], T["RtT"]
    m_strict_sr = K.ugt if n == 0 else K.uexc
    m_strict_rs = K.uexc if n == 0 else K.ugt
    m_incl_st = K.uinc if n == 0 else K.mskb
    res = {"TT": [], "AakT": [], "ArbT": [], "ArkT": []}

    def prod(lhs, rhs, hq, mask, pool="mat"):
        pp = K.psum()
        for hh in range(4):
            h = hq * 4 + hh
            S.op("pe", lambda e, hh=hh, h=h: e.matmul(pp[:, hh * 128:(hh + 1) * 128], lhs[:, h, :], rhs[:, h, :],
                                                      start=True, stop=True), reads=[lhs, rhs], writes=[pp])
        o = P[pool].get()
        S.op("dve", lambda e: e.tensor_tensor(out=o[:, :, :], in0=pp[:, :].rearrange("p (a b) -> p a b", a=4),
                                              in1=mask[:, :].unsqueeze(1).broadcast_to([128, 4, 128]), op=ALU.mult),
             reads=[pp, mask], writes=[o])
        return o

    def mm4(lhs, rhs, addto=None, eng="act", pool="mat"):
        pp = K.psum()
        for hh in range(4):
            S.op("pe", lambda e, hh=hh: e.matmul(pp[:, hh * 128:(hh + 1) * 128], lhs[:, hh, :], rhs[:, hh, :],
                                                 start=True, stop=True), reads=[lhs, rhs], writes=[pp])
        o = P[pool].get()
        if addto is None:
            S.op(eng, (lambda e: e.copy(out=o[:, :, :].rearrange("p a b -> p (a b)"), in_=pp[:, :])) if eng == "act" else
                 (lambda e: e.tensor_copy(out=o[:, :, :].rearrange("p a b -> p (a b)"), in_=pp[:, :])),
                 reads=[pp], writes=[o])
        else:
            S.op("dve", lambda e: e.tensor_tensor(out=o[:, :, :].rearrange("p a b -> p (a b)"), in0=pp[:, :],
                                                  in1=addto[:, :, :].rearrange("p a b -> p (a b)"), op=ALU.add),
                 reads=[pp, addto], writes=[o])
        return o

    Ms = [prod(AtT, BhT, hq, m_strict_sr) for hq in range(2)]
    MTs = [prod(BhT, AtT, hq, m_strict_rs) for hq in range(2)]
    TTs = []
    for hq in range(2):
        TT = P["mat"].get()
        S.op("pool", lambda e, TT=TT, MT=MTs[hq]: e.tensor_tensor(
            out=TT[:, :, :], in0=MT[:, :, :], in1=K.ident[:, :].unsqueeze(1).broadcast_to([128, 4, 128]), op=ALU.add),
            reads=[MTs[hq], K.ident], writes=[TT])
        TTs.append(TT)
    for hq in range(2):
        res["AakT"].append(prod(KhT, AtT, hq, m_strict_rs, pool="res"))
        res["ArbT"].append(prod(BhT, RtT, hq, m_incl_st, pool="res"))
        res["ArkT"].append(prod(KhT, RtT, hq, m_incl_st, pool="res"))
    for k in range(1, 7):
        M2s, MT2s = [], []
        for hq in range(2):
            M2s.append(mm4(MTs[hq], Ms[hq], eng="act"))
        for hq in range(2):
            MT2s.append(mm4(Ms[hq], MTs[hq], eng="act") if k < 6 else None)
        for hq in range(2):
            TTs[hq] = mm4(M2s[hq], TTs[hq], addto=TTs[hq], pool=("res" if k == 6 else "mat"))
        Ms, MTs = M2s, MT2s
    res["TT"] = TTs
    return res


def rwkv_seq(K, P, T, I_, St, Sb):
    S = K.S
    AtT, RtT, Vb, Bb, Kb, pl = T["AtT"], T["RtT"], T["Vb"], T["Bb"], T["Kb"], T["pl"]
    pw = K.psum()
    for h in range(8):
        hq, hh = divmod(h, 4)
        S.op("pe", lambda e, h=h: e.matmul(pw[:, h * 64:(h + 1) * 64], AtT[:, h, :], Sb[:, h, :], start=True, stop=False),
             reads=[AtT, Sb], writes=[pw])
        S.op("pe", lambda e, h=h, hq=hq, hh=hh: e.matmul(pw[:, h * 64:(h + 1) * 64], I_["AakT"][hq][:, hh, :],
                                                         Vb[:, h * 64:(h + 1) * 64], start=False, stop=True),
             reads=[I_["AakT"][hq], Vb], writes=[pw])
    Wb = P["tok"].get()
    S.op("act", lambda e: e.copy(out=Wb[:, :], in_=pw[:, :]), reads=[pw], writes=[Wb])
    pu = K.psum()
    for h in range(8):
        hq, hh = divmod(h, 4)
        S.op("pe", lambda e, h=h, hq=hq, hh=hh: e.matmul(pu[:, h * 64:(h + 1) * 64], I_["TT"][hq][:, hh, :],
                                                         Wb[:, h * 64:(h + 1) * 64], start=True, stop=True),
             reads=[I_["TT"][hq], Wb], writes=[pu])
    Ub = P["tok"].get()
    S.op("dve", lambda e: e.tensor_copy(out=Ub[:, :], in_=pu[:, :]), reads=[pu], writes=[Ub])
    py = K.psum()
    for h in range(8):
        hq, hh = divmod(h, 4)
        S.op("pe", lambda e, h=h: e.matmul(py[:, h * 64:(h + 1) * 64], RtT[:, h, :], Sb[:, h, :], start=True, stop=False),
             reads=[RtT, Sb], writes=[py])
        S.op("pe", lambda e, h=h, hq=hq, hh=hh: e.matmul(py[:, h * 64:(h + 1) * 64], I_["ArbT"][hq][:, hh, :],
                                                         Ub[:, h * 64:(h + 1) * 64], start=False, stop=False),
             reads=[I_["ArbT"][hq], Ub], writes=[py])
        S.op("pe", lambda e, h=h, hq=hq, hh=hh: e.matmul(py[:, h * 64:(h + 1) * 64], I_["ArkT"][hq][:, hh, :],
                                                         Vb[:, h * 64:(h + 1) * 64], start=False, stop=True),
             reads=[I_["ArkT"][hq], Vb], writes=[py])
    ps = K.psum()
    for h in range(8):
        S.op("pe", lambda e, h=h: e.matmul(ps[0:64, h * 64:(h + 1) * 64], Bb[:, h * 64:(h + 1) * 64],
                                           Ub[:, h * 64:(h + 1) * 64], start=True, stop=False),
             reads=[Bb, Ub], writes=[ps])
        S.op("pe", lambda e, h=h: e.matmul(ps[0:64, h * 64:(h + 1) * 64], Kb[:, h * 64:(h + 1) * 64],
                                           Vb[:, h * 64:(h + 1) * 64], start=False, stop=True),
             reads=[Kb, Vb], writes=[ps])
    S.op("dve", lambda e: e.tensor_tensor(out=St[:, :, :], in0=St[:, :, :],
                                          in1=pl[:, :].unsqueeze(2).broadcast_to([64, 8, 64]), op=ALU.mult),
         reads=[St, pl], writes=[St])
    S.op("dve", lambda e: e.tensor_tensor(out=St[:, :, :], in0=St[:, :, :],
                                          in1=ps[0:64, :].rearrange("p (a b) -> p a b", a=8), op=ALU.add),
         reads=[St, ps], writes=[St])
    S.op("act", lambda e: e.copy(out=Sb[:, :, :], in_=St[:, :, :]), reads=[St], writes=[Sb])
    return py


def rwkv_dir_pass(K, SC, C, nchunk, cps, n):
    S = K.S
    P = rwkv_pools(K)
    St = K.alloc([64, 8, 64], F32)
    Sb = K.alloc([64, 8, 64], BF16)
    S.op("dve", lambda e: e.memset(St[:, :, :], 0.0), writes=[St])
    S.op("dve", lambda e: e.memset(Sb[:, :, :], 0.0), writes=[Sb])
    ybp = Pool(K, 2, [128, 520], F32)
    y1p = Pool(K, 2, [128, 520], F32)
    yw = Pool(K, 2, [128, 8, 64], F32)
    yc = Pool(K, 2, [128, 8, 64], F32)
    smp = Pool(K, 4, [128, 32], F32)
    outp = Pool(K, 2, [128, 512], BF16)

    def body(c):
        seg, cis = divmod(c, cps)
        t0 = c * 128
        first = (cis == 0) if n == 0 else (cis == cps - 1)
        edge = (c == 0) if n == 0 else (c == nchunk - 1)
        if first and not edge:
            fl = K.flags[0:64, seg - 1:seg] if n == 0 else K.flags[0:64, seg:seg + 1]
            S.op("dve", lambda e: e.tensor_scalar(out=St[:, :, :], in0=St[:, :, :], scalar1=fl, scalar2=0.0,
                                                  op0=ALU.mult, op1=ALU.add), reads=[St, K.flags], writes=[St])
            S.op("act", lambda e: e.copy(out=Sb[:, :, :], in_=St[:, :, :]), reads=[St], writes=[Sb])
        T = rwkv_prep(K, SC, C, P, c, nchunk, cps, n, want_g=(n == 0))
        if T is None or RL <= 9:
            return
        I_ = rwkv_intra(K, P, T, n)
        if RL <= 10:
            return
        py = rwkv_seq(K, P, T, I_, St, Sb)
        if RL <= 11:
            return
        if n == 1:
            yb = ybp.get()
            S.op("act", lambda e: e.copy(out=yb[:, 0:512], in_=py[:, :]), reads=[py], writes=[yb])
            S.op("dve", lambda e: e.tensor_copy(out=yb[:, 512:520], in_=T["rk"][:, :]), reads=[T["rk"]], writes=[yb])
            S.dma("pool", SC["YB"][t0:t0 + 128, :], yb[:, :], reads=[yb])
            return
        y1 = y1p.get()
        S.dma("sp", y1[:, :], SC["YB"][t0:t0 + 128, :], writes=[y1])
        y = yw.get()
        yv = y[:, :, :].rearrange("p a b -> p (a b)")
        S.op("dve", lambda e: e.tensor_tensor(out=yv, in0=py[:, :], in1=y1[:, 0:512], op=ALU.add),
             reads=[py, y1], writes=[y])
        sm = smp.get()
        S.op("dve", lambda e: e.tensor_reduce(out=sm[:, 0:8], in_=y[:, :, :], axis=AX.X, op=ALU.add),
             reads=[y], writes=[sm])
        S.op("dve", lambda e: e.tensor_scalar(out=sm[:, 0:8], in0=sm[:, 0:8], scalar1=1.0 / 64, scalar2=0.0,
                                              op0=ALU.mult, op1=ALU.add), reads=[sm], writes=[sm])
        ycn = yc.get()
        S.op("dve", lambda e: e.tensor_tensor(out=ycn[:, :, :], in0=y[:, :, :],
                                              in1=sm[:, 0:8].unsqueeze(2).broadcast_to([128, 8, 64]), op=ALU.subtract),
             reads=[y, sm], writes=[ycn])
        S.op("pool", lambda e: e.tensor_tensor(out=y[:, :, :], in0=ycn[:, :, :], in1=ycn[:, :, :], op=ALU.mult),
             reads=[ycn], writes=[y])
        S.op("dve", lambda e: e.tensor_reduce(out=sm[:, 8:16], in_=y[:, :, :], axis=AX.X, op=ALU.add),
             reads=[y], writes=[sm])
        S.op("act", lambda e: e.activation(out=sm[:, 8:16], in_=sm[:, 8:16], func=AF.Sqrt, bias=C["gneps"][:, 0:1],
                                           scale=1.0 / 64), reads=[sm, C["gneps"]], writes=[sm])
        S.op("dve", lambda e: e.reciprocal(out=sm[:, 8:16], in_=sm[:, 8:16]), reads=[sm], writes=[sm])
        S.op("dve", lambda e: e.tensor_tensor(out=ycn[:, :, :], in0=ycn[:, :, :],
                                              in1=sm[:, 8:16].unsqueeze(2).broadcast_to([128, 8, 64]), op=ALU.mult),
             reads=[ycn, sm], writes=[ycn])
        ycv = ycn[:, :, :].rearrange("p a b -> p (a b)")
        S.op("pool", lambda e: e.tensor_tensor(out=ycv, in0=ycv, in1=C["lng"][:, :], op=ALU.mult),
             reads=[ycn, C["lng"]], writes=[ycn])
        S.op("pool", lambda e: e.tensor_tensor(out=ycv, in0=ycv, in1=C["lnb"][:, :], op=ALU.add),
             reads=[ycn, C["lnb"]], writes=[ycn])
        S.op("dve", lambda e: e.tensor_tensor(out=sm[:, 16:24], in0=T["rk"][:, :], in1=y1[:, 512:520], op=ALU.add),
             reads=[T["rk"], y1], writes=[sm])
        S.op("dve", lambda e: e.tensor_tensor(out=y[:, :, :], in0=T["Vf"][:, :].rearrange("p (a b) -> p a b", a=8),
                                              in1=sm[:, 16:24].unsqueeze(2).broadcast_to([128, 8, 64]), op=ALU.mult),
             reads=[T["Vf"], sm, y], writes=[y])
        S.op("pool", lambda e: e.tensor_tensor(out=ycv, in0=ycv, in1=yv, op=ALU.add), reads=[ycn, y], writes=[ycn])
        o = outp.get()
        S.op("dve", lambda e: e.tensor_tensor(out=o[:, :], in0=ycv, in1=T["g"][:, :], op=ALU.mult),
             reads=[ycn, T["g"]], writes=[o])
        S.dma("pool", SC["YR"][t0:t0 + 128, :], o[:, :], reads=[o])

    order = range(nchunk) if n == 0 else range(nchunk - 1, -1, -1)
    for c in order:
        body(c)


def cast_weights(K, src, dst, rows, cols):
    for r in range(0, rows, 128):
        K.S.dma("pool", dst[r:r + 128, :], src[r:r + 128, :])


def zero_fill(K, dst, rows, cols, dt):
    z = K.alloc([128, cols], dt)
    K.S.op("pool", lambda e: e.memset(z[:, :], 0.0), writes=[z])
    for r in range(0, rows, 128):
        K.S.dma("pool", dst[r:r + 128, :], z[:, :], reads=[z])


def build(ntok, seglen=4096, depth=DEPTH, mixer=True):
    nc = bass.Bass("TRN2", target_bir_lowering=False)
    es = ExitStack()
    A = {}
    nseg = ntok // seglen
    plan, pats = attn_plan(nseg, seglen)
    ncfg = pats[False].shape[0]

    def inp(name, shape, dt=F32):
        A[name] = nc.dram_tensor(name, list(shape), dt, kind="ExternalInput").ap()
        return A[name]

    import os
    dbg = os.environ.get("MK_DBG", "").split(",")

    def scr(name, shape, dt=F32):
        if name in dbg:
            return nc.dram_tensor(name, list(shape), dt, kind="ExternalOutput").ap()
        return nc.dram_tensor(name, list(shape), dt).ap()

    x = inp("x", [ntok, D])
    inp("norm_g", [DEPTH, 6, D])
    inp("ff_w_in", [DEPTH, 2, D, 2 * FF])
    inp("ff_w_out", [DEPTH, 2, FF, D])
    inp("w_in", [DEPTH, D, IN_W])
    inp("w_branch", [DEPTH, 2048, D])
    inp("w_out", [DEPTH, D, D])
    inp("atab", [DEPTH, ncfg, 128, 8, 128])
    inp("ssm_conv_w", [DEPTH, 5, 1536])
    inp("ssm_conv_b", [DEPTH, 1536])
    inp("ssm_dt_bias", [DEPTH, 2, 16])
    inp("ssm_a_log", [DEPTH, 2, 16])
    inp("ssm_d", [DEPTH, 2, 16])
    inp("ssm_norm_g", [DEPTH, 1024])
    inp("flags", [128, 4])
    inp("cf32", [128, 7, 128])
    inp("rwkv_mu", [DEPTH, 2, 1952])
    inp("rwkv_w0", [DEPTH, 2, 512])
    inp("rwkv_w_up", [DEPTH, 2, 64, 512])
    inp("rwkv_a0", [DEPTH, 2, 512])
    inp("rwkv_a_up", [DEPTH, 2, 64, 512])
    inp("rwkv_g_up", [DEPTH, 160, 512])
    inp("rwkv_k_k", [DEPTH, 512])
    inp("rwkv_k_a", [DEPTH, 512])
    inp("rwkv_r_k", [DEPTH, 8, 64])
    inp("rwkv_ln_g", [DEPTH, 512])
    inp("rwkv_ln_b", [DEPTH, 512])
    inp("ident", [128, 128], BF16)
    y = nc.dram_tensor("y", [ntok, D], F32, kind="ExternalOutput").ap()
    xs = scr("xs", [ntok, D])
    wi_bf = [[scr(f"wi{l}{f}", [D, 2 * FF], BF16) for f in range(2)] for l in range(DEPTH)]
    wo_bf = [[scr(f"wo{l}{f}", [FF, D], BF16) for f in range(2)] for l in range(DEPTH)]
    win_bf = [scr(f"win{l}", [D, IN_W], BF16) for l in range(DEPTH)]
    wbr_bf = [scr(f"wbr{l}", [2048, D], BF16) for l in range(DEPTH)]
    wout_bf = [scr(f"wout{l}", [D, D], BF16) for l in range(DEPTH)]
    SC = {"QT": scr("QT", [512, ntok], BF16), "KT": scr("KT", [512, ntok], BF16), "V": scr("V", [ntok, 512], BF16),
          "Z": scr("Z", [ntok, 1024]), "DT": scr("DT", [ntok, 32]), "G": scr("G", [ntok, 3072]),
          "XBCT": scr("XBCT", [1536, ntok]), "RWT": scr("RWT", [1952, ntok]),
          "HB": scr("HB", [ntok // 128, 128, 1024], BF16), "YB": scr("YB", [ntok, 520]), "RWS": scr("RWS", [1952, ntok]),
          "YA": scr("YA", [ntok, 512], BF16), "YS": scr("YS", [ntok, 1024], BF16), "YR": scr("YR", [ntok, 512], BF16)}
    if "DY" in dbg:
        SC["DY"] = scr("DY", [ntok, 1024]); SC["DV"] = scr("DV", [ntok, 256]); SC["DM"] = scr("DM", [ntok // 128, 128, 16, 128], BF16)
    K = Ctx(nc, es)
    S = K.S
    K.ident = K.alloc([128, 128], BF16, keep=True)
    S.dma("sp", K.ident[:, :], A["ident"][:, :], writes=[K.ident])
    cf = K.alloc([128, 7, 128], F32, keep=True)
    S.dma("sp", cf[:, :, :], A["cf32"][:, :, :], writes=[cf])
    K.identf, K.uinc, K.uexc, K.onesf, K.mskf, K.mskb, K.ugt = [Tl(cf.t[:, i, :], cf.res) for i in range(7)]
    K.flags = K.alloc([128, 4], F32, keep=True)
    S.dma("sp", K.flags[:, :], A["flags"][:, :], writes=[K.flags])
    K.onec = K.alloc([128, 4], F32, keep=True)
    S.op("dve", lambda e: e.memset(K.onec[:, :], 1.0), writes=[K.onec])
    K.epsc = K.alloc([128, 4], F32, keep=True)
    S.op("dve", lambda e: e.memset(K.epsc[:, 0:1], EPS), writes=[K.epsc])
    S.op("dve", lambda e: e.memset(K.epsc[:, 1:2], 4.0 * EPS), writes=[K.epsc])
    for l in range(depth):
        for f in range(2):
            cast_weights(K, A["ff_w_in"][l, f], wi_bf[l][f], D, 2 * FF)
            cast_weights(K, A["ff_w_out"][l, f], wo_bf[l][f], FF, D)
        if mixer:
            cast_weights(K, A["w_in"][l], win_bf[l], D, IN_W)
            cast_weights(K, A["w_branch"][l], wbr_bf[l], 2048, D)
            cast_weights(K, A["w_out"][l], wout_bf[l], D, D)
    if mixer:
        zero_fill(K, SC["YS"], ntok, 1024, BF16)
        zero_fill(K, SC["YR"], ntok, 512, BF16)
    cur = x
    for l in range(depth):
        g = A["norm_g"][l]
        ffn_pass(K, cur, xs, g[0:1, :], g[1:2, :], wi_bf[l][0], wo_bf[l][0], ntok)
        cur = xs
        if mixer:
            import os
            st = os.environ.get("MK_STAGES", "iasrm")
            if "i" in st:
                inproj_pass(K, xs, g[2:3, :], win_bf[l], SC, ntok)
            if "a" in st:
                attn_pass(K, SC, A["atab"][l], plan)
            if "s" in st:
                K.new_pass()
                Cs = ssd_setup(K, A, l)
                ssd_bwd_pass(K, SC, Cs, ntok // 128, seglen // 128)
                K.new_pass()
                Cs = ssd_setup(K, A, l)
                ssd_fwd_pass(K, SC, Cs, ntok // 128, seglen // 128)
            if "r" in st:
                shift_pass(K, SC, A, l, ntok, seglen)
                for n_ in (1, 0):
                    K.new_pass()
                    Cr = rwkv_setup(K, A, l)
                    rwkv_dir_pass(K, SC, Cr, ntok // 128, seglen // 128, n_)
            if "m" in st:
                merge_pass(K, SC, xs, g[3:4, :], wbr_bf[l], wout_bf[l], ntok)
        last = (l == depth - 1)
        ffn_pass(K, cur, y if last else xs, g[4:5, :], g[5:6, :], wi_bf[l][1], wo_bf[l][1], ntok)
    S.barrier()
    S.emit()
    es.close()
    return nc


def consts():
    i = np.arange(128)
    s_, l_ = i[:, None], i[None, :]
    cf = np.stack([np.eye(128), s_ <= l_, s_ < l_, np.ones((128, 128)), s_ <= l_, s_ >= l_, s_ > l_]).astype(np.float32)
    return {"ident": np.eye(128, dtype=np.float32).astype(ml_dtypes.bfloat16),
            "cf32": np.ascontiguousarray(cf.transpose(1, 0, 2))}


def flags_for(link):
    f = np.zeros((128, 4), np.float32)
    f[:, 0] = 1.0 if link else 0.0
    return f


NTOK = 12288
_NC_CACHE = {}


def _assign():
    segs = []
    for c in range(4):
        segs.append([("s", c, 0), ("s", c, 1), ("p", c, 0)])
    for c in range(4, 8):
        b = 4 + (c - 4) * 3
        segs.append([("p", b, 0), ("p", b + 1, 0), ("p", b + 2, 0)])
    return segs


def kernel(**inputs):
    xp = np.asarray(inputs["x_prompt"], dtype=np.float32)
    xs = np.asarray(inputs["x_sample"], dtype=np.float32)
    segs = _assign()
    if "nc" not in _NC_CACHE:
        _NC_CACHE["nc"] = build(NTOK, seglen=4096)
    nc = _NC_CACHE["nc"]
    shared = {k: np.ascontiguousarray(np.asarray(inputs[k], dtype=np.float32))
              for k in ("norm_g", "ff_w_in", "ff_w_out", "w_in", "w_branch", "w_out", "ssm_conv_w", "ssm_conv_b",
                        "ssm_dt_bias", "ssm_a_log", "ssm_d", "ssm_norm_g", "rwkv_mu", "rwkv_w0", "rwkv_w_up",
                        "rwkv_a0", "rwkv_a_up", "rwkv_g_up", "rwkv_k_k", "rwkv_k_a", "rwkv_r_k", "rwkv_ln_g",
                        "rwkv_ln_b")}
    shared.update(consts())
    plan, pats = attn_plan(3, 4096)
    rpb = np.asarray(inputs["attn_rpb"], dtype=np.float32)
    tabs = {link: attn_tables(rpb, pats[link]) for link in (False, True)}
    in_maps = []
    for c in range(8):
        parts = []
        for kind, b, h in segs[c]:
            parts.append(xs[b, h * 4096:(h + 1) * 4096] if kind == "s" else xp[b])
        m = {"x": np.ascontiguousarray(np.concatenate(parts, axis=0)), "atab": tabs[c < 4],
             "flags": flags_for(c < 4)}
        m.update(shared)
        in_maps.append(m)
    res = run_bass_kernel_spmd(nc, in_maps, core_ids=list(range(8)))
    yp = np.empty_like(xp)
    ys = np.empty_like(xs)
    for c in range(8):
        y = res.results[c]["y"]
        for i, (kind, b, h) in enumerate(segs[c]):
            blk = y[i * 4096:(i + 1) * 4096]
            if kind == "s":
                ys[b, h * 4096:(h + 1) * 4096] = blk
            else:
                yp[b] = blk
    return yp, ys
```

```python
import numpy as np
import ml_dtypes
from contextlib import ExitStack
import concourse.bass as bass
import concourse.mybir as mybir
from concourse.bass_utils import run_bass_kernel_spmd

F32 = mybir.dt.float32
BF16 = mybir.dt.bfloat16
AF = mybir.ActivationFunctionType
ALU = mybir.AluOpType
AX = mybir.AxisListType

D = 1024
FF = 2816
DEPTH = 2
EPS = 1e-6


class Res:
    __slots__ = ("w", "r")

    def __init__(self):
        self.w = None
        self.r = []


class Tl:
    def __init__(self, t, res=None):
        self.t = t
        self.res = res or Res()

    def __getitem__(self, idx):
        return self.t[idx]


class Sched:
    EPOCH = 60000
    NDMA = {"sp": 24, "pool": 12, "act": 8}

    def __init__(self, nc, es):
        self.nc = nc
        self.es = es
        self.names = ["sp", "act", "dve", "pool", "pe"]
        self.ops = {k: [] for k in self.names}
        self.n = {k: 0 for k in self.names}
        self.sems = {k: [] for k in self.names}
        self.seen = {k: {} for k in self.names}
        self.dsem = {}
        self.dval = {}
        self.drr = {}
        for q, n in self.NDMA.items():
            self.dsem[q] = [es.enter_context(nc.semaphore(f"d{q}{i}")) for i in range(n)]
            self.dval[q] = [0] * n
            self.drr[q] = 0
        self.last = {k: None for k in self.names}

    def _next_ev(self, e):
        n = self.n[e]
        ep = n // self.EPOCH
        while len(self.sems[e]) <= ep:
            self.sems[e].append(self.es.enter_context(self.nc.semaphore(f"s{e}{len(self.sems[e])}")))
        self.n[e] += 1
        ev = (self.sems[e][ep], n % self.EPOCH + 1, e)
        self.last[e] = ev
        return ev

    def _deps(self, e, reads, writes):
        deps = []
        for r in reads:
            r = r.res if isinstance(r, Tl) else r
            if r.w is not None:
                deps.append((r.w, 0))
        for w in writes:
            w = w.res if isinstance(w, Tl) else w
            if w.w is not None:
                deps.append((w.w, 0))
            for ev in w.r:
                deps.append((ev, 1))
        waits = []
        seen = self.seen[e]
        for (sem, val, src), war in deps:
            if src == e:
                if e == "pe" or war:
                    continue
            k = id(sem)
            if seen.get(k, 0) >= val:
                continue
            seen[k] = val
            waits.append((sem, val))
        return waits

    def _commit(self, ev, reads, writes):
        for r in reads:
            r = r.res if isinstance(r, Tl) else r
            r.r.append(ev)
        for w in writes:
            w = w.res if isinstance(w, Tl) else w
            w.w = ev
            w.r = []

    def op(self, e, fn, reads=(), writes=()):
        waits = self._deps(e, reads, writes)
        ev = self._next_ev(e)
        self.ops[e].append((waits, fn, ev, 1))
        self._commit(ev, reads, writes)

    def dma(self, q, out, in_, reads=(), writes=(), slow=False):
        waits = self._deps(q, reads, writes)
        i = self.drr[q]
        self.drr[q] = (i + 1) % len(self.dsem[q])
        sem = self.dsem[q][i]
        prev = self.dval[q][i]
        if prev > 0 and self.seen[q].get(id(sem), 0) < prev:
            waits.append((sem, prev))
            self.seen[q][id(sem)] = prev
        self.dval[q][i] = prev + 16
        ev = (sem, prev + 16, "dma")
        if slow:
            fn = (lambda e, o=out, i_=in_: e.dma_start(out=o, in_=i_, allow_slow_non_contiguous=True))
        else:
            fn = (lambda e, o=out, i_=in_: e.dma_start(out=o, in_=i_))
        self.ops[q].append((waits, fn, ev, 16))
        self._commit(ev, reads, writes)

    def barrier(self):
        evs = [self.last[k] for k in self.names if self.last[k] is not None]
        for q in self.dsem:
            for sem, v in zip(self.dsem[q], self.dval[q]):
                if v > 0:
                    evs.append((sem, v, "dma"))
        for e in self.names:
            waits = []
            for sem, val, src in evs:
                if src == e:
                    continue
                if self.seen[e].get(id(sem), 0) >= val:
                    continue
                self.seen[e][id(sem)] = val
                waits.append((sem, val))
            if waits:
                self.ops[e].append((waits, None, None, 0))

    def emit(self):
        block = self.es.enter_context(self.nc.Block())
        decos = {"sp": block.sync, "act": block.scalar, "dve": block.vector, "pool": block.gpsimd,
                 "pe": block.tensor}
        for name in self.names:
            ops = self.ops[name]

            def body(e, ops=ops):
                for waits, fn, ev, inc in ops:
                    for (s, v) in waits:
                        e.wait_ge(s, v)
                    if fn is not None:
                        fn(e).then_inc(ev[0], inc)

            decos[name](body)


class Ctx:
    BASE = 16640
    ARENA = 224 * 1024

    def __init__(self, nc, es):
        self.nc = nc
        self.es = es
        self.S = Sched(nc, es)
        self.off = self.BASE
        self.uid = 0
        self.keep = self.BASE
        self.ps = [Tl(es.enter_context(nc.psum_tensor(f"ps{i}", [128, 512], F32))) for i in range(8)]
        self.psi = 0
        self.psi6 = 0

    def alloc(self, shape, dtype, keep=False):
        nbytes = int(np.prod(shape[1:])) * (2 if dtype == BF16 else 4)
        nbytes = (nbytes + 31) // 32 * 32
        assert self.off + nbytes <= self.ARENA, f"SBUF arena overflow {self.off}+{nbytes}"
        self.uid += 1
        t = self.nc.alloc_sbuf_tensor_at(f"t{self.uid}", list(shape), dtype, offset=self.off)
        self.off += nbytes
        if keep:
            self.keep = self.off
        return Tl(t)

    def new_pass(self):
        self.S.barrier()
        self.off = self.keep

    def psum(self):
        p = self.ps[self.psi]
        self.psi = (self.psi + 1) % 8
        return p


class Pool:
    def __init__(self, K, n, shape, dtype):
        self.t = [K.alloc(shape, dtype) for _ in range(n)]
        self.i = 0

    def get(self):
        t = self.t[self.i]
        self.i = (self.i + 1) % len(self.t)
        return t


def norm_transpose(K, xt, xres, gbc, hnT, col0, P):
    S = K.S
    sm = P["small"].get()
    junk = P["junk"].get()
    S.op("act", lambda e: e.activation(out=junk[:, :], in_=xt, func=AF.Square, accum_out=sm[:, 0:1]),
         reads=[xres], writes=[junk, sm])
    S.op("act", lambda e: e.activation(out=sm[:, 1:2], in_=sm[:, 0:1], func=AF.Sqrt, bias=K.epsc[:, 0:1],
                                       scale=1.0 / D), reads=[sm, K.epsc], writes=[sm])
    S.op("dve", lambda e: e.reciprocal(out=sm[:, 2:3], in_=sm[:, 1:2]), reads=[sm], writes=[sm])
    hn = P["hn"].get()
    S.op("dve", lambda e: e.scalar_tensor_tensor(out=hn[:, :], in0=xt, scalar=sm[:, 2:3], in1=gbc[:, :],
                                                 op0=ALU.mult, op1=ALU.mult),
         reads=[xres, sm, gbc], writes=[hn])
    pt = K.psum()
    ptb = pt.t[:, :].bitcast(BF16)
    for kc in range(8):
        S.op("pe", lambda e, kc=kc: e.transpose(ptb[:, kc * 128:(kc + 1) * 128],
                                                hn[:, kc * 128:(kc + 1) * 128], K.ident[:, :]),
             reads=[hn, K.ident], writes=[pt])
    S.op("act", lambda e: e.copy(out=hnT[:, :, col0:col0 + 128],
                                 in_=ptb.rearrange("p (k c) -> p k c", k=8)), reads=[pt], writes=[hnT])


def ffn_pass(K, xin, xout, gin_ap, gout_ap, win_bf, wout_bf, ntok):
    S = K.S
    K.new_pass()
    gin = K.alloc([128, D], F32)
    gout = K.alloc([128, D], F32)
    S.dma("sp", gin[:, :], gin_ap.broadcast_to([128, D]), writes=[gin])
    S.dma("sp", gout[:, :], gout_ap.broadcast_to([128, D]), writes=[gout])
    wout = K.alloc([128, 22, D], BF16)
    wo_v = wout_bf.rearrange("(kc p) n -> p kc n", p=128)
    for kc in range(0, 22, 2):
        S.dma("sp", wout[:, kc:kc + 2, :], wo_v[:, kc:kc + 2, :], writes=[wout])
    xp = Pool(K, 2, [128, 4, D], F32)
    hnTp = Pool(K, 2, [128, 8, 512], BF16)
    hT = K.alloc([128, 22, 512], BF16)
    wp = Pool(K, 3, [128, 8, 2, 256], BF16)
    sg = Pool(K, 2, [128, 512], F32)
    tt = Pool(K, 2, [128, D], F32)
    P = {"small": Pool(K, 8, [128, 8], F32), "junk": Pool(K, 2, [128, D], BF16), "hn": Pool(K, 2, [128, D], BF16)}
    win_v = win_bf.rearrange("(kc p) n -> p kc n", p=128)
    for mt in range(ntok // 512):
        x = xp.get()
        S.dma("sp", x[:, :, :], xin[mt * 512:(mt + 1) * 512, :].rearrange("(s p) d -> p s d", p=128),
              writes=[x])
        hnT = hnTp.get()
        for s in range(4):
            norm_transpose(K, x[:, s, :], x, gin, hnT, s * 128, P)
        for j in range(11):
            w = wp.get()
            S.dma("sp", w[:, :, 0, :], win_v[:, :, j * 256:(j + 1) * 256], writes=[w])
            S.dma("sp", w[:, :, 1, :], win_v[:, :, FF + j * 256:FF + (j + 1) * 256], writes=[w])
            for c in range(2):
                pg = K.psum()
                pu = K.psum()
                for gi, pp in enumerate((pg, pu)):
                    for kc in range(8):
                        S.op("pe", lambda e, pp=pp, gi=gi, kc=kc, c=c, w=w, hnT=hnT: e.matmul(
                            pp[:, :], w[:, kc, gi, c * 128:(c + 1) * 128], hnT[:, kc, :],
                            start=(kc == 0), stop=(kc == 7)), reads=[w, hnT], writes=[pp])
                sgt = sg.get()
                S.op("act", lambda e, sgt=sgt, pg=pg: e.activation(out=sgt[:, :], in_=pg[:, :], func=AF.Silu),
                     reads=[pg], writes=[sgt])
                S.op("dve", lambda e, sgt=sgt, pu=pu, ch=j * 2 + c: e.tensor_tensor(
                    out=hT[:, ch, :], in0=sgt[:, :], in1=pu[:, :], op=ALU.mult),
                     reads=[sgt, pu], writes=[hT])
        for s in range(4):
            pp = [K.psum(), K.psum()]
            for nf in range(2):
                for kc in range(22):
                    S.op("pe", lambda e, p_=pp[nf], nf=nf, kc=kc, s=s: e.matmul(
                        p_[:, :], hT[:, kc, s * 128:(s + 1) * 128], wout[:, kc, nf * 512:(nf + 1) * 512],
                        start=(kc == 0), stop=(kc == 21)), reads=[hT, wout], writes=[pp[nf]])
            sm = P["small"].get()
            junk = P["junk"].get()
            for nf in range(2):
                S.op("act", lambda e, nf=nf, junk=junk, sm=sm, p_=pp[nf]: e.activation(
                    out=junk[:, 0:512], in_=p_[:, :], func=AF.Square, accum_out=sm[:, nf:nf + 1]),
                     reads=[pp[nf]], writes=[junk, sm])
            S.op("dve", lambda e, sm=sm: e.tensor_tensor(out=sm[:, 2:3], in0=sm[:, 0:1], in1=sm[:, 1:2],
                                                         op=ALU.add), reads=[sm], writes=[sm])
            S.op("act", lambda e, sm=sm: e.activation(out=sm[:, 3:4], in_=sm[:, 2:3], func=AF.Sqrt,
                                                      bias=K.epsc[:, 1:2], scale=4.0 / D),
                 reads=[sm, K.epsc], writes=[sm])
            S.op("dve", lambda e, sm=sm: e.reciprocal(out=sm[:, 4:5], in_=sm[:, 3:4]), reads=[sm], writes=[sm])
            t = tt.get()
            for nf in range(2):
                S.op("dve", lambda e, nf=nf, t=t, sm=sm, p_=pp[nf]: e.scalar_tensor_tensor(
                    out=t[:, nf * 512:(nf + 1) * 512], in0=p_[:, :], scalar=sm[:, 4:5],
                    in1=gout[:, nf * 512:(nf + 1) * 512], op0=ALU.mult, op1=ALU.mult),
                     reads=[pp[nf], sm, gout], writes=[t])
            S.op("pool", lambda e, t=t, x=x, s=s: e.tensor_tensor(out=x[:, s, :], in0=x[:, s, :], in1=t[:, :],
                                                                  op=ALU.add), reads=[t, x], writes=[x])
        S.dma("pool", xout[mt * 512:(mt + 1) * 512, :].rearrange("(s p) d -> p s d", p=128), x[:, :, :],
              reads=[x])


IN_W = 9152
OFF_Q, OFF_K, OFF_V, OFF_Z, OFF_XBC, OFF_DT, OFF_RW, OFF_G = 0, 512, 1024, 1536, 2560, 4096, 4128, 6080


def inproj_pass(K, xin, g_ap, w_bf, SC, ntok):
    S = K.S
    K.new_pass()
    gin = K.alloc([128, D], F32)
    S.dma("sp", gin[:, :], g_ap.broadcast_to([128, D]), writes=[gin])
    xp = Pool(K, 2, [128, 4, D], F32)
    hnTp = Pool(K, 2, [128, 8, 512], BF16)
    wp = Pool(K, 3, [128, 8, 512], BF16)
    ofm = Pool(K, 3, [128, 512], F32)
    obf = Pool(K, 3, [128, 512], BF16)
    P = {"small": Pool(K, 8, [128, 8], F32), "junk": Pool(K, 2, [128, D], BF16), "hn": Pool(K, 2, [128, D], BF16)}
    w_v = w_bf.rearrange("(kc p) n -> p kc n", p=128)
    blocks = []
    for c0 in range(0, 1024, 512):
        blocks.append((c0, 512, "F"))
    blocks.append((OFF_V, 512, "T"))
    blocks += [(OFF_Z, 512, "T"), (OFF_Z + 512, 512, "T")]
    blocks += [(OFF_XBC + i * 512, 512, "F") for i in range(3)]
    blocks.append((OFF_DT, 32, "T"))
    blocks += [(OFF_RW, 512, "F"), (OFF_RW + 512, 512, "F"), (OFF_RW + 1024, 512, "F"), (OFF_RW + 1536, 416, "F")]
    blocks += [(OFF_G + i * 512, 512, "T") for i in range(6)]
    for mt in range(ntok // 512):
        t0 = mt * 512
        x = xp.get()
        S.dma("sp", x[:, :, :], xin[t0:t0 + 512, :].rearrange("(s p) d -> p s d", p=128), writes=[x])
        hnT = hnTp.get()
        for s in range(4):
            norm_transpose(K, x[:, s, :], x, gin, hnT, s * 128, P)
        for (c0, ncol, mode) in blocks:
            w = wp.get()
            S.dma("sp", w[:, :, 0:ncol], w_v[:, :, c0:c0 + ncol], writes=[w])
            if mode == "F":
                for fc in range((ncol + 127) // 128):
                    nf = min(128, ncol - fc * 128)
                    pp = K.psum()
                    for kc in range(8):
                        S.op("pe", lambda e, pp=pp, w=w, kc=kc, fc=fc, nf=nf, hnT=hnT: e.matmul(
                            pp[0:nf, :], w[:, kc, fc * 128:fc * 128 + nf], hnT[:, kc, :],
                            start=(kc == 0), stop=(kc == 7)), reads=[w, hnT], writes=[pp])
                    f0 = c0 + fc * 128
                    if f0 < 1024:
                        o = obf.get()
                        sc = 0.125 if f0 < 512 else 1.0
                        S.op("act", lambda e, o=o, pp=pp, sc=sc: e.activation(out=o[:, :], in_=pp[:, :],
                                                                              func=AF.Copy, scale=sc),
                             reads=[pp], writes=[o])
                        dst = SC["QT"] if f0 < 512 else SC["KT"]
                        r0 = f0 % 512
                        S.dma("pool", dst[r0:r0 + 128, t0:t0 + 512], o[:, :], reads=[o])
                    else:
                        o = ofm.get()
                        S.op("act", lambda e, o=o, pp=pp, nf=nf: e.copy(out=o[0:nf, :], in_=pp[0:nf, :]),
                             reads=[pp], writes=[o])
                        if f0 < OFF_DT:
                            r0 = f0 - OFF_XBC
                            S.dma("pool", SC["XBCT"][r0:r0 + nf, t0:t0 + 512], o[0:nf, :], reads=[o])
                        else:
                            r0 = f0 - OFF_RW
                            S.dma("pool", SC["RWT"][r0:r0 + nf, t0:t0 + 512], o[0:nf, :], reads=[o])
            else:
                for s in range(4):
                    pp = K.psum()
                    for kc in range(8):
                        S.op("pe", lambda e, pp=pp, w=w, kc=kc, s=s, ncol=ncol, hnT=hnT: e.matmul(
                            pp[:, 0:ncol], hnT[:, kc, s * 128:(s + 1) * 128], w[:, kc, 0:ncol],
                            start=(kc == 0), stop=(kc == 7)), reads=[w, hnT], writes=[pp])
                    tk = t0 + s * 128
                    if c0 == OFF_V:
                        o = obf.get()
                        S.op("act", lambda e, o=o, pp=pp: e.copy(out=o[:, :], in_=pp[:, :]), reads=[pp], writes=[o])
                        S.dma("pool", SC["V"][tk:tk + 128, :], o[:, :], reads=[o])
                    elif c0 >= OFF_G:
                        o = ofm.get()
                        S.op("act", lambda e, o=o, pp=pp: e.activation(out=o[:, :], in_=pp[:, :], func=AF.Sigmoid),
                             reads=[pp], writes=[o])
                        S.dma("pool", SC["G"][tk:tk + 128, c0 - OFF_G:c0 - OFF_G + 512], o[:, :], reads=[o])
                    elif c0 == OFF_DT:
                        o = ofm.get()
                        S.op("act", lambda e, o=o, pp=pp: e.copy(out=o[:, 0:32], in_=pp[:, 0:32]), reads=[pp], writes=[o])
                        S.dma("pool", SC["DT"][tk:tk + 128, :], o[:, 0:32], reads=[o])
                    else:
                        o = ofm.get()
                        S.op("act", lambda e, o=o, pp=pp: e.copy(out=o[:, :], in_=pp[:, :]), reads=[pp], writes=[o])
                        S.dma("pool", SC["Z"][tk:tk + 128, c0 - OFF_Z:c0 - OFF_Z + 512], o[:, :], reads=[o])


def attn_plan(nseg, seglen):
    R = seglen // 64
    nrow = nseg * R

    def rs_of(g, link):
        seg, r = divmod(g, R)
        if link and seg < 2 and nseg >= 2:
            return int(np.clip(g - 4, 0, 2 * R - 8))
        return seg * R + int(np.clip(r - 4, 0, R - 8))

    qc = np.arange(64)
    wst = np.clip(qc - 8, 0, 48)
    pats = {}
    plan = []
    ids = {}
    for P in range(nrow // 2):
        kps = set()
        for g in (2 * P, 2 * P + 1):
            for link in (False, True):
                rs = rs_of(g, link)
                for kr in range(rs, rs + 8):
                    kps.add(kr // 2)
        ent = []
        for KP in sorted(kps):
            both = []
            for link in (False, True):
                idx = np.full((2, 64, 2, 64), -1, np.int32)
                for qr2 in range(2):
                    g = 2 * P + qr2
                    rs = rs_of(g, link)
                    for kr2 in range(2):
                        kr = 2 * KP + kr2
                        if not (rs <= kr < rs + 8):
                            continue
                        dr = kr - g + 7
                        kc = np.arange(64)[:, None]
                        ok = (kc >= wst[None, :]) & (kc < wst[None, :] + 16)
                        dc = np.clip(kc - qc[None, :] + 15, 0, 30)
                        idx[kr2, :, qr2, :] = np.where(ok, dr * 31 + dc, -1)
                both.append(idx.reshape(128, 128))
            key = both[0].tobytes() + both[1].tobytes()
            if key not in ids:
                ids[key] = len(ids)
                pats.setdefault(False, []).append(both[0])
                pats.setdefault(True, []).append(both[1])
            ent.append((KP, ids[key]))
        plan.append(ent)
    return plan, {k: np.stack(v) for k, v in pats.items()}


def attn_tables(rpb, pat):
    flat = rpb.reshape(rpb.shape[0], 8, 15 * 31)
    g = flat[:, :, np.clip(pat, 0, None)]
    g = np.where(pat[None, None] >= 0, g, np.float32(-30000.0)).astype(np.float32)
    return np.ascontiguousarray(g.transpose(0, 2, 3, 1, 4))


def attn_pass(K, SC, tab, plan):
    import os
    ALV = int(os.environ.get("MK_ALV", "3"))
    S = K.S
    K.new_pass()
    qp = Pool(K, 2, [64, 8, 128], BF16)
    kp = Pool(K, 3, [64, 8, 128], BF16)
    vp = Pool(K, 3, [128, 8, 80], BF16)
    tp = Pool(K, 3, [128, 8, 128], F32)
    sp_ = Pool(K, 2, [128, 512], F32)
    ptp = Pool(K, 3, [128, 8, 128], BF16)
    yp = Pool(K, 2, [128, 8, 64], BF16)
    sm = Pool(K, 2, [128, 8], F32)
    for v in vp.t:
        S.op("pool", lambda e, v=v: e.memset(v[:, :, 64:65], 1.0), writes=[v])
    QTv = SC["QT"].rearrange("(c p) t -> p c t", p=64)
    KTv = SC["KT"].rearrange("(c p) t -> p c t", p=64)
    for P, ent in enumerate(plan):
        q = qp.get()
        S.dma("sp", q[:, :, :], QTv[:, :, P * 128:(P + 1) * 128], writes=[q])
        ob = [K.ps[6], K.ps[7]]
        for ki, (KP, cfg) in enumerate(ent):
            k = kp.get()
            S.dma("sp", k[:, :, :], KTv[:, :, KP * 128:(KP + 1) * 128], writes=[k])
            v = vp.get()
            S.dma("sp", v[:, :, 0:64], SC["V"][KP * 128:(KP + 1) * 128, :].rearrange("t (h d) -> t h d", h=8),
                  writes=[v])
            tb = tp.get()
            S.dma("sp", tb[:, :, :], tab[cfg], writes=[tb])
            pt = ptp.get()
            for hb in range(2):
                pp = K.ps[K.psi6]
                K.psi6 = (K.psi6 + 1) % 6
                for hh in range(4):
                    h = hb * 4 + hh
                    S.op("pe", lambda e, pp=pp, k=k, q=q, h=h, hh=hh: e.matmul(
                        pp[:, hh * 128:(hh + 1) * 128], k[:, h, :], q[:, h, :], start=True, stop=True),
                        reads=[k, q], writes=[pp])
                sb = sp_.get()
                S.op("dve", lambda e, sb=sb, pp=pp, tb=tb, hb=hb: e.tensor_tensor(
                    out=sb[:, :], in0=pp[:, :], in1=tb[:, hb * 4:(hb + 1) * 4, :].rearrange("p a b -> p (a b)"),
                    op=ALU.add), reads=[pp, tb], writes=[sb])
                S.op("act", lambda e, sb=sb, pt=pt, hb=hb: e.activation(
                    out=pt[:, hb * 4:(hb + 1) * 4, :].rearrange("p a b -> p (a b)"), in_=sb[:, :], func=AF.Exp),
                    reads=[sb], writes=[pt])
            for h in range(8 if ALV >= 2 else 0):
                o_ = ob[h // 4]
                hh = h % 4
                S.op("pe", lambda e, o_=o_, pt=pt, v=v, h=h, hh=hh, ki=ki, n=len(ent): e.matmul(
                    o_[:, hh * 128:hh * 128 + 65], pt[:, h, :], v[:, h, 0:65], start=(ki == 0 and hh == 0), stop=(ki == n - 1)),
                    reads=[pt, v], writes=[o_])
        rc = sm.get()
        y = yp.get()
        for hb in range(2 if ALV >= 3 else 0):
            ov = ob[hb][:, :].rearrange("p (h d) -> p h d", h=4)
            S.op("dve", lambda e, rc=rc, ov=ov, hb=hb: e.reciprocal(out=rc[:, hb * 4:(hb + 1) * 4], in_=ov[:, :, 64]),
                 reads=[ob[hb]], writes=[rc])
            S.op("dve", lambda e, rc=rc, ov=ov, hb=hb, y=y: e.tensor_tensor(
                out=y[:, hb * 4:(hb + 1) * 4, :], in0=ov[:, :, 0:64],
                in1=rc[:, hb * 4:(hb + 1) * 4].unsqueeze(2).broadcast_to([128, 4, 64]), op=ALU.mult),
                reads=[ob[hb], rc], writes=[y])
        if ALV >= 3:
            S.dma("pool", SC["YA"][P * 128:(P + 1) * 128, :], y[:, :, :].rearrange("p h d -> p (h d)"), reads=[y])


def merge_pass(K, SC, xio, g_ap, wbr_bf, wout_bf, ntok):
    S = K.S
    K.new_pass()
    g3 = K.alloc([128, D], F32)
    S.dma("sp", g3[:, :], g_ap.broadcast_to([128, D]), writes=[g3])
    wbr = K.alloc([128, 16, D], BF16)
    wbv = wbr_bf.rearrange("(kc p) n -> p kc n", p=128)
    for kc in range(0, 16, 4):
        S.dma("sp", wbr[:, kc:kc + 4, :], wbv[:, kc:kc + 4, :], writes=[wbr])
    wo = K.alloc([128, 8, D], BF16)
    S.dma("sp", wo[:, :, :], wout_bf.rearrange("(kc p) n -> p kc n", p=128), writes=[wo])
    ybp = Pool(K, 2, [128, 2048], BF16)
    gp = Pool(K, 2, [128, 3, D], F32)
    xp = Pool(K, 2, [128, D], F32)
    yTp = Pool(K, 2, [128, 16, 128], BF16)
    mp = Pool(K, 2, [128, D], F32)
    tmp = Pool(K, 2, [128, 512], F32)
    mbp = Pool(K, 2, [128, D], BF16)
    mTp = Pool(K, 2, [128, 8, 128], BF16)
    tt = Pool(K, 2, [128, D], F32)
    sm = Pool(K, 4, [128, 8], F32)
    junk = Pool(K, 2, [128, 512], BF16)
    for t in range(ntok // 128):
        r0 = t * 128
        yb = ybp.get()
        S.dma("sp", yb[:, 0:512], SC["YA"][r0:r0 + 128, :], writes=[yb])
        S.dma("sp", yb[:, 512:1536], SC["YS"][r0:r0 + 128, :], writes=[yb])
        S.dma("sp", yb[:, 1536:2048], SC["YR"][r0:r0 + 128, :], writes=[yb])
        gt = gp.get()
        S.dma("sp", gt[:, :, :], SC["G"][r0:r0 + 128, :].rearrange("t (b d) -> t b d", b=3), writes=[gt])
        x = xp.get()
        S.dma("sp", x[:, :], xio[r0:r0 + 128, :], writes=[x])
        yT = yTp.get()
        for half in range(2):
            pt = K.psum()
            ptb = pt.t[:, :].bitcast(BF16)
            for c in range(8):
                kc = half * 8 + c
                S.op("pe", lambda e, ptb=ptb, c=c, kc=kc, yb=yb: e.transpose(
                    ptb[:, c * 128:(c + 1) * 128], yb[:, kc * 128:(kc + 1) * 128], K.ident[:, :]),
                    reads=[yb, K.ident], writes=[pt])
            S.op("act", lambda e, yT=yT, ptb=ptb, half=half: e.copy(
                out=yT[:, half * 8:(half + 1) * 8, :], in_=ptb.rearrange("p (k c) -> p k c", k=8)),
                reads=[pt], writes=[yT])
        m = mp.get()
        for b, (k0, nk) in enumerate(((0, 4), (4, 8), (12, 4))):
            for nf in range(2):
                pp = K.psum()
                for kc in range(nk):
                    S.op("pe", lambda e, pp=pp, yT=yT, kc=kc, k0=k0, nf=nf, nk=nk: e.matmul(
                        pp[:, :], yT[:, k0 + kc, :], wbr[:, k0 + kc, nf * 512:(nf + 1) * 512],
                        start=(kc == 0), stop=(kc == nk - 1)), reads=[yT, wbr], writes=[pp])
                if b == 0:
                    S.op("dve", lambda e, m=m, pp=pp, gt=gt, nf=nf, b=b: e.tensor_tensor(
                        out=m[:, nf * 512:(nf + 1) * 512], in0=pp[:, :], in1=gt[:, b, nf * 512:(nf + 1) * 512],
                        op=ALU.mult), reads=[pp, gt], writes=[m])
                else:
                    tm = tmp.get()
                    S.op("dve", lambda e, tm=tm, pp=pp, gt=gt, nf=nf, b=b: e.tensor_tensor(
                        out=tm[:, :], in0=pp[:, :], in1=gt[:, b, nf * 512:(nf + 1) * 512], op=ALU.mult),
                        reads=[pp, gt], writes=[tm])
                    S.op("pool", lambda e, tm=tm, m=m, nf=nf: e.tensor_tensor(
                        out=m[:, nf * 512:(nf + 1) * 512], in0=m[:, nf * 512:(nf + 1) * 512], in1=tm[:, :],
                        op=ALU.add), reads=[tm, m], writes=[m])
        mb = mbp.get()
        S.op("act", lambda e, mb=mb, m=m: e.copy(out=mb[:, :], in_=m[:, :]), reads=[m], writes=[mb])
        mT = mTp.get()
        pt = K.psum()
        ptb = pt.t[:, :].bitcast(BF16)
        for c in range(8):
            S.op("pe", lambda e, ptb=ptb, c=c, mb=mb: e.transpose(
                ptb[:, c * 128:(c + 1) * 128], mb[:, c * 128:(c + 1) * 128], K.ident[:, :]),
                reads=[mb, K.ident], writes=[pt])
        S.op("act", lambda e, mT=mT, ptb=ptb: e.copy(out=mT[:, :, :], in_=ptb.rearrange("p (k c) -> p k c", k=8)),
             reads=[pt], writes=[mT])
        pp = [K.psum(), K.psum()]
        for nf in range(2):
            for kc in range(8):
                S.op("pe", lambda e, p_=pp[nf], mT=mT, kc=kc, nf=nf: e.matmul(
                    p_[:, :], mT[:, kc, :], wo[:, kc, nf * 512:(nf + 1) * 512], start=(kc == 0), stop=(kc == 7)),
                    reads=[mT, wo], writes=[pp[nf]])
        s_ = sm.get()
        jk = junk.get()
        for nf in range(2):
            S.op("act", lambda e, nf=nf, jk=jk, s_=s_, p_=pp[nf]: e.activation(
                out=jk[:, :], in_=p_[:, :], func=AF.Square, accum_out=s_[:, nf:nf + 1]),
                reads=[pp[nf]], writes=[jk, s_])
        S.op("dve", lambda e, s_=s_: e.tensor_tensor(out=s_[:, 2:3], in0=s_[:, 0:1], in1=s_[:, 1:2], op=ALU.add),
             reads=[s_], writes=[s_])
        S.op("act", lambda e, s_=s_: e.activation(out=s_[:, 3:4], in_=s_[:, 2:3], func=AF.Sqrt,
                                                  bias=K.epsc[:, 0:1], scale=1.0 / D),
             reads=[s_, K.epsc], writes=[s_])
        S.op("dve", lambda e, s_=s_: e.reciprocal(out=s_[:, 4:5], in_=s_[:, 3:4]), reads=[s_], writes=[s_])
        t_ = tt.get()
        for nf in range(2):
            S.op("dve", lambda e, nf=nf, t_=t_, s_=s_, p_=pp[nf]: e.scalar_tensor_tensor(
                out=t_[:, nf * 512:(nf + 1) * 512], in0=p_[:, :], scalar=s_[:, 4:5],
                in1=g3[:, nf * 512:(nf + 1) * 512], op0=ALU.mult, op1=ALU.mult),
                reads=[pp[nf], s_, g3], writes=[t_])
        S.op("pool", lambda e, t_=t_, x=x: e.tensor_tensor(out=x[:, :], in0=x[:, :], in1=t_[:, :], op=ALU.add),
             reads=[t_, x], writes=[x])
        S.dma("pool", xio[r0:r0 + 128, :], x[:, :], reads=[x])


def ssd_setup(K, A, l):
    S = K.S
    C = {}
    C["cw"] = K.alloc([128, 12, 5], F32)
    for k in range(5):
        S.dma("sp", C["cw"][:, :, k], A["ssm_conv_w"][l, k].rearrange("(fc p) -> p fc", p=128), writes=[C["cw"]],
              slow=True)
    C["cb"] = K.alloc([128, 12], F32)
    S.dma("sp", C["cb"][:, :], A["ssm_conv_b"][l].rearrange("(fc p) -> p fc", p=128), writes=[C["cb"]], slow=True)
    C["dtb"] = K.alloc([128, 32], F32)
    S.dma("sp", C["dtb"][:, :], A["ssm_dt_bias"][l].rearrange("a b -> (a b)").unsqueeze(0).broadcast_to([128, 32]),
          writes=[C["dtb"]])
    C["abc"] = K.alloc([128, 32], F32)
    S.dma("sp", C["abc"][:, :], A["ssm_a_log"][l].rearrange("a b -> (a b)").unsqueeze(0).broadcast_to([128, 32]),
          writes=[C["abc"]])
    S.op("act", lambda e: e.activation(out=C["abc"][:, :], in_=C["abc"][:, :], func=AF.Exp), reads=[C["abc"]],
         writes=[C["abc"]])
    S.op("dve", lambda e: e.tensor_scalar(out=C["abc"][:, :], in0=C["abc"][:, :], scalar1=-1.0, scalar2=0.0,
                                          op0=ALU.mult, op1=ALU.add), reads=[C["abc"]], writes=[C["abc"]])
    dsk = K.alloc([128, 32], F32)
    S.dma("sp", dsk[:, :], A["ssm_d"][l].rearrange("a b -> (a b)").unsqueeze(0).broadcast_to([128, 32]), writes=[dsk])
    C["dsum"] = K.alloc([128, 16], F32)
    S.op("dve", lambda e: e.tensor_tensor(out=C["dsum"][:, :], in0=dsk[:, 0:16], in1=dsk[:, 16:32], op=ALU.add),
         reads=[dsk], writes=[C["dsum"]])
    C["dI"] = K.alloc([128, 16, 128], F32)
    for h in range(16):
        S.op("dve", lambda e, h=h: e.tensor_scalar(out=C["dI"][:, h, :], in0=K.identf[:, :], scalar1=C["dsum"][:, h:h + 1],
                                                   scalar2=0.0, op0=ALU.mult, op1=ALU.add),
             reads=[K.identf, C["dsum"]], writes=[C["dI"]])
    C["ng"] = K.alloc([128, 1024], F32)
    S.dma("sp", C["ng"][:, :], A["ssm_norm_g"][l:l + 1, :].broadcast_to([128, 1024]), writes=[C["ng"]])
    return C


import os as _os
NB = int(_os.environ.get('MK_NB', '2'))


def ssd_pools(K):
    P = {}
    P["xw"] = Pool(K, NB, [128, 12, 132], F32)
    P["accD"] = Pool(K, 2, [128, 8, 128], F32)
    P["tmpD"] = Pool(K, 1, [128, 8, 128], F32)
    P["accP"] = Pool(K, 2, [128, 4, 128], F32)
    P["tmpP"] = Pool(K, 1, [128, 4, 128], F32)
    P["xbcT"] = Pool(K, NB, [128, 12, 128], BF16)
    P["xs"] = Pool(K, NB, [128, 1024], BF16)
    P["bm"] = Pool(K, NB, [128, 256], BF16)
    P["dt"] = Pool(K, NB, [128, 32], F32)
    P["v"] = Pool(K, NB, [128, 256], F32)
    return P


def ssd_prep(K, SC, C, P, c, nchunk, cps, need_cm=True):
    S = K.S
    t0 = c * 128
    ntok = nchunk * 128
    seg, cis = divmod(c, cps)
    xw = P["xw"].get()
    XB = SC["XBCT"].rearrange("(fc p) t -> p fc t", p=128)
    lo, hi = max(t0 - 2, 0), min(t0 + 130, ntok)
    S.dma("sp", xw[:, :, lo - (t0 - 2):hi - (t0 - 2)], XB[:, :, lo:hi], writes=[xw])
    if t0 == 0:
        S.op("pool", lambda e: e.memset(xw[:, :, 0:2], 0.0), writes=[xw])
    elif cis == 0:
        S.op("pool", lambda e, seg=seg: e.tensor_scalar(out=xw[:, :, 0:2], in0=xw[:, :, 0:2],
                                                        scalar1=K.flags[:, seg - 1:seg], scalar2=0.0,
                                                        op0=ALU.mult, op1=ALU.add), reads=[xw, K.flags], writes=[xw])
    if t0 + 130 > ntok:
        S.op("pool", lambda e: e.memset(xw[:, :, 130:132], 0.0), writes=[xw])
    elif cis == cps - 1:
        S.op("pool", lambda e, seg=seg: e.tensor_scalar(out=xw[:, :, 130:132], in0=xw[:, :, 130:132],
                                                        scalar1=K.flags[:, seg:seg + 1], scalar2=0.0,
                                                        op0=ALU.mult, op1=ALU.add), reads=[xw, K.flags], writes=[xw])
    cw = C["cw"]
    accs = {}
    for eng, f0, f1, nm in (("dve", 0, 8, "D"), ("pool", 8, 12, "P")):
        nf = f1 - f0
        acc = P["acc" + nm].get()
        tmp = P["tmp" + nm].get()
        accs[nm] = acc
        for k in range(5):
            wk = cw[:, f0:f1, k:k + 1].broadcast_to([128, nf, 128])
            if k == 0:
                S.op(eng, lambda e, wk=wk, f0=f0, f1=f1, acc=acc: e.tensor_tensor(
                    out=acc[:, :, :], in0=xw[:, f0:f1, 0:128], in1=wk, op=ALU.mult), reads=[xw, cw], writes=[acc])
            else:
                S.op(eng, lambda e, wk=wk, k=k, f0=f0, f1=f1, tmp=tmp: e.tensor_tensor(
                    out=tmp[:, :, :], in0=xw[:, f0:f1, k:k + 128], in1=wk, op=ALU.mult), reads=[xw, cw], writes=[tmp])
                S.op(eng, lambda e, acc=acc, tmp=tmp: e.tensor_tensor(out=acc[:, :, :], in0=acc[:, :, :],
                                                                      in1=tmp[:, :, :], op=ALU.add),
                     reads=[acc, tmp], writes=[acc])
    xbcT = P["xbcT"].get()
    for fc in range(12):
        a_ = accs["D"] if fc < 8 else accs["P"]
        fi = fc if fc < 8 else fc - 8
        S.op("act", lambda e, fc=fc, a_=a_, fi=fi: e.activation(out=xbcT[:, fc, :], in_=a_[:, fi, :], func=AF.Silu,
                                                                bias=C["cb"][:, fc:fc + 1]), reads=[a_, C["cb"]],
             writes=[xbcT])
    xs = P["xs"].get()
    pt = K.psum()
    ptb = pt.t[:, :].bitcast(BF16)
    for j in range(8):
        S.op("pe", lambda e, j=j: e.transpose(ptb[:, j * 128:(j + 1) * 128], xbcT[:, j, :], K.ident[:, :]),
             reads=[xbcT, K.ident], writes=[pt])
    S.op("act", lambda e: e.copy(out=xs[:, :], in_=ptb[:, :]), reads=[pt], writes=[xs])
    bm = P["bm"].get()
    pt2 = K.psum()
    ptb2 = pt2.t[:, :].bitcast(BF16)
    for g in range(2):
        S.op("pe", lambda e, g=g: e.transpose(ptb2[:, g * 128:(g + 1) * 128], xbcT[:, 8 + g, :], K.ident[:, :]),
             reads=[xbcT, K.ident], writes=[pt2])
    S.op("act", lambda e: e.copy(out=bm[:, :], in_=ptb2[:, 0:256]), reads=[pt2], writes=[bm])
    dt = P["dt"].get()
    S.dma("sp", dt[:, :], SC["DT"][t0:t0 + 128, :], writes=[dt])
    V = P["v"].get()
    S.op("dve", lambda e: e.tensor_tensor(out=V[:, 0:32], in0=dt[:, :], in1=C["dtb"][:, :], op=ALU.add),
         reads=[dt, C["dtb"]], writes=[V])
    S.op("act", lambda e: e.activation(out=V[:, 0:32], in_=V[:, 0:32], func=AF.Exp), reads=[V], writes=[V])
    S.op("act", lambda e: e.activation(out=V[:, 0:32], in_=V[:, 0:32], func=AF.Ln, bias=K.onec[:, 0:1]),
         reads=[V, K.onec], writes=[V])
    S.op("act", lambda e: e.activation(out=V[:, 32:64], in_=V[:, 0:32], func=AF.Ln), reads=[V], writes=[V])
    S.op("dve", lambda e: e.tensor_tensor(out=V[:, 64:96], in0=V[:, 0:32], in1=C["abc"][:, :], op=ALU.mult),
         reads=[V, C["abc"]], writes=[V])
    pc = K.psum()
    S.op("pe", lambda e: e.matmul(pc[:, 0:16], K.uinc[:, :], V[:, 64:80], start=True, stop=True),
         reads=[K.uinc, V], writes=[pc])
    S.op("pe", lambda e: e.matmul(pc[:, 16:32], K.uexc[:, :], V[:, 80:96], start=True, stop=True),
         reads=[K.uexc, V], writes=[pc])
    S.op("pe", lambda e: e.matmul(pc[:, 32:64], K.onesf[:, :], V[:, 64:96], start=True, stop=True),
         reads=[K.onesf, V], writes=[pc])
    S.op("dve", lambda e: e.tensor_copy(out=V[:, 96:160], in_=pc[:, 0:64]), reads=[pc], writes=[V])
    S.op("dve", lambda e: e.tensor_tensor(out=V[:, 160:176], in0=V[:, 32:48], in1=V[:, 96:112], op=ALU.subtract),
         reads=[V], writes=[V])
    S.op("dve", lambda e: e.tensor_tensor(out=V[:, 176:192], in0=V[:, 48:64], in1=V[:, 112:128], op=ALU.add),
         reads=[V], writes=[V])
    S.op("dve", lambda e: e.tensor_tensor(out=V[:, 192:208], in0=V[:, 160:176], in1=V[:, 128:144], op=ALU.add),
         reads=[V], writes=[V])
    S.op("dve", lambda e: e.tensor_tensor(out=V[:, 240:256], in0=V[:, 144:160], in1=V[:, 112:128], op=ALU.subtract),
         reads=[V], writes=[V])
    S.op("dve", lambda e: e.tensor_copy(out=V[:, 208:240], in_=V[:, 176:192].unsqueeze(1).broadcast_to([128, 2, 16])
                                        .rearrange("p a b -> p (a b)")) if False else
         e.tensor_copy(out=V[:, 208:224], in_=V[:, 176:192]), reads=[V], writes=[V])
    S.op("dve", lambda e: e.tensor_copy(out=V[:, 224:240], in_=V[:, 96:112]), reads=[V], writes=[V])
    S.op("act", lambda e: e.activation(out=V[:, 192:256], in_=V[:, 192:256], func=AF.Exp), reads=[V], writes=[V])
    S.op("act", lambda e: e.activation(out=V[:, 128:160], in_=V[:, 128:160], func=AF.Exp), reads=[V], writes=[V])
    return {"xbcT": xbcT, "xs": xs, "bm": bm, "V": V}


def ssd_state_step(K, T, H, wcol, ecol, PS):
    S = K.S
    xs, bm, V = T["xs"], T["bm"], T["V"]
    xd = PS["xd"].get()
    S.op("pool", lambda e: e.tensor_tensor(out=xd[:, :].rearrange("p (h d) -> p h d", h=16),
                                           in0=xs[:, :].rearrange("p (h d) -> p h d", h=16),
                                           in1=V[:, wcol:wcol + 16].unsqueeze(2).broadcast_to([128, 16, 64]),
                                           op=ALU.mult), reads=[xs, V], writes=[xd])
    S.op("dve", lambda e: e.tensor_tensor(out=H[:, :].rearrange("p (h d) -> p h d", h=16),
                                          in0=H[:, :].rearrange("p (h d) -> p h d", h=16),
                                          in1=V[:, ecol:ecol + 16].unsqueeze(2).broadcast_to([128, 16, 64]),
                                          op=ALU.mult), reads=[H, V], writes=[H])
    for g in range(2):
        pp = K.psum()
        S.op("pe", lambda e, pp=pp, g=g: e.matmul(pp[:, :], bm[:, g * 128:(g + 1) * 128], xd[:, g * 512:(g + 1) * 512],
                                                  start=True, stop=True), reads=[bm, xd], writes=[pp])
        S.op("dve", lambda e, pp=pp, g=g: e.tensor_tensor(out=H[:, g * 512:(g + 1) * 512], in0=H[:, g * 512:(g + 1) * 512],
                                                          in1=pp[:, :], op=ALU.add), reads=[pp, H], writes=[H])


def ssd_bwd_pass(K, SC, C, nchunk, cps):
    S = K.S
    P = ssd_pools(K)
    PS = {"xd": Pool(K, NB, [128, 1024], BF16)}
    H = K.alloc([128, 1024], F32)
    hbp = Pool(K, NB, [128, 1024], BF16)
    S.op("dve", lambda e: e.memset(H[:, :], 0.0), writes=[H])
    def body(c):
        seg, cis = divmod(c, cps)
        if cis == cps - 1 and c != nchunk - 1:
            S.op("dve", lambda e, seg=seg: e.tensor_scalar(out=H[:, :], in0=H[:, :], scalar1=K.flags[:, seg:seg + 1],
                                                           scalar2=0.0, op0=ALU.mult, op1=ALU.add),
                 reads=[H, K.flags], writes=[H])
        T = Tn.pop(c)
        if c - 1 >= 0:
            Tn[c - 1] = ssd_prep(K, SC, C, P, c - 1, nchunk, cps)
        ssd_state_step(K, T, H, 208, 144, PS)
        hb = hbp.get()
        S.op("act", lambda e, hb=hb: e.copy(out=hb[:, :], in_=H[:, :]), reads=[H], writes=[hb])
        S.dma("pool", SC["HB"][c], hb[:, :], reads=[hb])

    Tn = {nchunk - 1: ssd_prep(K, SC, C, P, nchunk - 1, nchunk, cps)}
    for c in range(nchunk - 1, -1, -1):
        body(c)


def ssd_fwd_pass(K, SC, C, nchunk, cps):
    S = K.S
    P = ssd_pools(K)
    PS = {"xd": Pool(K, NB, [128, 1024], BF16)}
    H = K.alloc([128, 1024], F32)
    Hbf = K.alloc([128, 1024], BF16)
    S.op("dve", lambda e: e.memset(H[:, :], 0.0), writes=[H])
    hbp = Pool(K, NB, [128, 1024], BF16)
    cbp = Pool(K, NB, [128, 4, 128], F32)
    tq = Pool(K, NB, [128, 4, 128], F32)
    eq = Pool(K, NB, [128, 4, 128], F32)
    mq = Pool(K, NB, [128, 4, 128], F32)
    Mp = Pool(K, NB, [128, 16, 128], BF16)
    yp = Pool(K, NB, [128, 1024], F32)
    t1p = Pool(K, NB, [128, 512], F32)
    zp = Pool(K, NB, [128, 1024], F32)
    ybp = Pool(K, NB, [128, 1024], BF16)
    smp = Pool(K, 4, [128, 8], F32)
    jk = Pool(K, 1, [128, 512], BF16)
    def body(c):
        seg, cis = divmod(c, cps)
        t0 = c * 128
        if cis == 0 and c != 0:
            S.op("dve", lambda e, seg=seg: e.tensor_scalar(out=H[:, :], in0=H[:, :], scalar1=K.flags[:, seg - 1:seg],
                                                           scalar2=0.0, op0=ALU.mult, op1=ALU.add),
                 reads=[H, K.flags], writes=[H])
        S.op("act", lambda e: e.copy(out=Hbf[:, :], in_=H[:, :]), reads=[H], writes=[Hbf])
        hb = hbp.get()
        if c == nchunk - 1:
            S.op("pool", lambda e, hb=hb: e.memset(hb[:, :], 0.0), writes=[hb])
        else:
            S.dma("sp", hb[:, :], SC["HB"][c + 1], writes=[hb])
            if cis == cps - 1:
                S.op("pool", lambda e, hb=hb, seg=seg: e.tensor_scalar(
                    out=hb[:, :], in0=hb[:, :], scalar1=K.flags[:, seg:seg + 1], scalar2=0.0, op0=ALU.mult,
                    op1=ALU.add), reads=[hb, K.flags], writes=[hb])
        T = Tn.pop(c)
        if c + 1 < nchunk:
            Tn[c + 1] = ssd_prep(K, SC, C, P, c + 1, nchunk, cps)
        xbcT, xs, V = T["xbcT"], T["xs"], T["V"]
        cbm = cbp.get()
        for g in range(2):
            pp = K.psum()
            S.op("pe", lambda e, pp=pp, g=g: e.matmul(pp[:, 0:128], xbcT[:, 8 + g, :], xbcT[:, 10 + g, :],
                                                      start=True, stop=True), reads=[xbcT], writes=[pp])
            S.op("dve", lambda e, pp=pp, g=g: e.tensor_tensor(out=cbm[:, 2 * g, :], in0=pp[:, 0:128], in1=K.mskf[:, :],
                                                              op=ALU.mult), reads=[pp, K.mskf], writes=[cbm])
            S.op("dve", lambda e, pp=pp, g=g: e.tensor_tensor(out=cbm[:, 2 * g + 1, :], in0=pp[:, 0:128],
                                                              in1=K.mskb[:, :], op=ALU.mult),
                 reads=[pp, K.mskb], writes=[cbm])
        M = Mp.get()
        for qd in range(4):
            g = qd // 2
            mqs = []
            for d in range(2):
                pa = K.psum()
                for hh in range(4):
                    h = qd * 4 + hh
                    col = 64 + d * 16 + h
                    S.op("pe", lambda e, pa=pa, hh=hh, col=col, d=d: e.matmul(
                        pa[:, hh * 128:(hh + 1) * 128], V[:, col:col + 1].broadcast_to([128, 128]),
                        (K.uinc if d == 0 else K.uexc)[:, :], start=True, stop=True),
                        reads=[V, K.uinc, K.uexc], writes=[pa])
                t = tq.get()
                pav = pa[:, :].rearrange("p (a b) -> p a b", a=4)
                if d == 0:
                    vb = V[:, 96 + qd * 4:96 + qd * 4 + 4].unsqueeze(2).broadcast_to([128, 4, 128])
                    S.op("dve", lambda e, t=t, pav=pav, vb=vb: e.tensor_tensor(out=t[:, :, :], in0=pav, in1=vb,
                                                                               op=ALU.subtract),
                         reads=[pa, V], writes=[t])
                else:
                    vb = V[:, 112 + qd * 4:112 + qd * 4 + 4].unsqueeze(2).broadcast_to([128, 4, 128])
                    S.op("dve", lambda e, t=t, pav=pav, vb=vb: e.tensor_tensor(out=t[:, :, :], in0=vb, in1=pav,
                                                                               op=ALU.subtract),
                         reads=[pa, V], writes=[t])
                S.op("dve", lambda e, t=t: e.tensor_scalar(out=t[:, :, :], in0=t[:, :, :], scalar1=0.0, scalar2=0.0,
                                                            op0=ALU.min, op1=ALU.add), reads=[t], writes=[t])
                E = eq.get()
                for hh in range(4):
                    h = qd * 4 + hh
                    S.op("act", lambda e, E=E, t=t, hh=hh, h=h, d=d: e.activation(
                        out=E[:, hh, :], in_=t[:, hh, :], func=AF.Exp, bias=V[:, 32 + d * 16 + h:33 + d * 16 + h]),
                        reads=[t, V], writes=[E])
                m_ = mq.get()
                S.op("dve", lambda e, m_=m_, E=E, g=g, d=d: e.tensor_tensor(
                    out=m_[:, :, :], in0=E[:, :, :], in1=cbm[:, 2 * g + d:2 * g + d + 1, :].broadcast_to([128, 4, 128]),
                    op=ALU.mult), reads=[E, cbm], writes=[m_])
                mqs.append(m_)
            S.op("dve", lambda e, a=mqs[0], b=mqs[1]: e.tensor_tensor(out=a[:, :, :], in0=a[:, :, :], in1=b[:, :, :],
                                                                       op=ALU.add), reads=[mqs[0], mqs[1]], writes=[mqs[0]])
            S.op("dve", lambda e, a=mqs[0], qd=qd: e.tensor_tensor(out=M[:, qd * 4:(qd + 1) * 4, :], in0=a[:, :, :],
                                                                    in1=C["dI"][:, qd * 4:(qd + 1) * 4, :], op=ALU.add),
                 reads=[mqs[0], C["dI"]], writes=[M])
        yi = [K.psum(), K.psum()]
        yo = [[K.psum(), K.psum()], [K.psum(), K.psum()]]
        for h in range(16):
            g, hh = divmod(h, 8)
            S.op("pe", lambda e, h=h, g=g, hh=hh: e.matmul(yi[g][:, hh * 64:(hh + 1) * 64], M[:, h, :],
                                                           xs[:, h * 64:(h + 1) * 64], start=True, stop=True),
                 reads=[M, xs], writes=[yi[g]])
        for d, Hs in enumerate((Hbf, hb)):
            for g in range(2):
                S.op("pe", lambda e, d=d, g=g, Hs=Hs: e.matmul(yo[d][g][:, :], xbcT[:, 10 + g, :],
                                                               Hs[:, g * 512:(g + 1) * 512], start=True, stop=True),
                     reads=[xbcT, Hs], writes=[yo[d][g]])
        y = yp.get()
        for g in range(2):
            t1 = t1p.get()
            scf = V[:, 224 + g * 8:232 + g * 8].unsqueeze(2).broadcast_to([128, 8, 64])
            scb = V[:, 240 + g * 8:248 + g * 8].unsqueeze(2).broadcast_to([128, 8, 64])
            S.op("dve", lambda e, t1=t1, g=g, scf=scf: e.tensor_tensor(
                out=t1[:, :].rearrange("p (h d) -> p h d", h=8), in0=yo[0][g][:, :].rearrange("p (h d) -> p h d", h=8),
                in1=scf, op=ALU.mult), reads=[yo[0][g], V], writes=[t1])
            S.op("dve", lambda e, t1=t1, g=g: e.tensor_tensor(out=t1[:, :], in0=t1[:, :], in1=yi[g][:, :], op=ALU.add),
                 reads=[t1, yi[g]], writes=[t1])
            S.op("dve", lambda e, g=g, scb=scb, y=y: e.tensor_tensor(
                out=y[:, g * 512:(g + 1) * 512].rearrange("p (h d) -> p h d", h=8),
                in0=yo[1][g][:, :].rearrange("p (h d) -> p h d", h=8), in1=scb, op=ALU.mult),
                reads=[yo[1][g], V], writes=[y])
            S.op("pool", lambda e, g=g, t1=t1, y=y: e.tensor_tensor(out=y[:, g * 512:(g + 1) * 512],
                                                                    in0=y[:, g * 512:(g + 1) * 512], in1=t1[:, :],
                                                                    op=ALU.add), reads=[t1, y], writes=[y])
        if "DY" in SC:
            S.dma("pool", SC["DY"][t0:t0 + 128, :], y[:, :], reads=[y])
            S.dma("pool", SC["DV"][t0:t0 + 128, :], V[:, :], reads=[V])
            S.dma("pool", SC["DM"][c], M[:, :, :], reads=[M])
        z = zp.get()
        S.dma("sp", z[:, :], SC["Z"][t0:t0 + 128, :], writes=[z])
        S.op("act", lambda e, z=z: e.activation(out=z[:, :], in_=z[:, :], func=AF.Silu), reads=[z], writes=[z])
        S.op("pool", lambda e, z=z, y=y: e.tensor_tensor(out=y[:, :], in0=y[:, :], in1=z[:, :], op=ALU.mult),
             reads=[y, z], writes=[y])
        sm = smp.get()
        j_ = jk.get()
        yb = ybp.get()
        for g in range(2):
            S.op("act", lambda e, g=g, sm=sm, j_=j_, y=y: e.activation(
                out=j_[:, :], in_=y[:, g * 512:(g + 1) * 512], func=AF.Square, accum_out=sm[:, g:g + 1]),
                reads=[y], writes=[j_, sm])
        S.op("act", lambda e, sm=sm: e.activation(out=sm[:, 2:4], in_=sm[:, 0:2], func=AF.Sqrt, bias=K.epsc[:, 0:1],
                                                  scale=1.0 / 512), reads=[sm, K.epsc], writes=[sm])
        S.op("dve", lambda e, sm=sm: e.reciprocal(out=sm[:, 4:6], in_=sm[:, 2:4]), reads=[sm], writes=[sm])
        for g in range(2):
            S.op("dve", lambda e, g=g, sm=sm, y=y, yb=yb: e.scalar_tensor_tensor(
                out=yb[:, g * 512:(g + 1) * 512], in0=y[:, g * 512:(g + 1) * 512], scalar=sm[:, 4 + g:5 + g],
                in1=C["ng"][:, g * 512:(g + 1) * 512], op0=ALU.mult, op1=ALU.mult),
                reads=[y, sm, C["ng"]], writes=[yb])
        S.dma("pool", SC["YS"][t0:t0 + 128, :], yb[:, :], reads=[yb])
        ssd_state_step(K, T, H, 192, 128, PS)

    Tn = {0: ssd_prep(K, SC, C, P, 0, nchunk, cps)}
    for c in range(nchunk):
        body(c)


RL = float(_os.environ.get('MK_RL', '99'))
LW_C = 0.6065306597126334
GN_EPS = 64e-5


def rwkv_setup(K, A, l):
    S = K.S
    C = {}

    def ld(name, shape, dt, q, src, slow=False):
        C[name] = K.alloc(shape, dt)
        S.dma(q, C[name][tuple(slice(None) for _ in shape)], src, writes=[C[name]], slow=slow)

    ld("wup", [64, 2, 512], BF16, "pool", A["rwkv_w_up"][l].rearrange("n l c -> l n c"))
    ld("aup", [64, 2, 512], BF16, "pool", A["rwkv_a_up"][l].rearrange("n l c -> l n c"))
    C["gup"] = K.alloc([64, 3, 512], BF16)
    S.dma("pool", C["gup"][:, 0:2, :], A["rwkv_g_up"][l][0:128, :].rearrange("(q l) c -> l q c", l=64), writes=[C["gup"]])
    S.dma("pool", C["gup"][0:32, 2, :], A["rwkv_g_up"][l][128:160, :], writes=[C["gup"]])
    C["w0"] = K.alloc([128, 2, 512], F32)
    for n in range(2):
        S.dma("sp", C["w0"][:, n, :], A["rwkv_w0"][l][n:n + 1, :].broadcast_to([128, 512]), writes=[C["w0"]])
    C["a0"] = K.alloc([64, 2, 8], F32)
    for n in range(2):
        S.dma("sp", C["a0"][:, n, :], A["rwkv_a0"][l][n].rearrange("(h j) -> j h", j=64), writes=[C["a0"]], slow=True)
    for nm, src in (("kk_", A["rwkv_k_k"][l]), ("ka", A["rwkv_k_a"][l])):
        C[nm] = K.alloc([64, 8], F32)
        S.dma("sp", C[nm][:, :], src.rearrange("(h j) -> j h", j=64), writes=[C[nm]], slow=True)
    C["rk"] = K.alloc([64, 8], F32)
    S.dma("sp", C["rk"][:, :], A["rwkv_r_k"][l].rearrange("h j -> j h"), writes=[C["rk"]], slow=True)
    C["omka"] = K.alloc([64, 8], F32)
    S.op("dve", lambda e: e.tensor_scalar(out=C["omka"][:, :], in0=C["ka"][:, :], scalar1=-1.0, scalar2=1.0,
                                          op0=ALU.mult, op1=ALU.add), reads=[C["ka"]], writes=[C["omka"]])
    C["mu"] = K.alloc([64, 3, 31], F32)
    S.op("dve", lambda e: e.memset(C["mu"][:, :, :], 0.0), writes=[C["mu"]])
    for m in range(2):
        S.dma("sp", C["mu"][:, m, 0:30], A["rwkv_mu"][l][m, 0:1920].rearrange("(g j) -> j g", j=64), writes=[C["mu"]],
              slow=True)
        S.dma("sp", C["mu"][0:32, m, 30:31], A["rwkv_mu"][l][m, 1920:1952].rearrange("(g j) -> j g", j=32),
              writes=[C["mu"]], slow=True)
    S.op("dve", lambda e: e.tensor_tensor(out=C["mu"][:, 2, :], in0=C["mu"][:, 0, :], in1=C["mu"][:, 1, :], op=ALU.add),
         reads=[C["mu"]], writes=[C["mu"]])
    S.op("dve", lambda e: e.tensor_scalar(out=C["mu"][:, 2, :], in0=C["mu"][:, 2, :], scalar1=-1.0, scalar2=1.0,
                                          op0=ALU.mult, op1=ALU.add), reads=[C["mu"]], writes=[C["mu"]])
    C["lng"] = K.alloc([128, 512], F32)
    S.dma("sp", C["lng"][:, :], A["rwkv_ln_g"][l:l + 1, :].broadcast_to([128, 512]), writes=[C["lng"]])
    C["lnb"] = K.alloc([128, 512], F32)
    S.dma("sp", C["lnb"][:, :], A["rwkv_ln_b"][l:l + 1, :].broadcast_to([128, 512]), writes=[C["lnb"]])
    C["gneps"] = K.alloc([128, 1], F32)
    S.op("dve", lambda e: e.memset(C["gneps"][:, :], GN_EPS), writes=[C["gneps"]])
    return C


def shift_pass(K, SC, A, l, ntok, seglen):
    S = K.S
    K.new_pass()
    mu = K.alloc([128, 3, 16], F32)
    S.op("dve", lambda e: e.memset(mu[:, :, :], 0.0), writes=[mu])
    for m in range(2):
        S.dma("sp", mu[:, m, 0:15], A["rwkv_mu"][l][m, 0:1920].rearrange("(g j) -> j g", j=128), writes=[mu], slow=True)
        S.dma("sp", mu[0:32, m, 15:16], A["rwkv_mu"][l][m, 1920:1952].rearrange("(g j) -> j g", j=32), writes=[mu],
              slow=True)
    S.op("dve", lambda e: e.tensor_tensor(out=mu[:, 2, :], in0=mu[:, 0, :], in1=mu[:, 1, :], op=ALU.add),
         reads=[mu], writes=[mu])
    S.op("dve", lambda e: e.tensor_scalar(out=mu[:, 2, :], in0=mu[:, 2, :], scalar1=-1.0, scalar2=1.0, op0=ALU.mult,
                                          op1=ALU.add), reads=[mu], writes=[mu])
    xp = Pool(K, 2, [128, 16, 514], F32)
    ap = Pool(K, 2, [128, 16, 512], F32)
    RW = SC["RWT"]

    def body(mt):
        t0 = mt * 512
        X = xp.get()
        lo, hi = max(t0 - 1, 0), min(t0 + 513, ntok)
        a_, b_ = lo - (t0 - 1), hi - (t0 - 1)
        S.dma("sp", X[:, 0:15, a_:b_], RW[0:1920, lo:hi].rearrange("(g j) t -> j g t", j=128), writes=[X])
        S.dma("sp", X[0:32, 15, a_:b_], RW[1920:1952, lo:hi], writes=[X])
        seg = t0 // seglen
        if t0 == 0:
            S.op("pool", lambda e: e.memset(X[:, :, 0:1], 0.0), writes=[X])
        elif t0 % seglen == 0:
            S.op("pool", lambda e: e.tensor_scalar(out=X[:, :, 0:1], in0=X[:, :, 0:1], scalar1=K.flags[:, seg - 1:seg],
                                                   scalar2=0.0, op0=ALU.mult, op1=ALU.add), reads=[X, K.flags], writes=[X])
        if t0 + 512 >= ntok:
            S.op("pool", lambda e: e.memset(X[:, :, 513:514], 0.0), writes=[X])
        elif (t0 + 512) % seglen == 0:
            S.op("pool", lambda e: e.tensor_scalar(out=X[:, :, 513:514], in0=X[:, :, 513:514],
                                                   scalar1=K.flags[:, seg:seg + 1], scalar2=0.0, op0=ALU.mult,
                                                   op1=ALU.add), reads=[X, K.flags], writes=[X])
        acc = ap.get()
        for fc in range(16):
            eng = "dve"
            S.op("pool", lambda e, fc=fc: e.tensor_scalar(out=acc[:, fc, :], in0=X[:, fc, 1:513], scalar1=mu[:, 2, fc:fc + 1],
                                                       scalar2=0.0, op0=ALU.mult, op1=ALU.add),
                 reads=[X, mu], writes=[acc])
            S.op(eng, lambda e, fc=fc: e.scalar_tensor_tensor(out=acc[:, fc, :], in0=X[:, fc, 0:512],
                                                              scalar=mu[:, 0, fc:fc + 1], in1=acc[:, fc, :],
                                                              op0=ALU.mult, op1=ALU.add), reads=[X, mu, acc], writes=[acc])
            S.op(eng, lambda e, fc=fc: e.scalar_tensor_tensor(out=acc[:, fc, :], in0=X[:, fc, 2:514],
                                                              scalar=mu[:, 1, fc:fc + 1], in1=acc[:, fc, :],
                                                              op0=ALU.mult, op1=ALU.add), reads=[X, mu, acc], writes=[acc])
        S.dma("pool", SC["RWS"][0:1920, t0:t0 + 512].rearrange("(g j) t -> j g t", j=128), acc[:, 0:15, :], reads=[acc])
        S.dma("pool", SC["RWS"][1920:1952, t0:t0 + 512], acc[0:32, 15, :], reads=[acc])

    for mt in range(ntok // 512):
        body(mt)


def rwkv_pools(K):
    P = {}
    P["Pt"] = Pool(K, 2, [64, 31, 128], F32)
    P["tw"] = Pool(K, 2, [64, 128], BF16)
    P["ad"] = Pool(K, 2, [64, 128], BF16)
    P["sg"] = Pool(K, 2, [128, 512], F32)
    P["f8"] = Pool(K, 11, [64, 8, 128], F32)
    P["b8"] = Pool(K, 12, [64, 8, 128], BF16)
    P["tok"] = Pool(K, 12, [128, 512], BF16)
    P["vf"] = Pool(K, 2, [128, 512], F32)
    P["pl"] = Pool(K, 2, [64, 8], F32)
    P["mat"] = Pool(K, 16, [128, 4, 128], BF16)
    P["res"] = Pool(K, 12, [128, 4, 128], BF16)
    P["w"] = Pool(K, 2, [128, 512], F32)
    P["rkc"] = Pool(K, 2, [128, 8], F32)
    return P


def rwkv_prep(K, SC, C, P, c, nchunk, cps, n, want_g):
    S = K.S
    t0 = c * 128
    ntok = nchunk * 128
    seg, cis = divmod(c, cps)
    Pt = P["Pt"].get()
    S.dma("sp", Pt[:, 0:30, :], SC["RWS"][0:1920, t0:t0 + 128].rearrange("(g j) t -> j g t", j=64), writes=[Pt])
    S.dma("sp", Pt[0:32, 30, :], SC["RWS"][1920:1952, t0:t0 + 128], writes=[Pt])
    if RL <= 1:
        return None
    rT, kT, vT = Pt[:, 0:8, :], Pt[:, 8:16, :], Pt[:, 16:24, :]
    tw = P["tw"].get()
    S.op("act", lambda e: e.activation(out=tw[:, :], in_=Pt[:, 24 + n, :], func=AF.Tanh), reads=[Pt], writes=[tw])
    pu = K.psum()
    S.op("pe", lambda e: e.matmul(pu[:, :], tw[:, :], C["wup"][:, n, :], start=True, stop=True),
         reads=[tw, C["wup"]], writes=[pu])
    sg = P["sg"].get()
    S.op("dve", lambda e: e.tensor_tensor(out=sg[:, :], in0=pu[:, :], in1=C["w0"][:, n, :], op=ALU.add),
         reads=[pu, C["w0"]], writes=[sg])
    S.op("act", lambda e: e.activation(out=sg[:, :], in_=sg[:, :], func=AF.Sigmoid), reads=[sg], writes=[sg])
    if RL <= 2:
        return None
    uin = K.uinc if n == 0 else K.mskb
    uex = K.uexc if n == 0 else K.ugt
    eP, eN, ePx = P["f8"].get(), P["f8"].get(), P["f8"].get()
    for hq in range(2):
        pi, px = K.psum(), K.psum()
        for hh in range(4):
            h = hq * 4 + hh
            S.op("pe", lambda e, pi=pi, hh=hh, h=h: e.matmul(pi[0:64, hh * 128:(hh + 1) * 128], sg[:, h * 64:(h + 1) * 64],
                                                             uin[:, :], start=True, stop=True),
                 reads=[sg, uin], writes=[pi])
            S.op("pe", lambda e, px=px, hh=hh, h=h: e.matmul(px[0:64, hh * 128:(hh + 1) * 128], sg[:, h * 64:(h + 1) * 64],
                                                             uex[:, :], start=True, stop=True),
                 reads=[sg, uex], writes=[px])
        sl = slice(hq * 4, hq * 4 + 4)
        S.op("act", lambda e, pi=pi, sl=sl: e.activation(out=eP[:, sl, :].rearrange("p a b -> p (a b)"), in_=pi[0:64, :],
                                                         func=AF.Exp, scale=-LW_C), reads=[pi], writes=[eP])
        S.op("act", lambda e, pi=pi, sl=sl: e.activation(out=eN[:, sl, :].rearrange("p a b -> p (a b)"), in_=pi[0:64, :],
                                                         func=AF.Exp, scale=LW_C), reads=[pi], writes=[eN])
        S.op("act", lambda e, px=px, sl=sl: e.activation(out=ePx[:, sl, :].rearrange("p a b -> p (a b)"), in_=px[0:64, :],
                                                         func=AF.Exp, scale=-LW_C), reads=[px], writes=[ePx])
    last = 127 if n == 0 else 0
    pl = P["pl"].get()
    S.op("dve", lambda e: e.tensor_copy(out=pl[:, :], in_=eP[:, :, last]), reads=[eP], writes=[pl])
    if RL <= 3:
        return None
    ad = P["ad"].get()
    S.op("act", lambda e: e.copy(out=ad[:, :], in_=Pt[:, 26 + n, :]), reads=[Pt], writes=[ad])
    ic = P["f8"].get()
    for hq in range(2):
        pa = K.psum()
        for hh in range(4):
            h = hq * 4 + hh
            S.op("pe", lambda e, pa=pa, hh=hh, h=h: e.matmul(pa[0:64, hh * 128:(hh + 1) * 128],
                                                             C["aup"][:, n, h * 64:(h + 1) * 64], ad[:, :],
                                                             start=True, stop=True), reads=[C["aup"], ad], writes=[pa])
        sl = slice(hq * 4, hq * 4 + 4)
        S.op("dve", lambda e, pa=pa, sl=sl: e.tensor_tensor(
            out=ic[:, sl, :], in0=pa[0:64, :].rearrange("p (a b) -> p a b", a=4),
            in1=C["a0"][:, n, sl].unsqueeze(2).broadcast_to([64, 4, 128]), op=ALU.add), reads=[pa, C["a0"]], writes=[ic])
    S.op("act", lambda e: e.activation(out=ic[:, :, :], in_=ic[:, :, :], func=AF.Sigmoid), reads=[ic], writes=[ic])
    if RL <= 4:
        return None
    kk = P["f8"].get()
    sq = P["f8"].get()
    h8 = lambda t_: t_[:, :].unsqueeze(2).broadcast_to([64, 8, 128])
    S.op("dve", lambda e: e.tensor_tensor(out=kk[:, :, :], in0=kT, in1=h8(C["kk_"]), op=ALU.mult),
         reads=[Pt, C["kk_"]], writes=[kk])
    S.op("pool", lambda e: e.tensor_tensor(out=sq[:, :, :], in0=kk[:, :, :], in1=kk[:, :, :], op=ALU.mult),
         reads=[kk], writes=[sq])
    for hq in range(2):
        pn = K.psum()
        sl = slice(hq * 4, hq * 4 + 4)
        S.op("pe", lambda e, pn=pn, sl=sl: e.matmul(pn[0:64, :], K.onesf[0:64, 0:64],
                                                    sq[:, sl, :].rearrange("p a b -> p (a b)"), start=True, stop=True),
             reads=[K.onesf, sq], writes=[pn])
        S.op("dve", lambda e, pn=pn, sl=sl: e.tensor_scalar(out=sq[:, sl, :].rearrange("p a b -> p (a b)"), in0=pn[0:64, :],
                                                            scalar1=1e-24, scalar2=0.0, op0=ALU.max, op1=ALU.add),
             reads=[pn], writes=[sq])
    S.op("act", lambda e: e.activation(out=sq[:, :, :], in_=sq[:, :, :], func=AF.Ln), reads=[sq], writes=[sq])
    S.op("act", lambda e: e.activation(out=sq[:, :, :], in_=sq[:, :, :], func=AF.Exp, scale=-0.5), reads=[sq], writes=[sq])
    S.op("dve", lambda e: e.tensor_tensor(out=kk[:, :, :], in0=kk[:, :, :], in1=sq[:, :, :], op=ALU.mult),
         reads=[kk, sq], writes=[kk])
    if RL <= 5:
        return None
    km = P["f8"].get()
    S.op("pool", lambda e: e.tensor_tensor(out=km[:, :, :], in0=ic[:, :, :], in1=h8(C["ka"]), op=ALU.mult),
         reads=[ic, C["ka"]], writes=[km])
    S.op("pool", lambda e: e.tensor_tensor(out=km[:, :, :], in0=km[:, :, :], in1=h8(C["omka"]), op=ALU.add),
         reads=[km, C["omka"]], writes=[km])
    S.op("pool", lambda e: e.tensor_tensor(out=km[:, :, :], in0=km[:, :, :], in1=kT, op=ALU.mult),
         reads=[km, Pt], writes=[km])
    bb = P["f8"].get()
    S.op("dve", lambda e: e.tensor_tensor(out=bb[:, :, :], in0=kk[:, :, :], in1=ic[:, :, :], op=ALU.mult),
         reads=[kk, ic], writes=[bb])
    RtT, AtT, BhT, KhT = [P["b8"].get() for _ in range(4)]
    BbT, KbT = P["f8"].get(), P["f8"].get()
    S.op("dve", lambda e: e.tensor_tensor(out=RtT[:, :, :], in0=rT, in1=eP[:, :, :], op=ALU.mult),
         reads=[Pt, eP], writes=[RtT])
    S.op("dve", lambda e: e.scalar_tensor_tensor(out=AtT[:, :, :], in0=kk[:, :, :], scalar=-1.0, in1=ePx[:, :, :],
                                                 op0=ALU.mult, op1=ALU.mult), reads=[kk, ePx], writes=[AtT])
    S.op("pool", lambda e: e.tensor_tensor(out=bb[:, :, :], in0=bb[:, :, :], in1=eN[:, :, :], op=ALU.mult),
         reads=[bb, eN], writes=[bb])
    S.op("pool", lambda e: e.tensor_tensor(out=sq[:, :, :], in0=km[:, :, :], in1=eN[:, :, :], op=ALU.mult),
         reads=[km, eN, sq], writes=[sq])
    S.op("act", lambda e: e.copy(out=BhT[:, :, :], in_=bb[:, :, :]), reads=[bb], writes=[BhT])
    S.op("act", lambda e: e.copy(out=KhT[:, :, :], in_=sq[:, :, :]), reads=[sq], writes=[KhT])
    plb = pl[:, :].unsqueeze(2).broadcast_to([64, 8, 128])
    S.op("dve", lambda e: e.tensor_tensor(out=BbT[:, :, :], in0=bb[:, :, :], in1=plb, op=ALU.mult),
         reads=[bb, pl], writes=[BbT])
    S.op("dve", lambda e: e.tensor_tensor(out=KbT[:, :, :], in0=sq[:, :, :], in1=plb, op=ALU.mult),
         reads=[sq, pl], writes=[KbT])
    if RL <= 6:
        return None
    pv = K.psum()
    for h in range(8):
        S.op("pe", lambda e, h=h: e.matmul(pv[:, h * 64:(h + 1) * 64], Pt[:, 16 + h, :], K.identf[0:64, 0:64],
                                           start=True, stop=True), reads=[Pt, K.identf], writes=[pv])
    if RL <= 6.2:
        return None
    Vf = P["vf"].get()
    Vb = P["tok"].get()
    S.op("act", lambda e: e.copy(out=Vf[:, :], in_=pv[:, :]), reads=[pv], writes=[Vf])
    S.op("dve", lambda e: e.tensor_copy(out=Vb[:, :], in_=Vf[:, :]), reads=[Vf], writes=[Vb])
    if RL <= 6.5:
        return None
    outs = []
    for src in (BbT, KbT):
        pb = K.psum()
        for h in range(8):
            S.op("pe", lambda e, h=h, src=src, pb=pb: e.matmul(pb[:, h * 64:(h + 1) * 64], src[:, h, :],
                                                               K.identf[0:64, 0:64], start=True, stop=True),
                 reads=[src, K.identf], writes=[pb])
        o = P["tok"].get()
        S.op("act", lambda e, o=o, pb=pb: e.copy(out=o[:, :], in_=pb[:, :]), reads=[pb], writes=[o])
        outs.append(o)
    T = {"RtT": RtT, "AtT": AtT, "BhT": BhT, "KhT": KhT, "Vb": Vb, "Vf": Vf, "Bb": outs[0], "Kb": outs[1], "pl": pl}
    if RL <= 7:
        return None
    S.op("dve", lambda e: e.tensor_tensor(out=km[:, :, :], in0=km[:, :, :], in1=rT, op=ALU.mult),
         reads=[km, Pt], writes=[km])
    S.op("dve", lambda e: e.tensor_tensor(out=km[:, :, :], in0=km[:, :, :], in1=h8(C["rk"]), op=ALU.mult),
         reads=[km, C["rk"]], writes=[km])
    pr = K.psum()
    for h in range(8):
        S.op("pe", lambda e, h=h: e.matmul(pr[:, h:h + 1], km[:, h, :], K.onesf[0:64, 0:1], start=True, stop=True),
             reads=[km, K.onesf], writes=[pr])
    rkc = P["rkc"].get()
    S.op("dve", lambda e: e.tensor_copy(out=rkc[:, :], in_=pr[:, 0:8]), reads=[pr], writes=[rkc])
    T["rk"] = rkc
    if RL <= 8:
        return None
    if want_g:
        sgd = P["b8"].get()
        S.op("act", lambda e: e.activation(out=sgd[:, 0:3, :], in_=Pt[:, 28:31, :], func=AF.Sigmoid), reads=[Pt],
             writes=[sgd])
        pg = K.psum()
        for q in range(3):
            rows = 64 if q < 2 else 32
            S.op("pe", lambda e, q=q, rows=rows: e.matmul(pg[:, :], sgd[0:rows, q, :], C["gup"][0:rows, q, :],
                                                          start=(q == 0), stop=(q == 2)), reads=[sgd, C["gup"]], writes=[pg])
        gt = P["w"].get()
        S.op("act", lambda e: e.copy(out=gt[:, :], in_=pg[:, :]), reads=[pg], writes=[gt])
        T["g"] = gt
    return T


def rwkv_intra(K, P, T, n):
    S = K.S
    AtT, BhT, KhT, RtT = T["AtT"], T["BhT"], T["KhT"], T["RtT"]
    m_strict_sr = K.ugt if n == 0 else K.uexc
    m_strict_rs = K.uexc if n == 0 else K.ugt
    m_incl_st = K.uinc if n == 0 else K.mskb
    res = {"TT": [], "AakT": [], "ArbT": [], "ArkT": []}

    def prod(lhs, rhs, hq, mask, pool="mat"):
        pp = K.psum()
        for hh in range(4):
            h = hq * 4 + hh
            S.op("pe", lambda e, hh=hh, h=h: e.matmul(pp[:, hh * 128:(hh + 1) * 128], lhs[:, h, :], rhs[:, h, :],
                                                      start=True, stop=True), reads=[lhs, rhs], writes=[pp])
        o = P[pool].get()
        S.op("dve", lambda e: e.tensor_tensor(out=o[:, :, :], in0=pp[:, :].rearrange("p (a b) -> p a b", a=4),
                                              in1=mask[:, :].unsqueeze(1).broadcast_to([128, 4, 128]), op=ALU.mult),
             reads=[pp, mask], writes=[o])
        return o

    def mm4(lhs, rhs, addto=None, eng="act", pool="mat"):
        pp = K.psum()
        for hh in range(4):
            S.op("pe", lambda e, hh=hh: e.matmul(pp[:, hh * 128:(hh + 1) * 128], lhs[:, hh, :], rhs[:, hh, :],
                                                 start=True, stop=True), reads=[lhs, rhs], writes=[pp])
        o = P[pool].get()
        if addto is None:
            S.op(eng, (lambda e: e.copy(out=o[:, :, :].rearrange("p a b -> p (a b)"), in_=pp[:, :])) if eng == "act" else
                 (lambda e: e.tensor_copy(out=o[:, :, :].rearrange("p a b -> p (a b)"), in_=pp[:, :])),
                 reads=[pp], writes=[o])
        else:
            S.op("dve", lambda e: e.tensor_tensor(out=o[:, :, :].rearrange("p a b -> p (a b)"), in0=pp[:, :],
                                                  in1=addto[:, :, :].rearrange("p a b -> p (a b)"), op=ALU.add),
                 reads=[pp, addto], writes=[o])
        return o

    Ms = [prod(AtT, BhT, hq, m_strict_sr) for hq in range(2)]
    MTs = [prod(BhT, AtT, hq, m_strict_rs) for hq in range(2)]
    TTs = []
    for hq in range(2):
        TT = P["mat"].get()
        S.op("pool", lambda e, TT=TT, MT=MTs[hq]: e.tensor_tensor(
            out=TT[:, :, :], in0=MT[:, :, :], in1=K.ident[:, :].unsqueeze(1).broadcast_to([128, 4, 128]), op=ALU.add),
            reads=[MTs[hq], K.ident], writes=[TT])
        TTs.append(TT)
    for hq in range(2):
        res["AakT"].append(prod(KhT, AtT, hq, m_strict_rs, pool="res"))
        res["ArbT"].append(prod(BhT, RtT, hq, m_incl_st, pool="res"))
        res["ArkT"].append(prod(KhT, RtT, hq, m_incl_st, pool="res"))
    for k in range(1, 7):
        M2s, MT2s = [], []
        for hq in range(2):
            M2s.append(mm4(MTs[hq], Ms[hq], eng="act"))
        for hq in range(2):
            MT2s.append(mm4(Ms[hq], MTs[hq], eng="act") if k < 6 else None)
        for hq in range(2):
            TTs[hq] = mm4(M2s[hq], TTs[hq], addto=TTs[hq], pool=("res" if k == 6 else "mat"))
        Ms, MTs = M2s, MT2s
    res["TT"] = TTs
    return res


def rwkv_seq(K, P, T, I_, St, Sb):
    S = K.S
    AtT, RtT, Vb, Bb, Kb, pl = T["AtT"], T["RtT"], T["Vb"], T["Bb"], T["Kb"], T["pl"]
    pw = K.psum()
    for h in range(8):
        hq, hh = divmod(h, 4)
        S.op("pe", lambda e, h=h: e.matmul(pw[:, h * 64:(h + 1) * 64], AtT[:, h, :], Sb[:, h, :], start=True, stop=False),
             reads=[AtT, Sb], writes=[pw])
        S.op("pe", lambda e, h=h, hq=hq, hh=hh: e.matmul(pw[:, h * 64:(h + 1) * 64], I_["AakT"][hq][:, hh, :],
                                                         Vb[:, h * 64:(h + 1) * 64], start=False, stop=True),
             reads=[I_["AakT"][hq], Vb], writes=[pw])
    Wb = P["tok"].get()
    S.op("act", lambda e: e.copy(out=Wb[:, :], in_=pw[:, :]), reads=[pw], writes=[Wb])
    pu = K.psum()
    for h in range(8):
        hq, hh = divmod(h, 4)
        S.op("pe", lambda e, h=h, hq=hq, hh=hh: e.matmul(pu[:, h * 64:(h + 1) * 64], I_["TT"][hq][:, hh, :],
                                                         Wb[:, h * 64:(h + 1) * 64], start=True, stop=True),
             reads=[I_["TT"][hq], Wb], writes=[pu])
    Ub = P["tok"].get()
    S.op("dve", lambda e: e.tensor_copy(out=Ub[:, :], in_=pu[:, :]), reads=[pu], writes=[Ub])
    py = K.psum()
    for h in range(8):
        hq, hh = divmod(h, 4)
        S.op("pe", lambda e, h=h: e.matmul(py[:, h * 64:(h + 1) * 64], RtT[:, h, :], Sb[:, h, :], start=True, stop=False),
             reads=[RtT, Sb], writes=[py])
        S.op("pe", lambda e, h=h, hq=hq, hh=hh: e.matmul(py[:, h * 64:(h + 1) * 64], I_["ArbT"][hq][:, hh, :],
                                                         Ub[:, h * 64:(h + 1) * 64], start=False, stop=False),
             reads=[I_["ArbT"][hq], Ub], writes=[py])
        S.op("pe", lambda e, h=h, hq=hq, hh=hh: e.matmul(py[:, h * 64:(h + 1) * 64], I_["ArkT"][hq][:, hh, :],
                                                         Vb[:, h * 64:(h + 1) * 64], start=False, stop=True),
             reads=[I_["ArkT"][hq], Vb], writes=[py])
    ps = K.psum()
    for h in range(8):
        S.op("pe", lambda e, h=h: e.matmul(ps[0:64, h * 64:(h + 1) * 64], Bb[:, h * 64:(h + 1) * 64],
                                           Ub[:, h * 64:(h + 1) * 64], start=True, stop=False),
             reads=[Bb, Ub], writes=[ps])
        S.op("pe", lambda e, h=h: e.matmul(ps[0:64, h * 64:(h + 1) * 64], Kb[:, h * 64:(h + 1) * 64],
                                           Vb[:, h * 64:(h + 1) * 64], start=False, stop=True),
             reads=[Kb, Vb], writes=[ps])
    S.op("dve", lambda e: e.tensor_tensor(out=St[:, :, :], in0=St[:, :, :],
                                          in1=pl[:, :].unsqueeze(2).broadcast_to([64, 8, 64]), op=ALU.mult),
         reads=[St, pl], writes=[St])
    S.op("dve", lambda e: e.tensor_tensor(out=St[:, :, :], in0=St[:, :, :],
                                          in1=ps[0:64, :].rearrange("p (a b) -> p a b", a=8), op=ALU.add),
         reads=[St, ps], writes=[St])
    S.op("act", lambda e: e.copy(out=Sb[:, :, :], in_=St[:, :, :]), reads=[St], writes=[Sb])
    return py


def rwkv_dir_pass(K, SC, C, nchunk, cps, n):
    S = K.S
    P = rwkv_pools(K)
    St = K.alloc([64, 8, 64], F32)
    Sb = K.alloc([64, 8, 64], BF16)
    S.op("dve", lambda e: e.memset(St[:, :, :], 0.0), writes=[St])
    S.op("dve", lambda e: e.memset(Sb[:, :, :], 0.0), writes=[Sb])
    ybp = Pool(K, 2, [128, 520], F32)
    y1p = Pool(K, 2, [128, 520], F32)
    yw = Pool(K, 2, [128, 8, 64], F32)
    yc = Pool(K, 2, [128, 8, 64], F32)
    smp = Pool(K, 4, [128, 32], F32)
    outp = Pool(K, 2, [128, 512], BF16)

    def body(c):
        seg, cis = divmod(c, cps)
        t0 = c * 128
        first = (cis == 0) if n == 0 else (cis == cps - 1)
        edge = (c == 0) if n == 0 else (c == nchunk - 1)
        if first and not edge:
            fl = K.flags[0:64, seg - 1:seg] if n == 0 else K.flags[0:64, seg:seg + 1]
            S.op("dve", lambda e: e.tensor_scalar(out=St[:, :, :], in0=St[:, :, :], scalar1=fl, scalar2=0.0,
                                                  op0=ALU.mult, op1=ALU.add), reads=[St, K.flags], writes=[St])
            S.op("act", lambda e: e.copy(out=Sb[:, :, :], in_=St[:, :, :]), reads=[St], writes=[Sb])
        T = Tn.pop(c)
        nx = c + 1 if n == 0 else c - 1
        if 0 <= nx < nchunk:
            Tn[nx] = rwkv_prep(K, SC, C, P, nx, nchunk, cps, n, want_g=(n == 0))
        if T is None or RL <= 9:
            return
        I_ = rwkv_intra(K, P, T, n)
        if RL <= 10:
            return
        py = rwkv_seq(K, P, T, I_, St, Sb)
        if RL <= 11:
            return
        if n == 1:
            yb = ybp.get()
            S.op("act", lambda e: e.copy(out=yb[:, 0:512], in_=py[:, :]), reads=[py], writes=[yb])
            S.op("dve", lambda e: e.tensor_copy(out=yb[:, 512:520], in_=T["rk"][:, :]), reads=[T["rk"]], writes=[yb])
            S.dma("pool", SC["YB"][t0:t0 + 128, :], yb[:, :], reads=[yb])
            return
        y1 = y1p.get()
        S.dma("sp", y1[:, :], SC["YB"][t0:t0 + 128, :], writes=[y1])
        y = yw.get()
        yv = y[:, :, :].rearrange("p a b -> p (a b)")
        S.op("dve", lambda e: e.tensor_tensor(out=yv, in0=py[:, :], in1=y1[:, 0:512], op=ALU.add),
             reads=[py, y1], writes=[y])
        sm = smp.get()
        S.op("dve", lambda e: e.tensor_reduce(out=sm[:, 0:8], in_=y[:, :, :], axis=AX.X, op=ALU.add),
             reads=[y], writes=[sm])
        S.op("dve", lambda e: e.tensor_scalar(out=sm[:, 0:8], in0=sm[:, 0:8], scalar1=1.0 / 64, scalar2=0.0,
                                              op0=ALU.mult, op1=ALU.add), reads=[sm], writes=[sm])
        ycn = yc.get()
        S.op("dve", lambda e: e.tensor_tensor(out=ycn[:, :, :], in0=y[:, :, :],
                                              in1=sm[:, 0:8].unsqueeze(2).broadcast_to([128, 8, 64]), op=ALU.subtract),
             reads=[y, sm], writes=[ycn])
        S.op("pool", lambda e: e.tensor_tensor(out=y[:, :, :], in0=ycn[:, :, :], in1=ycn[:, :, :], op=ALU.mult),
             reads=[ycn], writes=[y])
        S.op("dve", lambda e: e.tensor_reduce(out=sm[:, 8:16], in_=y[:, :, :], axis=AX.X, op=ALU.add),
             reads=[y], writes=[sm])
        S.op("act", lambda e: e.activation(out=sm[:, 8:16], in_=sm[:, 8:16], func=AF.Sqrt, bias=C["gneps"][:, 0:1],
                                           scale=1.0 / 64), reads=[sm, C["gneps"]], writes=[sm])
        S.op("dve", lambda e: e.reciprocal(out=sm[:, 8:16], in_=sm[:, 8:16]), reads=[sm], writes=[sm])
        S.op("dve", lambda e: e.tensor_tensor(out=ycn[:, :, :], in0=ycn[:, :, :],
                                              in1=sm[:, 8:16].unsqueeze(2).broadcast_to([128, 8, 64]), op=ALU.mult),
             reads=[ycn, sm], writes=[ycn])
        ycv = ycn[:, :, :].rearrange("p a b -> p (a b)")
        S.op("pool", lambda e: e.tensor_tensor(out=ycv, in0=ycv, in1=C["lng"][:, :], op=ALU.mult),
             reads=[ycn, C["lng"]], writes=[ycn])
        S.op("pool", lambda e: e.tensor_tensor(out=ycv, in0=ycv, in1=C["lnb"][:, :], op=ALU.add),
             reads=[ycn, C["lnb"]], writes=[ycn])
        S.op("dve", lambda e: e.tensor_tensor(out=sm[:, 16:24], in0=T["rk"][:, :], in1=y1[:, 512:520], op=ALU.add),
             reads=[T["rk"], y1], writes=[sm])
        S.op("dve", lambda e: e.tensor_tensor(out=y[:, :, :], in0=T["Vf"][:, :].rearrange("p (a b) -> p a b", a=8),
                                              in1=sm[:, 16:24].unsqueeze(2).broadcast_to([128, 8, 64]), op=ALU.mult),
             reads=[T["Vf"], sm, y], writes=[y])
        S.op("pool", lambda e: e.tensor_tensor(out=ycv, in0=ycv, in1=yv, op=ALU.add), reads=[ycn, y], writes=[ycn])
        o = outp.get()
        S.op("dve", lambda e: e.tensor_tensor(out=o[:, :], in0=ycv, in1=T["g"][:, :], op=ALU.mult),
             reads=[ycn, T["g"]], writes=[o])
        S.dma("pool", SC["YR"][t0:t0 + 128, :], o[:, :], reads=[o])

    order = list(range(nchunk)) if n == 0 else list(range(nchunk - 1, -1, -1))
    Tn = {order[0]: rwkv_prep(K, SC, C, P, order[0], nchunk, cps, n, want_g=(n == 0))}
    for c in order:
        body(c)


def cast_weights(K, src, dst, rows, cols):
    for r in range(0, rows, 128):
        K.S.dma("pool", dst[r:r + 128, :], src[r:r + 128, :])


def zero_fill(K, dst, rows, cols, dt):
    z = K.alloc([128, cols], dt)
    K.S.op("pool", lambda e: e.memset(z[:, :], 0.0), writes=[z])
    for r in range(0, rows, 128):
        K.S.dma("pool", dst[r:r + 128, :], z[:, :], reads=[z])


def build(ntok, seglen=4096, depth=DEPTH, mixer=True):
    nc = bass.Bass("TRN2", target_bir_lowering=False)
    es = ExitStack()
    A = {}
    nseg = ntok // seglen
    plan, pats = attn_plan(nseg, seglen)
    ncfg = pats[False].shape[0]

    def inp(name, shape, dt=F32):
        A[name] = nc.dram_tensor(name, list(shape), dt, kind="ExternalInput").ap()
        return A[name]

    import os
    dbg = os.environ.get("MK_DBG", "").split(",")

    def scr(name, shape, dt=F32):
        if name in dbg:
            return nc.dram_tensor(name, list(shape), dt, kind="ExternalOutput").ap()
        return nc.dram_tensor(name, list(shape), dt).ap()

    x = inp("x", [ntok, D])
    inp("norm_g", [DEPTH, 6, D])
    inp("ff_w_in", [DEPTH, 2, D, 2 * FF])
    inp("ff_w_out", [DEPTH, 2, FF, D])
    inp("w_in", [DEPTH, D, IN_W])
    inp("w_branch", [DEPTH, 2048, D])
    inp("w_out", [DEPTH, D, D])
    inp("atab", [DEPTH, ncfg, 128, 8, 128])
    inp("ssm_conv_w", [DEPTH, 5, 1536])
    inp("ssm_conv_b", [DEPTH, 1536])
    inp("ssm_dt_bias", [DEPTH, 2, 16])
    inp("ssm_a_log", [DEPTH, 2, 16])
    inp("ssm_d", [DEPTH, 2, 16])
    inp("ssm_norm_g", [DEPTH, 1024])
    inp("flags", [128, 4])
    inp("cf32", [128, 7, 128])
    inp("rwkv_mu", [DEPTH, 2, 1952])
    inp("rwkv_w0", [DEPTH, 2, 512])
    inp("rwkv_w_up", [DEPTH, 2, 64, 512])
    inp("rwkv_a0", [DEPTH, 2, 512])
    inp("rwkv_a_up", [DEPTH, 2, 64, 512])
    inp("rwkv_g_up", [DEPTH, 160, 512])
    inp("rwkv_k_k", [DEPTH, 512])
    inp("rwkv_k_a", [DEPTH, 512])
    inp("rwkv_r_k", [DEPTH, 8, 64])
    inp("rwkv_ln_g", [DEPTH, 512])
    inp("rwkv_ln_b", [DEPTH, 512])
    inp("ident", [128, 128], BF16)
    y = nc.dram_tensor("y", [ntok, D], F32, kind="ExternalOutput").ap()
    xs = scr("xs", [ntok, D])
    wi_bf = [[scr(f"wi{l}{f}", [D, 2 * FF], BF16) for f in range(2)] for l in range(DEPTH)]
    wo_bf = [[scr(f"wo{l}{f}", [FF, D], BF16) for f in range(2)] for l in range(DEPTH)]
    win_bf = [scr(f"win{l}", [D, IN_W], BF16) for l in range(DEPTH)]
    wbr_bf = [scr(f"wbr{l}", [2048, D], BF16) for l in range(DEPTH)]
    wout_bf = [scr(f"wout{l}", [D, D], BF16) for l in range(DEPTH)]
    SC = {"QT": scr("QT", [512, ntok], BF16), "KT": scr("KT", [512, ntok], BF16), "V": scr("V", [ntok, 512], BF16),
          "Z": scr("Z", [ntok, 1024]), "DT": scr("DT", [ntok, 32]), "G": scr("G", [ntok, 3072]),
          "XBCT": scr("XBCT", [1536, ntok]), "RWT": scr("RWT", [1952, ntok]),
          "HB": scr("HB", [ntok // 128, 128, 1024], BF16), "YB": scr("YB", [ntok, 520]), "RWS": scr("RWS", [1952, ntok]),
          "YA": scr("YA", [ntok, 512], BF16), "YS": scr("YS", [ntok, 1024], BF16), "YR": scr("YR", [ntok, 512], BF16)}
    if "DY" in dbg:
        SC["DY"] = scr("DY", [ntok, 1024]); SC["DV"] = scr("DV", [ntok, 256]); SC["DM"] = scr("DM", [ntok // 128, 128, 16, 128], BF16)
    K = Ctx(nc, es)
    S = K.S
    K.ident = K.alloc([128, 128], BF16, keep=True)
    S.dma("sp", K.ident[:, :], A["ident"][:, :], writes=[K.ident])
    cf = K.alloc([128, 7, 128], F32, keep=True)
    S.dma("sp", cf[:, :, :], A["cf32"][:, :, :], writes=[cf])
    K.identf, K.uinc, K.uexc, K.onesf, K.mskf, K.mskb, K.ugt = [Tl(cf.t[:, i, :], cf.res) for i in range(7)]
    K.flags = K.alloc([128, 4], F32, keep=True)
    S.dma("sp", K.flags[:, :], A["flags"][:, :], writes=[K.flags])
    K.onec = K.alloc([128, 4], F32, keep=True)
    S.op("dve", lambda e: e.memset(K.onec[:, :], 1.0), writes=[K.onec])
    K.epsc = K.alloc([128, 4], F32, keep=True)
    S.op("dve", lambda e: e.memset(K.epsc[:, 0:1], EPS), writes=[K.epsc])
    S.op("dve", lambda e: e.memset(K.epsc[:, 1:2], 4.0 * EPS), writes=[K.epsc])
    for l in range(depth):
        for f in range(2):
            cast_weights(K, A["ff_w_in"][l, f], wi_bf[l][f], D, 2 * FF)
            cast_weights(K, A["ff_w_out"][l, f], wo_bf[l][f], FF, D)
        if mixer:
            cast_weights(K, A["w_in"][l], win_bf[l], D, IN_W)
            cast_weights(K, A["w_branch"][l], wbr_bf[l], 2048, D)
            cast_weights(K, A["w_out"][l], wout_bf[l], D, D)
    if mixer:
        zero_fill(K, SC["YS"], ntok, 1024, BF16)
        zero_fill(K, SC["YR"], ntok, 512, BF16)
    cur = x
    for l in range(depth):
        g = A["norm_g"][l]
        ffn_pass(K, cur, xs, g[0:1, :], g[1:2, :], wi_bf[l][0], wo_bf[l][0], ntok)
        cur = xs
        if mixer:
            import os
            st = os.environ.get("MK_STAGES", "iasrm")
            if "i" in st:
                inproj_pass(K, xs, g[2:3, :], win_bf[l], SC, ntok)
            if "a" in st:
                attn_pass(K, SC, A["atab"][l], plan)
            if "s" in st:
                K.new_pass()
                Cs = ssd_setup(K, A, l)
                ssd_bwd_pass(K, SC, Cs, ntok // 128, seglen // 128)
                K.new_pass()
                Cs = ssd_setup(K, A, l)
                ssd_fwd_pass(K, SC, Cs, ntok // 128, seglen // 128)
            if "r" in st:
                shift_pass(K, SC, A, l, ntok, seglen)
                for n_ in (1, 0):
                    K.new_pass()
                    Cr = rwkv_setup(K, A, l)
                    rwkv_dir_pass(K, SC, Cr, ntok // 128, seglen // 128, n_)
            if "m" in st:
                merge_pass(K, SC, xs, g[3:4, :], wbr_bf[l], wout_bf[l], ntok)
        last = (l == depth - 1)
        ffn_pass(K, cur, y if last else xs, g[4:5, :], g[5:6, :], wi_bf[l][1], wo_bf[l][1], ntok)
    S.barrier()
    S.emit()
    es.close()
    return nc


def consts():
    i = np.arange(128)
    s_, l_ = i[:, None], i[None, :]
    cf = np.stack([np.eye(128), s_ <= l_, s_ < l_, np.ones((128, 128)), s_ <= l_, s_ >= l_, s_ > l_]).astype(np.float32)
    return {"ident": np.eye(128, dtype=np.float32).astype(ml_dtypes.bfloat16),
            "cf32": np.ascontiguousarray(cf.transpose(1, 0, 2))}


def flags_for(link):
    f = np.zeros((128, 4), np.float32)
    f[:, 0] = 1.0 if link else 0.0
    return f


NTOK = 12288
_NC_CACHE = {}


def _assign():
    segs = []
    for c in range(4):
        segs.append([("s", c, 0), ("s", c, 1), ("p", c, 0)])
    for c in range(4, 8):
        b = 4 + (c - 4) * 3
        segs.append([("p", b, 0), ("p", b + 1, 0), ("p", b + 2, 0)])
    return segs


def kernel(**inputs):
    xp = np.asarray(inputs["x_prompt"], dtype=np.float32)
    xs = np.asarray(inputs["x_sample"], dtype=np.float32)
    segs = _assign()
    if "nc" not in _NC_CACHE:
        _NC_CACHE["nc"] = build(NTOK, seglen=4096)
    nc = _NC_CACHE["nc"]
    shared = {k: np.ascontiguousarray(np.asarray(inputs[k], dtype=np.float32))
              for k in ("norm_g", "ff_w_in", "ff_w_out", "w_in", "w_branch", "w_out", "ssm_conv_w", "ssm_conv_b",
                        "ssm_dt_bias", "ssm_a_log", "ssm_d", "ssm_norm_g", "rwkv_mu", "rwkv_w0", "rwkv_w_up",
                        "rwkv_a0", "rwkv_a_up", "rwkv_g_up", "rwkv_k_k", "rwkv_k_a", "rwkv_r_k", "rwkv_ln_g",
                        "rwkv_ln_b")}
    shared.update(consts())
    plan, pats = attn_plan(3, 4096)
    rpb = np.asarray(inputs["attn_rpb"], dtype=np.float32)
    tabs = {link: attn_tables(rpb, pats[link]) for link in (False, True)}
    in_maps = []
    for c in range(8):
        parts = []
        for kind, b, h in segs[c]:
            parts.append(xs[b, h * 4096:(h + 1) * 4096] if kind == "s" else xp[b])
        m = {"x": np.ascontiguousarray(np.concatenate(parts, axis=0)), "atab": tabs[c < 4],
             "flags": flags_for(c < 4)}
        m.update(shared)
        in_maps.append(m)
    res = run_bass_kernel_spmd(nc, in_maps, core_ids=list(range(8)))
    yp = np.empty_like(xp)
    ys = np.empty_like(xs)
    for c in range(8):
        y = res.results[c]["y"]
        for i, (kind, b, h) in enumerate(segs[c]):
            blk = y[i * 4096:(i + 1) * 4096]
            if kind == "s":
                ys[b, h * 4096:(h + 1) * 4096] = blk
            else:
                yp[b] = blk
    return yp, ys
```

```python
import numpy as np
import ml_dtypes
from contextlib import ExitStack
import concourse.bass as bass
import concourse.mybir as mybir
from concourse.bass_utils import run_bass_kernel_spmd

F32 = mybir.dt.float32
BF16 = mybir.dt.bfloat16
AF = mybir.ActivationFunctionType
ALU = mybir.AluOpType
AX = mybir.AxisListType

D = 1024
FF = 2816
DEPTH = 2
EPS = 1e-6


class Res:
    __slots__ = ("w", "r")

    def __init__(self):
        self.w = None
        self.r = []


class Tl:
    def __init__(self, t, res=None):
        self.t = t
        self.res = res or Res()

    def __getitem__(self, idx):
        return self.t[idx]


class Sched:
    EPOCH = 60000
    NDMA = {"sp": 24, "pool": 12, "act": 8}

    def __init__(self, nc, es):
        self.nc = nc
        self.es = es
        self.names = ["sp", "act", "dve", "pool", "pe"]
        self.ops = {k: [] for k in self.names}
        self.n = {k: 0 for k in self.names}
        self.sems = {k: [] for k in self.names}
        self.seen = {k: {} for k in self.names}
        self.dsem = {}
        self.dval = {}
        self.drr = {}
        for q, n in self.NDMA.items():
            self.dsem[q] = [es.enter_context(nc.semaphore(f"d{q}{i}")) for i in range(n)]
            self.dval[q] = [0] * n
            self.drr[q] = 0
        self.last = {k: None for k in self.names}

    def _next_ev(self, e):
        n = self.n[e]
        ep = n // self.EPOCH
        while len(self.sems[e]) <= ep:
            self.sems[e].append(self.es.enter_context(self.nc.semaphore(f"s{e}{len(self.sems[e])}")))
        self.n[e] += 1
        ev = (self.sems[e][ep], n % self.EPOCH + 1, e)
        self.last[e] = ev
        return ev

    def _deps(self, e, reads, writes):
        deps = []
        for r in reads:
            r = r.res if isinstance(r, Tl) else r
            if r.w is not None:
                deps.append((r.w, 0))
        for w in writes:
            w = w.res if isinstance(w, Tl) else w
            if w.w is not None:
                deps.append((w.w, 0))
            for ev in w.r:
                deps.append((ev, 1))
        waits = []
        seen = self.seen[e]
        for (sem, val, src), war in deps:
            if src == e:
                if e == "pe" or war:
                    continue
            k = id(sem)
            if seen.get(k, 0) >= val:
                continue
            seen[k] = val
            waits.append((sem, val))
        return waits

    def _commit(self, ev, reads, writes):
        for r in reads:
            r = r.res if isinstance(r, Tl) else r
            r.r.append(ev)
        for w in writes:
            w = w.res if isinstance(w, Tl) else w
            w.w = ev
            w.r = []

    def op(self, e, fn, reads=(), writes=()):
        waits = self._deps(e, reads, writes)
        ev = self._next_ev(e)
        self.ops[e].append((waits, fn, ev, 1))
        self._commit(ev, reads, writes)

    def dma(self, q, out, in_, reads=(), writes=(), slow=False):
        waits = self._deps(q, reads, writes)
        i = self.drr[q]
        self.drr[q] = (i + 1) % len(self.dsem[q])
        sem = self.dsem[q][i]
        prev = self.dval[q][i]
        if prev > 0 and self.seen[q].get(id(sem), 0) < prev:
            waits.append((sem, prev))
            self.seen[q][id(sem)] = prev
        self.dval[q][i] = prev + 16
        ev = (sem, prev + 16, "dma")
        if slow:
            fn = (lambda e, o=out, i_=in_: e.dma_start(out=o, in_=i_, allow_slow_non_contiguous=True))
        else:
            fn = (lambda e, o=out, i_=in_: e.dma_start(out=o, in_=i_))
        self.ops[q].append((waits, fn, ev, 16))
        self._commit(ev, reads, writes)

    def barrier(self):
        evs = [self.last[k] for k in self.names if self.last[k] is not None]
        for q in self.dsem:
            for sem, v in zip(self.dsem[q], self.dval[q]):
                if v > 0:
                    evs.append((sem, v, "dma"))
        for e in self.names:
            waits = []
            for sem, val, src in evs:
                if src == e:
                    continue
                if self.seen[e].get(id(sem), 0) >= val:
                    continue
                self.seen[e][id(sem)] = val
                waits.append((sem, val))
            if waits:
                self.ops[e].append((waits, None, None, 0))

    def emit(self):
        block = self.es.enter_context(self.nc.Block())
        decos = {"sp": block.sync, "act": block.scalar, "dve": block.vector, "pool": block.gpsimd,
                 "pe": block.tensor}
        for name in self.names:
            ops = self.ops[name]

            def body(e, ops=ops):
                for waits, fn, ev, inc in ops:
                    for (s, v) in waits:
                        e.wait_ge(s, v)
                    if fn is not None:
                        fn(e).then_inc(ev[0], inc)

            decos[name](body)


class Ctx:
    BASE = 16640
    ARENA = 224 * 1024

    def __init__(self, nc, es):
        self.nc = nc
        self.es = es
        self.S = Sched(nc, es)
        self.off = self.BASE
        self.uid = 0
        self.keep = self.BASE
        self.ps = [Tl(es.enter_context(nc.psum_tensor(f"ps{i}", [128, 512], F32))) for i in range(8)]
        self.psi = 0
        self.psi6 = 0

    def alloc(self, shape, dtype, keep=False):
        nbytes = int(np.prod(shape[1:])) * (2 if dtype == BF16 else 4)
        nbytes = (nbytes + 31) // 32 * 32
        assert self.off + nbytes <= self.ARENA, f"SBUF arena overflow {self.off}+{nbytes}"
        self.uid += 1
        t = self.nc.alloc_sbuf_tensor_at(f"t{self.uid}", list(shape), dtype, offset=self.off)
        self.off += nbytes
        if keep:
            self.keep = self.off
        return Tl(t)

    def new_pass(self):
        self.S.barrier()
        self.off = self.keep

    def psum(self):
        p = self.ps[self.psi]
        self.psi = (self.psi + 1) % 8
        return p


class Pool:
    def __init__(self, K, n, shape, dtype):
        self.t = [K.alloc(shape, dtype) for _ in range(n)]
        self.i = 0

    def get(self):
        t = self.t[self.i]
        self.i = (self.i + 1) % len(self.t)
        return t


def norm_transpose(K, xt, xres, gbc, hnT, col0, P):
    S = K.S
    sm = P["small"].get()
    junk = P["junk"].get()
    S.op("act", lambda e: e.activation(out=junk[:, :], in_=xt, func=AF.Square, accum_out=sm[:, 0:1]),
         reads=[xres], writes=[junk, sm])
    S.op("act", lambda e: e.activation(out=sm[:, 1:2], in_=sm[:, 0:1], func=AF.Sqrt, bias=K.epsc[:, 0:1],
                                       scale=1.0 / D), reads=[sm, K.epsc], writes=[sm])
    S.op("dve", lambda e: e.reciprocal(out=sm[:, 2:3], in_=sm[:, 1:2]), reads=[sm], writes=[sm])
    hn = P["hn"].get()
    S.op("dve", lambda e: e.scalar_tensor_tensor(out=hn[:, :], in0=xt, scalar=sm[:, 2:3], in1=gbc[:, :],
                                                 op0=ALU.mult, op1=ALU.mult),
         reads=[xres, sm, gbc], writes=[hn])
    pt = K.psum()
    ptb = pt.t[:, :].bitcast(BF16)
    for kc in range(8):
        S.op("pe", lambda e, kc=kc: e.transpose(ptb[:, kc * 128:(kc + 1) * 128],
                                                hn[:, kc * 128:(kc + 1) * 128], K.ident[:, :]),
             reads=[hn, K.ident], writes=[pt])
    S.op("act", lambda e: e.copy(out=hnT[:, :, col0:col0 + 128],
                                 in_=ptb.rearrange("p (k c) -> p k c", k=8)), reads=[pt], writes=[hnT])


def ffn_pass(K, xin, xout, gin_ap, gout_ap, win_bf, wout_bf, ntok):
    S = K.S
    K.new_pass()
    gin = K.alloc([128, D], F32)
    gout = K.alloc([128, D], F32)
    S.dma("sp", gin[:, :], gin_ap.broadcast_to([128, D]), writes=[gin])
    S.dma("sp", gout[:, :], gout_ap.broadcast_to([128, D]), writes=[gout])
    wout = K.alloc([128, 22, D], BF16)
    wo_v = wout_bf.rearrange("(kc p) n -> p kc n", p=128)
    for kc in range(0, 22, 2):
        S.dma("sp", wout[:, kc:kc + 2, :], wo_v[:, kc:kc + 2, :], writes=[wout])
    xp = Pool(K, 2, [128, 4, D], F32)
    hnTp = Pool(K, 2, [128, 8, 512], BF16)
    hT = K.alloc([128, 22, 512], BF16)
    wp = Pool(K, 3, [128, 8, 2, 256], BF16)
    sg = Pool(K, 2, [128, 512], F32)
    tt = Pool(K, 2, [128, D], F32)
    P = {"small": Pool(K, 8, [128, 8], F32), "junk": Pool(K, 2, [128, D], BF16), "hn": Pool(K, 2, [128, D], BF16)}
    win_v = win_bf.rearrange("(kc p) n -> p kc n", p=128)
    for mt in range(ntok // 512):
        x = xp.get()
        S.dma("sp", x[:, :, :], xin[mt * 512:(mt + 1) * 512, :].rearrange("(s p) d -> p s d", p=128),
              writes=[x])
        hnT = hnTp.get()
        for s in range(4):
            norm_transpose(K, x[:, s, :], x, gin, hnT, s * 128, P)
        for j in range(11):
            w = wp.get()
            S.dma("sp", w[:, :, 0, :], win_v[:, :, j * 256:(j + 1) * 256], writes=[w])
            S.dma("sp", w[:, :, 1, :], win_v[:, :, FF + j * 256:FF + (j + 1) * 256], writes=[w])
            for c in range(2):
                pg = K.psum()
                pu = K.psum()
                for gi, pp in enumerate((pg, pu)):
                    for kc in range(8):
                        S.op("pe", lambda e, pp=pp, gi=gi, kc=kc, c=c, w=w, hnT=hnT: e.matmul(
                            pp[:, :], w[:, kc, gi, c * 128:(c + 1) * 128], hnT[:, kc, :],
                            start=(kc == 0), stop=(kc == 7)), reads=[w, hnT], writes=[pp])
                sgt = sg.get()
                S.op("act", lambda e, sgt=sgt, pg=pg: e.activation(out=sgt[:, :], in_=pg[:, :], func=AF.Silu),
                     reads=[pg], writes=[sgt])
                S.op("dve", lambda e, sgt=sgt, pu=pu, ch=j * 2 + c: e.tensor_tensor(
                    out=hT[:, ch, :], in0=sgt[:, :], in1=pu[:, :], op=ALU.mult),
                     reads=[sgt, pu], writes=[hT])
        for s in range(4):
            pp = [K.psum(), K.psum()]
            for nf in range(2):
                for kc in range(22):
                    S.op("pe", lambda e, p_=pp[nf], nf=nf, kc=kc, s=s: e.matmul(
                        p_[:, :], hT[:, kc, s * 128:(s + 1) * 128], wout[:, kc, nf * 512:(nf + 1) * 512],
                        start=(kc == 0), stop=(kc == 21)), reads=[hT, wout], writes=[pp[nf]])
            sm = P["small"].get()
            junk = P["junk"].get()
            for nf in range(2):
                S.op("act", lambda e, nf=nf, junk=junk, sm=sm, p_=pp[nf]: e.activation(
                    out=junk[:, 0:512], in_=p_[:, :], func=AF.Square, accum_out=sm[:, nf:nf + 1]),
                     reads=[pp[nf]], writes=[junk, sm])
            S.op("dve", lambda e, sm=sm: e.tensor_tensor(out=sm[:, 2:3], in0=sm[:, 0:1], in1=sm[:, 1:2],
                                                         op=ALU.add), reads=[sm], writes=[sm])
            S.op("act", lambda e, sm=sm: e.activation(out=sm[:, 3:4], in_=sm[:, 2:3], func=AF.Sqrt,
                                                      bias=K.epsc[:, 1:2], scale=4.0 / D),
                 reads=[sm, K.epsc], writes=[sm])
            S.op("dve", lambda e, sm=sm: e.reciprocal(out=sm[:, 4:5], in_=sm[:, 3:4]), reads=[sm], writes=[sm])
            t = tt.get()
            for nf in range(2):
                S.op("dve", lambda e, nf=nf, t=t, sm=sm, p_=pp[nf]: e.scalar_tensor_tensor(
                    out=t[:, nf * 512:(nf + 1) * 512], in0=p_[:, :], scalar=sm[:, 4:5],
                    in1=gout[:, nf * 512:(nf + 1) * 512], op0=ALU.mult, op1=ALU.mult),
                     reads=[pp[nf], sm, gout], writes=[t])
            S.op("pool", lambda e, t=t, x=x, s=s: e.tensor_tensor(out=x[:, s, :], in0=x[:, s, :], in1=t[:, :],
                                                                  op=ALU.add), reads=[t, x], writes=[x])
        S.dma("pool", xout[mt * 512:(mt + 1) * 512, :].rearrange("(s p) d -> p s d", p=128), x[:, :, :],
              reads=[x])


IN_W = 9152
OFF_Q, OFF_K, OFF_V, OFF_Z, OFF_XBC, OFF_DT, OFF_RW, OFF_G = 0, 512, 1024, 1536, 2560, 4096, 4128, 6080


def inproj_pass(K, xin, g_ap, w_bf, SC, ntok):
    S = K.S
    K.new_pass()
    gin = K.alloc([128, D], F32)
    S.dma("sp", gin[:, :], g_ap.broadcast_to([128, D]), writes=[gin])
    xp = Pool(K, 2, [128, 4, D], F32)
    hnTp = Pool(K, 2, [128, 8, 512], BF16)
    wp = Pool(K, 3, [128, 8, 512], BF16)
    ofm = Pool(K, 3, [128, 512], F32)
    obf = Pool(K, 3, [128, 512], BF16)
    P = {"small": Pool(K, 8, [128, 8], F32), "junk": Pool(K, 2, [128, D], BF16), "hn": Pool(K, 2, [128, D], BF16)}
    w_v = w_bf.rearrange("(kc p) n -> p kc n", p=128)
    blocks = []
    for c0 in range(0, 1024, 512):
        blocks.append((c0, 512, "F"))
    blocks.append((OFF_V, 512, "T"))
    blocks += [(OFF_Z, 512, "T"), (OFF_Z + 512, 512, "T")]
    blocks += [(OFF_XBC + i * 512, 512, "F") for i in range(3)]
    blocks.append((OFF_DT, 32, "T"))
    blocks += [(OFF_RW, 512, "F"), (OFF_RW + 512, 512, "F"), (OFF_RW + 1024, 512, "F"), (OFF_RW + 1536, 416, "F")]
    blocks += [(OFF_G + i * 512, 512, "T") for i in range(6)]
    for mt in range(ntok // 512):
        t0 = mt * 512
        x = xp.get()
        S.dma("sp", x[:, :, :], xin[t0:t0 + 512, :].rearrange("(s p) d -> p s d", p=128), writes=[x])
        hnT = hnTp.get()
        for s in range(4):
            norm_transpose(K, x[:, s, :], x, gin, hnT, s * 128, P)
        for (c0, ncol, mode) in blocks:
            w = wp.get()
            S.dma("sp", w[:, :, 0:ncol], w_v[:, :, c0:c0 + ncol], writes=[w])
            if mode == "F":
                for fc in range((ncol + 127) // 128):
                    nf = min(128, ncol - fc * 128)
                    pp = K.psum()
                    for kc in range(8):
                        S.op("pe", lambda e, pp=pp, w=w, kc=kc, fc=fc, nf=nf, hnT=hnT: e.matmul(
                            pp[0:nf, :], w[:, kc, fc * 128:fc * 128 + nf], hnT[:, kc, :],
                            start=(kc == 0), stop=(kc == 7)), reads=[w, hnT], writes=[pp])
                    f0 = c0 + fc * 128
                    if f0 < 1024:
                        o = obf.get()
                        sc = 0.125 if f0 < 512 else 1.0
                        S.op("act", lambda e, o=o, pp=pp, sc=sc: e.activation(out=o[:, :], in_=pp[:, :],
                                                                              func=AF.Copy, scale=sc),
                             reads=[pp], writes=[o])
                        dst = SC["QT"] if f0 < 512 else SC["KT"]
                        r0 = f0 % 512
                        S.dma("pool", dst[r0:r0 + 128, t0:t0 + 512], o[:, :], reads=[o])
                    else:
                        o = ofm.get()
                        S.op("act", lambda e, o=o, pp=pp, nf=nf: e.copy(out=o[0:nf, :], in_=pp[0:nf, :]),
                             reads=[pp], writes=[o])
                        if f0 < OFF_DT:
                            r0 = f0 - OFF_XBC
                            S.dma("pool", SC["XBCT"][r0:r0 + nf, t0:t0 + 512], o[0:nf, :], reads=[o])
                        else:
                            r0 = f0 - OFF_RW
                            S.dma("pool", SC["RWT"][r0:r0 + nf, t0:t0 + 512], o[0:nf, :], reads=[o])
            else:
                for s in range(4):
                    pp = K.psum()
                    for kc in range(8):
                        S.op("pe", lambda e, pp=pp, w=w, kc=kc, s=s, ncol=ncol, hnT=hnT: e.matmul(
                            pp[:, 0:ncol], hnT[:, kc, s * 128:(s + 1) * 128], w[:, kc, 0:ncol],
                            start=(kc == 0), stop=(kc == 7)), reads=[w, hnT], writes=[pp])
                    tk = t0 + s * 128
                    if c0 == OFF_V:
                        o = obf.get()
                        S.op("act", lambda e, o=o, pp=pp: e.copy(out=o[:, :], in_=pp[:, :]), reads=[pp], writes=[o])
                        S.dma("pool", SC["V"][tk:tk + 128, :], o[:, :], reads=[o])
                    elif c0 >= OFF_G:
                        o = ofm.get()
                        S.op("act", lambda e, o=o, pp=pp: e.activation(out=o[:, :], in_=pp[:, :], func=AF.Sigmoid),
                             reads=[pp], writes=[o])
                        S.dma("pool", SC["G"][tk:tk + 128, c0 - OFF_G:c0 - OFF_G + 512], o[:, :], reads=[o])
                    elif c0 == OFF_DT:
                        o = ofm.get()
                        S.op("act", lambda e, o=o, pp=pp: e.copy(out=o[:, 0:32], in_=pp[:, 0:32]), reads=[pp], writes=[o])
                        S.dma("pool", SC["DT"][tk:tk + 128, :], o[:, 0:32], reads=[o])
                    else:
                        o = ofm.get()
                        S.op("act", lambda e, o=o, pp=pp: e.copy(out=o[:, :], in_=pp[:, :]), reads=[pp], writes=[o])
                        S.dma("pool", SC["Z"][tk:tk + 128, c0 - OFF_Z:c0 - OFF_Z + 512], o[:, :], reads=[o])


def attn_plan(nseg, seglen):
    R = seglen // 64
    nrow = nseg * R

    def rs_of(g, link):
        seg, r = divmod(g, R)
        if link and seg < 2 and nseg >= 2:
            return int(np.clip(g - 4, 0, 2 * R - 8))
        return seg * R + int(np.clip(r - 4, 0, R - 8))

    qc = np.arange(64)
    wst = np.clip(qc - 8, 0, 48)
    pats = {}
    plan = []
    ids = {}
    for P in range(nrow // 2):
        kps = set()
        for g in (2 * P, 2 * P + 1):
            for link in (False, True):
                rs = rs_of(g, link)
                for kr in range(rs, rs + 8):
                    kps.add(kr // 2)
        ent = []
        for KP in sorted(kps):
            both = []
            for link in (False, True):
                idx = np.full((2, 64, 2, 64), -1, np.int32)
                for qr2 in range(2):
                    g = 2 * P + qr2
                    rs = rs_of(g, link)
                    for kr2 in range(2):
                        kr = 2 * KP + kr2
                        if not (rs <= kr < rs + 8):
                            continue
                        dr = kr - g + 7
                        kc = np.arange(64)[:, None]
                        ok = (kc >= wst[None, :]) & (kc < wst[None, :] + 16)
                        dc = np.clip(kc - qc[None, :] + 15, 0, 30)
                        idx[kr2, :, qr2, :] = np.where(ok, dr * 31 + dc, -1)
                both.append(idx.reshape(128, 128))
            key = both[0].tobytes() + both[1].tobytes()
            if key not in ids:
                ids[key] = len(ids)
                pats.setdefault(False, []).append(both[0])
                pats.setdefault(True, []).append(both[1])
            ent.append((KP, ids[key]))
        plan.append(ent)
    return plan, {k: np.stack(v) for k, v in pats.items()}


def attn_tables(rpb, pat):
    flat = rpb.reshape(rpb.shape[0], 8, 15 * 31)
    g = flat[:, :, np.clip(pat, 0, None)]
    g = np.where(pat[None, None] >= 0, g, np.float32(-30000.0)).astype(np.float32)
    return np.ascontiguousarray(g.transpose(0, 2, 3, 1, 4))


def attn_pass(K, SC, tab, plan):
    import os
    ALV = int(os.environ.get("MK_ALV", "3"))
    S = K.S
    K.new_pass()
    qp = Pool(K, 2, [64, 8, 128], BF16)
    kp = Pool(K, 3, [64, 8, 128], BF16)
    vp = Pool(K, 3, [128, 8, 80], BF16)
    tp = Pool(K, 3, [128, 8, 128], F32)
    sp_ = Pool(K, 2, [128, 512], F32)
    ptp = Pool(K, 3, [128, 8, 128], BF16)
    yp = Pool(K, 2, [128, 8, 64], BF16)
    sm = Pool(K, 2, [128, 8], F32)
    for v in vp.t:
        S.op("pool", lambda e, v=v: e.memset(v[:, :, 64:65], 1.0), writes=[v])
    QTv = SC["QT"].rearrange("(c p) t -> p c t", p=64)
    KTv = SC["KT"].rearrange("(c p) t -> p c t", p=64)
    for P, ent in enumerate(plan):
        q = qp.get()
        S.dma("sp", q[:, :, :], QTv[:, :, P * 128:(P + 1) * 128], writes=[q])
        ob = [K.ps[6], K.ps[7]]
        for ki, (KP, cfg) in enumerate(ent):
            k = kp.get()
            S.dma("sp", k[:, :, :], KTv[:, :, KP * 128:(KP + 1) * 128], writes=[k])
            v = vp.get()
            S.dma("sp", v[:, :, 0:64], SC["V"][KP * 128:(KP + 1) * 128, :].rearrange("t (h d) -> t h d", h=8),
                  writes=[v])
            tb = tp.get()
            S.dma("sp", tb[:, :, :], tab[cfg], writes=[tb])
            pt = ptp.get()
            for hb in range(2):
                pp = K.ps[K.psi6]
                K.psi6 = (K.psi6 + 1) % 6
                for hh in range(4):
                    h = hb * 4 + hh
                    S.op("pe", lambda e, pp=pp, k=k, q=q, h=h, hh=hh: e.matmul(
                        pp[:, hh * 128:(hh + 1) * 128], k[:, h, :], q[:, h, :], start=True, stop=True),
                        reads=[k, q], writes=[pp])
                sb = sp_.get()
                S.op("dve", lambda e, sb=sb, pp=pp, tb=tb, hb=hb: e.tensor_tensor(
                    out=sb[:, :], in0=pp[:, :], in1=tb[:, hb * 4:(hb + 1) * 4, :].rearrange("p a b -> p (a b)"),
                    op=ALU.add), reads=[pp, tb], writes=[sb])
                S.op("act", lambda e, sb=sb, pt=pt, hb=hb: e.activation(
                    out=pt[:, hb * 4:(hb + 1) * 4, :].rearrange("p a b -> p (a b)"), in_=sb[:, :], func=AF.Exp),
                    reads=[sb], writes=[pt])
            for h in range(8 if ALV >= 2 else 0):
                o_ = ob[h // 4]
                hh = h % 4
                S.op("pe", lambda e, o_=o_, pt=pt, v=v, h=h, hh=hh, ki=ki, n=len(ent): e.matmul(
                    o_[:, hh * 128:hh * 128 + 65], pt[:, h, :], v[:, h, 0:65], start=(ki == 0 and hh == 0), stop=(ki == n - 1)),
                    reads=[pt, v], writes=[o_])
        rc = sm.get()
        y = yp.get()
        for hb in range(2 if ALV >= 3 else 0):
            ov = ob[hb][:, :].rearrange("p (h d) -> p h d", h=4)
            S.op("dve", lambda e, rc=rc, ov=ov, hb=hb: e.reciprocal(out=rc[:, hb * 4:(hb + 1) * 4], in_=ov[:, :, 64]),
                 reads=[ob[hb]], writes=[rc])
            S.op("dve", lambda e, rc=rc, ov=ov, hb=hb, y=y: e.tensor_tensor(
                out=y[:, hb * 4:(hb + 1) * 4, :], in0=ov[:, :, 0:64],
                in1=rc[:, hb * 4:(hb + 1) * 4].unsqueeze(2).broadcast_to([128, 4, 64]), op=ALU.mult),
                reads=[ob[hb], rc], writes=[y])
        if ALV >= 3:
            S.dma("pool", SC["YA"][P * 128:(P + 1) * 128, :], y[:, :, :].rearrange("p h d -> p (h d)"), reads=[y])


def merge_pass(K, SC, xio, g_ap, wbr_bf, wout_bf, ntok):
    S = K.S
    K.new_pass()
    g3 = K.alloc([128, D], F32)
    S.dma("sp", g3[:, :], g_ap.broadcast_to([128, D]), writes=[g3])
    wbr = K.alloc([128, 16, D], BF16)
    wbv = wbr_bf.rearrange("(kc p) n -> p kc n", p=128)
    for kc in range(0, 16, 4):
        S.dma("sp", wbr[:, kc:kc + 4, :], wbv[:, kc:kc + 4, :], writes=[wbr])
    wo = K.alloc([128, 8, D], BF16)
    S.dma("sp", wo[:, :, :], wout_bf.rearrange("(kc p) n -> p kc n", p=128), writes=[wo])
    ybp = Pool(K, 2, [128, 2048], BF16)
    gp = Pool(K, 2, [128, 3, D], F32)
    xp = Pool(K, 2, [128, D], F32)
    yTp = Pool(K, 2, [128, 16, 128], BF16)
    mp = Pool(K, 2, [128, D], F32)
    tmp = Pool(K, 2, [128, 512], F32)
    mbp = Pool(K, 2, [128, D], BF16)
    mTp = Pool(K, 2, [128, 8, 128], BF16)
    tt = Pool(K, 2, [128, D], F32)
    sm = Pool(K, 4, [128, 8], F32)
    junk = Pool(K, 2, [128, 512], BF16)
    for t in range(ntok // 128):
        r0 = t * 128
        yb = ybp.get()
        S.dma("sp", yb[:, 0:512], SC["YA"][r0:r0 + 128, :], writes=[yb])
        S.dma("sp", yb[:, 512:1536], SC["YS"][r0:r0 + 128, :], writes=[yb])
        S.dma("sp", yb[:, 1536:2048], SC["YR"][r0:r0 + 128, :], writes=[yb])
        gt = gp.get()
        S.dma("sp", gt[:, :, :], SC["G"][r0:r0 + 128, :].rearrange("t (b d) -> t b d", b=3), writes=[gt])
        x = xp.get()
        S.dma("sp", x[:, :], xio[r0:r0 + 128, :], writes=[x])
        yT = yTp.get()
        for half in range(2):
            pt = K.psum()
            ptb = pt.t[:, :].bitcast(BF16)
            for c in range(8):
                kc = half * 8 + c
                S.op("pe", lambda e, ptb=ptb, c=c, kc=kc, yb=yb: e.transpose(
                    ptb[:, c * 128:(c + 1) * 128], yb[:, kc * 128:(kc + 1) * 128], K.ident[:, :]),
                    reads=[yb, K.ident], writes=[pt])
            S.op("act", lambda e, yT=yT, ptb=ptb, half=half: e.copy(
                out=yT[:, half * 8:(half + 1) * 8, :], in_=ptb.rearrange("p (k c) -> p k c", k=8)),
                reads=[pt], writes=[yT])
        m = mp.get()
        for b, (k0, nk) in enumerate(((0, 4), (4, 8), (12, 4))):
            for nf in range(2):
                pp = K.psum()
                for kc in range(nk):
                    S.op("pe", lambda e, pp=pp, yT=yT, kc=kc, k0=k0, nf=nf, nk=nk: e.matmul(
                        pp[:, :], yT[:, k0 + kc, :], wbr[:, k0 + kc, nf * 512:(nf + 1) * 512],
                        start=(kc == 0), stop=(kc == nk - 1)), reads=[yT, wbr], writes=[pp])
                if b == 0:
                    S.op("dve", lambda e, m=m, pp=pp, gt=gt, nf=nf, b=b: e.tensor_tensor(
                        out=m[:, nf * 512:(nf + 1) * 512], in0=pp[:, :], in1=gt[:, b, nf * 512:(nf + 1) * 512],
                        op=ALU.mult), reads=[pp, gt], writes=[m])
                else:
                    tm = tmp.get()
                    S.op("dve", lambda e, tm=tm, pp=pp, gt=gt, nf=nf, b=b: e.tensor_tensor(
                        out=tm[:, :], in0=pp[:, :], in1=gt[:, b, nf * 512:(nf + 1) * 512], op=ALU.mult),
                        reads=[pp, gt], writes=[tm])
                    S.op("pool", lambda e, tm=tm, m=m, nf=nf: e.tensor_tensor(
                        out=m[:, nf * 512:(nf + 1) * 512], in0=m[:, nf * 512:(nf + 1) * 512], in1=tm[:, :],
                        op=ALU.add), reads=[tm, m], writes=[m])
        mb = mbp.get()
        S.op("act", lambda e, mb=mb, m=m: e.copy(out=mb[:, :], in_=m[:, :]), reads=[m], writes=[mb])
        mT = mTp.get()
        pt = K.psum()
        ptb = pt.t[:, :].bitcast(BF16)
        for c in range(8):
            S.op("pe", lambda e, ptb=ptb, c=c, mb=mb: e.transpose(
                ptb[:, c * 128:(c + 1) * 128], mb[:, c * 128:(c + 1) * 128], K.ident[:, :]),
                reads=[mb, K.ident], writes=[pt])
        S.op("act", lambda e, mT=mT, ptb=ptb: e.copy(out=mT[:, :, :], in_=ptb.rearrange("p (k c) -> p k c", k=8)),
             reads=[pt], writes=[mT])
        pp = [K.psum(), K.psum()]
        for nf in range(2):
            for kc in range(8):
                S.op("pe", lambda e, p_=pp[nf], mT=mT, kc=kc, nf=nf: e.matmul(
                    p_[:, :], mT[:, kc, :], wo[:, kc, nf * 512:(nf + 1) * 512], start=(kc == 0), stop=(kc == 7)),
                    reads=[mT, wo], writes=[pp[nf]])
        s_ = sm.get()
        jk = junk.get()
        for nf in range(2):
            S.op("act", lambda e, nf=nf, jk=jk, s_=s_, p_=pp[nf]: e.activation(
                out=jk[:, :], in_=p_[:, :], func=AF.Square, accum_out=s_[:, nf:nf + 1]),
                reads=[pp[nf]], writes=[jk, s_])
        S.op("dve", lambda e, s_=s_: e.tensor_tensor(out=s_[:, 2:3], in0=s_[:, 0:1], in1=s_[:, 1:2], op=ALU.add),
             reads=[s_], writes=[s_])
        S.op("act", lambda e, s_=s_: e.activation(out=s_[:, 3:4], in_=s_[:, 2:3], func=AF.Sqrt,
                                                  bias=K.epsc[:, 0:1], scale=1.0 / D),
             reads=[s_, K.epsc], writes=[s_])
        S.op("dve", lambda e, s_=s_: e.reciprocal(out=s_[:, 4:5], in_=s_[:, 3:4]), reads=[s_], writes=[s_])
        t_ = tt.get()
        for nf in range(2):
            S.op("dve", lambda e, nf=nf, t_=t_, s_=s_, p_=pp[nf]: e.scalar_tensor_tensor(
                out=t_[:, nf * 512:(nf + 1) * 512], in0=p_[:, :], scalar=s_[:, 4:5],
                in1=g3[:, nf * 512:(nf + 1) * 512], op0=ALU.mult, op1=ALU.mult),
                reads=[pp[nf], s_, g3], writes=[t_])
        S.op("pool", lambda e, t_=t_, x=x: e.tensor_tensor(out=x[:, :], in0=x[:, :], in1=t_[:, :], op=ALU.add),
             reads=[t_, x], writes=[x])
        S.dma("pool", xio[r0:r0 + 128, :], x[:, :], reads=[x])


def ssd_setup(K, A, l):
    S = K.S
    C = {}
    C["cw"] = K.alloc([128, 12, 5], F32)
    for k in range(5):
        S.dma("sp", C["cw"][:, :, k], A["ssm_conv_w"][l, k].rearrange("(fc p) -> p fc", p=128), writes=[C["cw"]],
              slow=True)
    C["cb"] = K.alloc([128, 12], F32)
    S.dma("sp", C["cb"][:, :], A["ssm_conv_b"][l].rearrange("(fc p) -> p fc", p=128), writes=[C["cb"]], slow=True)
    C["dtb"] = K.alloc([128, 32], F32)
    S.dma("sp", C["dtb"][:, :], A["ssm_dt_bias"][l].rearrange("a b -> (a b)").unsqueeze(0).broadcast_to([128, 32]),
          writes=[C["dtb"]])
    C["abc"] = K.alloc([128, 32], F32)
    S.dma("sp", C["abc"][:, :], A["ssm_a_log"][l].rearrange("a b -> (a b)").unsqueeze(0).broadcast_to([128, 32]),
          writes=[C["abc"]])
    S.op("act", lambda e: e.activation(out=C["abc"][:, :], in_=C["abc"][:, :], func=AF.Exp), reads=[C["abc"]],
         writes=[C["abc"]])
    S.op("dve", lambda e: e.tensor_scalar(out=C["abc"][:, :], in0=C["abc"][:, :], scalar1=-1.0, scalar2=0.0,
                                          op0=ALU.mult, op1=ALU.add), reads=[C["abc"]], writes=[C["abc"]])
    dsk = K.alloc([128, 32], F32)
    S.dma("sp", dsk[:, :], A["ssm_d"][l].rearrange("a b -> (a b)").unsqueeze(0).broadcast_to([128, 32]), writes=[dsk])
    C["dsum"] = K.alloc([128, 16], F32)
    S.op("dve", lambda e: e.tensor_tensor(out=C["dsum"][:, :], in0=dsk[:, 0:16], in1=dsk[:, 16:32], op=ALU.add),
         reads=[dsk], writes=[C["dsum"]])
    C["dI"] = K.alloc([128, 16, 128], F32)
    for h in range(16):
        S.op("dve", lambda e, h=h: e.tensor_scalar(out=C["dI"][:, h, :], in0=K.identf[:, :], scalar1=C["dsum"][:, h:h + 1],
                                                   scalar2=0.0, op0=ALU.mult, op1=ALU.add),
             reads=[K.identf, C["dsum"]], writes=[C["dI"]])
    C["ng"] = K.alloc([128, 1024], F32)
    S.dma("sp", C["ng"][:, :], A["ssm_norm_g"][l:l + 1, :].broadcast_to([128, 1024]), writes=[C["ng"]])
    return C


import os as _os
NB = int(_os.environ.get('MK_NB', '2'))


def ssd_pools(K):
    P = {}
    P["xw"] = Pool(K, NB, [128, 12, 132], F32)
    P["accD"] = Pool(K, 2, [128, 8, 128], F32)
    P["tmpD"] = Pool(K, 1, [128, 8, 128], F32)
    P["accP"] = Pool(K, 2, [128, 4, 128], F32)
    P["tmpP"] = Pool(K, 1, [128, 4, 128], F32)
    P["xbcT"] = Pool(K, NB, [128, 12, 128], BF16)
    P["xs"] = Pool(K, NB, [128, 1024], BF16)
    P["bm"] = Pool(K, NB, [128, 256], BF16)
    P["dt"] = Pool(K, NB, [128, 32], F32)
    P["v"] = Pool(K, NB, [128, 256], F32)
    return P


def ssd_prep(K, SC, C, P, c, nchunk, cps, need_cm=True):
    S = K.S
    t0 = c * 128
    ntok = nchunk * 128
    seg, cis = divmod(c, cps)
    xw = P["xw"].get()
    XB = SC["XBCT"].rearrange("(fc p) t -> p fc t", p=128)
    lo, hi = max(t0 - 2, 0), min(t0 + 130, ntok)
    S.dma("sp", xw[:, :, lo - (t0 - 2):hi - (t0 - 2)], XB[:, :, lo:hi], writes=[xw])
    if t0 == 0:
        S.op("pool", lambda e: e.memset(xw[:, :, 0:2], 0.0), writes=[xw])
    elif cis == 0:
        S.op("pool", lambda e, seg=seg: e.tensor_scalar(out=xw[:, :, 0:2], in0=xw[:, :, 0:2],
                                                        scalar1=K.flags[:, seg - 1:seg], scalar2=0.0,
                                                        op0=ALU.mult, op1=ALU.add), reads=[xw, K.flags], writes=[xw])
    if t0 + 130 > ntok:
        S.op("pool", lambda e: e.memset(xw[:, :, 130:132], 0.0), writes=[xw])
    elif cis == cps - 1:
        S.op("pool", lambda e, seg=seg: e.tensor_scalar(out=xw[:, :, 130:132], in0=xw[:, :, 130:132],
                                                        scalar1=K.flags[:, seg:seg + 1], scalar2=0.0,
                                                        op0=ALU.mult, op1=ALU.add), reads=[xw, K.flags], writes=[xw])
    cw = C["cw"]
    accs = {}
    for eng, f0, f1, nm in (("dve", 0, 8, "D"), ("pool", 8, 12, "P")):
        nf = f1 - f0
        acc = P["acc" + nm].get()
        tmp = P["tmp" + nm].get()
        accs[nm] = acc
        for k in range(5):
            wk = cw[:, f0:f1, k:k + 1].broadcast_to([128, nf, 128])
            if k == 0:
                S.op(eng, lambda e, wk=wk, f0=f0, f1=f1, acc=acc: e.tensor_tensor(
                    out=acc[:, :, :], in0=xw[:, f0:f1, 0:128], in1=wk, op=ALU.mult), reads=[xw, cw], writes=[acc])
            else:
                S.op(eng, lambda e, wk=wk, k=k, f0=f0, f1=f1, tmp=tmp: e.tensor_tensor(
                    out=tmp[:, :, :], in0=xw[:, f0:f1, k:k + 128], in1=wk, op=ALU.mult), reads=[xw, cw], writes=[tmp])
                S.op(eng, lambda e, acc=acc, tmp=tmp: e.tensor_tensor(out=acc[:, :, :], in0=acc[:, :, :],
                                                                      in1=tmp[:, :, :], op=ALU.add),
                     reads=[acc, tmp], writes=[acc])
    xbcT = P["xbcT"].get()
    for fc in range(12):
        a_ = accs["D"] if fc < 8 else accs["P"]
        fi = fc if fc < 8 else fc - 8
        S.op("act", lambda e, fc=fc, a_=a_, fi=fi: e.activation(out=xbcT[:, fc, :], in_=a_[:, fi, :], func=AF.Silu,
                                                                bias=C["cb"][:, fc:fc + 1]), reads=[a_, C["cb"]],
             writes=[xbcT])
    xs = P["xs"].get()
    pt = K.psum()
    ptb = pt.t[:, :].bitcast(BF16)
    for j in range(8):
        S.op("pe", lambda e, j=j: e.transpose(ptb[:, j * 128:(j + 1) * 128], xbcT[:, j, :], K.ident[:, :]),
             reads=[xbcT, K.ident], writes=[pt])
    S.op("act", lambda e: e.copy(out=xs[:, :], in_=ptb[:, :]), reads=[pt], writes=[xs])
    bm = P["bm"].get()
    pt2 = K.psum()
    ptb2 = pt2.t[:, :].bitcast(BF16)
    for g in range(2):
        S.op("pe", lambda e, g=g: e.transpose(ptb2[:, g * 128:(g + 1) * 128], xbcT[:, 8 + g, :], K.ident[:, :]),
             reads=[xbcT, K.ident], writes=[pt2])
    S.op("act", lambda e: e.copy(out=bm[:, :], in_=ptb2[:, 0:256]), reads=[pt2], writes=[bm])
    dt = P["dt"].get()
    S.dma("sp", dt[:, :], SC["DT"][t0:t0 + 128, :], writes=[dt])
    V = P["v"].get()
    S.op("dve", lambda e: e.tensor_tensor(out=V[:, 0:32], in0=dt[:, :], in1=C["dtb"][:, :], op=ALU.add),
         reads=[dt, C["dtb"]], writes=[V])
    S.op("act", lambda e: e.activation(out=V[:, 0:32], in_=V[:, 0:32], func=AF.Exp), reads=[V], writes=[V])
    S.op("act", lambda e: e.activation(out=V[:, 0:32], in_=V[:, 0:32], func=AF.Ln, bias=K.onec[:, 0:1]),
         reads=[V, K.onec], writes=[V])
    S.op("act", lambda e: e.activation(out=V[:, 32:64], in_=V[:, 0:32], func=AF.Ln), reads=[V], writes=[V])
    S.op("dve", lambda e: e.tensor_tensor(out=V[:, 64:96], in0=V[:, 0:32], in1=C["abc"][:, :], op=ALU.mult),
         reads=[V, C["abc"]], writes=[V])
    pc = K.psum()
    S.op("pe", lambda e: e.matmul(pc[:, 0:16], K.uinc[:, :], V[:, 64:80], start=True, stop=True),
         reads=[K.uinc, V], writes=[pc])
    S.op("pe", lambda e: e.matmul(pc[:, 16:32], K.uexc[:, :], V[:, 80:96], start=True, stop=True),
         reads=[K.uexc, V], writes=[pc])
    S.op("pe", lambda e: e.matmul(pc[:, 32:64], K.onesf[:, :], V[:, 64:96], start=True, stop=True),
         reads=[K.onesf, V], writes=[pc])
    S.op("dve", lambda e: e.tensor_copy(out=V[:, 96:160], in_=pc[:, 0:64]), reads=[pc], writes=[V])
    S.op("dve", lambda e: e.tensor_tensor(out=V[:, 160:176], in0=V[:, 32:48], in1=V[:, 96:112], op=ALU.subtract),
         reads=[V], writes=[V])
    S.op("dve", lambda e: e.tensor_tensor(out=V[:, 176:192], in0=V[:, 48:64], in1=V[:, 112:128], op=ALU.add),
         reads=[V], writes=[V])
    S.op("dve", lambda e: e.tensor_tensor(out=V[:, 192:208], in0=V[:, 160:176], in1=V[:, 128:144], op=ALU.add),
         reads=[V], writes=[V])
    S.op("dve", lambda e: e.tensor_tensor(out=V[:, 240:256], in0=V[:, 144:160], in1=V[:, 112:128], op=ALU.subtract),
         reads=[V], writes=[V])
    S.op("dve", lambda e: e.tensor_copy(out=V[:, 208:240], in_=V[:, 176:192].unsqueeze(1).broadcast_to([128, 2, 16])
                                        .rearrange("p a b -> p (a b)")) if False else
         e.tensor_copy(out=V[:, 208:224], in_=V[:, 176:192]), reads=[V], writes=[V])
    S.op("dve", lambda e: e.tensor_copy(out=V[:, 224:240], in_=V[:, 96:112]), reads=[V], writes=[V])
    S.op("act", lambda e: e.activation(out=V[:, 192:256], in_=V[:, 192:256], func=AF.Exp), reads=[V], writes=[V])
    S.op("act", lambda e: e.activation(out=V[:, 128:160], in_=V[:, 128:160], func=AF.Exp), reads=[V], writes=[V])
    return {"xbcT": xbcT, "xs": xs, "bm": bm, "V": V}


def ssd_load(K, SC, P, c):
    S = K.S
    xbcT, xs, bm, V = P["xbcT"].get(), P["xs"].get(), P["bm"].get(), P["v"].get()
    S.dma("sp", xbcT[:, :, :], SC["PX"][c], writes=[xbcT])
    S.dma("sp", xs[:, :], SC["PS"][c], writes=[xs])
    S.dma("sp", bm[:, :], SC["PB"][c], writes=[bm])
    S.dma("sp", V[:, :], SC["PV"][c], writes=[V])
    return {"xbcT": xbcT, "xs": xs, "bm": bm, "V": V}


def ssd_state_step(K, T, H, wcol, ecol, PS):
    S = K.S
    xs, bm, V = T["xs"], T["bm"], T["V"]
    xd = PS["xd"].get()
    S.op("pool", lambda e: e.tensor_tensor(out=xd[:, :].rearrange("p (h d) -> p h d", h=16),
                                           in0=xs[:, :].rearrange("p (h d) -> p h d", h=16),
                                           in1=V[:, wcol:wcol + 16].unsqueeze(2).broadcast_to([128, 16, 64]),
                                           op=ALU.mult), reads=[xs, V], writes=[xd])
    S.op("dve", lambda e: e.tensor_tensor(out=H[:, :].rearrange("p (h d) -> p h d", h=16),
                                          in0=H[:, :].rearrange("p (h d) -> p h d", h=16),
                                          in1=V[:, ecol:ecol + 16].unsqueeze(2).broadcast_to([128, 16, 64]),
                                          op=ALU.mult), reads=[H, V], writes=[H])
    for g in range(2):
        pp = K.psum()
        S.op("pe", lambda e, pp=pp, g=g: e.matmul(pp[:, :], bm[:, g * 128:(g + 1) * 128], xd[:, g * 512:(g + 1) * 512],
                                                  start=True, stop=True), reads=[bm, xd], writes=[pp])
        S.op("dve", lambda e, pp=pp, g=g: e.tensor_tensor(out=H[:, g * 512:(g + 1) * 512], in0=H[:, g * 512:(g + 1) * 512],
                                                          in1=pp[:, :], op=ALU.add), reads=[pp, H], writes=[H])


def ssd_bwd_pass(K, SC, C, nchunk, cps):
    S = K.S
    P = ssd_pools(K)
    PS = {"xd": Pool(K, NB, [128, 1024], BF16)}
    H = K.alloc([128, 1024], F32)
    hbp = Pool(K, NB, [128, 1024], BF16)
    S.op("dve", lambda e: e.memset(H[:, :], 0.0), writes=[H])
    def body(c):
        seg, cis = divmod(c, cps)
        if cis == cps - 1 and c != nchunk - 1:
            S.op("dve", lambda e, seg=seg: e.tensor_scalar(out=H[:, :], in0=H[:, :], scalar1=K.flags[:, seg:seg + 1],
                                                           scalar2=0.0, op0=ALU.mult, op1=ALU.add),
                 reads=[H, K.flags], writes=[H])
        T = Tn.pop(c)
        S.dma("pool", SC["PX"][c], T["xbcT"][:, :, :], reads=[T["xbcT"]])
        S.dma("pool", SC["PS"][c], T["xs"][:, :], reads=[T["xs"]])
        S.dma("pool", SC["PB"][c], T["bm"][:, :], reads=[T["bm"]])
        S.dma("pool", SC["PV"][c], T["V"][:, :], reads=[T["V"]])
        if c - 1 >= 0:
            Tn[c - 1] = ssd_prep(K, SC, C, P, c - 1, nchunk, cps)
        ssd_state_step(K, T, H, 208, 144, PS)
        hb = hbp.get()
        S.op("act", lambda e, hb=hb: e.copy(out=hb[:, :], in_=H[:, :]), reads=[H], writes=[hb])
        S.dma("pool", SC["HB"][c], hb[:, :], reads=[hb])

    Tn = {nchunk - 1: ssd_prep(K, SC, C, P, nchunk - 1, nchunk, cps)}
    for c in range(nchunk - 1, -1, -1):
        body(c)


def ssd_fwd_pass(K, SC, C, nchunk, cps):
    S = K.S
    P = ssd_pools(K)
    PS = {"xd": Pool(K, NB, [128, 1024], BF16)}
    H = K.alloc([128, 1024], F32)
    Hbf = K.alloc([128, 1024], BF16)
    S.op("dve", lambda e: e.memset(H[:, :], 0.0), writes=[H])
    hbp = Pool(K, NB, [128, 1024], BF16)
    cbp = Pool(K, NB, [128, 4, 128], F32)
    tq = Pool(K, NB, [128, 4, 128], F32)
    eq = Pool(K, NB, [128, 4, 128], F32)
    mq = Pool(K, NB, [128, 4, 128], F32)
    Mp = Pool(K, NB, [128, 16, 128], BF16)
    yp = Pool(K, NB, [128, 1024], F32)
    t1p = Pool(K, NB, [128, 512], F32)
    zp = Pool(K, NB, [128, 1024], F32)
    ybp = Pool(K, NB, [128, 1024], BF16)
    smp = Pool(K, 4, [128, 8], F32)
    jk = Pool(K, 1, [128, 512], BF16)
    def body(c):
        seg, cis = divmod(c, cps)
        t0 = c * 128
        if cis == 0 and c != 0:
            S.op("dve", lambda e, seg=seg: e.tensor_scalar(out=H[:, :], in0=H[:, :], scalar1=K.flags[:, seg - 1:seg],
                                                           scalar2=0.0, op0=ALU.mult, op1=ALU.add),
                 reads=[H, K.flags], writes=[H])
        S.op("act", lambda e: e.copy(out=Hbf[:, :], in_=H[:, :]), reads=[H], writes=[Hbf])
        hb = hbp.get()
        if c == nchunk - 1:
            S.op("pool", lambda e, hb=hb: e.memset(hb[:, :], 0.0), writes=[hb])
        else:
            S.dma("sp", hb[:, :], SC["HB"][c + 1], writes=[hb])
            if cis == cps - 1:
                S.op("pool", lambda e, hb=hb, seg=seg: e.tensor_scalar(
                    out=hb[:, :], in0=hb[:, :], scalar1=K.flags[:, seg:seg + 1], scalar2=0.0, op0=ALU.mult,
                    op1=ALU.add), reads=[hb, K.flags], writes=[hb])
        T = Tn.pop(c)
        if c + 1 < nchunk:
            Tn[c + 1] = ssd_load(K, SC, P, c + 1)
        xbcT, xs, V = T["xbcT"], T["xs"], T["V"]
        cbm = cbp.get()
        for g in range(2):
            pp = K.psum()
            S.op("pe", lambda e, pp=pp, g=g: e.matmul(pp[:, 0:128], xbcT[:, 8 + g, :], xbcT[:, 10 + g, :],
                                                      start=True, stop=True), reads=[xbcT], writes=[pp])
            S.op("dve", lambda e, pp=pp, g=g: e.tensor_tensor(out=cbm[:, 2 * g, :], in0=pp[:, 0:128], in1=K.mskf[:, :],
                                                              op=ALU.mult), reads=[pp, K.mskf], writes=[cbm])
            S.op("dve", lambda e, pp=pp, g=g: e.tensor_tensor(out=cbm[:, 2 * g + 1, :], in0=pp[:, 0:128],
                                                              in1=K.mskb[:, :], op=ALU.mult),
                 reads=[pp, K.mskb], writes=[cbm])
        M = Mp.get()
        for qd in range(4):
            g = qd // 2
            mqs = []
            for d in range(2):
                pa = K.psum()
                for hh in range(4):
                    h = qd * 4 + hh
                    col = 64 + d * 16 + h
                    S.op("pe", lambda e, pa=pa, hh=hh, col=col, d=d: e.matmul(
                        pa[:, hh * 128:(hh + 1) * 128], V[:, col:col + 1].broadcast_to([128, 128]),
                        (K.uinc if d == 0 else K.uexc)[:, :], start=True, stop=True),
                        reads=[V, K.uinc, K.uexc], writes=[pa])
                t = tq.get()
                pav = pa[:, :].rearrange("p (a b) -> p a b", a=4)
                if d == 0:
                    vb = V[:, 96 + qd * 4:96 + qd * 4 + 4].unsqueeze(2).broadcast_to([128, 4, 128])
                    S.op("dve", lambda e, t=t, pav=pav, vb=vb: e.tensor_tensor(out=t[:, :, :], in0=pav, in1=vb,
                                                                               op=ALU.subtract),
                         reads=[pa, V], writes=[t])
                else:
                    vb = V[:, 112 + qd * 4:112 + qd * 4 + 4].unsqueeze(2).broadcast_to([128, 4, 128])
                    S.op("dve", lambda e, t=t, pav=pav, vb=vb: e.tensor_tensor(out=t[:, :, :], in0=vb, in1=pav,
                                                                               op=ALU.subtract),
                         reads=[pa, V], writes=[t])
                S.op("dve", lambda e, t=t: e.tensor_scalar(out=t[:, :, :], in0=t[:, :, :], scalar1=0.0, scalar2=0.0,
                                                            op0=ALU.min, op1=ALU.add), reads=[t], writes=[t])
                E = eq.get()
                for hh in range(4):
                    h = qd * 4 + hh
                    S.op("act", lambda e, E=E, t=t, hh=hh, h=h, d=d: e.activation(
                        out=E[:, hh, :], in_=t[:, hh, :], func=AF.Exp, bias=V[:, 32 + d * 16 + h:33 + d * 16 + h]),
                        reads=[t, V], writes=[E])
                m_ = mq.get()
                S.op("dve", lambda e, m_=m_, E=E, g=g, d=d: e.tensor_tensor(
                    out=m_[:, :, :], in0=E[:, :, :], in1=cbm[:, 2 * g + d:2 * g + d + 1, :].broadcast_to([128, 4, 128]),
                    op=ALU.mult), reads=[E, cbm], writes=[m_])
                mqs.append(m_)
            S.op("dve", lambda e, a=mqs[0], b=mqs[1]: e.tensor_tensor(out=a[:, :, :], in0=a[:, :, :], in1=b[:, :, :],
                                                                       op=ALU.add), reads=[mqs[0], mqs[1]], writes=[mqs[0]])
            S.op("dve", lambda e, a=mqs[0], qd=qd: e.tensor_tensor(out=M[:, qd * 4:(qd + 1) * 4, :], in0=a[:, :, :],
                                                                    in1=C["dI"][:, qd * 4:(qd + 1) * 4, :], op=ALU.add),
                 reads=[mqs[0], C["dI"]], writes=[M])
        yi = [K.psum(), K.psum()]
        yo = [[K.psum(), K.psum()], [K.psum(), K.psum()]]
        for h in range(16):
            g, hh = divmod(h, 8)
            S.op("pe", lambda e, h=h, g=g, hh=hh: e.matmul(yi[g][:, hh * 64:(hh + 1) * 64], M[:, h, :],
                                                           xs[:, h * 64:(h + 1) * 64], start=True, stop=True),
                 reads=[M, xs], writes=[yi[g]])
        for d, Hs in enumerate((Hbf, hb)):
            for g in range(2):
                S.op("pe", lambda e, d=d, g=g, Hs=Hs: e.matmul(yo[d][g][:, :], xbcT[:, 10 + g, :],
                                                               Hs[:, g * 512:(g + 1) * 512], start=True, stop=True),
                     reads=[xbcT, Hs], writes=[yo[d][g]])
        y = yp.get()
        for g in range(2):
            t1 = t1p.get()
            scf = V[:, 224 + g * 8:232 + g * 8].unsqueeze(2).broadcast_to([128, 8, 64])
            scb = V[:, 240 + g * 8:248 + g * 8].unsqueeze(2).broadcast_to([128, 8, 64])
            S.op("dve", lambda e, t1=t1, g=g, scf=scf: e.tensor_tensor(
                out=t1[:, :].rearrange("p (h d) -> p h d", h=8), in0=yo[0][g][:, :].rearrange("p (h d) -> p h d", h=8),
                in1=scf, op=ALU.mult), reads=[yo[0][g], V], writes=[t1])
            S.op("dve", lambda e, t1=t1, g=g: e.tensor_tensor(out=t1[:, :], in0=t1[:, :], in1=yi[g][:, :], op=ALU.add),
                 reads=[t1, yi[g]], writes=[t1])
            S.op("dve", lambda e, g=g, scb=scb, y=y: e.tensor_tensor(
                out=y[:, g * 512:(g + 1) * 512].rearrange("p (h d) -> p h d", h=8),
                in0=yo[1][g][:, :].rearrange("p (h d) -> p h d", h=8), in1=scb, op=ALU.mult),
                reads=[yo[1][g], V], writes=[y])
            S.op("pool", lambda e, g=g, t1=t1, y=y: e.tensor_tensor(out=y[:, g * 512:(g + 1) * 512],
                                                                    in0=y[:, g * 512:(g + 1) * 512], in1=t1[:, :],
                                                                    op=ALU.add), reads=[t1, y], writes=[y])
        if "DY" in SC:
            S.dma("pool", SC["DY"][t0:t0 + 128, :], y[:, :], reads=[y])
            S.dma("pool", SC["DV"][t0:t0 + 128, :], V[:, :], reads=[V])
            S.dma("pool", SC["DM"][c], M[:, :, :], reads=[M])
        z = zp.get()
        S.dma("sp", z[:, :], SC["Z"][t0:t0 + 128, :], writes=[z])
        S.op("act", lambda e, z=z: e.activation(out=z[:, :], in_=z[:, :], func=AF.Silu), reads=[z], writes=[z])
        S.op("pool", lambda e, z=z, y=y: e.tensor_tensor(out=y[:, :], in0=y[:, :], in1=z[:, :], op=ALU.mult),
             reads=[y, z], writes=[y])
        sm = smp.get()
        j_ = jk.get()
        yb = ybp.get()
        for g in range(2):
            S.op("act", lambda e, g=g, sm=sm, j_=j_, y=y: e.activation(
                out=j_[:, :], in_=y[:, g * 512:(g + 1) * 512], func=AF.Square, accum_out=sm[:, g:g + 1]),
                reads=[y], writes=[j_, sm])
        S.op("act", lambda e, sm=sm: e.activation(out=sm[:, 2:4], in_=sm[:, 0:2], func=AF.Sqrt, bias=K.epsc[:, 0:1],
                                                  scale=1.0 / 512), reads=[sm, K.epsc], writes=[sm])
        S.op("dve", lambda e, sm=sm: e.reciprocal(out=sm[:, 4:6], in_=sm[:, 2:4]), reads=[sm], writes=[sm])
        for g in range(2):
            S.op("dve", lambda e, g=g, sm=sm, y=y, yb=yb: e.scalar_tensor_tensor(
                out=yb[:, g * 512:(g + 1) * 512], in0=y[:, g * 512:(g + 1) * 512], scalar=sm[:, 4 + g:5 + g],
                in1=C["ng"][:, g * 512:(g + 1) * 512], op0=ALU.mult, op1=ALU.mult),
                reads=[y, sm, C["ng"]], writes=[yb])
        S.dma("pool", SC["YS"][t0:t0 + 128, :], yb[:, :], reads=[yb])
        ssd_state_step(K, T, H, 192, 128, PS)

    Tn = {0: ssd_load(K, SC, P, 0)}
    for c in range(nchunk):
        body(c)


RL = float(_os.environ.get('MK_RL', '99'))
LW_C = 0.6065306597126334
GN_EPS = 64e-5


def rwkv_setup(K, A, l):
    S = K.S
    C = {}

    def ld(name, shape, dt, q, src, slow=False):
        C[name] = K.alloc(shape, dt)
        S.dma(q, C[name][tuple(slice(None) for _ in shape)], src, writes=[C[name]], slow=slow)

    ld("wup", [64, 2, 512], BF16, "pool", A["rwkv_w_up"][l].rearrange("n l c -> l n c"))
    ld("aup", [64, 2, 512], BF16, "pool", A["rwkv_a_up"][l].rearrange("n l c -> l n c"))
    C["gup"] = K.alloc([64, 3, 512], BF16)
    S.dma("pool", C["gup"][:, 0:2, :], A["rwkv_g_up"][l][0:128, :].rearrange("(q l) c -> l q c", l=64), writes=[C["gup"]])
    S.dma("pool", C["gup"][0:32, 2, :], A["rwkv_g_up"][l][128:160, :], writes=[C["gup"]])
    C["w0"] = K.alloc([128, 2, 512], F32)
    for n in range(2):
        S.dma("sp", C["w0"][:, n, :], A["rwkv_w0"][l][n:n + 1, :].broadcast_to([128, 512]), writes=[C["w0"]])
    C["a0"] = K.alloc([64, 2, 8], F32)
    for n in range(2):
        S.dma("sp", C["a0"][:, n, :], A["rwkv_a0"][l][n].rearrange("(h j) -> j h", j=64), writes=[C["a0"]], slow=True)
    for nm, src in (("kk_", A["rwkv_k_k"][l]), ("ka", A["rwkv_k_a"][l])):
        C[nm] = K.alloc([64, 8], F32)
        S.dma("sp", C[nm][:, :], src.rearrange("(h j) -> j h", j=64), writes=[C[nm]], slow=True)
    C["rk"] = K.alloc([64, 8], F32)
    S.dma("sp", C["rk"][:, :], A["rwkv_r_k"][l].rearrange("h j -> j h"), writes=[C["rk"]], slow=True)
    C["omka"] = K.alloc([64, 8], F32)
    S.op("dve", lambda e: e.tensor_scalar(out=C["omka"][:, :], in0=C["ka"][:, :], scalar1=-1.0, scalar2=1.0,
                                          op0=ALU.mult, op1=ALU.add), reads=[C["ka"]], writes=[C["omka"]])
    C["mu"] = K.alloc([64, 3, 31], F32)
    S.op("dve", lambda e: e.memset(C["mu"][:, :, :], 0.0), writes=[C["mu"]])
    for m in range(2):
        S.dma("sp", C["mu"][:, m, 0:30], A["rwkv_mu"][l][m, 0:1920].rearrange("(g j) -> j g", j=64), writes=[C["mu"]],
              slow=True)
        S.dma("sp", C["mu"][0:32, m, 30:31], A["rwkv_mu"][l][m, 1920:1952].rearrange("(g j) -> j g", j=32),
              writes=[C["mu"]], slow=True)
    S.op("dve", lambda e: e.tensor_tensor(out=C["mu"][:, 2, :], in0=C["mu"][:, 0, :], in1=C["mu"][:, 1, :], op=ALU.add),
         reads=[C["mu"]], writes=[C["mu"]])
    S.op("dve", lambda e: e.tensor_scalar(out=C["mu"][:, 2, :], in0=C["mu"][:, 2, :], scalar1=-1.0, scalar2=1.0,
                                          op0=ALU.mult, op1=ALU.add), reads=[C["mu"]], writes=[C["mu"]])
    C["lng"] = K.alloc([128, 512], F32)
    S.dma("sp", C["lng"][:, :], A["rwkv_ln_g"][l:l + 1, :].broadcast_to([128, 512]), writes=[C["lng"]])
    C["lnb"] = K.alloc([128, 512], F32)
    S.dma("sp", C["lnb"][:, :], A["rwkv_ln_b"][l:l + 1, :].broadcast_to([128, 512]), writes=[C["lnb"]])
    C["gneps"] = K.alloc([128, 1], F32)
    S.op("dve", lambda e: e.memset(C["gneps"][:, :], GN_EPS), writes=[C["gneps"]])
    return C


def shift_pass(K, SC, A, l, ntok, seglen):
    S = K.S
    K.new_pass()
    mu = K.alloc([128, 3, 16], F32)
    S.op("dve", lambda e: e.memset(mu[:, :, :], 0.0), writes=[mu])
    for m in range(2):
        S.dma("sp", mu[:, m, 0:15], A["rwkv_mu"][l][m, 0:1920].rearrange("(g j) -> j g", j=128), writes=[mu], slow=True)
        S.dma("sp", mu[0:32, m, 15:16], A["rwkv_mu"][l][m, 1920:1952].rearrange("(g j) -> j g", j=32), writes=[mu],
              slow=True)
    S.op("dve", lambda e: e.tensor_tensor(out=mu[:, 2, :], in0=mu[:, 0, :], in1=mu[:, 1, :], op=ALU.add),
         reads=[mu], writes=[mu])
    S.op("dve", lambda e: e.tensor_scalar(out=mu[:, 2, :], in0=mu[:, 2, :], scalar1=-1.0, scalar2=1.0, op0=ALU.mult,
                                          op1=ALU.add), reads=[mu], writes=[mu])
    xp = Pool(K, 2, [128, 16, 514], F32)
    ap = Pool(K, 2, [128, 16, 512], F32)
    RW = SC["RWT"]

    def body(mt):
        t0 = mt * 512
        X = xp.get()
        lo, hi = max(t0 - 1, 0), min(t0 + 513, ntok)
        a_, b_ = lo - (t0 - 1), hi - (t0 - 1)
        S.dma("sp", X[:, 0:15, a_:b_], RW[0:1920, lo:hi].rearrange("(g j) t -> j g t", j=128), writes=[X])
        S.dma("sp", X[0:32, 15, a_:b_], RW[1920:1952, lo:hi], writes=[X])
        seg = t0 // seglen
        if t0 == 0:
            S.op("pool", lambda e: e.memset(X[:, :, 0:1], 0.0), writes=[X])
        elif t0 % seglen == 0:
            S.op("pool", lambda e: e.tensor_scalar(out=X[:, :, 0:1], in0=X[:, :, 0:1], scalar1=K.flags[:, seg - 1:seg],
                                                   scalar2=0.0, op0=ALU.mult, op1=ALU.add), reads=[X, K.flags], writes=[X])
        if t0 + 512 >= ntok:
            S.op("pool", lambda e: e.memset(X[:, :, 513:514], 0.0), writes=[X])
        elif (t0 + 512) % seglen == 0:
            S.op("pool", lambda e: e.tensor_scalar(out=X[:, :, 513:514], in0=X[:, :, 513:514],
                                                   scalar1=K.flags[:, seg:seg + 1], scalar2=0.0, op0=ALU.mult,
                                                   op1=ALU.add), reads=[X, K.flags], writes=[X])
        acc = ap.get()
        for fc in range(16):
            eng = "dve"
            S.op("pool", lambda e, fc=fc: e.tensor_scalar(out=acc[:, fc, :], in0=X[:, fc, 1:513], scalar1=mu[:, 2, fc:fc + 1],
                                                       scalar2=0.0, op0=ALU.mult, op1=ALU.add),
                 reads=[X, mu], writes=[acc])
            S.op(eng, lambda e, fc=fc: e.scalar_tensor_tensor(out=acc[:, fc, :], in0=X[:, fc, 0:512],
                                                              scalar=mu[:, 0, fc:fc + 1], in1=acc[:, fc, :],
                                                              op0=ALU.mult, op1=ALU.add), reads=[X, mu, acc], writes=[acc])
            S.op(eng, lambda e, fc=fc: e.scalar_tensor_tensor(out=acc[:, fc, :], in0=X[:, fc, 2:514],
                                                              scalar=mu[:, 1, fc:fc + 1], in1=acc[:, fc, :],
                                                              op0=ALU.mult, op1=ALU.add), reads=[X, mu, acc], writes=[acc])
        S.dma("pool", SC["RWS"][0:1920, t0:t0 + 512].rearrange("(g j) t -> j g t", j=128), acc[:, 0:15, :], reads=[acc])
        S.dma("pool", SC["RWS"][1920:1952, t0:t0 + 512], acc[0:32, 15, :], reads=[acc])

    for mt in range(ntok // 512):
        body(mt)


def rwkv_pools(K):
    P = {}
    P["Pt"] = Pool(K, 2, [64, 31, 128], F32)
    P["tw"] = Pool(K, 2, [64, 128], BF16)
    P["ad"] = Pool(K, 2, [64, 128], BF16)
    P["sg"] = Pool(K, 2, [128, 512], F32)
    P["f8"] = Pool(K, 11, [64, 8, 128], F32)
    P["b8"] = Pool(K, 12, [64, 8, 128], BF16)
    P["tok"] = Pool(K, 12, [128, 512], BF16)
    P["vf"] = Pool(K, 2, [128, 512], F32)
    P["pl"] = Pool(K, 2, [64, 8], F32)
    P["mat"] = Pool(K, 16, [128, 4, 128], BF16)
    P["res"] = Pool(K, 12, [128, 4, 128], BF16)
    P["w"] = Pool(K, 2, [128, 512], F32)
    P["rkc"] = Pool(K, 2, [128, 8], F32)
    return P


def rwkv_prep(K, SC, C, P, c, nchunk, cps, n, want_g):
    S = K.S
    t0 = c * 128
    ntok = nchunk * 128
    seg, cis = divmod(c, cps)
    Pt = P["Pt"].get()
    S.dma("sp", Pt[:, 0:30, :], SC["RWS"][0:1920, t0:t0 + 128].rearrange("(g j) t -> j g t", j=64), writes=[Pt])
    S.dma("sp", Pt[0:32, 30, :], SC["RWS"][1920:1952, t0:t0 + 128], writes=[Pt])
    if RL <= 1:
        return None
    rT, kT, vT = Pt[:, 0:8, :], Pt[:, 8:16, :], Pt[:, 16:24, :]
    tw = P["tw"].get()
    S.op("act", lambda e: e.activation(out=tw[:, :], in_=Pt[:, 24 + n, :], func=AF.Tanh), reads=[Pt], writes=[tw])
    pu = K.psum()
    S.op("pe", lambda e: e.matmul(pu[:, :], tw[:, :], C["wup"][:, n, :], start=True, stop=True),
         reads=[tw, C["wup"]], writes=[pu])
    sg = P["sg"].get()
    S.op("dve", lambda e: e.tensor_tensor(out=sg[:, :], in0=pu[:, :], in1=C["w0"][:, n, :], op=ALU.add),
         reads=[pu, C["w0"]], writes=[sg])
    S.op("act", lambda e: e.activation(out=sg[:, :], in_=sg[:, :], func=AF.Sigmoid), reads=[sg], writes=[sg])
    if RL <= 2:
        return None
    uin = K.uinc if n == 0 else K.mskb
    uex = K.uexc if n == 0 else K.ugt
    eP, eN, ePx = P["f8"].get(), P["f8"].get(), P["f8"].get()
    for hq in range(2):
        pi, px = K.psum(), K.psum()
        for hh in range(4):
            h = hq * 4 + hh
            S.op("pe", lambda e, pi=pi, hh=hh, h=h: e.matmul(pi[0:64, hh * 128:(hh + 1) * 128], sg[:, h * 64:(h + 1) * 64],
                                                             uin[:, :], start=True, stop=True),
                 reads=[sg, uin], writes=[pi])
            S.op("pe", lambda e, px=px, hh=hh, h=h: e.matmul(px[0:64, hh * 128:(hh + 1) * 128], sg[:, h * 64:(h + 1) * 64],
                                                             uex[:, :], start=True, stop=True),
                 reads=[sg, uex], writes=[px])
        sl = slice(hq * 4, hq * 4 + 4)
        S.op("act", lambda e, pi=pi, sl=sl: e.activation(out=eP[:, sl, :].rearrange("p a b -> p (a b)"), in_=pi[0:64, :],
                                                         func=AF.Exp, scale=-LW_C), reads=[pi], writes=[eP])
        S.op("act", lambda e, pi=pi, sl=sl: e.activation(out=eN[:, sl, :].rearrange("p a b -> p (a b)"), in_=pi[0:64, :],
                                                         func=AF.Exp, scale=LW_C), reads=[pi], writes=[eN])
        S.op("act", lambda e, px=px, sl=sl: e.activation(out=ePx[:, sl, :].rearrange("p a b -> p (a b)"), in_=px[0:64, :],
                                                         func=AF.Exp, scale=-LW_C), reads=[px], writes=[ePx])
    last = 127 if n == 0 else 0
    pl = P["pl"].get()
    S.op("dve", lambda e: e.tensor_copy(out=pl[:, :], in_=eP[:, :, last]), reads=[eP], writes=[pl])
    if RL <= 3:
        return None
    ad = P["ad"].get()
    S.op("act", lambda e: e.copy(out=ad[:, :], in_=Pt[:, 26 + n, :]), reads=[Pt], writes=[ad])
    ic = P["f8"].get()
    for hq in range(2):
        pa = K.psum()
        for hh in range(4):
            h = hq * 4 + hh
            S.op("pe", lambda e, pa=pa, hh=hh, h=h: e.matmul(pa[0:64, hh * 128:(hh + 1) * 128],
                                                             C["aup"][:, n, h * 64:(h + 1) * 64], ad[:, :],
                                                             start=True, stop=True), reads=[C["aup"], ad], writes=[pa])
        sl = slice(hq * 4, hq * 4 + 4)
        S.op("dve", lambda e, pa=pa, sl=sl: e.tensor_tensor(
            out=ic[:, sl, :], in0=pa[0:64, :].rearrange("p (a b) -> p a b", a=4),
            in1=C["a0"][:, n, sl].unsqueeze(2).broadcast_to([64, 4, 128]), op=ALU.add), reads=[pa, C["a0"]], writes=[ic])
    S.op("act", lambda e: e.activation(out=ic[:, :, :], in_=ic[:, :, :], func=AF.Sigmoid), reads=[ic], writes=[ic])
    if RL <= 4:
        return None
    kk = P["f8"].get()
    sq = P["f8"].get()
    h8 = lambda t_: t_[:, :].unsqueeze(2).broadcast_to([64, 8, 128])
    S.op("dve", lambda e: e.tensor_tensor(out=kk[:, :, :], in0=kT, in1=h8(C["kk_"]), op=ALU.mult),
         reads=[Pt, C["kk_"]], writes=[kk])
    S.op("pool", lambda e: e.tensor_tensor(out=sq[:, :, :], in0=kk[:, :, :], in1=kk[:, :, :], op=ALU.mult),
         reads=[kk], writes=[sq])
    for hq in range(2):
        pn = K.psum()
        sl = slice(hq * 4, hq * 4 + 4)
        S.op("pe", lambda e, pn=pn, sl=sl: e.matmul(pn[0:64, :], K.onesf[0:64, 0:64],
                                                    sq[:, sl, :].rearrange("p a b -> p (a b)"), start=True, stop=True),
             reads=[K.onesf, sq], writes=[pn])
        S.op("dve", lambda e, pn=pn, sl=sl: e.tensor_scalar(out=sq[:, sl, :].rearrange("p a b -> p (a b)"), in0=pn[0:64, :],
                                                            scalar1=1e-24, scalar2=0.0, op0=ALU.max, op1=ALU.add),
             reads=[pn], writes=[sq])
    S.op("act", lambda e: e.activation(out=sq[:, :, :], in_=sq[:, :, :], func=AF.Ln), reads=[sq], writes=[sq])
    S.op("act", lambda e: e.activation(out=sq[:, :, :], in_=sq[:, :, :], func=AF.Exp, scale=-0.5), reads=[sq], writes=[sq])
    S.op("dve", lambda e: e.tensor_tensor(out=kk[:, :, :], in0=kk[:, :, :], in1=sq[:, :, :], op=ALU.mult),
         reads=[kk, sq], writes=[kk])
    if RL <= 5:
        return None
    km = P["f8"].get()
    S.op("pool", lambda e: e.tensor_tensor(out=km[:, :, :], in0=ic[:, :, :], in1=h8(C["ka"]), op=ALU.mult),
         reads=[ic, C["ka"]], writes=[km])
    S.op("pool", lambda e: e.tensor_tensor(out=km[:, :, :], in0=km[:, :, :], in1=h8(C["omka"]), op=ALU.add),
         reads=[km, C["omka"]], writes=[km])
    S.op("pool", lambda e: e.tensor_tensor(out=km[:, :, :], in0=km[:, :, :], in1=kT, op=ALU.mult),
         reads=[km, Pt], writes=[km])
    bb = P["f8"].get()
    S.op("dve", lambda e: e.tensor_tensor(out=bb[:, :, :], in0=kk[:, :, :], in1=ic[:, :, :], op=ALU.mult),
         reads=[kk, ic], writes=[bb])
    RtT, AtT, BhT, KhT = [P["b8"].get() for _ in range(4)]
    BbT, KbT = P["f8"].get(), P["f8"].get()
    S.op("dve", lambda e: e.tensor_tensor(out=RtT[:, :, :], in0=rT, in1=eP[:, :, :], op=ALU.mult),
         reads=[Pt, eP], writes=[RtT])
    S.op("dve", lambda e: e.scalar_tensor_tensor(out=AtT[:, :, :], in0=kk[:, :, :], scalar=-1.0, in1=ePx[:, :, :],
                                                 op0=ALU.mult, op1=ALU.mult), reads=[kk, ePx], writes=[AtT])
    S.op("pool", lambda e: e.tensor_tensor(out=bb[:, :, :], in0=bb[:, :, :], in1=eN[:, :, :], op=ALU.mult),
         reads=[bb, eN], writes=[bb])
    S.op("pool", lambda e: e.tensor_tensor(out=sq[:, :, :], in0=km[:, :, :], in1=eN[:, :, :], op=ALU.mult),
         reads=[km, eN, sq], writes=[sq])
    S.op("act", lambda e: e.copy(out=BhT[:, :, :], in_=bb[:, :, :]), reads=[bb], writes=[BhT])
    S.op("act", lambda e: e.copy(out=KhT[:, :, :], in_=sq[:, :, :]), reads=[sq], writes=[KhT])
    plb = pl[:, :].unsqueeze(2).broadcast_to([64, 8, 128])
    S.op("dve", lambda e: e.tensor_tensor(out=BbT[:, :, :], in0=bb[:, :, :], in1=plb, op=ALU.mult),
         reads=[bb, pl], writes=[BbT])
    S.op("dve", lambda e: e.tensor_tensor(out=KbT[:, :, :], in0=sq[:, :, :], in1=plb, op=ALU.mult),
         reads=[sq, pl], writes=[KbT])
    if RL <= 6:
        return None
    pv = K.psum()
    for h in range(8):
        S.op("pe", lambda e, h=h: e.matmul(pv[:, h * 64:(h + 1) * 64], Pt[:, 16 + h, :], K.identf[0:64, 0:64],
                                           start=True, stop=True), reads=[Pt, K.identf], writes=[pv])
    if RL <= 6.2:
        return None
    Vf = P["vf"].get()
    Vb = P["tok"].get()
    S.op("act", lambda e: e.copy(out=Vf[:, :], in_=pv[:, :]), reads=[pv], writes=[Vf])
    S.op("dve", lambda e: e.tensor_copy(out=Vb[:, :], in_=Vf[:, :]), reads=[Vf], writes=[Vb])
    if RL <= 6.5:
        return None
    outs = []
    for src in (BbT, KbT):
        pb = K.psum()
        for h in range(8):
            S.op("pe", lambda e, h=h, src=src, pb=pb: e.matmul(pb[:, h * 64:(h + 1) * 64], src[:, h, :],
                                                               K.identf[0:64, 0:64], start=True, stop=True),
                 reads=[src, K.identf], writes=[pb])
        o = P["tok"].get()
        S.op("act", lambda e, o=o, pb=pb: e.copy(out=o[:, :], in_=pb[:, :]), reads=[pb], writes=[o])
        outs.append(o)
    T = {"RtT": RtT, "AtT": AtT, "BhT": BhT, "KhT": KhT, "Vb": Vb, "Vf": Vf, "Bb": outs[0], "Kb": outs[1], "pl": pl}
    if RL <= 7:
        return None
    S.op("dve", lambda e: e.tensor_tensor(out=km[:, :, :], in0=km[:, :, :], in1=rT, op=ALU.mult),
         reads=[km, Pt], writes=[km])
    S.op("dve", lambda e: e.tensor_tensor(out=km[:, :, :], in0=km[:, :, :], in1=h8(C["rk"]), op=ALU.mult),
         reads=[km, C["rk"]], writes=[km])
    pr = K.psum()
    for h in range(8):
        S.op("pe", lambda e, h=h: e.matmul(pr[:, h:h + 1], km[:, h, :], K.onesf[0:64, 0:1], start=True, stop=True),
             reads=[km, K.onesf], writes=[pr])
    rkc = P["rkc"].get()
    S.op("dve", lambda e: e.tensor_copy(out=rkc[:, :], in_=pr[:, 0:8]), reads=[pr], writes=[rkc])
    T["rk"] = rkc
    if RL <= 8:
        return None
    if want_g:
        sgd = P["b8"].get()
        S.op("act", lambda e: e.activation(out=sgd[:, 0:3, :], in_=Pt[:, 28:31, :], func=AF.Sigmoid), reads=[Pt],
             writes=[sgd])
        pg = K.psum()
        for q in range(3):
            rows = 64 if q < 2 else 32
            S.op("pe", lambda e, q=q, rows=rows: e.matmul(pg[:, :], sgd[0:rows, q, :], C["gup"][0:rows, q, :],
                                                          start=(q == 0), stop=(q == 2)), reads=[sgd, C["gup"]], writes=[pg])
        gt = P["w"].get()
        S.op("act", lambda e: e.copy(out=gt[:, :], in_=pg[:, :]), reads=[pg], writes=[gt])
        T["g"] = gt
    return T


def rwkv_intra(K, P, T, n):
    S = K.S
    AtT, BhT, KhT, RtT = T["AtT"], T["BhT"], T["KhT"], T["RtT"]
    m_strict_sr = K.ugt if n == 0 else K.uexc
    m_strict_rs = K.uexc if n == 0 else K.ugt
    m_incl_st = K.uinc if n == 0 else K.mskb
    res = {"TT": [], "AakT": [], "ArbT": [], "ArkT": []}

    def prod(lhs, rhs, hq, mask, pool="mat"):
        pp = K.psum()
        for hh in range(4):
            h = hq * 4 + hh
            S.op("pe", lambda e, hh=hh, h=h: e.matmul(pp[:, hh * 128:(hh + 1) * 128], lhs[:, h, :], rhs[:, h, :],
                                                      start=True, stop=True), reads=[lhs, rhs], writes=[pp])
        o = P[pool].get()
        S.op("dve", lambda e: e.tensor_tensor(out=o[:, :, :], in0=pp[:, :].rearrange("p (a b) -> p a b", a=4),
                                              in1=mask[:, :].unsqueeze(1).broadcast_to([128, 4, 128]), op=ALU.mult),
             reads=[pp, mask], writes=[o])
        return o

    def mm4(lhs, rhs, addto=None, eng="act", pool="mat"):
        pp = K.psum()
        for hh in range(4):
            S.op("pe", lambda e, hh=hh: e.matmul(pp[:, hh * 128:(hh + 1) * 128], lhs[:, hh, :], rhs[:, hh, :],
                                                 start=True, stop=True), reads=[lhs, rhs], writes=[pp])
        o = P[pool].get()
        if addto is None:
            S.op(eng, (lambda e: e.copy(out=o[:, :, :].rearrange("p a b -> p (a b)"), in_=pp[:, :])) if eng == "act" else
                 (lambda e: e.tensor_copy(out=o[:, :, :].rearrange("p a b -> p (a b)"), in_=pp[:, :])),
                 reads=[pp], writes=[o])
        else:
            S.op("dve", lambda e: e.tensor_tensor(out=o[:, :, :].rearrange("p a b -> p (a b)"), in0=pp[:, :],
                                                  in1=addto[:, :, :].rearrange("p a b -> p (a b)"), op=ALU.add),
                 reads=[pp, addto], writes=[o])
        return o

    Ms = [prod(AtT, BhT, hq, m_strict_sr) for hq in range(2)]
    MTs = [prod(BhT, AtT, hq, m_strict_rs) for hq in range(2)]
    TTs = []
    for hq in range(2):
        TT = P["mat"].get()
        S.op("pool", lambda e, TT=TT, MT=MTs[hq]: e.tensor_tensor(
            out=TT[:, :, :], in0=MT[:, :, :], in1=K.ident[:, :].unsqueeze(1).broadcast_to([128, 4, 128]), op=ALU.add),
            reads=[MTs[hq], K.ident], writes=[TT])
        TTs.append(TT)
    for hq in range(2):
        res["AakT"].append(prod(KhT, AtT, hq, m_strict_rs, pool="res"))
        res["ArbT"].append(prod(BhT, RtT, hq, m_incl_st, pool="res"))
        res["ArkT"].append(prod(KhT, RtT, hq, m_incl_st, pool="res"))
    for k in range(1, 7):
        M2s, MT2s = [], []
        for hq in range(2):
            M2s.append(mm4(MTs[hq], Ms[hq], eng="act"))
        for hq in range(2):
            MT2s.append(mm4(Ms[hq], MTs[hq], eng="act") if k < 6 else None)
        for hq in range(2):
            TTs[hq] = mm4(M2s[hq], TTs[hq], addto=TTs[hq], pool=("res" if k == 6 else "mat"))
        Ms, MTs = M2s, MT2s
    res["TT"] = TTs
    return res


def rwkv_seq(K, P, T, I_, St, Sb):
    S = K.S
    AtT, RtT, Vb, Bb, Kb, pl = T["AtT"], T["RtT"], T["Vb"], T["Bb"], T["Kb"], T["pl"]
    pw = K.psum()
    for h in range(8):
        hq, hh = divmod(h, 4)
        S.op("pe", lambda e, h=h: e.matmul(pw[:, h * 64:(h + 1) * 64], AtT[:, h, :], Sb[:, h, :], start=True, stop=False),
             reads=[AtT, Sb], writes=[pw])
        S.op("pe", lambda e, h=h, hq=hq, hh=hh: e.matmul(pw[:, h * 64:(h + 1) * 64], I_["AakT"][hq][:, hh, :],
                                                         Vb[:, h * 64:(h + 1) * 64], start=False, stop=True),
             reads=[I_["AakT"][hq], Vb], writes=[pw])
    Wb = P["tok"].get()
    S.op("act", lambda e: e.copy(out=Wb[:, :], in_=pw[:, :]), reads=[pw], writes=[Wb])
    pu = K.psum()
    for h in range(8):
        hq, hh = divmod(h, 4)
        S.op("pe", lambda e, h=h, hq=hq, hh=hh: e.matmul(pu[:, h * 64:(h + 1) * 64], I_["TT"][hq][:, hh, :],
                                                         Wb[:, h * 64:(h + 1) * 64], start=True, stop=True),
             reads=[I_["TT"][hq], Wb], writes=[pu])
    Ub = P["tok"].get()
    S.op("dve", lambda e: e.tensor_copy(out=Ub[:, :], in_=pu[:, :]), reads=[pu], writes=[Ub])
    py = K.psum()
    for h in range(8):
        hq, hh = divmod(h, 4)
        S.op("pe", lambda e, h=h: e.matmul(py[:, h * 64:(h + 1) * 64], RtT[:, h, :], Sb[:, h, :], start=True, stop=False),
             reads=[RtT, Sb], writes=[py])
        S.op("pe", lambda e, h=h, hq=hq, hh=hh: e.matmul(py[:, h * 64:(h + 1) * 64], I_["ArbT"][hq][:, hh, :],
                                                         Ub[:, h * 64:(h + 1) * 64], start=False, stop=False),
             reads=[I_["ArbT"][hq], Ub], writes=[py])
        S.op("pe", lambda e, h=h, hq=hq, hh=hh: e.matmul(py[:, h * 64:(h + 1) * 64], I_["ArkT"][hq][:, hh, :],
                                                         Vb[:, h * 64:(h + 1) * 64], start=False, stop=True),
             reads=[I_["ArkT"][hq], Vb], writes=[py])
    ps = K.psum()
    for h in range(8):
        S.op("pe", lambda e, h=h: e.matmul(ps[0:64, h * 64:(h + 1) * 64], Bb[:, h * 64:(h + 1) * 64],
                                           Ub[:, h * 64:(h + 1) * 64], start=True, stop=False),
             reads=[Bb, Ub], writes=[ps])
        S.op("pe", lambda e, h=h: e.matmul(ps[0:64, h * 64:(h + 1) * 64], Kb[:, h * 64:(h + 1) * 64],
                                           Vb[:, h * 64:(h + 1) * 64], start=False, stop=True),
             reads=[Kb, Vb], writes=[ps])
    S.op("dve", lambda e: e.tensor_tensor(out=St[:, :, :], in0=St[:, :, :],
                                          in1=pl[:, :].unsqueeze(2).broadcast_to([64, 8, 64]), op=ALU.mult),
         reads=[St, pl], writes=[St])
    S.op("dve", lambda e: e.tensor_tensor(out=St[:, :, :], in0=St[:, :, :],
                                          in1=ps[0:64, :].rearrange("p (a b) -> p a b", a=8), op=ALU.add),
         reads=[St, ps], writes=[St])
    S.op("act", lambda e: e.copy(out=Sb[:, :, :], in_=St[:, :, :]), reads=[St], writes=[Sb])
    return py


def rwkv_dir_pass(K, SC, C, nchunk, cps, n):
    S = K.S
    P = rwkv_pools(K)
    St = K.alloc([64, 8, 64], F32)
    Sb = K.alloc([64, 8, 64], BF16)
    S.op("dve", lambda e: e.memset(St[:, :, :], 0.0), writes=[St])
    S.op("dve", lambda e: e.memset(Sb[:, :, :], 0.0), writes=[Sb])
    ybp = Pool(K, 2, [128, 520], F32)
    y1p = Pool(K, 2, [128, 520], F32)
    yw = Pool(K, 2, [128, 8, 64], F32)
    yc = Pool(K, 2, [128, 8, 64], F32)
    smp = Pool(K, 4, [128, 32], F32)
    outp = Pool(K, 2, [128, 512], BF16)

    def body(c):
        seg, cis = divmod(c, cps)
        t0 = c * 128
        first = (cis == 0) if n == 0 else (cis == cps - 1)
        edge = (c == 0) if n == 0 else (c == nchunk - 1)
        if first and not edge:
            fl = K.flags[0:64, seg - 1:seg] if n == 0 else K.flags[0:64, seg:seg + 1]
            S.op("dve", lambda e: e.tensor_scalar(out=St[:, :, :], in0=St[:, :, :], scalar1=fl, scalar2=0.0,
                                                  op0=ALU.mult, op1=ALU.add), reads=[St, K.flags], writes=[St])
            S.op("act", lambda e: e.copy(out=Sb[:, :, :], in_=St[:, :, :]), reads=[St], writes=[Sb])
        T = Tn.pop(c)
        nx = c + 1 if n == 0 else c - 1
        if 0 <= nx < nchunk:
            Tn[nx] = rwkv_prep(K, SC, C, P, nx, nchunk, cps, n, want_g=(n == 0))
        if T is None or RL <= 9:
            return
        I_ = rwkv_intra(K, P, T, n)
        if RL <= 10:
            return
        py = rwkv_seq(K, P, T, I_, St, Sb)
        if RL <= 11:
            return
        if n == 1:
            yb = ybp.get()
            S.op("act", lambda e: e.copy(out=yb[:, 0:512], in_=py[:, :]), reads=[py], writes=[yb])
            S.op("dve", lambda e: e.tensor_copy(out=yb[:, 512:520], in_=T["rk"][:, :]), reads=[T["rk"]], writes=[yb])
            S.dma("pool", SC["YB"][t0:t0 + 128, :], yb[:, :], reads=[yb])
            return
        y1 = y1p.get()
        S.dma("sp", y1[:, :], SC["YB"][t0:t0 + 128, :], writes=[y1])
        y = yw.get()
        yv = y[:, :, :].rearrange("p a b -> p (a b)")
        S.op("dve", lambda e: e.tensor_tensor(out=yv, in0=py[:, :], in1=y1[:, 0:512], op=ALU.add),
             reads=[py, y1], writes=[y])
        sm = smp.get()
        S.op("dve", lambda e: e.tensor_reduce(out=sm[:, 0:8], in_=y[:, :, :], axis=AX.X, op=ALU.add),
             reads=[y], writes=[sm])
        S.op("dve", lambda e: e.tensor_scalar(out=sm[:, 0:8], in0=sm[:, 0:8], scalar1=1.0 / 64, scalar2=0.0,
                                              op0=ALU.mult, op1=ALU.add), reads=[sm], writes=[sm])
        ycn = yc.get()
        S.op("dve", lambda e: e.tensor_tensor(out=ycn[:, :, :], in0=y[:, :, :],
                                              in1=sm[:, 0:8].unsqueeze(2).broadcast_to([128, 8, 64]), op=ALU.subtract),
             reads=[y, sm], writes=[ycn])
        S.op("pool", lambda e: e.tensor_tensor(out=y[:, :, :], in0=ycn[:, :, :], in1=ycn[:, :, :], op=ALU.mult),
             reads=[ycn], writes=[y])
        S.op("dve", lambda e: e.tensor_reduce(out=sm[:, 8:16], in_=y[:, :, :], axis=AX.X, op=ALU.add),
             reads=[y], writes=[sm])
        S.op("act", lambda e: e.activation(out=sm[:, 8:16], in_=sm[:, 8:16], func=AF.Sqrt, bias=C["gneps"][:, 0:1],
                                           scale=1.0 / 64), reads=[sm, C["gneps"]], writes=[sm])
        S.op("dve", lambda e: e.reciprocal(out=sm[:, 8:16], in_=sm[:, 8:16]), reads=[sm], writes=[sm])
        S.op("dve", lambda e: e.tensor_tensor(out=ycn[:, :, :], in0=ycn[:, :, :],
                                              in1=sm[:, 8:16].unsqueeze(2).broadcast_to([128, 8, 64]), op=ALU.mult),
             reads=[ycn, sm], writes=[ycn])
        ycv = ycn[:, :, :].rearrange("p a b -> p (a b)")
        S.op("pool", lambda e: e.tensor_tensor(out=ycv, in0=ycv, in1=C["lng"][:, :], op=ALU.mult),
             reads=[ycn, C["lng"]], writes=[ycn])
        S.op("pool", lambda e: e.tensor_tensor(out=ycv, in0=ycv, in1=C["lnb"][:, :], op=ALU.add),
             reads=[ycn, C["lnb"]], writes=[ycn])
        S.op("dve", lambda e: e.tensor_tensor(out=sm[:, 16:24], in0=T["rk"][:, :], in1=y1[:, 512:520], op=ALU.add),
             reads=[T["rk"], y1], writes=[sm])
        S.op("dve", lambda e: e.tensor_tensor(out=y[:, :, :], in0=T["Vf"][:, :].rearrange("p (a b) -> p a b", a=8),
                                              in1=sm[:, 16:24].unsqueeze(2).broadcast_to([128, 8, 64]), op=ALU.mult),
             reads=[T["Vf"], sm, y], writes=[y])
        S.op("pool", lambda e: e.tensor_tensor(out=ycv, in0=ycv, in1=yv, op=ALU.add), reads=[ycn, y], writes=[ycn])
        o = outp.get()
        S.op("dve", lambda e: e.tensor_tensor(out=o[:, :], in0=ycv, in1=T["g"][:, :], op=ALU.mult),
             reads=[ycn, T["g"]], writes=[o])
        S.dma("pool", SC["YR"][t0:t0 + 128, :], o[:, :], reads=[o])

    order = list(range(nchunk)) if n == 0 else list(range(nchunk - 1, -1, -1))
    Tn = {order[0]: rwkv_prep(K, SC, C, P, order[0], nchunk, cps, n, want_g=(n == 0))}
    for c in order:
        body(c)


def cast_weights(K, src, dst, rows, cols):
    for r in range(0, rows, 128):
        K.S.dma("pool", dst[r:r + 128, :], src[r:r + 128, :])


def zero_fill(K, dst, rows, cols, dt):
    z = K.alloc([128, cols], dt)
    K.S.op("pool", lambda e: e.memset(z[:, :], 0.0), writes=[z])
    for r in range(0, rows, 128):
        K.S.dma("pool", dst[r:r + 128, :], z[:, :], reads=[z])


def build(ntok, seglen=4096, depth=DEPTH, mixer=True):
    nc = bass.Bass("TRN2", target_bir_lowering=False)
    es = ExitStack()
    A = {}
    nseg = ntok // seglen
    plan, pats = attn_plan(nseg, seglen)
    ncfg = pats[False].shape[0]

    def inp(name, shape, dt=F32):
        A[name] = nc.dram_tensor(name, list(shape), dt, kind="ExternalInput").ap()
        return A[name]

    import os
    dbg = os.environ.get("MK_DBG", "").split(",")

    def scr(name, shape, dt=F32):
        if name in dbg:
            return nc.dram_tensor(name, list(shape), dt, kind="ExternalOutput").ap()
        return nc.dram_tensor(name, list(shape), dt).ap()

    x = inp("x", [ntok, D])
    inp("norm_g", [DEPTH, 6, D])
    inp("ff_w_in", [DEPTH, 2, D, 2 * FF])
    inp("ff_w_out", [DEPTH, 2, FF, D])
    inp("w_in", [DEPTH, D, IN_W])
    inp("w_branch", [DEPTH, 2048, D])
    inp("w_out", [DEPTH, D, D])
    inp("atab", [DEPTH, ncfg, 128, 8, 128])
    inp("ssm_conv_w", [DEPTH, 5, 1536])
    inp("ssm_conv_b", [DEPTH, 1536])
    inp("ssm_dt_bias", [DEPTH, 2, 16])
    inp("ssm_a_log", [DEPTH, 2, 16])
    inp("ssm_d", [DEPTH, 2, 16])
    inp("ssm_norm_g", [DEPTH, 1024])
    inp("flags", [128, 4])
    inp("cf32", [128, 7, 128])
    inp("rwkv_mu", [DEPTH, 2, 1952])
    inp("rwkv_w0", [DEPTH, 2, 512])
    inp("rwkv_w_up", [DEPTH, 2, 64, 512])
    inp("rwkv_a0", [DEPTH, 2, 512])
    inp("rwkv_a_up", [DEPTH, 2, 64, 512])
    inp("rwkv_g_up", [DEPTH, 160, 512])
    inp("rwkv_k_k", [DEPTH, 512])
    inp("rwkv_k_a", [DEPTH, 512])
    inp("rwkv_r_k", [DEPTH, 8, 64])
    inp("rwkv_ln_g", [DEPTH, 512])
    inp("rwkv_ln_b", [DEPTH, 512])
    inp("ident", [128, 128], BF16)
    y = nc.dram_tensor("y", [ntok, D], F32, kind="ExternalOutput").ap()
    xs = scr("xs", [ntok, D])
    wi_bf = [[scr(f"wi{l}{f}", [D, 2 * FF], BF16) for f in range(2)] for l in range(DEPTH)]
    wo_bf = [[scr(f"wo{l}{f}", [FF, D], BF16) for f in range(2)] for l in range(DEPTH)]
    win_bf = [scr(f"win{l}", [D, IN_W], BF16) for l in range(DEPTH)]
    wbr_bf = [scr(f"wbr{l}", [2048, D], BF16) for l in range(DEPTH)]
    wout_bf = [scr(f"wout{l}", [D, D], BF16) for l in range(DEPTH)]
    SC = {"QT": scr("QT", [512, ntok], BF16), "KT": scr("KT", [512, ntok], BF16), "V": scr("V", [ntok, 512], BF16),
          "Z": scr("Z", [ntok, 1024]), "DT": scr("DT", [ntok, 32]), "G": scr("G", [ntok, 3072]),
          "XBCT": scr("XBCT", [1536, ntok]), "RWT": scr("RWT", [1952, ntok]),
          "HB": scr("HB", [ntok // 128, 128, 1024], BF16), "YB": scr("YB", [ntok, 520]), "RWS": scr("RWS", [1952, ntok]),
          "PX": scr("PX", [ntok // 128, 128, 12, 128], BF16), "PS": scr("PS", [ntok // 128, 128, 1024], BF16),
          "PB": scr("PB", [ntok // 128, 128, 256], BF16), "PV": scr("PV", [ntok // 128, 128, 256]),
          "YA": scr("YA", [ntok, 512], BF16), "YS": scr("YS", [ntok, 1024], BF16), "YR": scr("YR", [ntok, 512], BF16)}
    if "DY" in dbg:
        SC["DY"] = scr("DY", [ntok, 1024]); SC["DV"] = scr("DV", [ntok, 256]); SC["DM"] = scr("DM", [ntok // 128, 128, 16, 128], BF16)
    K = Ctx(nc, es)
    S = K.S
    K.ident = K.alloc([128, 128], BF16, keep=True)
    S.dma("sp", K.ident[:, :], A["ident"][:, :], writes=[K.ident])
    cf = K.alloc([128, 7, 128], F32, keep=True)
    S.dma("sp", cf[:, :, :], A["cf32"][:, :, :], writes=[cf])
    K.identf, K.uinc, K.uexc, K.onesf, K.mskf, K.mskb, K.ugt = [Tl(cf.t[:, i, :], cf.res) for i in range(7)]
    K.flags = K.alloc([128, 4], F32, keep=True)
    S.dma("sp", K.flags[:, :], A["flags"][:, :], writes=[K.flags])
    K.onec = K.alloc([128, 4], F32, keep=True)
    S.op("dve", lambda e: e.memset(K.onec[:, :], 1.0), writes=[K.onec])
    K.epsc = K.alloc([128, 4], F32, keep=True)
    S.op("dve", lambda e: e.memset(K.epsc[:, 0:1], EPS), writes=[K.epsc])
    S.op("dve", lambda e: e.memset(K.epsc[:, 1:2], 4.0 * EPS), writes=[K.epsc])
    for l in range(depth):
        for f in range(2):
            cast_weights(K, A["ff_w_in"][l, f], wi_bf[l][f], D, 2 * FF)
            cast_weights(K, A["ff_w_out"][l, f], wo_bf[l][f], FF, D)
        if mixer:
            cast_weights(K, A["w_in"][l], win_bf[l], D, IN_W)
            cast_weights(K, A["w_branch"][l], wbr_bf[l], 2048, D)
            cast_weights(K, A["w_out"][l], wout_bf[l], D, D)
    if mixer:
        zero_fill(K, SC["YS"], ntok, 1024, BF16)
        zero_fill(K, SC["YR"], ntok, 512, BF16)
    cur = x
    for l in range(depth):
        g = A["norm_g"][l]
        ffn_pass(K, cur, xs, g[0:1, :], g[1:2, :], wi_bf[l][0], wo_bf[l][0], ntok)
        cur = xs
        if mixer:
            import os
            st = os.environ.get("MK_STAGES", "iasrm")
            if "i" in st:
                inproj_pass(K, xs, g[2:3, :], win_bf[l], SC, ntok)
            if "a" in st:
                attn_pass(K, SC, A["atab"][l], plan)
            if "s" in st:
                K.new_pass()
                Cs = ssd_setup(K, A, l)
                ssd_bwd_pass(K, SC, Cs, ntok // 128, seglen // 128)
                K.new_pass()
                Cs = ssd_setup(K, A, l)
                ssd_fwd_pass(K, SC, Cs, ntok // 128, seglen // 128)
            if "r" in st:
                shift_pass(K, SC, A, l, ntok, seglen)
                for n_ in (1, 0):
                    K.new_pass()
                    Cr = rwkv_setup(K, A, l)
                    rwkv_dir_pass(K, SC, Cr, ntok // 128, seglen // 128, n_)
            if "m" in st:
                merge_pass(K, SC, xs, g[3:4, :], wbr_bf[l], wout_bf[l], ntok)
        last = (l == depth - 1)
        ffn_pass(K, cur, y if last else xs, g[4:5, :], g[5:6, :], wi_bf[l][1], wo_bf[l][1], ntok)
    S.barrier()
    S.emit()
    es.close()
    return nc


def consts():
    i = np.arange(128)
    s_, l_ = i[:, None], i[None, :]
    cf = np.stack([np.eye(128), s_ <= l_, s_ < l_, np.ones((128, 128)), s_ <= l_, s_ >= l_, s_ > l_]).astype(np.float32)
    return {"ident": np.eye(128, dtype=np.float32).astype(ml_dtypes.bfloat16),
            "cf32": np.ascontiguousarray(cf.transpose(1, 0, 2))}


def flags_for(link):
    f = np.zeros((128, 4), np.float32)
    f[:, 0] = 1.0 if link else 0.0
    return f


NTOK = 12288
_NC_CACHE = {}


def _assign():
    segs = []
    for c in range(4):
        segs.append([("s", c, 0), ("s", c, 1), ("p", c, 0)])
    for c in range(4, 8):
        b = 4 + (c - 4) * 3
        segs.append([("p", b, 0), ("p", b + 1, 0), ("p", b + 2, 0)])
    return segs


def kernel(**inputs):
    xp = np.asarray(inputs["x_prompt"], dtype=np.float32)
    xs = np.asarray(inputs["x_sample"], dtype=np.float32)
    segs = _assign()
    if "nc" not in _NC_CACHE:
        _NC_CACHE["nc"] = build(NTOK, seglen=4096)
    nc = _NC_CACHE["nc"]
    shared = {k: np.ascontiguousarray(np.asarray(inputs[k], dtype=np.float32))
              for k in ("norm_g", "ff_w_in", "ff_w_out", "w_in", "w_branch", "w_out", "ssm_conv_w", "ssm_conv_b",
                        "ssm_dt_bias", "ssm_a_log", "ssm_d", "ssm_norm_g", "rwkv_mu", "rwkv_w0", "rwkv_w_up",
                        "rwkv_a0", "rwkv_a_up", "rwkv_g_up", "rwkv_k_k", "rwkv_k_a", "rwkv_r_k", "rwkv_ln_g",
                        "rwkv_ln_b")}
    shared.update(consts())
    plan, pats = attn_plan(3, 4096)
    rpb = np.asarray(inputs["attn_rpb"], dtype=np.float32)
    tabs = {link: attn_tables(rpb, pats[link]) for link in (False, True)}
    in_maps = []
    for c in range(8):
        parts = []
        for kind, b, h in segs[c]:
            parts.append(xs[b, h * 4096:(h + 1) * 4096] if kind == "s" else xp[b])
        m = {"x": np.ascontiguousarray(np.concatenate(parts, axis=0)), "atab": tabs[c < 4],
             "flags": flags_for(c < 4)}
        m.update(shared)
        in_maps.append(m)
    res = run_bass_kernel_spmd(nc, in_maps, core_ids=list(range(8)))
    yp = np.empty_like(xp)
    ys = np.empty_like(xs)
    for c in range(8):
        y = res.results[c]["y"]
        for i, (kind, b, h) in enumerate(segs[c]):
            blk = y[i * 4096:(i + 1) * 4096]
            if kind == "s":
                ys[b, h * 4096:(h + 1) * 4096] = blk
            else:
                yp[b] = blk
    return yp, ys
```

```python
import numpy as np
import ml_dtypes
from contextlib import ExitStack
import concourse.bass as bass
import concourse.mybir as mybir
from concourse.bass_utils import run_bass_kernel_spmd

F32 = mybir.dt.float32
BF16 = mybir.dt.bfloat16
AF = mybir.ActivationFunctionType
ALU = mybir.AluOpType
AX = mybir.AxisListType

D = 1024
FF = 2816
DEPTH = 2
EPS = 1e-6


class Res:
    __slots__ = ("w", "r")

    def __init__(self):
        self.w = None
        self.r = []


class Tl:
    def __init__(self, t, res=None):
        self.t = t
        self.res = res or Res()

    def __getitem__(self, idx):
        return self.t[idx]


class Sched:
    EPOCH = 60000
    NDMA = {"sp": 24, "pool": 12, "act": 8}

    def __init__(self, nc, es):
        self.nc = nc
        self.es = es
        self.names = ["sp", "act", "dve", "pool", "pe"]
        self.ops = {k: [] for k in self.names}
        self.n = {k: 0 for k in self.names}
        self.sems = {k: [] for k in self.names}
        self.seen = {k: {} for k in self.names}
        self.dsem = {}
        self.dval = {}
        self.drr = {}
        for q, n in self.NDMA.items():
            self.dsem[q] = [es.enter_context(nc.semaphore(f"d{q}{i}")) for i in range(n)]
            self.dval[q] = [0] * n
            self.drr[q] = 0
        self.last = {k: None for k in self.names}

    def _next_ev(self, e):
        n = self.n[e]
        ep = n // self.EPOCH
        while len(self.sems[e]) <= ep:
            self.sems[e].append(self.es.enter_context(self.nc.semaphore(f"s{e}{len(self.sems[e])}")))
        self.n[e] += 1
        ev = (self.sems[e][ep], n % self.EPOCH + 1, e)
        self.last[e] = ev
        return ev

    def _deps(self, e, reads, writes):
        deps = []
        for r in reads:
            r = r.res if isinstance(r, Tl) else r
            if r.w is not None:
                deps.append((r.w, 0))
        for w in writes:
            w = w.res if isinstance(w, Tl) else w
            if w.w is not None:
                deps.append((w.w, 0))
            for ev in w.r:
                deps.append((ev, 1))
        waits = []
        seen = self.seen[e]
        for (sem, val, src), war in deps:
            if src == e:
                if e == "pe" or war:
                    continue
            k = id(sem)
            if seen.get(k, 0) >= val:
                continue
            seen[k] = val
            waits.append((sem, val))
        return waits

    def _commit(self, ev, reads, writes):
        for r in reads:
            r = r.res if isinstance(r, Tl) else r
            r.r.append(ev)
        for w in writes:
            w = w.res if isinstance(w, Tl) else w
            w.w = ev
            w.r = []

    def op(self, e, fn, reads=(), writes=()):
        waits = self._deps(e, reads, writes)
        ev = self._next_ev(e)
        self.ops[e].append((waits, fn, ev, 1))
        self._commit(ev, reads, writes)

    def dma(self, q, out, in_, reads=(), writes=(), slow=False):
        waits = self._deps(q, reads, writes)
        i = self.drr[q]
        self.drr[q] = (i + 1) % len(self.dsem[q])
        sem = self.dsem[q][i]
        prev = self.dval[q][i]
        if prev > 0 and self.seen[q].get(id(sem), 0) < prev:
            waits.append((sem, prev))
            self.seen[q][id(sem)] = prev
        self.dval[q][i] = prev + 16
        ev = (sem, prev + 16, "dma")
        if slow:
            fn = (lambda e, o=out, i_=in_: e.dma_start(out=o, in_=i_, allow_slow_non_contiguous=True))
        else:
            fn = (lambda e, o=out, i_=in_: e.dma_start(out=o, in_=i_))
        self.ops[q].append((waits, fn, ev, 16))
        self._commit(ev, reads, writes)

    def barrier(self):
        evs = [self.last[k] for k in self.names if self.last[k] is not None]
        for q in self.dsem:
            for sem, v in zip(self.dsem[q], self.dval[q]):
                if v > 0:
                    evs.append((sem, v, "dma"))
        for e in self.names:
            waits = []
            for sem, val, src in evs:
                if src == e:
                    continue
                if self.seen[e].get(id(sem), 0) >= val:
                    continue
                self.seen[e][id(sem)] = val
                waits.append((sem, val))
            if waits:
                self.ops[e].append((waits, None, None, 0))

    def emit(self):
        block = self.es.enter_context(self.nc.Block())
        decos = {"sp": block.sync, "act": block.scalar, "dve": block.vector, "pool": block.gpsimd,
                 "pe": block.tensor}
        for name in self.names:
            ops = self.ops[name]

            def body(e, ops=ops):
                for waits, fn, ev, inc in ops:
                    for (s, v) in waits:
                        e.wait_ge(s, v)
                    if fn is not None:
                        fn(e).then_inc(ev[0], inc)

            decos[name](body)


class Ctx:
    BASE = 16640
    ARENA = 224 * 1024

    def __init__(self, nc, es):
        self.nc = nc
        self.es = es
        self.S = Sched(nc, es)
        self.off = self.BASE
        self.uid = 0
        self.keep = self.BASE
        self.ps = [Tl(es.enter_context(nc.psum_tensor(f"ps{i}", [128, 512], F32))) for i in range(8)]
        self.psi = 0
        self.psi6 = 0

    def alloc(self, shape, dtype, keep=False):
        nbytes = int(np.prod(shape[1:])) * (2 if dtype == BF16 else 4)
        nbytes = (nbytes + 31) // 32 * 32
        assert self.off + nbytes <= self.ARENA, f"SBUF arena overflow {self.off}+{nbytes}"
        self.uid += 1
        t = self.nc.alloc_sbuf_tensor_at(f"t{self.uid}", list(shape), dtype, offset=self.off)
        self.off += nbytes
        if keep:
            self.keep = self.off
        return Tl(t)

    def new_pass(self):
        self.S.barrier()
        self.off = self.keep
        if getattr(self, "cast_jobs", None):
            self.cast_jobs.pop(0)()

    def psum(self):
        p = self.ps[self.psi]
        self.psi = (self.psi + 1) % 8
        return p


class Pool:
    def __init__(self, K, n, shape, dtype):
        self.t = [K.alloc(shape, dtype) for _ in range(n)]
        self.i = 0

    def get(self):
        t = self.t[self.i]
        self.i = (self.i + 1) % len(self.t)
        return t


def norm_transpose(K, xt, xres, gbc, hnT, col0, P):
    S = K.S
    sm = P["small"].get()
    junk = P["junk"].get()
    S.op("act", lambda e: e.activation(out=junk[:, :], in_=xt, func=AF.Square, accum_out=sm[:, 0:1]),
         reads=[xres], writes=[junk, sm])
    S.op("act", lambda e: e.activation(out=sm[:, 1:2], in_=sm[:, 0:1], func=AF.Sqrt, bias=K.epsc[:, 0:1],
                                       scale=1.0 / D), reads=[sm, K.epsc], writes=[sm])
    S.op("dve", lambda e: e.reciprocal(out=sm[:, 2:3], in_=sm[:, 1:2]), reads=[sm], writes=[sm])
    hn = P["hn"].get()
    S.op("dve", lambda e: e.scalar_tensor_tensor(out=hn[:, :], in0=xt, scalar=sm[:, 2:3], in1=gbc[:, :],
                                                 op0=ALU.mult, op1=ALU.mult),
         reads=[xres, sm, gbc], writes=[hn])
    pt = K.psum()
    ptb = pt.t[:, :].bitcast(BF16)
    for kc in range(8):
        S.op("pe", lambda e, kc=kc: e.transpose(ptb[:, kc * 128:(kc + 1) * 128],
                                                hn[:, kc * 128:(kc + 1) * 128], K.ident[:, :]),
             reads=[hn, K.ident], writes=[pt])
    S.op("act", lambda e: e.copy(out=hnT[:, :, col0:col0 + 128],
                                 in_=ptb.rearrange("p (k c) -> p k c", k=8)), reads=[pt], writes=[hnT])


def ffn_pass(K, xin, xout, gin_ap, gout_ap, win_bf, wout_bf, ntok):
    S = K.S
    K.new_pass()
    gin = K.alloc([128, D], F32)
    gout = K.alloc([128, D], F32)
    S.dma("sp", gin[:, :], gin_ap.broadcast_to([128, D]), writes=[gin])
    S.dma("sp", gout[:, :], gout_ap.broadcast_to([128, D]), writes=[gout])
    wout = K.alloc([128, 22, D], BF16)
    wo_v = wout_bf.rearrange("(kc p) n -> p kc n", p=128)
    for kc in range(0, 22, 2):
        S.dma("sp", wout[:, kc:kc + 2, :], wo_v[:, kc:kc + 2, :], writes=[wout])
    xp = Pool(K, 2, [128, 4, D], F32)
    hnTp = Pool(K, 2, [128, 8, 512], BF16)
    hT = K.alloc([128, 22, 512], BF16)
    wp = Pool(K, 3, [128, 8, 2, 256], BF16)
    sg = Pool(K, 2, [128, 512], F32)
    tt = Pool(K, 2, [128, D], F32)
    P = {"small": Pool(K, 8, [128, 8], F32), "junk": Pool(K, 2, [128, D], BF16), "hn": Pool(K, 2, [128, D], BF16)}
    win_v = win_bf.rearrange("(kc p) n -> p kc n", p=128)
    for mt in range(ntok // 512):
        x = xp.get()
        S.dma("sp", x[:, :, :], xin[mt * 512:(mt + 1) * 512, :].rearrange("(s p) d -> p s d", p=128),
              writes=[x])
        hnT = hnTp.get()
        for s in range(4):
            norm_transpose(K, x[:, s, :], x, gin, hnT, s * 128, P)
        for j in range(11):
            w = wp.get()
            S.dma("sp", w[:, :, 0, :], win_v[:, :, j * 256:(j + 1) * 256], writes=[w])
            S.dma("sp", w[:, :, 1, :], win_v[:, :, FF + j * 256:FF + (j + 1) * 256], writes=[w])
            for c in range(2):
                pg = K.psum()
                pu = K.psum()
                for gi, pp in enumerate((pg, pu)):
                    for kc in range(8):
                        S.op("pe", lambda e, pp=pp, gi=gi, kc=kc, c=c, w=w, hnT=hnT: e.matmul(
                            pp[:, :], w[:, kc, gi, c * 128:(c + 1) * 128], hnT[:, kc, :],
                            start=(kc == 0), stop=(kc == 7)), reads=[w, hnT], writes=[pp])
                sgt = sg.get()
                S.op("act", lambda e, sgt=sgt, pg=pg: e.activation(out=sgt[:, :], in_=pg[:, :], func=AF.Silu),
                     reads=[pg], writes=[sgt])
                S.op("dve", lambda e, sgt=sgt, pu=pu, ch=j * 2 + c: e.tensor_tensor(
                    out=hT[:, ch, :], in0=sgt[:, :], in1=pu[:, :], op=ALU.mult),
                     reads=[sgt, pu], writes=[hT])
        for s in range(4):
            pp = [K.psum(), K.psum()]
            for nf in range(2):
                for kc in range(22):
                    S.op("pe", lambda e, p_=pp[nf], nf=nf, kc=kc, s=s: e.matmul(
                        p_[:, :], hT[:, kc, s * 128:(s + 1) * 128], wout[:, kc, nf * 512:(nf + 1) * 512],
                        start=(kc == 0), stop=(kc == 21)), reads=[hT, wout], writes=[pp[nf]])
            sm = P["small"].get()
            junk = P["junk"].get()
            for nf in range(2):
                S.op("act", lambda e, nf=nf, junk=junk, sm=sm, p_=pp[nf]: e.activation(
                    out=junk[:, 0:512], in_=p_[:, :], func=AF.Square, accum_out=sm[:, nf:nf + 1]),
                     reads=[pp[nf]], writes=[junk, sm])
            S.op("dve", lambda e, sm=sm: e.tensor_tensor(out=sm[:, 2:3], in0=sm[:, 0:1], in1=sm[:, 1:2],
                                                         op=ALU.add), reads=[sm], writes=[sm])
            S.op("act", lambda e, sm=sm: e.activation(out=sm[:, 3:4], in_=sm[:, 2:3], func=AF.Sqrt,
                                                      bias=K.epsc[:, 1:2], scale=4.0 / D),
                 reads=[sm, K.epsc], writes=[sm])
            S.op("dve", lambda e, sm=sm: e.reciprocal(out=sm[:, 4:5], in_=sm[:, 3:4]), reads=[sm], writes=[sm])
            t = tt.get()
            for nf in range(2):
                S.op("dve", lambda e, nf=nf, t=t, sm=sm, p_=pp[nf]: e.scalar_tensor_tensor(
                    out=t[:, nf * 512:(nf + 1) * 512], in0=p_[:, :], scalar=sm[:, 4:5],
                    in1=gout[:, nf * 512:(nf + 1) * 512], op0=ALU.mult, op1=ALU.mult),
                     reads=[pp[nf], sm, gout], writes=[t])
            S.op("pool", lambda e, t=t, x=x, s=s: e.tensor_tensor(out=x[:, s, :], in0=x[:, s, :], in1=t[:, :],
                                                                  op=ALU.add), reads=[t, x], writes=[x])
        S.dma("pool", xout[mt * 512:(mt + 1) * 512, :].rearrange("(s p) d -> p s d", p=128), x[:, :, :],
              reads=[x])


IN_W = 9152
OFF_Q, OFF_K, OFF_V, OFF_Z, OFF_XBC, OFF_DT, OFF_RW, OFF_G = 0, 512, 1024, 1536, 2560, 4096, 4128, 6080


def inproj_pass(K, xin, g_ap, w_bf, SC, ntok):
    S = K.S
    K.new_pass()
    gin = K.alloc([128, D], F32)
    S.dma("sp", gin[:, :], g_ap.broadcast_to([128, D]), writes=[gin])
    xp = Pool(K, 2, [128, 4, D], F32)
    hnTp = Pool(K, 2, [128, 8, 512], BF16)
    wp = Pool(K, 3, [128, 8, 512], BF16)
    ofm = Pool(K, 3, [128, 512], F32)
    obf = Pool(K, 3, [128, 512], BF16)
    P = {"small": Pool(K, 8, [128, 8], F32), "junk": Pool(K, 2, [128, D], BF16), "hn": Pool(K, 2, [128, D], BF16)}
    w_v = w_bf.rearrange("(kc p) n -> p kc n", p=128)
    blocks = []
    for c0 in range(0, 1024, 512):
        blocks.append((c0, 512, "F"))
    blocks.append((OFF_V, 512, "T"))
    blocks += [(OFF_Z, 512, "T"), (OFF_Z + 512, 512, "T")]
    blocks += [(OFF_XBC + i * 512, 512, "F") for i in range(3)]
    blocks.append((OFF_DT, 32, "T"))
    blocks += [(OFF_RW, 512, "F"), (OFF_RW + 512, 512, "F"), (OFF_RW + 1024, 512, "F"), (OFF_RW + 1536, 416, "F")]
    blocks += [(OFF_G + i * 512, 512, "T") for i in range(6)]
    for mt in range(ntok // 512):
        t0 = mt * 512
        x = xp.get()
        S.dma("sp", x[:, :, :], xin[t0:t0 + 512, :].rearrange("(s p) d -> p s d", p=128), writes=[x])
        hnT = hnTp.get()
        for s in range(4):
            norm_transpose(K, x[:, s, :], x, gin, hnT, s * 128, P)
        for (c0, ncol, mode) in blocks:
            w = wp.get()
            S.dma("sp", w[:, :, 0:ncol], w_v[:, :, c0:c0 + ncol], writes=[w])
            if mode == "F":
                for fc in range((ncol + 127) // 128):
                    nf = min(128, ncol - fc * 128)
                    pp = K.psum()
                    for kc in range(8):
                        S.op("pe", lambda e, pp=pp, w=w, kc=kc, fc=fc, nf=nf, hnT=hnT: e.matmul(
                            pp[0:nf, :], w[:, kc, fc * 128:fc * 128 + nf], hnT[:, kc, :],
                            start=(kc == 0), stop=(kc == 7)), reads=[w, hnT], writes=[pp])
                    f0 = c0 + fc * 128
                    if f0 < 1024:
                        o = obf.get()
                        sc = 0.125 if f0 < 512 else 1.0
                        S.op("act", lambda e, o=o, pp=pp, sc=sc: e.activation(out=o[:, :], in_=pp[:, :],
                                                                              func=AF.Copy, scale=sc),
                             reads=[pp], writes=[o])
                        dst = SC["QT"] if f0 < 512 else SC["KT"]
                        r0 = f0 % 512
                        S.dma("pool", dst[r0:r0 + 128, t0:t0 + 512], o[:, :], reads=[o])
                    else:
                        o = ofm.get()
                        S.op("act", lambda e, o=o, pp=pp, nf=nf: e.copy(out=o[0:nf, :], in_=pp[0:nf, :]),
                             reads=[pp], writes=[o])
                        if f0 < OFF_DT:
                            r0 = f0 - OFF_XBC
                            S.dma("pool", SC["XBCT"][r0:r0 + nf, t0:t0 + 512], o[0:nf, :], reads=[o])
                        else:
                            r0 = f0 - OFF_RW
                            S.dma("pool", SC["RWT"][r0:r0 + nf, t0:t0 + 512], o[0:nf, :], reads=[o])
            else:
                for s in range(4):
                    pp = K.psum()
                    for kc in range(8):
                        S.op("pe", lambda e, pp=pp, w=w, kc=kc, s=s, ncol=ncol, hnT=hnT: e.matmul(
                            pp[:, 0:ncol], hnT[:, kc, s * 128:(s + 1) * 128], w[:, kc, 0:ncol],
                            start=(kc == 0), stop=(kc == 7)), reads=[w, hnT], writes=[pp])
                    tk = t0 + s * 128
                    if c0 == OFF_V:
                        o = obf.get()
                        S.op("act", lambda e, o=o, pp=pp: e.copy(out=o[:, :], in_=pp[:, :]), reads=[pp], writes=[o])
                        S.dma("pool", SC["V"][tk:tk + 128, :], o[:, :], reads=[o])
                    elif c0 >= OFF_G:
                        o = ofm.get()
                        S.op("act", lambda e, o=o, pp=pp: e.activation(out=o[:, :], in_=pp[:, :], func=AF.Sigmoid),
                             reads=[pp], writes=[o])
                        S.dma("pool", SC["G"][tk:tk + 128, c0 - OFF_G:c0 - OFF_G + 512], o[:, :], reads=[o])
                    elif c0 == OFF_DT:
                        o = ofm.get()
                        S.op("act", lambda e, o=o, pp=pp: e.copy(out=o[:, 0:32], in_=pp[:, 0:32]), reads=[pp], writes=[o])
                        S.dma("pool", SC["DT"][tk:tk + 128, :], o[:, 0:32], reads=[o])
                    else:
                        o = ofm.get()
                        S.op("act", lambda e, o=o, pp=pp: e.copy(out=o[:, :], in_=pp[:, :]), reads=[pp], writes=[o])
                        S.dma("pool", SC["Z"][tk:tk + 128, c0 - OFF_Z:c0 - OFF_Z + 512], o[:, :], reads=[o])


def attn_plan(nseg, seglen):
    R = seglen // 64
    nrow = nseg * R

    def rs_of(g, link):
        seg, r = divmod(g, R)
        if link and seg < 2 and nseg >= 2:
            return int(np.clip(g - 4, 0, 2 * R - 8))
        return seg * R + int(np.clip(r - 4, 0, R - 8))

    qc = np.arange(64)
    wst = np.clip(qc - 8, 0, 48)
    pats = {}
    plan = []
    ids = {}
    for P in range(nrow // 2):
        kps = set()
        for g in (2 * P, 2 * P + 1):
            for link in (False, True):
                rs = rs_of(g, link)
                for kr in range(rs, rs + 8):
                    kps.add(kr // 2)
        ent = []
        for KP in sorted(kps):
            both = []
            for link in (False, True):
                idx = np.full((2, 64, 2, 64), -1, np.int32)
                for qr2 in range(2):
                    g = 2 * P + qr2
                    rs = rs_of(g, link)
                    for kr2 in range(2):
                        kr = 2 * KP + kr2
                        if not (rs <= kr < rs + 8):
                            continue
                        dr = kr - g + 7
                        kc = np.arange(64)[:, None]
                        ok = (kc >= wst[None, :]) & (kc < wst[None, :] + 16)
                        dc = np.clip(kc - qc[None, :] + 15, 0, 30)
                        idx[kr2, :, qr2, :] = np.where(ok, dr * 31 + dc, -1)
                both.append(idx.reshape(128, 128))
            key = both[0].tobytes() + both[1].tobytes()
            if key not in ids:
                ids[key] = len(ids)
                pats.setdefault(False, []).append(both[0])
                pats.setdefault(True, []).append(both[1])
            ent.append((KP, ids[key]))
        plan.append(ent)
    return plan, {k: np.stack(v) for k, v in pats.items()}


def attn_tables(rpb, pat):
    flat = rpb.reshape(rpb.shape[0], 8, 15 * 31)
    g = flat[:, :, np.clip(pat, 0, None)]
    g = np.where(pat[None, None] >= 0, g, np.float32(-30000.0)).astype(np.float32)
    return np.ascontiguousarray(g.transpose(0, 2, 3, 1, 4))


def attn_pass(K, SC, tab, plan):
    import os
    ALV = int(os.environ.get("MK_ALV", "3"))
    S = K.S
    K.new_pass()
    qp = Pool(K, 2, [64, 8, 128], BF16)
    kp = Pool(K, 3, [64, 8, 128], BF16)
    vp = Pool(K, 3, [128, 8, 80], BF16)
    tp = Pool(K, 3, [128, 8, 128], F32)
    sp_ = Pool(K, 2, [128, 512], F32)
    ptp = Pool(K, 3, [128, 8, 128], BF16)
    yp = Pool(K, 2, [128, 8, 64], BF16)
    sm = Pool(K, 2, [128, 8], F32)
    for v in vp.t:
        S.op("pool", lambda e, v=v: e.memset(v[:, :, 64:65], 1.0), writes=[v])
    QTv = SC["QT"].rearrange("(c p) t -> p c t", p=64)
    KTv = SC["KT"].rearrange("(c p) t -> p c t", p=64)
    for P, ent in enumerate(plan):
        q = qp.get()
        S.dma("sp", q[:, :, :], QTv[:, :, P * 128:(P + 1) * 128], writes=[q])
        ob = [K.ps[6], K.ps[7]]
        for ki, (KP, cfg) in enumerate(ent):
            k = kp.get()
            S.dma("sp", k[:, :, :], KTv[:, :, KP * 128:(KP + 1) * 128], writes=[k])
            v = vp.get()
            S.dma("sp", v[:, :, 0:64], SC["V"][KP * 128:(KP + 1) * 128, :].rearrange("t (h d) -> t h d", h=8),
                  writes=[v])
            tb = tp.get()
            S.dma("sp", tb[:, :, :], tab[cfg], writes=[tb])
            pt = ptp.get()
            for hb in range(2):
                pp = K.ps[K.psi6]
                K.psi6 = (K.psi6 + 1) % 6
                for hh in range(4):
                    h = hb * 4 + hh
                    S.op("pe", lambda e, pp=pp, k=k, q=q, h=h, hh=hh: e.matmul(
                        pp[:, hh * 128:(hh + 1) * 128], k[:, h, :], q[:, h, :], start=True, stop=True),
                        reads=[k, q], writes=[pp])
                sb = sp_.get()
                S.op("dve", lambda e, sb=sb, pp=pp, tb=tb, hb=hb: e.tensor_tensor(
                    out=sb[:, :], in0=pp[:, :], in1=tb[:, hb * 4:(hb + 1) * 4, :].rearrange("p a b -> p (a b)"),
                    op=ALU.add), reads=[pp, tb], writes=[sb])
                S.op("act", lambda e, sb=sb, pt=pt, hb=hb: e.activation(
                    out=pt[:, hb * 4:(hb + 1) * 4, :].rearrange("p a b -> p (a b)"), in_=sb[:, :], func=AF.Exp),
                    reads=[sb], writes=[pt])
            for h in range(8 if ALV >= 2 else 0):
                o_ = ob[h // 4]
                hh = h % 4
                S.op("pe", lambda e, o_=o_, pt=pt, v=v, h=h, hh=hh, ki=ki, n=len(ent): e.matmul(
                    o_[:, hh * 128:hh * 128 + 65], pt[:, h, :], v[:, h, 0:65], start=(ki == 0 and hh == 0), stop=(ki == n - 1)),
                    reads=[pt, v], writes=[o_])
        rc = sm.get()
        y = yp.get()
        for hb in range(2 if ALV >= 3 else 0):
            ov = ob[hb][:, :].rearrange("p (h d) -> p h d", h=4)
            S.op("dve", lambda e, rc=rc, ov=ov, hb=hb: e.reciprocal(out=rc[:, hb * 4:(hb + 1) * 4], in_=ov[:, :, 64]),
                 reads=[ob[hb]], writes=[rc])
            S.op("dve", lambda e, rc=rc, ov=ov, hb=hb, y=y: e.tensor_tensor(
                out=y[:, hb * 4:(hb + 1) * 4, :], in0=ov[:, :, 0:64],
                in1=rc[:, hb * 4:(hb + 1) * 4].unsqueeze(2).broadcast_to([128, 4, 64]), op=ALU.mult),
                reads=[ob[hb], rc], writes=[y])
        if ALV >= 3:
            S.dma("pool", SC["YA"][P * 128:(P + 1) * 128, :], y[:, :, :].rearrange("p h d -> p (h d)"), reads=[y])


def merge_pass(K, SC, xio, g_ap, wbr_bf, wout_bf, ntok):
    S = K.S
    K.new_pass()
    g3 = K.alloc([128, D], F32)
    S.dma("sp", g3[:, :], g_ap.broadcast_to([128, D]), writes=[g3])
    wbr = K.alloc([128, 16, D], BF16)
    wbv = wbr_bf.rearrange("(kc p) n -> p kc n", p=128)
    for kc in range(0, 16, 4):
        S.dma("sp", wbr[:, kc:kc + 4, :], wbv[:, kc:kc + 4, :], writes=[wbr])
    wo = K.alloc([128, 8, D], BF16)
    S.dma("sp", wo[:, :, :], wout_bf.rearrange("(kc p) n -> p kc n", p=128), writes=[wo])
    ybp = Pool(K, 2, [128, 2048], BF16)
    gp = Pool(K, 2, [128, 3, D], F32)
    xp = Pool(K, 2, [128, D], F32)
    yTp = Pool(K, 2, [128, 16, 128], BF16)
    mp = Pool(K, 2, [128, D], F32)
    tmp = Pool(K, 2, [128, 512], F32)
    mbp = Pool(K, 2, [128, D], BF16)
    mTp = Pool(K, 2, [128, 8, 128], BF16)
    tt = Pool(K, 2, [128, D], F32)
    sm = Pool(K, 4, [128, 8], F32)
    junk = Pool(K, 2, [128, 512], BF16)
    for t in range(ntok // 128):
        r0 = t * 128
        yb = ybp.get()
        S.dma("sp", yb[:, 0:512], SC["YA"][r0:r0 + 128, :], writes=[yb])
        S.dma("sp", yb[:, 512:1536], SC["YS"][r0:r0 + 128, :], writes=[yb])
        S.dma("sp", yb[:, 1536:2048], SC["YR"][r0:r0 + 128, :], writes=[yb])
        gt = gp.get()
        S.dma("sp", gt[:, :, :], SC["G"][r0:r0 + 128, :].rearrange("t (b d) -> t b d", b=3), writes=[gt])
        x = xp.get()
        S.dma("sp", x[:, :], xio[r0:r0 + 128, :], writes=[x])
        yT = yTp.get()
        for half in range(2):
            pt = K.psum()
            ptb = pt.t[:, :].bitcast(BF16)
            for c in range(8):
                kc = half * 8 + c
                S.op("pe", lambda e, ptb=ptb, c=c, kc=kc, yb=yb: e.transpose(
                    ptb[:, c * 128:(c + 1) * 128], yb[:, kc * 128:(kc + 1) * 128], K.ident[:, :]),
                    reads=[yb, K.ident], writes=[pt])
            S.op("act", lambda e, yT=yT, ptb=ptb, half=half: e.copy(
                out=yT[:, half * 8:(half + 1) * 8, :], in_=ptb.rearrange("p (k c) -> p k c", k=8)),
                reads=[pt], writes=[yT])
        m = mp.get()
        for b, (k0, nk) in enumerate(((0, 4), (4, 8), (12, 4))):
            for nf in range(2):
                pp = K.psum()
                for kc in range(nk):
                    S.op("pe", lambda e, pp=pp, yT=yT, kc=kc, k0=k0, nf=nf, nk=nk: e.matmul(
                        pp[:, :], yT[:, k0 + kc, :], wbr[:, k0 + kc, nf * 512:(nf + 1) * 512],
                        start=(kc == 0), stop=(kc == nk - 1)), reads=[yT, wbr], writes=[pp])
                if b == 0:
                    S.op("dve", lambda e, m=m, pp=pp, gt=gt, nf=nf, b=b: e.tensor_tensor(
                        out=m[:, nf * 512:(nf + 1) * 512], in0=pp[:, :], in1=gt[:, b, nf * 512:(nf + 1) * 512],
                        op=ALU.mult), reads=[pp, gt], writes=[m])
                else:
                    tm = tmp.get()
                    S.op("dve", lambda e, tm=tm, pp=pp, gt=gt, nf=nf, b=b: e.tensor_tensor(
                        out=tm[:, :], in0=pp[:, :], in1=gt[:, b, nf * 512:(nf + 1) * 512], op=ALU.mult),
                        reads=[pp, gt], writes=[tm])
                    S.op("pool", lambda e, tm=tm, m=m, nf=nf: e.tensor_tensor(
                        out=m[:, nf * 512:(nf + 1) * 512], in0=m[:, nf * 512:(nf + 1) * 512], in1=tm[:, :],
                        op=ALU.add), reads=[tm, m], writes=[m])
        mb = mbp.get()
        S.op("act", lambda e, mb=mb, m=m: e.copy(out=mb[:, :], in_=m[:, :]), reads=[m], writes=[mb])
        mT = mTp.get()
        pt = K.psum()
        ptb = pt.t[:, :].bitcast(BF16)
        for c in range(8):
            S.op("pe", lambda e, ptb=ptb, c=c, mb=mb: e.transpose(
                ptb[:, c * 128:(c + 1) * 128], mb[:, c * 128:(c + 1) * 128], K.ident[:, :]),
                reads=[mb, K.ident], writes=[pt])
        S.op("act", lambda e, mT=mT, ptb=ptb: e.copy(out=mT[:, :, :], in_=ptb.rearrange("p (k c) -> p k c", k=8)),
             reads=[pt], writes=[mT])
        pp = [K.psum(), K.psum()]
        for nf in range(2):
            for kc in range(8):
                S.op("pe", lambda e, p_=pp[nf], mT=mT, kc=kc, nf=nf: e.matmul(
                    p_[:, :], mT[:, kc, :], wo[:, kc, nf * 512:(nf + 1) * 512], start=(kc == 0), stop=(kc == 7)),
                    reads=[mT, wo], writes=[pp[nf]])
        s_ = sm.get()
        jk = junk.get()
        for nf in range(2):
            S.op("act", lambda e, nf=nf, jk=jk, s_=s_, p_=pp[nf]: e.activation(
                out=jk[:, :], in_=p_[:, :], func=AF.Square, accum_out=s_[:, nf:nf + 1]),
                reads=[pp[nf]], writes=[jk, s_])
        S.op("dve", lambda e, s_=s_: e.tensor_tensor(out=s_[:, 2:3], in0=s_[:, 0:1], in1=s_[:, 1:2], op=ALU.add),
             reads=[s_], writes=[s_])
        S.op("act", lambda e, s_=s_: e.activation(out=s_[:, 3:4], in_=s_[:, 2:3], func=AF.Sqrt,
                                                  bias=K.epsc[:, 0:1], scale=1.0 / D),
             reads=[s_, K.epsc], writes=[s_])
        S.op("dve", lambda e, s_=s_: e.reciprocal(out=s_[:, 4:5], in_=s_[:, 3:4]), reads=[s_], writes=[s_])
        t_ = tt.get()
        for nf in range(2):
            S.op("dve", lambda e, nf=nf, t_=t_, s_=s_, p_=pp[nf]: e.scalar_tensor_tensor(
                out=t_[:, nf * 512:(nf + 1) * 512], in0=p_[:, :], scalar=s_[:, 4:5],
                in1=g3[:, nf * 512:(nf + 1) * 512], op0=ALU.mult, op1=ALU.mult),
                reads=[pp[nf], s_, g3], writes=[t_])
        S.op("pool", lambda e, t_=t_, x=x: e.tensor_tensor(out=x[:, :], in0=x[:, :], in1=t_[:, :], op=ALU.add),
             reads=[t_, x], writes=[x])
        S.dma("pool", xio[r0:r0 + 128, :], x[:, :], reads=[x])


def ssd_setup(K, A, l):
    S = K.S
    C = {}
    C["cw"] = K.alloc([128, 12, 5], F32)
    for k in range(5):
        S.dma("sp", C["cw"][:, :, k], A["ssm_conv_w"][l, k].rearrange("(fc p) -> p fc", p=128), writes=[C["cw"]],
              slow=True)
    C["cb"] = K.alloc([128, 12], F32)
    S.dma("sp", C["cb"][:, :], A["ssm_conv_b"][l].rearrange("(fc p) -> p fc", p=128), writes=[C["cb"]], slow=True)
    C["dtb"] = K.alloc([128, 32], F32)
    S.dma("sp", C["dtb"][:, :], A["ssm_dt_bias"][l].rearrange("a b -> (a b)").unsqueeze(0).broadcast_to([128, 32]),
          writes=[C["dtb"]])
    C["abc"] = K.alloc([128, 32], F32)
    S.dma("sp", C["abc"][:, :], A["ssm_a_log"][l].rearrange("a b -> (a b)").unsqueeze(0).broadcast_to([128, 32]),
          writes=[C["abc"]])
    S.op("act", lambda e: e.activation(out=C["abc"][:, :], in_=C["abc"][:, :], func=AF.Exp), reads=[C["abc"]],
         writes=[C["abc"]])
    S.op("dve", lambda e: e.tensor_scalar(out=C["abc"][:, :], in0=C["abc"][:, :], scalar1=-1.0, scalar2=0.0,
                                          op0=ALU.mult, op1=ALU.add), reads=[C["abc"]], writes=[C["abc"]])
    dsk = K.alloc([128, 32], F32)
    S.dma("sp", dsk[:, :], A["ssm_d"][l].rearrange("a b -> (a b)").unsqueeze(0).broadcast_to([128, 32]), writes=[dsk])
    C["dsum"] = K.alloc([128, 16], F32)
    S.op("dve", lambda e: e.tensor_tensor(out=C["dsum"][:, :], in0=dsk[:, 0:16], in1=dsk[:, 16:32], op=ALU.add),
         reads=[dsk], writes=[C["dsum"]])
    C["dI"] = K.alloc([128, 16, 128], F32)
    for h in range(16):
        S.op("dve", lambda e, h=h: e.tensor_scalar(out=C["dI"][:, h, :], in0=K.identf[:, :], scalar1=C["dsum"][:, h:h + 1],
                                                   scalar2=0.0, op0=ALU.mult, op1=ALU.add),
             reads=[K.identf, C["dsum"]], writes=[C["dI"]])
    C["ng"] = K.alloc([128, 1024], F32)
    S.dma("sp", C["ng"][:, :], A["ssm_norm_g"][l:l + 1, :].broadcast_to([128, 1024]), writes=[C["ng"]])
    return C


import os as _os
NB = int(_os.environ.get('MK_NB', '2'))


def ssd_pools(K):
    P = {}
    P["xw"] = Pool(K, NB, [128, 12, 132], F32)
    P["accD"] = Pool(K, 2, [128, 8, 128], F32)
    P["tmpD"] = Pool(K, 1, [128, 8, 128], F32)
    P["accP"] = Pool(K, 2, [128, 4, 128], F32)
    P["tmpP"] = Pool(K, 1, [128, 4, 128], F32)
    P["xbcT"] = Pool(K, NB, [128, 12, 128], BF16)
    P["xs"] = Pool(K, NB, [128, 1024], BF16)
    P["bm"] = Pool(K, NB, [128, 256], BF16)
    P["dt"] = Pool(K, NB, [128, 32], F32)
    P["v"] = Pool(K, NB, [128, 256], F32)
    return P


def ssd_prep(K, SC, C, P, c, nchunk, cps, need_cm=True):
    S = K.S
    t0 = c * 128
    ntok = nchunk * 128
    seg, cis = divmod(c, cps)
    xw = P["xw"].get()
    XB = SC["XBCT"].rearrange("(fc p) t -> p fc t", p=128)
    lo, hi = max(t0 - 2, 0), min(t0 + 130, ntok)
    S.dma("sp", xw[:, :, lo - (t0 - 2):hi - (t0 - 2)], XB[:, :, lo:hi], writes=[xw])
    if t0 == 0:
        S.op("pool", lambda e: e.memset(xw[:, :, 0:2], 0.0), writes=[xw])
    elif cis == 0:
        S.op("pool", lambda e, seg=seg: e.tensor_scalar(out=xw[:, :, 0:2], in0=xw[:, :, 0:2],
                                                        scalar1=K.flags[:, seg - 1:seg], scalar2=0.0,
                                                        op0=ALU.mult, op1=ALU.add), reads=[xw, K.flags], writes=[xw])
    if t0 + 130 > ntok:
        S.op("pool", lambda e: e.memset(xw[:, :, 130:132], 0.0), writes=[xw])
    elif cis == cps - 1:
        S.op("pool", lambda e, seg=seg: e.tensor_scalar(out=xw[:, :, 130:132], in0=xw[:, :, 130:132],
                                                        scalar1=K.flags[:, seg:seg + 1], scalar2=0.0,
                                                        op0=ALU.mult, op1=ALU.add), reads=[xw, K.flags], writes=[xw])
    cw = C["cw"]
    accs = {}
    for eng, f0, f1, nm in (("dve", 0, 8, "D"), ("pool", 8, 12, "P")):
        nf = f1 - f0
        acc = P["acc" + nm].get()
        tmp = P["tmp" + nm].get()
        accs[nm] = acc
        for k in range(5):
            wk = cw[:, f0:f1, k:k + 1].broadcast_to([128, nf, 128])
            if k == 0:
                S.op(eng, lambda e, wk=wk, f0=f0, f1=f1, acc=acc: e.tensor_tensor(
                    out=acc[:, :, :], in0=xw[:, f0:f1, 0:128], in1=wk, op=ALU.mult), reads=[xw, cw], writes=[acc])
            else:
                S.op(eng, lambda e, wk=wk, k=k, f0=f0, f1=f1, tmp=tmp: e.tensor_tensor(
                    out=tmp[:, :, :], in0=xw[:, f0:f1, k:k + 128], in1=wk, op=ALU.mult), reads=[xw, cw], writes=[tmp])
                S.op(eng, lambda e, acc=acc, tmp=tmp: e.tensor_tensor(out=acc[:, :, :], in0=acc[:, :, :],
                                                                      in1=tmp[:, :, :], op=ALU.add),
                     reads=[acc, tmp], writes=[acc])
    xbcT = P["xbcT"].get()
    for fc in range(12):
        a_ = accs["D"] if fc < 8 else accs["P"]
        fi = fc if fc < 8 else fc - 8
        S.op("act", lambda e, fc=fc, a_=a_, fi=fi: e.activation(out=xbcT[:, fc, :], in_=a_[:, fi, :], func=AF.Silu,
                                                                bias=C["cb"][:, fc:fc + 1]), reads=[a_, C["cb"]],
             writes=[xbcT])
    xs = P["xs"].get()
    pt = K.psum()
    ptb = pt.t[:, :].bitcast(BF16)
    for j in range(8):
        S.op("pe", lambda e, j=j: e.transpose(ptb[:, j * 128:(j + 1) * 128], xbcT[:, j, :], K.ident[:, :]),
             reads=[xbcT, K.ident], writes=[pt])
    S.op("act", lambda e: e.copy(out=xs[:, :], in_=ptb[:, :]), reads=[pt], writes=[xs])
    bm = P["bm"].get()
    pt2 = K.psum()
    ptb2 = pt2.t[:, :].bitcast(BF16)
    for g in range(2):
        S.op("pe", lambda e, g=g: e.transpose(ptb2[:, g * 128:(g + 1) * 128], xbcT[:, 8 + g, :], K.ident[:, :]),
             reads=[xbcT, K.ident], writes=[pt2])
    S.op("act", lambda e: e.copy(out=bm[:, :], in_=ptb2[:, 0:256]), reads=[pt2], writes=[bm])
    dt = P["dt"].get()
    S.dma("sp", dt[:, :], SC["DT"][t0:t0 + 128, :], writes=[dt])
    V = P["v"].get()
    S.op("dve", lambda e: e.tensor_tensor(out=V[:, 0:32], in0=dt[:, :], in1=C["dtb"][:, :], op=ALU.add),
         reads=[dt, C["dtb"]], writes=[V])
    S.op("act", lambda e: e.activation(out=V[:, 0:32], in_=V[:, 0:32], func=AF.Exp), reads=[V], writes=[V])
    S.op("act", lambda e: e.activation(out=V[:, 0:32], in_=V[:, 0:32], func=AF.Ln, bias=K.onec[:, 0:1]),
         reads=[V, K.onec], writes=[V])
    S.op("act", lambda e: e.activation(out=V[:, 32:64], in_=V[:, 0:32], func=AF.Ln), reads=[V], writes=[V])
    S.op("dve", lambda e: e.tensor_tensor(out=V[:, 64:96], in0=V[:, 0:32], in1=C["abc"][:, :], op=ALU.mult),
         reads=[V, C["abc"]], writes=[V])
    pc = K.psum()
    S.op("pe", lambda e: e.matmul(pc[:, 0:16], K.uinc[:, :], V[:, 64:80], start=True, stop=True),
         reads=[K.uinc, V], writes=[pc])
    S.op("pe", lambda e: e.matmul(pc[:, 16:32], K.uexc[:, :], V[:, 80:96], start=True, stop=True),
         reads=[K.uexc, V], writes=[pc])
    S.op("pe", lambda e: e.matmul(pc[:, 32:64], K.onesf[:, :], V[:, 64:96], start=True, stop=True),
         reads=[K.onesf, V], writes=[pc])
    S.op("dve", lambda e: e.tensor_copy(out=V[:, 96:160], in_=pc[:, 0:64]), reads=[pc], writes=[V])
    S.op("dve", lambda e: e.tensor_tensor(out=V[:, 160:176], in0=V[:, 32:48], in1=V[:, 96:112], op=ALU.subtract),
         reads=[V], writes=[V])
    S.op("dve", lambda e: e.tensor_tensor(out=V[:, 176:192], in0=V[:, 48:64], in1=V[:, 112:128], op=ALU.add),
         reads=[V], writes=[V])
    S.op("dve", lambda e: e.tensor_tensor(out=V[:, 192:208], in0=V[:, 160:176], in1=V[:, 128:144], op=ALU.add),
         reads=[V], writes=[V])
    S.op("dve", lambda e: e.tensor_tensor(out=V[:, 240:256], in0=V[:, 144:160], in1=V[:, 112:128], op=ALU.subtract),
         reads=[V], writes=[V])
    S.op("dve", lambda e: e.tensor_copy(out=V[:, 208:240], in_=V[:, 176:192].unsqueeze(1).broadcast_to([128, 2, 16])
                                        .rearrange("p a b -> p (a b)")) if False else
         e.tensor_copy(out=V[:, 208:224], in_=V[:, 176:192]), reads=[V], writes=[V])
    S.op("dve", lambda e: e.tensor_copy(out=V[:, 224:240], in_=V[:, 96:112]), reads=[V], writes=[V])
    S.op("act", lambda e: e.activation(out=V[:, 192:256], in_=V[:, 192:256], func=AF.Exp), reads=[V], writes=[V])
    S.op("act", lambda e: e.activation(out=V[:, 128:160], in_=V[:, 128:160], func=AF.Exp), reads=[V], writes=[V])
    return {"xbcT": xbcT, "xs": xs, "bm": bm, "V": V}


def ssd_load(K, SC, P, c):
    S = K.S
    xbcT, xs, bm, V = P["xbcT"].get(), P["xs"].get(), P["bm"].get(), P["v"].get()
    S.dma("sp", xbcT[:, :, :], SC["PX"][c], writes=[xbcT])
    S.dma("sp", xs[:, :], SC["PS"][c], writes=[xs])
    S.dma("sp", bm[:, :], SC["PB"][c], writes=[bm])
    S.dma("sp", V[:, :], SC["PV"][c], writes=[V])
    return {"xbcT": xbcT, "xs": xs, "bm": bm, "V": V}


def ssd_state_step(K, T, H, wcol, ecol, PS):
    S = K.S
    xs, bm, V = T["xs"], T["bm"], T["V"]
    xd = PS["xd"].get()
    S.op("pool", lambda e: e.tensor_tensor(out=xd[:, :].rearrange("p (h d) -> p h d", h=16),
                                           in0=xs[:, :].rearrange("p (h d) -> p h d", h=16),
                                           in1=V[:, wcol:wcol + 16].unsqueeze(2).broadcast_to([128, 16, 64]),
                                           op=ALU.mult), reads=[xs, V], writes=[xd])
    S.op("dve", lambda e: e.tensor_tensor(out=H[:, :].rearrange("p (h d) -> p h d", h=16),
                                          in0=H[:, :].rearrange("p (h d) -> p h d", h=16),
                                          in1=V[:, ecol:ecol + 16].unsqueeze(2).broadcast_to([128, 16, 64]),
                                          op=ALU.mult), reads=[H, V], writes=[H])
    for g in range(2):
        pp = K.psum()
        S.op("pe", lambda e, pp=pp, g=g: e.matmul(pp[:, :], bm[:, g * 128:(g + 1) * 128], xd[:, g * 512:(g + 1) * 512],
                                                  start=True, stop=True), reads=[bm, xd], writes=[pp])
        S.op("dve", lambda e, pp=pp, g=g: e.tensor_tensor(out=H[:, g * 512:(g + 1) * 512], in0=H[:, g * 512:(g + 1) * 512],
                                                          in1=pp[:, :], op=ALU.add), reads=[pp, H], writes=[H])


def ssd_bwd_pass(K, SC, C, nchunk, cps):
    S = K.S
    P = ssd_pools(K)
    PS = {"xd": Pool(K, NB, [128, 1024], BF16)}
    H = K.alloc([128, 1024], F32)
    hbp = Pool(K, NB, [128, 1024], BF16)
    S.op("dve", lambda e: e.memset(H[:, :], 0.0), writes=[H])
    def body(c):
        seg, cis = divmod(c, cps)
        if cis == cps - 1 and c != nchunk - 1:
            S.op("dve", lambda e, seg=seg: e.tensor_scalar(out=H[:, :], in0=H[:, :], scalar1=K.flags[:, seg:seg + 1],
                                                           scalar2=0.0, op0=ALU.mult, op1=ALU.add),
                 reads=[H, K.flags], writes=[H])
        T = Tn.pop(c)
        S.dma("pool", SC["PX"][c], T["xbcT"][:, :, :], reads=[T["xbcT"]])
        S.dma("pool", SC["PS"][c], T["xs"][:, :], reads=[T["xs"]])
        S.dma("pool", SC["PB"][c], T["bm"][:, :], reads=[T["bm"]])
        S.dma("pool", SC["PV"][c], T["V"][:, :], reads=[T["V"]])
        if c - 1 >= 0:
            Tn[c - 1] = ssd_prep(K, SC, C, P, c - 1, nchunk, cps)
        ssd_state_step(K, T, H, 208, 144, PS)
        hb = hbp.get()
        S.op("act", lambda e, hb=hb: e.copy(out=hb[:, :], in_=H[:, :]), reads=[H], writes=[hb])
        S.dma("pool", SC["HB"][c], hb[:, :], reads=[hb])

    Tn = {nchunk - 1: ssd_prep(K, SC, C, P, nchunk - 1, nchunk, cps)}
    for c in range(nchunk - 1, -1, -1):
        body(c)


def ssd_fwd_pass(K, SC, C, nchunk, cps):
    S = K.S
    P = ssd_pools(K)
    PS = {"xd": Pool(K, NB, [128, 1024], BF16)}
    H = K.alloc([128, 1024], F32)
    Hbf = K.alloc([128, 1024], BF16)
    S.op("dve", lambda e: e.memset(H[:, :], 0.0), writes=[H])
    hbp = Pool(K, NB, [128, 1024], BF16)
    cbp = Pool(K, NB, [128, 4, 128], F32)
    tq = Pool(K, NB, [128, 4, 128], F32)
    eq = Pool(K, NB, [128, 4, 128], F32)
    mq = Pool(K, NB, [128, 4, 128], F32)
    Mp = Pool(K, NB, [128, 16, 128], BF16)
    yp = Pool(K, NB, [128, 1024], F32)
    t1p = Pool(K, NB, [128, 512], F32)
    zp = Pool(K, NB, [128, 1024], F32)
    ybp = Pool(K, NB, [128, 1024], BF16)
    smp = Pool(K, 4, [128, 8], F32)
    jk = Pool(K, 1, [128, 512], BF16)
    def body(c):
        seg, cis = divmod(c, cps)
        t0 = c * 128
        if cis == 0 and c != 0:
            S.op("dve", lambda e, seg=seg: e.tensor_scalar(out=H[:, :], in0=H[:, :], scalar1=K.flags[:, seg - 1:seg],
                                                           scalar2=0.0, op0=ALU.mult, op1=ALU.add),
                 reads=[H, K.flags], writes=[H])
        S.op("act", lambda e: e.copy(out=Hbf[:, :], in_=H[:, :]), reads=[H], writes=[Hbf])
        hb = hbp.get()
        if c == nchunk - 1:
            S.op("pool", lambda e, hb=hb: e.memset(hb[:, :], 0.0), writes=[hb])
        else:
            S.dma("sp", hb[:, :], SC["HB"][c + 1], writes=[hb])
            if cis == cps - 1:
                S.op("pool", lambda e, hb=hb, seg=seg: e.tensor_scalar(
                    out=hb[:, :], in0=hb[:, :], scalar1=K.flags[:, seg:seg + 1], scalar2=0.0, op0=ALU.mult,
                    op1=ALU.add), reads=[hb, K.flags], writes=[hb])
        T = Tn.pop(c)
        if c + 1 < nchunk:
            Tn[c + 1] = ssd_load(K, SC, P, c + 1)
        xbcT, xs, V = T["xbcT"], T["xs"], T["V"]
        cbm = cbp.get()
        for g in range(2):
            pp = K.psum()
            S.op("pe", lambda e, pp=pp, g=g: e.matmul(pp[:, 0:128], xbcT[:, 8 + g, :], xbcT[:, 10 + g, :],
                                                      start=True, stop=True), reads=[xbcT], writes=[pp])
            S.op("dve", lambda e, pp=pp, g=g: e.tensor_tensor(out=cbm[:, 2 * g, :], in0=pp[:, 0:128], in1=K.mskf[:, :],
                                                              op=ALU.mult), reads=[pp, K.mskf], writes=[cbm])
            S.op("dve", lambda e, pp=pp, g=g: e.tensor_tensor(out=cbm[:, 2 * g + 1, :], in0=pp[:, 0:128],
                                                              in1=K.mskb[:, :], op=ALU.mult),
                 reads=[pp, K.mskb], writes=[cbm])
        M = Mp.get()
        for qd in range(4):
            g = qd // 2
            mqs = []
            for d in range(2):
                pa = K.psum()
                for hh in range(4):
                    h = qd * 4 + hh
                    col = 64 + d * 16 + h
                    S.op("pe", lambda e, pa=pa, hh=hh, col=col, d=d: e.matmul(
                        pa[:, hh * 128:(hh + 1) * 128], V[:, col:col + 1].broadcast_to([128, 128]),
                        (K.uinc if d == 0 else K.uexc)[:, :], start=True, stop=True),
                        reads=[V, K.uinc, K.uexc], writes=[pa])
                t = tq.get()
                pav = pa[:, :].rearrange("p (a b) -> p a b", a=4)
                if d == 0:
                    vb = V[:, 96 + qd * 4:96 + qd * 4 + 4].unsqueeze(2).broadcast_to([128, 4, 128])
                    S.op("dve", lambda e, t=t, pav=pav, vb=vb: e.tensor_tensor(out=t[:, :, :], in0=pav, in1=vb,
                                                                               op=ALU.subtract),
                         reads=[pa, V], writes=[t])
                else:
                    vb = V[:, 112 + qd * 4:112 + qd * 4 + 4].unsqueeze(2).broadcast_to([128, 4, 128])
                    S.op("dve", lambda e, t=t, pav=pav, vb=vb: e.tensor_tensor(out=t[:, :, :], in0=vb, in1=pav,
                                                                               op=ALU.subtract),
                         reads=[pa, V], writes=[t])
                S.op("dve", lambda e, t=t: e.tensor_scalar(out=t[:, :, :], in0=t[:, :, :], scalar1=0.0, scalar2=0.0,
                                                            op0=ALU.min, op1=ALU.add), reads=[t], writes=[t])
                E = eq.get()
                for hh in range(4):
                    h = qd * 4 + hh
                    S.op("act", lambda e, E=E, t=t, hh=hh, h=h, d=d: e.activation(
                        out=E[:, hh, :], in_=t[:, hh, :], func=AF.Exp, bias=V[:, 32 + d * 16 + h:33 + d * 16 + h]),
                        reads=[t, V], writes=[E])
                m_ = mq.get()
                S.op("dve", lambda e, m_=m_, E=E, g=g, d=d: e.tensor_tensor(
                    out=m_[:, :, :], in0=E[:, :, :], in1=cbm[:, 2 * g + d:2 * g + d + 1, :].broadcast_to([128, 4, 128]),
                    op=ALU.mult), reads=[E, cbm], writes=[m_])
                mqs.append(m_)
            S.op("dve", lambda e, a=mqs[0], b=mqs[1]: e.tensor_tensor(out=a[:, :, :], in0=a[:, :, :], in1=b[:, :, :],
                                                                       op=ALU.add), reads=[mqs[0], mqs[1]], writes=[mqs[0]])
            S.op("dve", lambda e, a=mqs[0], qd=qd: e.tensor_tensor(out=M[:, qd * 4:(qd + 1) * 4, :], in0=a[:, :, :],
                                                                    in1=C["dI"][:, qd * 4:(qd + 1) * 4, :], op=ALU.add),
                 reads=[mqs[0], C["dI"]], writes=[M])
        yi = [K.psum(), K.psum()]
        yo = [[K.psum(), K.psum()], [K.psum(), K.psum()]]
        for h in range(16):
            g, hh = divmod(h, 8)
            S.op("pe", lambda e, h=h, g=g, hh=hh: e.matmul(yi[g][:, hh * 64:(hh + 1) * 64], M[:, h, :],
                                                           xs[:, h * 64:(h + 1) * 64], start=True, stop=True),
                 reads=[M, xs], writes=[yi[g]])
        for d, Hs in enumerate((Hbf, hb)):
            for g in range(2):
                S.op("pe", lambda e, d=d, g=g, Hs=Hs: e.matmul(yo[d][g][:, :], xbcT[:, 10 + g, :],
                                                               Hs[:, g * 512:(g + 1) * 512], start=True, stop=True),
                     reads=[xbcT, Hs], writes=[yo[d][g]])
        y = yp.get()
        for g in range(2):
            t1 = t1p.get()
            scf = V[:, 224 + g * 8:232 + g * 8].unsqueeze(2).broadcast_to([128, 8, 64])
            scb = V[:, 240 + g * 8:248 + g * 8].unsqueeze(2).broadcast_to([128, 8, 64])
            S.op("dve", lambda e, t1=t1, g=g, scf=scf: e.tensor_tensor(
                out=t1[:, :].rearrange("p (h d) -> p h d", h=8), in0=yo[0][g][:, :].rearrange("p (h d) -> p h d", h=8),
                in1=scf, op=ALU.mult), reads=[yo[0][g], V], writes=[t1])
            S.op("dve", lambda e, t1=t1, g=g: e.tensor_tensor(out=t1[:, :], in0=t1[:, :], in1=yi[g][:, :], op=ALU.add),
                 reads=[t1, yi[g]], writes=[t1])
            S.op("dve", lambda e, g=g, scb=scb, y=y: e.tensor_tensor(
                out=y[:, g * 512:(g + 1) * 512].rearrange("p (h d) -> p h d", h=8),
                in0=yo[1][g][:, :].rearrange("p (h d) -> p h d", h=8), in1=scb, op=ALU.mult),
                reads=[yo[1][g], V], writes=[y])
            S.op("pool", lambda e, g=g, t1=t1, y=y: e.tensor_tensor(out=y[:, g * 512:(g + 1) * 512],
                                                                    in0=y[:, g * 512:(g + 1) * 512], in1=t1[:, :],
                                                                    op=ALU.add), reads=[t1, y], writes=[y])
        if "DY" in SC:
            S.dma("pool", SC["DY"][t0:t0 + 128, :], y[:, :], reads=[y])
            S.dma("pool", SC["DV"][t0:t0 + 128, :], V[:, :], reads=[V])
            S.dma("pool", SC["DM"][c], M[:, :, :], reads=[M])
        z = zp.get()
        S.dma("sp", z[:, :], SC["Z"][t0:t0 + 128, :], writes=[z])
        S.op("act", lambda e, z=z: e.activation(out=z[:, :], in_=z[:, :], func=AF.Silu), reads=[z], writes=[z])
        S.op("pool", lambda e, z=z, y=y: e.tensor_tensor(out=y[:, :], in0=y[:, :], in1=z[:, :], op=ALU.mult),
             reads=[y, z], writes=[y])
        sm = smp.get()
        j_ = jk.get()
        yb = ybp.get()
        for g in range(2):
            S.op("act", lambda e, g=g, sm=sm, j_=j_, y=y: e.activation(
                out=j_[:, :], in_=y[:, g * 512:(g + 1) * 512], func=AF.Square, accum_out=sm[:, g:g + 1]),
                reads=[y], writes=[j_, sm])
        S.op("act", lambda e, sm=sm: e.activation(out=sm[:, 2:4], in_=sm[:, 0:2], func=AF.Sqrt, bias=K.epsc[:, 0:1],
                                                  scale=1.0 / 512), reads=[sm, K.epsc], writes=[sm])
        S.op("dve", lambda e, sm=sm: e.reciprocal(out=sm[:, 4:6], in_=sm[:, 2:4]), reads=[sm], writes=[sm])
        for g in range(2):
            S.op("dve", lambda e, g=g, sm=sm, y=y, yb=yb: e.scalar_tensor_tensor(
                out=yb[:, g * 512:(g + 1) * 512], in0=y[:, g * 512:(g + 1) * 512], scalar=sm[:, 4 + g:5 + g],
                in1=C["ng"][:, g * 512:(g + 1) * 512], op0=ALU.mult, op1=ALU.mult),
                reads=[y, sm, C["ng"]], writes=[yb])
        S.dma("pool", SC["YS"][t0:t0 + 128, :], yb[:, :], reads=[yb])
        ssd_state_step(K, T, H, 192, 128, PS)

    Tn = {0: ssd_load(K, SC, P, 0)}
    for c in range(nchunk):
        body(c)


RL = float(_os.environ.get('MK_RL', '99'))
LW_C = 0.6065306597126334
GN_EPS = 64e-5


def rwkv_setup(K, A, l):
    S = K.S
    C = {}

    def ld(name, shape, dt, q, src, slow=False):
        C[name] = K.alloc(shape, dt)
        S.dma(q, C[name][tuple(slice(None) for _ in shape)], src, writes=[C[name]], slow=slow)

    ld("wup", [64, 2, 512], BF16, "pool", A["rwkv_w_up"][l].rearrange("n l c -> l n c"))
    ld("aup", [64, 2, 512], BF16, "pool", A["rwkv_a_up"][l].rearrange("n l c -> l n c"))
    C["gup"] = K.alloc([64, 3, 512], BF16)
    S.dma("pool", C["gup"][:, 0:2, :], A["rwkv_g_up"][l][0:128, :].rearrange("(q l) c -> l q c", l=64), writes=[C["gup"]])
    S.dma("pool", C["gup"][0:32, 2, :], A["rwkv_g_up"][l][128:160, :], writes=[C["gup"]])
    C["w0"] = K.alloc([128, 2, 512], F32)
    for n in range(2):
        S.dma("sp", C["w0"][:, n, :], A["rwkv_w0"][l][n:n + 1, :].broadcast_to([128, 512]), writes=[C["w0"]])
    C["a0"] = K.alloc([64, 2, 8], F32)
    for n in range(2):
        S.dma("sp", C["a0"][:, n, :], A["rwkv_a0"][l][n].rearrange("(h j) -> j h", j=64), writes=[C["a0"]], slow=True)
    for nm, src in (("kk_", A["rwkv_k_k"][l]), ("ka", A["rwkv_k_a"][l])):
        C[nm] = K.alloc([64, 8], F32)
        S.dma("sp", C[nm][:, :], src.rearrange("(h j) -> j h", j=64), writes=[C[nm]], slow=True)
    C["rk"] = K.alloc([64, 8], F32)
    S.dma("sp", C["rk"][:, :], A["rwkv_r_k"][l].rearrange("h j -> j h"), writes=[C["rk"]], slow=True)
    C["omka"] = K.alloc([64, 8], F32)
    S.op("dve", lambda e: e.tensor_scalar(out=C["omka"][:, :], in0=C["ka"][:, :], scalar1=-1.0, scalar2=1.0,
                                          op0=ALU.mult, op1=ALU.add), reads=[C["ka"]], writes=[C["omka"]])
    C["mu"] = K.alloc([64, 3, 31], F32)
    S.op("dve", lambda e: e.memset(C["mu"][:, :, :], 0.0), writes=[C["mu"]])
    for m in range(2):
        S.dma("sp", C["mu"][:, m, 0:30], A["rwkv_mu"][l][m, 0:1920].rearrange("(g j) -> j g", j=64), writes=[C["mu"]],
              slow=True)
        S.dma("sp", C["mu"][0:32, m, 30:31], A["rwkv_mu"][l][m, 1920:1952].rearrange("(g j) -> j g", j=32),
              writes=[C["mu"]], slow=True)
    S.op("dve", lambda e: e.tensor_tensor(out=C["mu"][:, 2, :], in0=C["mu"][:, 0, :], in1=C["mu"][:, 1, :], op=ALU.add),
         reads=[C["mu"]], writes=[C["mu"]])
    S.op("dve", lambda e: e.tensor_scalar(out=C["mu"][:, 2, :], in0=C["mu"][:, 2, :], scalar1=-1.0, scalar2=1.0,
                                          op0=ALU.mult, op1=ALU.add), reads=[C["mu"]], writes=[C["mu"]])
    C["lng"] = K.alloc([128, 512], F32)
    S.dma("sp", C["lng"][:, :], A["rwkv_ln_g"][l:l + 1, :].broadcast_to([128, 512]), writes=[C["lng"]])
    C["lnb"] = K.alloc([128, 512], F32)
    S.dma("sp", C["lnb"][:, :], A["rwkv_ln_b"][l:l + 1, :].broadcast_to([128, 512]), writes=[C["lnb"]])
    C["gneps"] = K.alloc([128, 1], F32)
    S.op("dve", lambda e: e.memset(C["gneps"][:, :], GN_EPS), writes=[C["gneps"]])
    return C


def shift_pass(K, SC, A, l, ntok, seglen):
    S = K.S
    K.new_pass()
    mu = K.alloc([128, 3, 16], F32)
    S.op("dve", lambda e: e.memset(mu[:, :, :], 0.0), writes=[mu])
    for m in range(2):
        S.dma("sp", mu[:, m, 0:15], A["rwkv_mu"][l][m, 0:1920].rearrange("(g j) -> j g", j=128), writes=[mu], slow=True)
        S.dma("sp", mu[0:32, m, 15:16], A["rwkv_mu"][l][m, 1920:1952].rearrange("(g j) -> j g", j=32), writes=[mu],
              slow=True)
    S.op("dve", lambda e: e.tensor_tensor(out=mu[:, 2, :], in0=mu[:, 0, :], in1=mu[:, 1, :], op=ALU.add),
         reads=[mu], writes=[mu])
    S.op("dve", lambda e: e.tensor_scalar(out=mu[:, 2, :], in0=mu[:, 2, :], scalar1=-1.0, scalar2=1.0, op0=ALU.mult,
                                          op1=ALU.add), reads=[mu], writes=[mu])
    xp = Pool(K, 2, [128, 16, 514], F32)
    ap = Pool(K, 2, [128, 16, 512], F32)
    RW = SC["RWT"]

    def body(mt):
        t0 = mt * 512
        X = xp.get()
        lo, hi = max(t0 - 1, 0), min(t0 + 513, ntok)
        a_, b_ = lo - (t0 - 1), hi - (t0 - 1)
        S.dma("sp", X[:, 0:15, a_:b_], RW[0:1920, lo:hi].rearrange("(g j) t -> j g t", j=128), writes=[X])
        S.dma("sp", X[0:32, 15, a_:b_], RW[1920:1952, lo:hi], writes=[X])
        seg = t0 // seglen
        if t0 == 0:
            S.op("pool", lambda e: e.memset(X[:, :, 0:1], 0.0), writes=[X])
        elif t0 % seglen == 0:
            S.op("pool", lambda e: e.tensor_scalar(out=X[:, :, 0:1], in0=X[:, :, 0:1], scalar1=K.flags[:, seg - 1:seg],
                                                   scalar2=0.0, op0=ALU.mult, op1=ALU.add), reads=[X, K.flags], writes=[X])
        if t0 + 512 >= ntok:
            S.op("pool", lambda e: e.memset(X[:, :, 513:514], 0.0), writes=[X])
        elif (t0 + 512) % seglen == 0:
            S.op("pool", lambda e: e.tensor_scalar(out=X[:, :, 513:514], in0=X[:, :, 513:514],
                                                   scalar1=K.flags[:, seg:seg + 1], scalar2=0.0, op0=ALU.mult,
                                                   op1=ALU.add), reads=[X, K.flags], writes=[X])
        acc = ap.get()
        for fc in range(16):
            eng = "dve"
            S.op("pool", lambda e, fc=fc: e.tensor_scalar(out=acc[:, fc, :], in0=X[:, fc, 1:513], scalar1=mu[:, 2, fc:fc + 1],
                                                       scalar2=0.0, op0=ALU.mult, op1=ALU.add),
                 reads=[X, mu], writes=[acc])
            S.op(eng, lambda e, fc=fc: e.scalar_tensor_tensor(out=acc[:, fc, :], in0=X[:, fc, 0:512],
                                                              scalar=mu[:, 0, fc:fc + 1], in1=acc[:, fc, :],
                                                              op0=ALU.mult, op1=ALU.add), reads=[X, mu, acc], writes=[acc])
            S.op(eng, lambda e, fc=fc: e.scalar_tensor_tensor(out=acc[:, fc, :], in0=X[:, fc, 2:514],
                                                              scalar=mu[:, 1, fc:fc + 1], in1=acc[:, fc, :],
                                                              op0=ALU.mult, op1=ALU.add), reads=[X, mu, acc], writes=[acc])
        S.dma("pool", SC["RWS"][0:1920, t0:t0 + 512].rearrange("(g j) t -> j g t", j=128), acc[:, 0:15, :], reads=[acc])
        S.dma("pool", SC["RWS"][1920:1952, t0:t0 + 512], acc[0:32, 15, :], reads=[acc])

    for mt in range(ntok // 512):
        body(mt)


def rwkv_pools(K):
    P = {}
    P["Pt"] = Pool(K, 2, [64, 31, 128], F32)
    P["tw"] = Pool(K, 2, [64, 128], BF16)
    P["ad"] = Pool(K, 2, [64, 128], BF16)
    P["sg"] = Pool(K, 2, [128, 512], F32)
    P["f8"] = Pool(K, 11, [64, 8, 128], F32)
    P["b8"] = Pool(K, 12, [64, 8, 128], BF16)
    P["tok"] = Pool(K, 12, [128, 512], BF16)
    P["vf"] = Pool(K, 2, [128, 512], F32)
    P["pl"] = Pool(K, 2, [64, 8], F32)
    P["mat"] = Pool(K, 16, [128, 4, 128], BF16)
    P["res"] = Pool(K, 12, [128, 4, 128], BF16)
    P["w"] = Pool(K, 2, [128, 512], F32)
    P["rkc"] = Pool(K, 2, [128, 8], F32)
    return P


def rwkv_prep(K, SC, C, P, c, nchunk, cps, n, want_g):
    S = K.S
    t0 = c * 128
    ntok = nchunk * 128
    seg, cis = divmod(c, cps)
    Pt = P["Pt"].get()
    S.dma("sp", Pt[:, 0:30, :], SC["RWS"][0:1920, t0:t0 + 128].rearrange("(g j) t -> j g t", j=64), writes=[Pt])
    S.dma("sp", Pt[0:32, 30, :], SC["RWS"][1920:1952, t0:t0 + 128], writes=[Pt])
    if RL <= 1:
        return None
    rT, kT, vT = Pt[:, 0:8, :], Pt[:, 8:16, :], Pt[:, 16:24, :]
    tw = P["tw"].get()
    S.op("act", lambda e: e.activation(out=tw[:, :], in_=Pt[:, 24 + n, :], func=AF.Tanh), reads=[Pt], writes=[tw])
    pu = K.psum()
    S.op("pe", lambda e: e.matmul(pu[:, :], tw[:, :], C["wup"][:, n, :], start=True, stop=True),
         reads=[tw, C["wup"]], writes=[pu])
    sg = P["sg"].get()
    S.op("dve", lambda e: e.tensor_tensor(out=sg[:, :], in0=pu[:, :], in1=C["w0"][:, n, :], op=ALU.add),
         reads=[pu, C["w0"]], writes=[sg])
    S.op("act", lambda e: e.activation(out=sg[:, :], in_=sg[:, :], func=AF.Sigmoid), reads=[sg], writes=[sg])
    if RL <= 2:
        return None
    uin = K.uinc if n == 0 else K.mskb
    uex = K.uexc if n == 0 else K.ugt
    eP, eN, ePx = P["f8"].get(), P["f8"].get(), P["f8"].get()
    for hq in range(2):
        pi, px = K.psum(), K.psum()
        for hh in range(4):
            h = hq * 4 + hh
            S.op("pe", lambda e, pi=pi, hh=hh, h=h: e.matmul(pi[0:64, hh * 128:(hh + 1) * 128], sg[:, h * 64:(h + 1) * 64],
                                                             uin[:, :], start=True, stop=True),
                 reads=[sg, uin], writes=[pi])
            S.op("pe", lambda e, px=px, hh=hh, h=h: e.matmul(px[0:64, hh * 128:(hh + 1) * 128], sg[:, h * 64:(h + 1) * 64],
                                                             uex[:, :], start=True, stop=True),
                 reads=[sg, uex], writes=[px])
        sl = slice(hq * 4, hq * 4 + 4)
        S.op("act", lambda e, pi=pi, sl=sl: e.activation(out=eP[:, sl, :].rearrange("p a b -> p (a b)"), in_=pi[0:64, :],
                                                         func=AF.Exp, scale=-LW_C), reads=[pi], writes=[eP])
        S.op("act", lambda e, pi=pi, sl=sl: e.activation(out=eN[:, sl, :].rearrange("p a b -> p (a b)"), in_=pi[0:64, :],
                                                         func=AF.Exp, scale=LW_C), reads=[pi], writes=[eN])
        S.op("act", lambda e, px=px, sl=sl: e.activation(out=ePx[:, sl, :].rearrange("p a b -> p (a b)"), in_=px[0:64, :],
                                                         func=AF.Exp, scale=-LW_C), reads=[px], writes=[ePx])
    last = 127 if n == 0 else 0
    pl = P["pl"].get()
    S.op("dve", lambda e: e.tensor_copy(out=pl[:, :], in_=eP[:, :, last]), reads=[eP], writes=[pl])
    if RL <= 3:
        return None
    ad = P["ad"].get()
    S.op("act", lambda e: e.copy(out=ad[:, :], in_=Pt[:, 26 + n, :]), reads=[Pt], writes=[ad])
    ic = P["f8"].get()
    for hq in range(2):
        pa = K.psum()
        for hh in range(4):
            h = hq * 4 + hh
            S.op("pe", lambda e, pa=pa, hh=hh, h=h: e.matmul(pa[0:64, hh * 128:(hh + 1) * 128],
                                                             C["aup"][:, n, h * 64:(h + 1) * 64], ad[:, :],
                                                             start=True, stop=True), reads=[C["aup"], ad], writes=[pa])
        sl = slice(hq * 4, hq * 4 + 4)
        S.op("dve", lambda e, pa=pa, sl=sl: e.tensor_tensor(
            out=ic[:, sl, :], in0=pa[0:64, :].rearrange("p (a b) -> p a b", a=4),
            in1=C["a0"][:, n, sl].unsqueeze(2).broadcast_to([64, 4, 128]), op=ALU.add), reads=[pa, C["a0"]], writes=[ic])
    S.op("act", lambda e: e.activation(out=ic[:, :, :], in_=ic[:, :, :], func=AF.Sigmoid), reads=[ic], writes=[ic])
    if RL <= 4:
        return None
    kk = P["f8"].get()
    sq = P["f8"].get()
    h8 = lambda t_: t_[:, :].unsqueeze(2).broadcast_to([64, 8, 128])
    S.op("dve", lambda e: e.tensor_tensor(out=kk[:, :, :], in0=kT, in1=h8(C["kk_"]), op=ALU.mult),
         reads=[Pt, C["kk_"]], writes=[kk])
    S.op("pool", lambda e: e.tensor_tensor(out=sq[:, :, :], in0=kk[:, :, :], in1=kk[:, :, :], op=ALU.mult),
         reads=[kk], writes=[sq])
    for hq in range(2):
        pn = K.psum()
        sl = slice(hq * 4, hq * 4 + 4)
        S.op("pe", lambda e, pn=pn, sl=sl: e.matmul(pn[0:64, :], K.onesf[0:64, 0:64],
                                                    sq[:, sl, :].rearrange("p a b -> p (a b)"), start=True, stop=True),
             reads=[K.onesf, sq], writes=[pn])
        S.op("dve", lambda e, pn=pn, sl=sl: e.tensor_scalar(out=sq[:, sl, :].rearrange("p a b -> p (a b)"), in0=pn[0:64, :],
                                                            scalar1=1e-24, scalar2=0.0, op0=ALU.max, op1=ALU.add),
             reads=[pn], writes=[sq])
    S.op("act", lambda e: e.activation(out=sq[:, :, :], in_=sq[:, :, :], func=AF.Ln), reads=[sq], writes=[sq])
    S.op("act", lambda e: e.activation(out=sq[:, :, :], in_=sq[:, :, :], func=AF.Exp, scale=-0.5), reads=[sq], writes=[sq])
    S.op("dve", lambda e: e.tensor_tensor(out=kk[:, :, :], in0=kk[:, :, :], in1=sq[:, :, :], op=ALU.mult),
         reads=[kk, sq], writes=[kk])
    if RL <= 5:
        return None
    km = P["f8"].get()
    S.op("pool", lambda e: e.tensor_tensor(out=km[:, :, :], in0=ic[:, :, :], in1=h8(C["ka"]), op=ALU.mult),
         reads=[ic, C["ka"]], writes=[km])
    S.op("pool", lambda e: e.tensor_tensor(out=km[:, :, :], in0=km[:, :, :], in1=h8(C["omka"]), op=ALU.add),
         reads=[km, C["omka"]], writes=[km])
    S.op("pool", lambda e: e.tensor_tensor(out=km[:, :, :], in0=km[:, :, :], in1=kT, op=ALU.mult),
         reads=[km, Pt], writes=[km])
    bb = P["f8"].get()
    S.op("dve", lambda e: e.tensor_tensor(out=bb[:, :, :], in0=kk[:, :, :], in1=ic[:, :, :], op=ALU.mult),
         reads=[kk, ic], writes=[bb])
    RtT, AtT, BhT, KhT = [P["b8"].get() for _ in range(4)]
    BbT, KbT = P["f8"].get(), P["f8"].get()
    S.op("dve", lambda e: e.tensor_tensor(out=RtT[:, :, :], in0=rT, in1=eP[:, :, :], op=ALU.mult),
         reads=[Pt, eP], writes=[RtT])
    S.op("dve", lambda e: e.scalar_tensor_tensor(out=AtT[:, :, :], in0=kk[:, :, :], scalar=-1.0, in1=ePx[:, :, :],
                                                 op0=ALU.mult, op1=ALU.mult), reads=[kk, ePx], writes=[AtT])
    S.op("pool", lambda e: e.tensor_tensor(out=bb[:, :, :], in0=bb[:, :, :], in1=eN[:, :, :], op=ALU.mult),
         reads=[bb, eN], writes=[bb])
    S.op("pool", lambda e: e.tensor_tensor(out=sq[:, :, :], in0=km[:, :, :], in1=eN[:, :, :], op=ALU.mult),
         reads=[km, eN, sq], writes=[sq])
    S.op("act", lambda e: e.copy(out=BhT[:, :, :], in_=bb[:, :, :]), reads=[bb], writes=[BhT])
    S.op("act", lambda e: e.copy(out=KhT[:, :, :], in_=sq[:, :, :]), reads=[sq], writes=[KhT])
    plb = pl[:, :].unsqueeze(2).broadcast_to([64, 8, 128])
    S.op("dve", lambda e: e.tensor_tensor(out=BbT[:, :, :], in0=bb[:, :, :], in1=plb, op=ALU.mult),
         reads=[bb, pl], writes=[BbT])
    S.op("dve", lambda e: e.tensor_tensor(out=KbT[:, :, :], in0=sq[:, :, :], in1=plb, op=ALU.mult),
         reads=[sq, pl], writes=[KbT])
    if RL <= 6:
        return None
    pv = K.psum()
    for h in range(8):
        S.op("pe", lambda e, h=h: e.matmul(pv[:, h * 64:(h + 1) * 64], Pt[:, 16 + h, :], K.identf[0:64, 0:64],
                                           start=True, stop=True), reads=[Pt, K.identf], writes=[pv])
    if RL <= 6.2:
        return None
    Vf = P["vf"].get()
    Vb = P["tok"].get()
    S.op("act", lambda e: e.copy(out=Vf[:, :], in_=pv[:, :]), reads=[pv], writes=[Vf])
    S.op("dve", lambda e: e.tensor_copy(out=Vb[:, :], in_=Vf[:, :]), reads=[Vf], writes=[Vb])
    if RL <= 6.5:
        return None
    outs = []
    for src in (BbT, KbT):
        pb = K.psum()
        for h in range(8):
            S.op("pe", lambda e, h=h, src=src, pb=pb: e.matmul(pb[:, h * 64:(h + 1) * 64], src[:, h, :],
                                                               K.identf[0:64, 0:64], start=True, stop=True),
                 reads=[src, K.identf], writes=[pb])
        o = P["tok"].get()
        S.op("act", lambda e, o=o, pb=pb: e.copy(out=o[:, :], in_=pb[:, :]), reads=[pb], writes=[o])
        outs.append(o)
    T = {"RtT": RtT, "AtT": AtT, "BhT": BhT, "KhT": KhT, "Vb": Vb, "Vf": Vf, "Bb": outs[0], "Kb": outs[1], "pl": pl}
    if RL <= 7:
        return None
    S.op("dve", lambda e: e.tensor_tensor(out=km[:, :, :], in0=km[:, :, :], in1=rT, op=ALU.mult),
         reads=[km, Pt], writes=[km])
    S.op("dve", lambda e: e.tensor_tensor(out=km[:, :, :], in0=km[:, :, :], in1=h8(C["rk"]), op=ALU.mult),
         reads=[km, C["rk"]], writes=[km])
    pr = K.psum()
    for h in range(8):
        S.op("pe", lambda e, h=h: e.matmul(pr[:, h:h + 1], km[:, h, :], K.onesf[0:64, 0:1], start=True, stop=True),
             reads=[km, K.onesf], writes=[pr])
    rkc = P["rkc"].get()
    S.op("dve", lambda e: e.tensor_copy(out=rkc[:, :], in_=pr[:, 0:8]), reads=[pr], writes=[rkc])
    T["rk"] = rkc
    if RL <= 8:
        return None
    if want_g:
        sgd = P["b8"].get()
        S.op("act", lambda e: e.activation(out=sgd[:, 0:3, :], in_=Pt[:, 28:31, :], func=AF.Sigmoid), reads=[Pt],
             writes=[sgd])
        pg = K.psum()
        for q in range(3):
            rows = 64 if q < 2 else 32
            S.op("pe", lambda e, q=q, rows=rows: e.matmul(pg[:, :], sgd[0:rows, q, :], C["gup"][0:rows, q, :],
                                                          start=(q == 0), stop=(q == 2)), reads=[sgd, C["gup"]], writes=[pg])
        gt = P["w"].get()
        S.op("act", lambda e: e.copy(out=gt[:, :], in_=pg[:, :]), reads=[pg], writes=[gt])
        T["g"] = gt
    return T


def rwkv_intra(K, P, T, n):
    S = K.S
    AtT, BhT, KhT, RtT = T["AtT"], T["BhT"], T["KhT"], T["RtT"]
    m_strict_sr = K.ugt if n == 0 else K.uexc
    m_strict_rs = K.uexc if n == 0 else K.ugt
    m_incl_st = K.uinc if n == 0 else K.mskb
    res = {"TT": [], "AakT": [], "ArbT": [], "ArkT": []}

    def prod(lhs, rhs, hq, mask, pool="mat"):
        pp = K.psum()
        for hh in range(4):
            h = hq * 4 + hh
            S.op("pe", lambda e, hh=hh, h=h: e.matmul(pp[:, hh * 128:(hh + 1) * 128], lhs[:, h, :], rhs[:, h, :],
                                                      start=True, stop=True), reads=[lhs, rhs], writes=[pp])
        o = P[pool].get()
        S.op("dve", lambda e: e.tensor_tensor(out=o[:, :, :], in0=pp[:, :].rearrange("p (a b) -> p a b", a=4),
                                              in1=mask[:, :].unsqueeze(1).broadcast_to([128, 4, 128]), op=ALU.mult),
             reads=[pp, mask], writes=[o])
        return o

    def mm4(lhs, rhs, addto=None, eng="act", pool="mat"):
        pp = K.psum()
        for hh in range(4):
            S.op("pe", lambda e, hh=hh: e.matmul(pp[:, hh * 128:(hh + 1) * 128], lhs[:, hh, :], rhs[:, hh, :],
                                                 start=True, stop=True), reads=[lhs, rhs], writes=[pp])
        o = P[pool].get()
        if addto is None:
            S.op(eng, (lambda e: e.copy(out=o[:, :, :].rearrange("p a b -> p (a b)"), in_=pp[:, :])) if eng == "act" else
                 (lambda e: e.tensor_copy(out=o[:, :, :].rearrange("p a b -> p (a b)"), in_=pp[:, :])),
                 reads=[pp], writes=[o])
        else:
            S.op("dve", lambda e: e.tensor_tensor(out=o[:, :, :].rearrange("p a b -> p (a b)"), in0=pp[:, :],
                                                  in1=addto[:, :, :].rearrange("p a b -> p (a b)"), op=ALU.add),
                 reads=[pp, addto], writes=[o])
        return o

    Ms = [prod(AtT, BhT, hq, m_strict_sr) for hq in range(2)]
    MTs = [prod(BhT, AtT, hq, m_strict_rs) for hq in range(2)]
    TTs = []
    for hq in range(2):
        TT = P["mat"].get()
        S.op("pool", lambda e, TT=TT, MT=MTs[hq]: e.tensor_tensor(
            out=TT[:, :, :], in0=MT[:, :, :], in1=K.ident[:, :].unsqueeze(1).broadcast_to([128, 4, 128]), op=ALU.add),
            reads=[MTs[hq], K.ident], writes=[TT])
        TTs.append(TT)
    for hq in range(2):
        res["AakT"].append(prod(KhT, AtT, hq, m_strict_rs, pool="res"))
        res["ArbT"].append(prod(BhT, RtT, hq, m_incl_st, pool="res"))
        res["ArkT"].append(prod(KhT, RtT, hq, m_incl_st, pool="res"))
    for k in range(1, 7):
        M2s, MT2s = [], []
        for hq in range(2):
            M2s.append(mm4(MTs[hq], Ms[hq], eng="act"))
        for hq in range(2):
            MT2s.append(mm4(Ms[hq], MTs[hq], eng="act") if k < 6 else None)
        for hq in range(2):
            TTs[hq] = mm4(M2s[hq], TTs[hq], addto=TTs[hq], pool=("res" if k == 6 else "mat"))
        Ms, MTs = M2s, MT2s
    res["TT"] = TTs
    return res


def rwkv_seq(K, P, T, I_, St, Sb):
    S = K.S
    AtT, RtT, Vb, Bb, Kb, pl = T["AtT"], T["RtT"], T["Vb"], T["Bb"], T["Kb"], T["pl"]
    pw = K.psum()
    for h in range(8):
        hq, hh = divmod(h, 4)
        S.op("pe", lambda e, h=h: e.matmul(pw[:, h * 64:(h + 1) * 64], AtT[:, h, :], Sb[:, h, :], start=True, stop=False),
             reads=[AtT, Sb], writes=[pw])
        S.op("pe", lambda e, h=h, hq=hq, hh=hh: e.matmul(pw[:, h * 64:(h + 1) * 64], I_["AakT"][hq][:, hh, :],
                                                         Vb[:, h * 64:(h + 1) * 64], start=False, stop=True),
             reads=[I_["AakT"][hq], Vb], writes=[pw])
    Wb = P["tok"].get()
    S.op("act", lambda e: e.copy(out=Wb[:, :], in_=pw[:, :]), reads=[pw], writes=[Wb])
    pu = K.psum()
    for h in range(8):
        hq, hh = divmod(h, 4)
        S.op("pe", lambda e, h=h, hq=hq, hh=hh: e.matmul(pu[:, h * 64:(h + 1) * 64], I_["TT"][hq][:, hh, :],
                                                         Wb[:, h * 64:(h + 1) * 64], start=True, stop=True),
             reads=[I_["TT"][hq], Wb], writes=[pu])
    Ub = P["tok"].get()
    S.op("dve", lambda e: e.tensor_copy(out=Ub[:, :], in_=pu[:, :]), reads=[pu], writes=[Ub])
    py = K.psum()
    for h in range(8):
        hq, hh = divmod(h, 4)
        S.op("pe", lambda e, h=h: e.matmul(py[:, h * 64:(h + 1) * 64], RtT[:, h, :], Sb[:, h, :], start=True, stop=False),
             reads=[RtT, Sb], writes=[py])
        S.op("pe", lambda e, h=h, hq=hq, hh=hh: e.matmul(py[:, h * 64:(h + 1) * 64], I_["ArbT"][hq][:, hh, :],
                                                         Ub[:, h * 64:(h + 1) * 64], start=False, stop=False),
             reads=[I_["ArbT"][hq], Ub], writes=[py])
        S.op("pe", lambda e, h=h, hq=hq, hh=hh: e.matmul(py[:, h * 64:(h + 1) * 64], I_["ArkT"][hq][:, hh, :],
                                                         Vb[:, h * 64:(h + 1) * 64], start=False, stop=True),
             reads=[I_["ArkT"][hq], Vb], writes=[py])
    ps = K.psum()
    for h in range(8):
        S.op("pe", lambda e, h=h: e.matmul(ps[0:64, h * 64:(h + 1) * 64], Bb[:, h * 64:(h + 1) * 64],
                                           Ub[:, h * 64:(h + 1) * 64], start=True, stop=False),
             reads=[Bb, Ub], writes=[ps])
        S.op("pe", lambda e, h=h: e.matmul(ps[0:64, h * 64:(h + 1) * 64], Kb[:, h * 64:(h + 1) * 64],
                                           Vb[:, h * 64:(h + 1) * 64], start=False, stop=True),
             reads=[Kb, Vb], writes=[ps])
    S.op("dve", lambda e: e.tensor_tensor(out=St[:, :, :], in0=St[:, :, :],
                                          in1=pl[:, :].unsqueeze(2).broadcast_to([64, 8, 64]), op=ALU.mult),
         reads=[St, pl], writes=[St])
    S.op("dve", lambda e: e.tensor_tensor(out=St[:, :, :], in0=St[:, :, :],
                                          in1=ps[0:64, :].rearrange("p (a b) -> p a b", a=8), op=ALU.add),
         reads=[St, ps], writes=[St])
    S.op("act", lambda e: e.copy(out=Sb[:, :, :], in_=St[:, :, :]), reads=[St], writes=[Sb])
    return py


def rwkv_dir_pass(K, SC, C, nchunk, cps, n):
    S = K.S
    P = rwkv_pools(K)
    St = K.alloc([64, 8, 64], F32)
    Sb = K.alloc([64, 8, 64], BF16)
    S.op("dve", lambda e: e.memset(St[:, :, :], 0.0), writes=[St])
    S.op("dve", lambda e: e.memset(Sb[:, :, :], 0.0), writes=[Sb])
    ybp = Pool(K, 2, [128, 520], F32)
    y1p = Pool(K, 2, [128, 520], F32)
    yw = Pool(K, 2, [128, 8, 64], F32)
    yc = Pool(K, 2, [128, 8, 64], F32)
    smp = Pool(K, 4, [128, 32], F32)
    outp = Pool(K, 2, [128, 512], BF16)

    def body(c):
        seg, cis = divmod(c, cps)
        t0 = c * 128
        first = (cis == 0) if n == 0 else (cis == cps - 1)
        edge = (c == 0) if n == 0 else (c == nchunk - 1)
        if first and not edge:
            fl = K.flags[0:64, seg - 1:seg] if n == 0 else K.flags[0:64, seg:seg + 1]
            S.op("dve", lambda e: e.tensor_scalar(out=St[:, :, :], in0=St[:, :, :], scalar1=fl, scalar2=0.0,
                                                  op0=ALU.mult, op1=ALU.add), reads=[St, K.flags], writes=[St])
            S.op("act", lambda e: e.copy(out=Sb[:, :, :], in_=St[:, :, :]), reads=[St], writes=[Sb])
        T = Tn.pop(c)
        nx = c + 1 if n == 0 else c - 1
        if 0 <= nx < nchunk:
            Tn[nx] = rwkv_prep(K, SC, C, P, nx, nchunk, cps, n, want_g=(n == 0))
        if T is None or RL <= 9:
            return
        I_ = rwkv_intra(K, P, T, n)
        if RL <= 10:
            return
        py = rwkv_seq(K, P, T, I_, St, Sb)
        if RL <= 11:
            return
        if n == 1:
            yb = ybp.get()
            S.op("act", lambda e: e.copy(out=yb[:, 0:512], in_=py[:, :]), reads=[py], writes=[yb])
            S.op("dve", lambda e: e.tensor_copy(out=yb[:, 512:520], in_=T["rk"][:, :]), reads=[T["rk"]], writes=[yb])
            S.dma("pool", SC["YB"][t0:t0 + 128, :], yb[:, :], reads=[yb])
            return
        y1 = y1p.get()
        S.dma("sp", y1[:, :], SC["YB"][t0:t0 + 128, :], writes=[y1])
        y = yw.get()
        yv = y[:, :, :].rearrange("p a b -> p (a b)")
        S.op("dve", lambda e: e.tensor_tensor(out=yv, in0=py[:, :], in1=y1[:, 0:512], op=ALU.add),
             reads=[py, y1], writes=[y])
        sm = smp.get()
        S.op("dve", lambda e: e.tensor_reduce(out=sm[:, 0:8], in_=y[:, :, :], axis=AX.X, op=ALU.add),
             reads=[y], writes=[sm])
        S.op("dve", lambda e: e.tensor_scalar(out=sm[:, 0:8], in0=sm[:, 0:8], scalar1=1.0 / 64, scalar2=0.0,
                                              op0=ALU.mult, op1=ALU.add), reads=[sm], writes=[sm])
        ycn = yc.get()
        S.op("dve", lambda e: e.tensor_tensor(out=ycn[:, :, :], in0=y[:, :, :],
                                              in1=sm[:, 0:8].unsqueeze(2).broadcast_to([128, 8, 64]), op=ALU.subtract),
             reads=[y, sm], writes=[ycn])
        S.op("pool", lambda e: e.tensor_tensor(out=y[:, :, :], in0=ycn[:, :, :], in1=ycn[:, :, :], op=ALU.mult),
             reads=[ycn], writes=[y])
        S.op("dve", lambda e: e.tensor_reduce(out=sm[:, 8:16], in_=y[:, :, :], axis=AX.X, op=ALU.add),
             reads=[y], writes=[sm])
        S.op("act", lambda e: e.activation(out=sm[:, 8:16], in_=sm[:, 8:16], func=AF.Sqrt, bias=C["gneps"][:, 0:1],
                                           scale=1.0 / 64), reads=[sm, C["gneps"]], writes=[sm])
        S.op("dve", lambda e: e.reciprocal(out=sm[:, 8:16], in_=sm[:, 8:16]), reads=[sm], writes=[sm])
        S.op("dve", lambda e: e.tensor_tensor(out=ycn[:, :, :], in0=ycn[:, :, :],
                                              in1=sm[:, 8:16].unsqueeze(2).broadcast_to([128, 8, 64]), op=ALU.mult),
             reads=[ycn, sm], writes=[ycn])
        ycv = ycn[:, :, :].rearrange("p a b -> p (a b)")
        S.op("pool", lambda e: e.tensor_tensor(out=ycv, in0=ycv, in1=C["lng"][:, :], op=ALU.mult),
             reads=[ycn, C["lng"]], writes=[ycn])
        S.op("pool", lambda e: e.tensor_tensor(out=ycv, in0=ycv, in1=C["lnb"][:, :], op=ALU.add),
             reads=[ycn, C["lnb"]], writes=[ycn])
        S.op("dve", lambda e: e.tensor_tensor(out=sm[:, 16:24], in0=T["rk"][:, :], in1=y1[:, 512:520], op=ALU.add),
             reads=[T["rk"], y1], writes=[sm])
        S.op("dve", lambda e: e.tensor_tensor(out=y[:, :, :], in0=T["Vf"][:, :].rearrange("p (a b) -> p a b", a=8),
                                              in1=sm[:, 16:24].unsqueeze(2).broadcast_to([128, 8, 64]), op=ALU.mult),
             reads=[T["Vf"], sm, y], writes=[y])
        S.op("pool", lambda e: e.tensor_tensor(out=ycv, in0=ycv, in1=yv, op=ALU.add), reads=[ycn, y], writes=[ycn])
        o = outp.get()
        S.op("dve", lambda e: e.tensor_tensor(out=o[:, :], in0=ycv, in1=T["g"][:, :], op=ALU.mult),
             reads=[ycn, T["g"]], writes=[o])
        S.dma("pool", SC["YR"][t0:t0 + 128, :], o[:, :], reads=[o])

    order = list(range(nchunk)) if n == 0 else list(range(nchunk - 1, -1, -1))
    Tn = {order[0]: rwkv_prep(K, SC, C, P, order[0], nchunk, cps, n, want_g=(n == 0))}
    for c in order:
        body(c)


def cast_weights(K, src, dst, rows, cols):
    for r in range(0, rows, 128):
        K.S.dma("pool", dst[r:r + 128, :], src[r:r + 128, :])


def zero_fill(K, dst, rows, cols, dt):
    z = K.alloc([128, cols], dt)
    K.S.op("pool", lambda e: e.memset(z[:, :], 0.0), writes=[z])
    for r in range(0, rows, 128):
        K.S.dma("pool", dst[r:r + 128, :], z[:, :], reads=[z])


def build(ntok, seglen=4096, depth=DEPTH, mixer=True):
    nc = bass.Bass("TRN2", target_bir_lowering=False)
    es = ExitStack()
    A = {}
    nseg = ntok // seglen
    plan, pats = attn_plan(nseg, seglen)
    ncfg = pats[False].shape[0]

    def inp(name, shape, dt=F32):
        A[name] = nc.dram_tensor(name, list(shape), dt, kind="ExternalInput").ap()
        return A[name]

    import os
    dbg = os.environ.get("MK_DBG", "").split(",")

    def scr(name, shape, dt=F32):
        if name in dbg:
            return nc.dram_tensor(name, list(shape), dt, kind="ExternalOutput").ap()
        return nc.dram_tensor(name, list(shape), dt).ap()

    x = inp("x", [ntok, D])
    inp("norm_g", [DEPTH, 6, D])
    inp("ff_w_in", [DEPTH, 2, D, 2 * FF])
    inp("ff_w_out", [DEPTH, 2, FF, D])
    inp("w_in", [DEPTH, D, IN_W])
    inp("w_branch", [DEPTH, 2048, D])
    inp("w_out", [DEPTH, D, D])
    inp("atab", [DEPTH, ncfg, 128, 8, 128])
    inp("ssm_conv_w", [DEPTH, 5, 1536])
    inp("ssm_conv_b", [DEPTH, 1536])
    inp("ssm_dt_bias", [DEPTH, 2, 16])
    inp("ssm_a_log", [DEPTH, 2, 16])
    inp("ssm_d", [DEPTH, 2, 16])
    inp("ssm_norm_g", [DEPTH, 1024])
    inp("flags", [128, 4])
    inp("cf32", [128, 7, 128])
    inp("rwkv_mu", [DEPTH, 2, 1952])
    inp("rwkv_w0", [DEPTH, 2, 512])
    inp("rwkv_w_up", [DEPTH, 2, 64, 512])
    inp("rwkv_a0", [DEPTH, 2, 512])
    inp("rwkv_a_up", [DEPTH, 2, 64, 512])
    inp("rwkv_g_up", [DEPTH, 160, 512])
    inp("rwkv_k_k", [DEPTH, 512])
    inp("rwkv_k_a", [DEPTH, 512])
    inp("rwkv_r_k", [DEPTH, 8, 64])
    inp("rwkv_ln_g", [DEPTH, 512])
    inp("rwkv_ln_b", [DEPTH, 512])
    inp("ident", [128, 128], BF16)
    y = nc.dram_tensor("y", [ntok, D], F32, kind="ExternalOutput").ap()
    xs = scr("xs", [ntok, D])
    wi_bf = [[scr(f"wi{l}{f}", [D, 2 * FF], BF16) for f in range(2)] for l in range(DEPTH)]
    wo_bf = [[scr(f"wo{l}{f}", [FF, D], BF16) for f in range(2)] for l in range(DEPTH)]
    win_bf = [scr(f"win{l}", [D, IN_W], BF16) for l in range(DEPTH)]
    wbr_bf = [scr(f"wbr{l}", [2048, D], BF16) for l in range(DEPTH)]
    wout_bf = [scr(f"wout{l}", [D, D], BF16) for l in range(DEPTH)]
    SC = {"QT": scr("QT", [512, ntok], BF16), "KT": scr("KT", [512, ntok], BF16), "V": scr("V", [ntok, 512], BF16),
          "Z": scr("Z", [ntok, 1024]), "DT": scr("DT", [ntok, 32]), "G": scr("G", [ntok, 3072]),
          "XBCT": scr("XBCT", [1536, ntok]), "RWT": scr("RWT", [1952, ntok]),
          "HB": scr("HB", [ntok // 128, 128, 1024], BF16), "YB": scr("YB", [ntok, 520]), "RWS": scr("RWS", [1952, ntok]),
          "PX": scr("PX", [ntok // 128, 128, 12, 128], BF16), "PS": scr("PS", [ntok // 128, 128, 1024], BF16),
          "PB": scr("PB", [ntok // 128, 128, 256], BF16), "PV": scr("PV", [ntok // 128, 128, 256]),
          "YA": scr("YA", [ntok, 512], BF16), "YS": scr("YS", [ntok, 1024], BF16), "YR": scr("YR", [ntok, 512], BF16)}
    if "DY" in dbg:
        SC["DY"] = scr("DY", [ntok, 1024]); SC["DV"] = scr("DV", [ntok, 256]); SC["DM"] = scr("DM", [ntok // 128, 128, 16, 128], BF16)
    K = Ctx(nc, es)
    S = K.S
    K.ident = K.alloc([128, 128], BF16, keep=True)
    S.dma("sp", K.ident[:, :], A["ident"][:, :], writes=[K.ident])
    cf = K.alloc([128, 7, 128], F32, keep=True)
    S.dma("sp", cf[:, :, :], A["cf32"][:, :, :], writes=[cf])
    K.identf, K.uinc, K.uexc, K.onesf, K.mskf, K.mskb, K.ugt = [Tl(cf.t[:, i, :], cf.res) for i in range(7)]
    K.flags = K.alloc([128, 4], F32, keep=True)
    S.dma("sp", K.flags[:, :], A["flags"][:, :], writes=[K.flags])
    K.onec = K.alloc([128, 4], F32, keep=True)
    S.op("dve", lambda e: e.memset(K.onec[:, :], 1.0), writes=[K.onec])
    K.epsc = K.alloc([128, 4], F32, keep=True)
    S.op("dve", lambda e: e.memset(K.epsc[:, 0:1], EPS), writes=[K.epsc])
    S.op("dve", lambda e: e.memset(K.epsc[:, 1:2], 4.0 * EPS), writes=[K.epsc])
    jobs = []
    for l in range(depth):
        jobs.append(lambda l=l: (cast_weights(K, A["ff_w_in"][l, 0], wi_bf[l][0], D, 2 * FF),
                                 cast_weights(K, A["ff_w_out"][l, 0], wo_bf[l][0], FF, D)))
        if mixer:
            jobs.append(lambda l=l: (cast_weights(K, A["w_in"][l], win_bf[l], D, IN_W),
                                     cast_weights(K, A["w_branch"][l], wbr_bf[l], 2048, D),
                                     cast_weights(K, A["w_out"][l], wout_bf[l], D, D)))
        jobs.append(lambda l=l: (cast_weights(K, A["ff_w_in"][l, 1], wi_bf[l][1], D, 2 * FF),
                                 cast_weights(K, A["ff_w_out"][l, 1], wo_bf[l][1], FF, D)))
    jobs.pop(0)()
    K.cast_jobs = jobs
    if mixer:
        zero_fill(K, SC["YS"], ntok, 1024, BF16)
        zero_fill(K, SC["YR"], ntok, 512, BF16)
    cur = x
    for l in range(depth):
        g = A["norm_g"][l]
        ffn_pass(K, cur, xs, g[0:1, :], g[1:2, :], wi_bf[l][0], wo_bf[l][0], ntok)
        cur = xs
        if mixer:
            import os
            st = os.environ.get("MK_STAGES", "iasrm")
            if "i" in st:
                inproj_pass(K, xs, g[2:3, :], win_bf[l], SC, ntok)
            if "a" in st:
                attn_pass(K, SC, A["atab"][l], plan)
            if "s" in st:
                K.new_pass()
                Cs = ssd_setup(K, A, l)
                ssd_bwd_pass(K, SC, Cs, ntok // 128, seglen // 128)
                K.new_pass()
                Cs = ssd_setup(K, A, l)
                ssd_fwd_pass(K, SC, Cs, ntok // 128, seglen // 128)
            if "r" in st:
                shift_pass(K, SC, A, l, ntok, seglen)
                for n_ in (1, 0):
                    K.new_pass()
                    Cr = rwkv_setup(K, A, l)
                    rwkv_dir_pass(K, SC, Cr, ntok // 128, seglen // 128, n_)
            if "m" in st:
                merge_pass(K, SC, xs, g[3:4, :], wbr_bf[l], wout_bf[l], ntok)
        last = (l == depth - 1)
        ffn_pass(K, cur, y if last else xs, g[4:5, :], g[5:6, :], wi_bf[l][1], wo_bf[l][1], ntok)
    S.barrier()
    S.emit()
    es.close()
    return nc


def consts():
    i = np.arange(128)
    s_, l_ = i[:, None], i[None, :]
    cf = np.stack([np.eye(128), s_ <= l_, s_ < l_, np.ones((128, 128)), s_ <= l_, s_ >= l_, s_ > l_]).astype(np.float32)
    return {"ident": np.eye(128, dtype=np.float32).astype(ml_dtypes.bfloat16),
            "cf32": np.ascontiguousarray(cf.transpose(1, 0, 2))}


def flags_for(link):
    f = np.zeros((128, 4), np.float32)
    f[:, 0] = 1.0 if link else 0.0
    return f


NTOK = 12288
_NC_CACHE = {}


def _assign():
    segs = []
    for c in range(4):
        segs.append([("s", c, 0), ("s", c, 1), ("p", c, 0)])
    for c in range(4, 8):
        b = 4 + (c - 4) * 3
        segs.append([("p", b, 0), ("p", b + 1, 0), ("p", b + 2, 0)])
    return segs


def kernel(**inputs):
    xp = np.asarray(inputs["x_prompt"], dtype=np.float32)
    xs = np.asarray(inputs["x_sample"], dtype=np.float32)
    segs = _assign()
    if "nc" not in _NC_CACHE:
        _NC_CACHE["nc"] = build(NTOK, seglen=4096)
    nc = _NC_CACHE["nc"]
    shared = {k: np.ascontiguousarray(np.asarray(inputs[k], dtype=np.float32))
              for k in ("norm_g", "ff_w_in", "ff_w_out", "w_in", "w_branch", "w_out", "ssm_conv_w", "ssm_conv_b",
                        "ssm_dt_bias", "ssm_a_log", "ssm_d", "ssm_norm_g", "rwkv_mu", "rwkv_w0", "rwkv_w_up",
                        "rwkv_a0", "rwkv_a_up", "rwkv_g_up", "rwkv_k_k", "rwkv_k_a", "rwkv_r_k", "rwkv_ln_g",
                        "rwkv_ln_b")}
    shared.update(consts())
    plan, pats = attn_plan(3, 4096)
    rpb = np.asarray(inputs["attn_rpb"], dtype=np.float32)
    tabs = {link: attn_tables(rpb, pats[link]) for link in (False, True)}
    in_maps = []
    for c in range(8):
        parts = []
        for kind, b, h in segs[c]:
            parts.append(xs[b, h * 4096:(h + 1) * 4096] if kind == "s" else xp[b])
        m = {"x": np.ascontiguousarray(np.concatenate(parts, axis=0)), "atab": tabs[c < 4],
             "flags": flags_for(c < 4)}
        m.update(shared)
        in_maps.append(m)
    res = run_bass_kernel_spmd(nc, in_maps, core_ids=list(range(8)))
    yp = np.empty_like(xp)
    ys = np.empty_like(xs)
    for c in range(8):
        y = res.results[c]["y"]
        for i, (kind, b, h) in enumerate(segs[c]):
            blk = y[i * 4096:(i + 1) * 4096]
            if kind == "s":
                ys[b, h * 4096:(h + 1) * 4096] = blk
            else:
                yp[b] = blk
    return yp, ys
```
